# Optimizing a Trainium2 kernel written in Bass

```python
import jax, jax.numpy as jnp
from jax import lax
import numpy as np

D_MODEL = 1024
BATCH = 4
SEQ = 8192
DEPTH = 2

GROUP_WIDTH = D_MODEL // 2
D_MIX = 3 * GROUP_WIDTH
BLOCK = 128
RMS_EPS = 1e-6
NEG_INF = -1e30

MLA_HEADS = 8
MLA_NOPE_DIM = 64
MLA_ROPE_DIM = 32
MLA_QK_DIM = MLA_NOPE_DIM + MLA_ROPE_DIM
MLA_V_DIM = 64
MLA_Q_LORA = 256
MLA_KV_LORA = 128
ROPE_THETA = 10000.0

CONV_DIM = GROUP_WIDTH
CONV_WIDTH = 3

SWA_HEADS = 8
SWA_KV_HEADS = 2
SWA_GROUP = SWA_HEADS // SWA_KV_HEADS
SWA_HEAD_DIM = 64
SWA_WINDOW = 128

IN_SPLITS = (
    MLA_Q_LORA, MLA_KV_LORA, MLA_ROPE_DIM, GROUP_WIDTH,
    CONV_DIM, CONV_DIM, CONV_DIM, GROUP_WIDTH,
    SWA_HEADS * SWA_HEAD_DIM, SWA_KV_HEADS * SWA_HEAD_DIM,
    SWA_KV_HEADS * SWA_HEAD_DIM, GROUP_WIDTH,
)
IN_COLS = sum(IN_SPLITS)

kernel_name = "hymba_mla_shortconv_swa_hybrid"


def rms_norm(x, g):
    xf = x.astype(jnp.float32)
    y = xf * lax.rsqrt(jnp.mean(xf * xf, axis=-1, keepdims=True) + RMS_EPS)
    return (y * g.astype(jnp.float32)).astype(x.dtype)


def apply_rope(x, pos):
    half = x.shape[-1] // 2
    inv_freq = jnp.power(jnp.float32(ROPE_THETA), -jnp.arange(half, dtype=jnp.float32) / half)
    ang = pos[:, None] * inv_freq[None, :]
    cos = jnp.cos(ang)[:, None, :]
    sin = jnp.sin(ang)[:, None, :]
    xf = x.astype(jnp.float32)
    x1, x2 = xf[..., :half], xf[..., half:]
    out = jnp.concatenate([x1 * cos - x2 * sin, x2 * cos + x1 * sin], axis=-1)
    return out.astype(x.dtype)


def mla_mixer(q_lat, kv_lat, k_rope, q_a_norm, w_qb, kv_a_norm, w_kvb, q_norm, k_norm):
    b, s, _ = q_lat.shape
    q = (rms_norm(q_lat, q_a_norm) @ w_qb).reshape(b, s, MLA_HEADS, MLA_QK_DIM)
    kv = (rms_norm(kv_lat, kv_a_norm) @ w_kvb).reshape(b, s, MLA_HEADS, MLA_NOPE_DIM + MLA_V_DIM)
    k_nope, v = kv[..., :MLA_NOPE_DIM], kv[..., MLA_NOPE_DIM:]
    k_pe = jnp.broadcast_to(k_rope[:, :, None, :], (b, s, MLA_HEADS, MLA_ROPE_DIM))
    k = jnp.concatenate([k_nope, k_pe], axis=-1)
    q = rms_norm(q, q_norm)
    k = rms_norm(k, k_norm)
    pos = jnp.arange(s, dtype=jnp.float32)
    q = jnp.concatenate([q[..., :MLA_NOPE_DIM], apply_rope(q[..., MLA_NOPE_DIM:], pos)], axis=-1)
    k = jnp.concatenate([k[..., :MLA_NOPE_DIM], apply_rope(k[..., MLA_NOPE_DIM:], pos)], axis=-1)
    scale = MLA_QK_DIM ** -0.5
    nb = s // BLOCK
    q_blocks = q.reshape(b, nb, BLOCK, MLA_HEADS, MLA_QK_DIM).transpose(1, 0, 2, 3, 4)
    starts = jnp.arange(nb, dtype=jnp.int32) * BLOCK
    key_pos = jnp.arange(s, dtype=jnp.int32)

    def attend_block(args):
        qb, start = args
        sc = jnp.einsum('bqhd,bkhd->bhqk', qb, k, preferred_element_type=jnp.float32) * scale
        q_pos = start + jnp.arange(BLOCK, dtype=jnp.int32)
        causal = key_pos[None, :] <= q_pos[:, None]
        sc = jnp.where(causal[None, None], sc, NEG_INF)
        p = jax.nn.softmax(sc, axis=-1).astype(v.dtype)
        return jnp.einsum('bhqk,bkhd->bqhd', p, v)

    o = lax.map(attend_block, (q_blocks, starts))
    return o.transpose(1, 0, 2, 3, 4).reshape(b, s, MLA_HEADS * MLA_V_DIM)


def short_conv_mixer(h, b_gate, c_gate, conv_w):
    u = c_gate * h
    y = lax.conv_general_dilated(
        u, conv_w[:, None, :].astype(u.dtype), window_strides=(1,),
        padding=[(CONV_WIDTH - 1, 0)], dimension_numbers=('NWC', 'WIO', 'NWC'),
        feature_group_count=CONV_DIM)
    return b_gate * y


def _band(t):
    b, s, kv, d = t.shape
    nb = s // BLOCK
    tp = jnp.concatenate([jnp.zeros((b, BLOCK, kv, d), t.dtype), t], axis=1)
    tp = tp.reshape(b, nb + 1, BLOCK, kv, d)
    return jnp.concatenate([tp[:, :-1], tp[:, 1:]], axis=2)


def swa_mixer(q, k, v, q_norm, k_norm, sinks):
    b, s, _ = q.shape
    nb = s // BLOCK
    q = rms_norm(q.reshape(b, s, SWA_HEADS, SWA_HEAD_DIM), q_norm)
    k = rms_norm(k.reshape(b, s, SWA_KV_HEADS, SWA_HEAD_DIM), k_norm)
    v = v.reshape(b, s, SWA_KV_HEADS, SWA_HEAD_DIM)
    qb = q.reshape(b, nb, BLOCK, SWA_KV_HEADS, SWA_GROUP, SWA_HEAD_DIM)
    kb, vb = _band(k), _band(v)
    sc = jnp.einsum('bnqkgd,bnskd->bnkgqs', qb, kb,
                    preferred_element_type=jnp.float32) * (SWA_HEAD_DIM ** -0.5)
    q_idx = jnp.arange(BLOCK, dtype=jnp.int32)[:, None]
    k_idx = jnp.arange(2 * BLOCK, dtype=jnp.int32)[None, :]
    dist = BLOCK + q_idx - k_idx
    key_pos = (jnp.arange(nb, dtype=jnp.int32)[:, None] - 1) * BLOCK + k_idx
    valid = ((dist >= 0) & (dist < SWA_WINDOW))[None] & (key_pos >= 0)[:, None, :]
    slopes = jnp.exp2(-8.0 * jnp.arange(1, SWA_HEADS + 1, dtype=jnp.float32) / SWA_HEADS)
    slopes = slopes.reshape(SWA_KV_HEADS, SWA_GROUP)
    sc = sc - slopes[None, None, :, :, None, None] * dist.astype(jnp.float32)[None, None, None, None]
    sc = jnp.where(valid[None, :, None, None], sc, NEG_INF)
    sink = jnp.broadcast_to(
        sinks.astype(jnp.float32).reshape(SWA_KV_HEADS, SWA_GROUP)[None, None, :, :, None, None],
        sc.shape[:-1] + (1,))
    p = jax.nn.softmax(jnp.concatenate([sc, sink], axis=-1), axis=-1)[..., :-1].astype(v.dtype)
    o = jnp.einsum('bnkgqs,bnskd->bnqkgd', p, vb)
    return o.reshape(b, s, SWA_HEADS * SWA_HEAD_DIM)


def hybrid_layer(x, norm_g, w_in, mla_q_a_norm, mla_w_qb, mla_kv_a_norm, mla_w_kvb,
                 mla_q_norm, mla_k_norm, conv_w, swa_q_norm, swa_k_norm, swa_sinks, w_out):
    h = rms_norm(x, norm_g)
    proj = h @ w_in
    offsets = [int(o) for o in np.cumsum(IN_SPLITS)[:-1]]
    (q_lat, kv_lat, k_rope, g_mla,
     c_h, c_b, c_c, g_conv,
     s_q, s_k, s_v, g_swa) = jnp.split(proj, offsets, axis=-1)
    y_mla = mla_mixer(q_lat, kv_lat, k_rope, mla_q_a_norm, mla_w_qb, mla_kv_a_norm,
                      mla_w_kvb, mla_q_norm, mla_k_norm) * jax.nn.silu(g_mla)
    y_conv = short_conv_mixer(c_h, c_b, c_c, conv_w) * jax.nn.silu(g_conv)
    y_swa = swa_mixer(s_q, s_k, s_v, swa_q_norm, swa_k_norm, swa_sinks) * jax.nn.silu(g_swa)
    y = jnp.concatenate([y_mla, y_conv, y_swa], axis=-1) @ w_out
    return x + y


def setup_inputs(seed: int = 0) -> dict:
    key = jax.random.key(seed)
    ks = jax.random.split(key, 14)
    f32 = jnp.float32

    def nrm(k, shape, scale):
        return jax.random.normal(k, shape, f32) * scale

    def gain(k, n):
        return 1.0 + 0.02 * jax.random.normal(k, (DEPTH, n), f32)

    return {
        "x": jax.random.normal(ks[0], (BATCH, SEQ, D_MODEL), f32),
        "norm_g": gain(ks[1], D_MODEL),
        "w_in": nrm(ks[2], (DEPTH, D_MODEL, IN_COLS), D_MODEL ** -0.5),
        "mla_q_a_norm": gain(ks[3], MLA_Q_LORA),
        "mla_w_qb": nrm(ks[4], (DEPTH, MLA_Q_LORA, MLA_HEADS * MLA_QK_DIM), MLA_Q_LORA ** -0.5),
        "mla_kv_a_norm": gain(ks[5], MLA_KV_LORA),
        "mla_w_kvb": nrm(ks[6], (DEPTH, MLA_KV_LORA, MLA_HEADS * (MLA_NOPE_DIM + MLA_V_DIM)), MLA_KV_LORA ** -0.5),
        "mla_q_norm": gain(ks[7], MLA_QK_DIM),
        "mla_k_norm": gain(ks[8], MLA_QK_DIM),
        "conv_w": nrm(ks[9], (DEPTH, CONV_WIDTH, CONV_DIM), CONV_WIDTH ** -0.5),
        "swa_q_norm": gain(ks[10], SWA_HEAD_DIM),
        "swa_k_norm": gain(ks[11], SWA_HEAD_DIM),
        "swa_sinks": nrm(ks[12], (DEPTH, SWA_HEADS), 0.5),
        "w_out": nrm(ks[13], (DEPTH, D_MIX, D_MODEL), D_MIX ** -0.5),
    }


def reference(x, norm_g, w_in, mla_q_a_norm, mla_w_qb, mla_kv_a_norm, mla_w_kvb,
              mla_q_norm, mla_k_norm, conv_w, swa_q_norm, swa_k_norm, swa_sinks, w_out):
    for l in range(DEPTH):
        x = hybrid_layer(x, norm_g[l], w_in[l], mla_q_a_norm[l], mla_w_qb[l], mla_kv_a_norm[l],
                         mla_w_kvb[l], mla_q_norm[l], mla_k_norm[l], conv_w[l], swa_q_norm[l],
                         swa_k_norm[l], swa_sinks[l], w_out[l])
    return x
```

```python
import numpy as np
import ml_dtypes
from contextlib import ExitStack
import concourse.bass as bass
import concourse.mybir as mybir
from concourse.bass_utils import run_bass_kernel_spmd

F32 = mybir.dt.float32
BF16 = mybir.dt.bfloat16
ALU = mybir.AluOpType
AF = mybir.ActivationFunctionType
AX = mybir.AxisListType

NB = 32
NS = 64
D = 1024
NCOL = 4256
EPS = 1e-6
C_KVL, C_KR, C_SK, C_SV, C_QL, C_SQ, C_GM, C_GS, C_CH, C_CC, C_CB, C_GC = (
    0, 128, 160, 288, 416, 672, 1184, 1696, 2208, 2720, 3232, 3744)
PERM = np.concatenate([np.arange(256, 384), np.arange(384, 416), np.arange(3488, 3616), np.arange(3616, 3744),
                       np.arange(0, 256), np.arange(2976, 3488), np.arange(416, 928), np.arange(3744, 4256),
                       np.arange(928, 1440), np.arange(1952, 2464), np.arange(1440, 1952), np.arange(2464, 2976)])

COMPUTE = ("pe", "act", "dve", "pool")
ALLENG = COMPUTE + ("sp",)


class Buf:
    __slots__ = ("name", "lw", "rd")

    def __init__(self, name):
        self.name = name
        self.lw = None
        self.rd = []


class Op:
    __slots__ = ("eng", "fn", "waits", "signal", "dma", "slot", "slot_cnt")

    def __init__(self, eng, fn, dma=False, slot=None):
        self.eng = eng
        self.fn = fn
        self.waits = []
        self.signal = False
        self.dma = dma
        self.slot = slot
        self.slot_cnt = 0


class Prog:
    def __init__(self, nc):
        self.nc = nc
        self.ops = {e: [] for e in ALLENG}
        self.seen = {e: {} for e in ALLENG}
        self.pending = {e: [] for e in ALLENG}
        self.slot_counts = {}
        self.stack = ExitStack()
        self.nbuf = 0

    def sbuf(self, name, shape, dtype):
        return self.stack.enter_context(self.nc.sbuf_tensor("sb_" + name, list(shape), dtype))

    def psum(self, name, shape, dtype):
        return self.stack.enter_context(self.nc.psum_tensor("ps_" + name, list(shape), dtype))

    def buf(self, name=None):
        self.nbuf += 1
        return Buf(name or f"b{self.nbuf}")

    def _dep(self, op, eng, key):
        kind, k, v = key
        if kind == "eng" and k == "pe" and eng == "pe":
            return
        seen = self.seen[eng]
        if seen.get((kind, k), -1) >= v:
            return
        seen[(kind, k)] = v
        op.waits.append(key)
        if kind == "eng":
            self.ops[k][v].signal = True

    limit = None
    total = 0
    trace = None

    def add(self, eng, fn, reads=(), writes=(), dma=False, slot=None):
        self.total += 1
        if self.limit is not None and self.total > self.limit:
            return None
        op = Op(eng, fn, dma=dma, slot=slot)
        if self.trace is not None:
            import sys as _s
            f = _s._getframe(1)
            while f is not None and f.f_code.co_name != "build_program":
                f = f.f_back
            self.trace.append((self.total, eng, f.f_lineno if f else -1))
        idx = len(self.ops[eng])
        for key in self.pending[eng]:
            self._dep(op, eng, key)
        self.pending[eng] = []
        if dma:
            cnt = self.slot_counts.get(slot, 0)
            if cnt > 0:
                self._dep(op, eng, ("slot", slot, cnt))
            self.slot_counts[slot] = cnt + 1
            op.slot_cnt = cnt + 1
            me = ("slot", slot, cnt + 1)
        else:
            me = ("eng", eng, idx)
        for b in reads:
            if b.lw is not None:
                self._dep(op, eng, b.lw)
        for b in writes:
            if b.lw is not None:
                self._dep(op, eng, b.lw)
            for r in b.rd:
                self._dep(op, eng, r)
        for b in reads:
            b.rd.append(me)
        for b in writes:
            b.lw = me
            b.rd = []
        self.ops[eng].append(op)
        return op

    def barrier(self):
        keys = []
        for e in ALLENG:
            for i in range(len(self.ops[e]) - 1, -1, -1):
                if not self.ops[e][i].dma:
                    keys.append(("eng", e, i))
                    break
        for s, c in self.slot_counts.items():
            keys.append(("slot", s, c))
        for e in ALLENG:
            self.pending[e] = self.pending[e] + [k for k in keys if not (k[0] == "eng" and k[1] == e)]

    def dma(self, out, in_, reads=(), writes=(), slot="d0", eng="sp"):
        return self.add(eng, lambda e: e.dma_start(out=out, in_=in_), reads, writes, dma=True, slot=slot)

    def emit(self, final_slots=()):
        nc = self.nc
        st = self.stack
        esem = {e: st.enter_context(nc.semaphore(f"s_{e}")) for e in ALLENG}
        ssem = {s: st.enter_context(nc.semaphore(f"d_{s}")) for s in self.slot_counts}
        cnt = {}
        for e, lst in self.ops.items():
            c = 0
            for i, op in enumerate(lst):
                if op.signal and not op.dma:
                    c += 1
                cnt[(e, i)] = c

        def run(engname, handle):
            for op in self.ops[engname]:
                for kind, k, v in op.waits:
                    if kind == "eng":
                        handle.wait_ge(esem[k], cnt[(k, v)])
                    else:
                        handle.wait_ge(ssem[k], 16 * v)
                ins = op.fn(handle)
                if op.dma:
                    ins.then_inc(ssem[op.slot], 16)
                elif op.signal:
                    ins.then_inc(esem[engname], 1)
            if engname == "sp":
                for s in final_slots:
                    handle.wait_ge(ssem[s], 16 * self.slot_counts[s])

        block = st.enter_context(nc.Block())

        @block.sync
        def _(e):
            run("sp", e)

        @block.tensor
        def _(e):
            run("pe", e)

        @block.scalar
        def _(e):
            run("act", e)

        @block.vector
        def _(e):
            run("dve", e)

        @block.gpsimd
        def _(e):
            run("pool", e)

    def close(self):
        self.stack.close()


WNAMES = ["w_in", "norm_g", "w_kvb", "kva_g", "w_qb", "qa_g", "q_g", "k_g", "conv_w", "sq_g", "sk_g",
          "sinks", "w_out"]
WSHAPES = {"w_in": [D, NCOL], "norm_g": [128, 8], "w_kvb": [128, 1024], "kva_g": [128, 1], "w_qb": [256, 768],
           "qa_g": [128, 2], "q_g": [96], "k_g": [96], "conv_w": [128, 3, 4], "sq_g": [64], "sk_g": [64],
           "sinks": [8], "w_out": [1536, 1024]}


class _Stop(Exception):
    pass


def build_program(nlayers=1, dbg=None):
    nc = bass.Bass("TRN2", target_bir_lowering=False)
    P = Prog(nc)
    import os
    if os.environ.get("K_LIMIT"):
        P.limit = int(os.environ["K_LIMIT"])

    def dram_in(name, shape, dt=F32):
        return nc.dram_tensor(name, list(shape), dt, kind="ExternalInput").ap()

    x_own = dram_in("x_own", [NB, 128, D])
    x_prev = dram_in("x_prev", [NB, 128, D])
    Wd = [{n: dram_in(f"{n}_{l}", WSHAPES[n]) for n in WNAMES} for l in range(nlayers)]
    cosT_d = dram_in("cosT", [128, NS, 32])
    sinT_d = dram_in("sinT", [128, NS, 32])
    maskM_d = dram_in("maskM", [128, 8, 512], BF16)
    swaM_d = dram_in("swaM", [128, 2, 8, 128])
    vvalid_d = dram_in("vvalid", [128, NS])
    y_out = nc.dram_tensor("y_out", [NB, 128, D], F32, kind="ExternalOutput").ap()

    skind = "ExternalOutput"
    kT_scr = nc.dram_tensor("kT_scr", [8, 96, NS * 128], BF16, kind=skind).ap()
    v_scr = nc.dram_tensor("v_scr", [NS, 128, 8, 128], BF16, kind=skind).ap()
    qT_scr = nc.dram_tensor("qT_scr", [8, 96, NB * 128], BF16, kind=skind).ap()
    sg_scr = nc.dram_tensor("sg_scr", [4, 128, NB * 128], BF16, kind=skind).ap()
    part_scr = nc.dram_tensor("part_scr", [NB, 128, D], F32, kind=skind).ap()

    ident = P.sbuf("ident", [128, 128], BF16); B_ident = P.buf("ident")
    idf = P.sbuf("idf", [128, 128], F32)
    w_kvb = P.sbuf("w_kvb", [128, 1024], BF16)
    w_qb = P.sbuf("w_qb", [128, 2, 768], BF16)
    w_out = P.sbuf("w_out", [128, 12, 1024], BF16)
    swaM = P.sbuf("swaM", [128, 2, 8, 128], F32)
    vvalid = P.sbuf("vvalid", [128, NS], F32)
    gq_b = P.sbuf("gq_b", [128, 96], F32)
    gk_b = P.sbuf("gk_b", [128, 96], F32)
    gsq_b = P.sbuf("gsq_b", [128, 64], F32)
    gsk_b = P.sbuf("gsk_b", [128, 64], F32)
    esink = P.sbuf("esink", [128, 8], F32)
    normg = P.sbuf("normg", [128, 8], F32)
    convw = P.sbuf("convw", [128, 3, 4], F32)
    kvag = P.sbuf("kvag", [128, 1], F32)
    qag = P.sbuf("qag", [128, 2], F32)
    eps_t = P.sbuf("eps_t", [128, 1], F32)
    B_W = P.buf("weights")
    B_C = P.buf("consts")

    ARENA_B = 150 * 1024
    arena = P.sbuf("arena", [128, ARENA_B // 2], BF16)

    class Arena:
        def __init__(self):
            self.off = 0

        def alloc(self, shape, dt, parts=128):
            n = int(np.prod(shape[1:]))
            nbytes = n * (4 if dt == F32 else 2)
            nbytes = (nbytes + 31) // 32 * 32
            o = self.off
            self.off += nbytes
            assert self.off <= ARENA_B, f"arena overflow {self.off}"
            v = arena[:, o // 2:(o + nbytes) // 2]
            if dt == F32:
                v = v.bitcast(F32)
            v = v[:, 0:n]
            if len(shape) == 3:
                v = v.rearrange("p (a b) -> p a b", a=shape[1])
            elif len(shape) == 4:
                v = v.rearrange("p (a b c) -> p a b c", a=shape[1], b=shape[2])
            return v

    banks = [P.psum(f"bank{i}", [128, 512], F32) for i in range(3)]
    big = P.psum("big", [128, 1024], F32)
    banks += [P.psum(f"bank{i}", [128, 512], F32) for i in range(5, 8)]
    bk0, bk1, bk2, bk5, bk6, bk7 = banks
    B_bk = {id(b): P.buf(f"bk{i}") for i, b in enumerate(banks)}
    B_big = P.buf("big")

    def bf(ps):
        return ps[:].bitcast(BF16)

    def tt(eng, out, in0, in1, op, R, W):
        return P.add(eng, lambda e: e.tensor_tensor(out=out, in0=in0, in1=in1, op=op), R, W)

    def ts(eng, out, in0, s1, op0, R, W, s2=None, op1=None):
        if op1 is None:
            return P.add(eng, lambda e: e.tensor_scalar(out=out, in0=in0, scalar1=s1, scalar2=None, op0=op0), R, W)
        return P.add(eng, lambda e: e.tensor_scalar(out=out, in0=in0, scalar1=s1, scalar2=s2, op0=op0, op1=op1), R, W)

    def stt(eng, out, in0, scalar, in1, op0, op1, R, W):
        return P.add(eng, lambda e: e.scalar_tensor_tensor(out=out, in0=in0, scalar=scalar, in1=in1, op0=op0, op1=op1), R, W)

    def act(out, in_, func, R, W, scale=None, bias=None, accum=None):
        kw = {}
        if scale is not None:
            kw["scale"] = scale
        if bias is not None:
            kw["bias"] = bias
        if accum is not None:
            kw["accum_out"] = accum
        return P.add("act", lambda e: e.activation(out=out, in_=in_, func=func, **kw), R, W)

    def cp(eng, out, in_, R, W):
        if eng == "act":
            return P.add("act", lambda e: e.activation(out=out, in_=in_, func=AF.Copy), R, W)
        return P.add(eng, lambda e: e.tensor_copy(out=out, in_=in_), R, W)

    def red(eng, out, in_, R, W):
        return P.add(eng, lambda e: e.tensor_reduce(out=out, in_=in_, axis=AX.X, op=ALU.add), R, W)

    def rcp(out, in_, R, W):
        return P.add("dve", lambda e: e.reciprocal(out=out, in_=in_), R, W)

    def mm(out, lhsT, rhs, start, stop, R, W):
        return P.add("pe", lambda e: e.matmul(out, lhsT=lhsT, rhs=rhs, start=start, stop=stop), R, W)

    def tr(out, in_, R, W):
        return P.add("pe", lambda e: e.transpose(out=out, in_=in_, identity=ident[:]), list(R) + [B_ident], W)

    def rstd(ss_ap, out_ap, scale, R_ss, B_out, tmp_ap, B_tmp):
        act(tmp_ap, ss_ap, AF.Sqrt, [R_ss, B_C], [B_tmp], scale=scale, bias=eps_t[:, 0:1])
        rcp(out_ap, tmp_ap, [B_tmp], [B_out])

    P.add("pool", lambda e: e.memset(idf[:], 0.0), [], [B_ident])
    P.add("pool", lambda e: e.affine_select(out=idf[:], in_=idf[:], pattern=[[-1, 128]], compare_op=ALU.not_equal,
                                            fill=1.0, base=0, channel_multiplier=1), [B_ident], [B_ident])
    P.add("pool", lambda e: e.tensor_copy(out=ident[:], in_=idf[:]), [B_ident], [B_ident])
    P.add("pool", lambda e: e.memset(eps_t[:], EPS), [], [B_C])
    P.dma(swaM[:], swaM_d, writes=[B_C], slot="c0")
    P.dma(vvalid[:], vvalid_d, writes=[B_C], slot="c1")

    for l in range(nlayers):
      try:
          W = Wd[l]
          last = (l == nlayers - 1)
          P.barrier()
          ar = Arena()
          w_in = ar.alloc([128, 8, NCOL], BF16)
          stage = [ar.alloc([128, 2128], F32) for _ in range(2)]
          B_st = [P.buf("st0"), P.buf("st1")]
          mark = ar.off
          P.dma(normg[:], W["norm_g"], writes=[B_W], slot="c0")
          P.dma(convw[:], W["conv_w"], writes=[B_W], slot="c1")
          P.dma(kvag[:], W["kva_g"], writes=[B_W], slot="c0")
          P.dma(qag[:], W["qa_g"], writes=[B_W], slot="c1")
          P.dma(gq_b[:], W["q_g"].partition_broadcast(128), writes=[B_W], slot="c0")
          P.dma(gk_b[:], W["k_g"].partition_broadcast(128), writes=[B_W], slot="c1")
          P.dma(gsq_b[:], W["sq_g"].partition_broadcast(128), writes=[B_W], slot="c0")
          P.dma(gsk_b[:], W["sk_g"].partition_broadcast(128), writes=[B_W], slot="c1")
          P.dma(esink[:], W["sinks"].partition_broadcast(128), writes=[B_W], slot="c0")
          act(esink[:], esink[:], AF.Exp, [B_W], [B_W])
          n = 0
          engs = ["dve", "pool"]
          win_d = W["w_in"].rearrange("(kc p) c -> p kc c", p=128)
          for kc in range(8):
              for hf in range(2):
                  s = n % 2
                  P.dma(stage[s][:], win_d[:, kc, hf * 2128:(hf + 1) * 2128], writes=[B_st[s]], slot=f"st{s}")
                  ts(engs[n % 2], w_in[:, kc, hf * 2128:(hf + 1) * 2128], stage[s][:], normg[:, kc:kc + 1], ALU.mult,
                     [B_st[s], B_W], [B_W])
                  n += 1
          wout_d = W["w_out"].rearrange("(kc p) c -> p kc c", p=128)
          for kc in range(12):
              s = n % 2
              P.dma(stage[s][:, 0:1024], wout_d[:, kc, :], writes=[B_st[s]], slot=f"st{s}")
              cp(engs[n % 2], w_out[:, kc, :], stage[s][:, 0:1024], [B_st[s]], [B_W])
              n += 1
          s = n % 2
          P.dma(stage[s][:, 0:1024], W["w_kvb"], writes=[B_st[s]], slot=f"st{s}")
          ts(engs[n % 2], w_kvb[:], stage[s][:, 0:1024], kvag[:, 0:1], ALU.mult, [B_st[s], B_W], [B_W])
          n += 1
          wqb_d = W["w_qb"].rearrange("(c p) n -> p c n", p=128)
          for c in range(2):
              s = n % 2
              P.dma(stage[s][:, 0:768], wqb_d[:, c, :], writes=[B_st[s]], slot=f"st{s}")
              ts(engs[n % 2], w_qb[:, c, :], stage[s][:, 0:768], qag[:, c:c + 1], ALU.mult, [B_st[s], B_W], [B_W])
              n += 1

          if dbg == "w":
              raise _Stop()
          P.barrier()
          ar.off = mark - 2 * ((2128 * 4 + 31) // 32 * 32)
          A = ar.alloc
          x_sb = [A([128, D], F32) for _ in range(2)]; B_x = [P.buf("x0"), P.buf("x1")]
          cs_t = [A([128, 32], F32) for _ in range(2)]; sn_t = [A([128, 32], F32) for _ in range(2)]
          B_cs = [P.buf("cs0"), P.buf("cs1")]
          junk = A([128, D], BF16); B_junk = P.buf("junk")
          st1 = A([128, 1], F32); st1b = A([128, 1], F32); B_st1 = P.buf(); B_st1b = P.buf()
          rstdx = A([128, 1], F32); B_rx = P.buf()
          h_bf = A([128, D], BF16); B_h = P.buf("h")
          hT = {"P": A([128, 8, 130], BF16), "O": A([128, 8, 130], BF16)}
          B_hT = {"P": P.buf("hTP"), "O": P.buf("hTO")}
          kvn = A([128, 128], BF16); B_kvn = P.buf()
          kvnT = A([128, 128], BF16); B_kvnT = P.buf()
          vst = A([128, 8, 128], BF16); B_vst = P.buf("vst")
          sqk = A([128, 8, 96], F32); B_sqk = P.buf("sqk")
          s8a = A([128, 8], F32); s8b = A([128, 8], F32); s8c = A([128, 8], F32)
          B_s8a = P.buf(); B_s8b = P.buf(); B_s8c = P.buf()
          ktmp = A([128, 8, 96], F32); B_ktmp = P.buf("ktmp")
          kt = A([128, 8, 96], BF16); B_kt = P.buf("kt")
          kr = A([128, 8, 32], F32); B_kr = P.buf("kr")
          rt1 = A([128, 8, 32], F32); rt2 = A([128, 8, 32], F32); B_rt1 = P.buf(); B_rt2 = P.buf()
          krr = A([128, 32], F32); B_krr = P.buf()
          kTst = A([128, 8, 128], BF16); B_kTst = P.buf("kTst")
          ks1 = A([128, 2, 64], F32); B_ks1 = P.buf()
          ksd = A([128, 2, 2, 64], BF16); B_ksd = P.buf()
          skT = {"P": A([128, 2, 128], BF16), "O": A([128, 2, 128], BF16)}
          B_skT = {"P": P.buf("skTP"), "O": P.buf("skTO")}
          sv = {"P": A([128, 2, 128], BF16), "O": A([128, 2, 128], BF16)}
          B_sv = {"P": P.buf("svP"), "O": P.buf("svO")}
          qln = A([128, 256], BF16); B_qln = P.buf()
          qlnT = A([128, 2, 128], BF16); B_qlnT = P.buf()
          qt = A([128, 8, 96], BF16); B_qt = P.buf("qt")
          qTst = A([128, 8, 128], BF16); B_qTst = P.buf("qTst")
          sqn = A([128, 8, 64], F32); B_sqn = P.buf()
          sqb = A([128, 8, 64], BF16); B_sqb = P.buf()
          sqT = A([128, 8, 128], BF16); B_sqT = P.buf()
          Ebuf = [A([128, 512], F32) for _ in range(2)]; B_E = [P.buf(), P.buf()]
          Pm = [A([128, 512], BF16) for _ in range(2)]; B_Pm = [P.buf(), P.buf()]
          dtot = A([128, 512], F32); B_dtot = P.buf()
          rden = A([128, 512], F32); B_rden = P.buf()
          tsw = A([128, 4, 128], F32); B_tsw = P.buf()
          sgs = A([128, 4, 128], F32); B_sgs = P.buf()
          sgm = A([128, 4, 128], BF16); B_sgm = P.buf()
          ch_sb = A([128, 130], F32); B_ch = P.buf()
          u_sb = A([128, 130], F32); B_u = P.buf()
          cy = [A([128, 128], F32) for _ in range(2)]; B_cy = [P.buf(), P.buf()]
          sgc = A([128, 128], F32); B_sgc = P.buf()
          tcb = A([128, 128], F32); B_tcb = P.buf()
          yT = A([128, 8, 128], BF16); B_yT = P.buf("yT")
          osb = A([128, D], F32); B_osb = P.buf("osb")
          p1_end = ar.off

          blocks = []
          for i in range(NB):
              blocks.append(("P", i))
              blocks.append(("O", i))

          def x_src(kind, i):
              if l == 0:
                  return (x_prev if kind == "P" else x_own)[i]
              raise NotImplementedError

          def load_x(n):
              kind, i = blocks[n]
              s = n % 2
              P.dma(x_sb[s][:], x_src(kind, i), writes=[B_x[s]], slot=f"x{s}")
              P.dma(cs_t[s][:], cosT_d[:, n, :], writes=[B_cs[s]], slot=f"cs{s}")
              P.dma(sn_t[s][:], sinT_d[:, n, :], writes=[B_cs[s]], slot=f"sn{s}")

          load_x(0)
          for n, (kind, i) in enumerate(blocks):
              if dbg is not None and dbg.startswith("b") and n >= int(dbg[1:]):
                  raise _Stop()
              s = n % 2
              own = kind == "O"
              if n + 1 < len(blocks):
                  load_x(n + 1)
              xs = x_sb[s]; Bx = B_x[s]
              cs = cs_t[s]; sn = sn_t[s]; Bcs = B_cs[s]
              hTk = hT[kind]; BhT = B_hT[kind]
              act(junk[:], xs[:], AF.Square, [Bx], [B_junk, B_st1], accum=st1[:, 0:1])
              rstd(st1[:, 0:1], rstdx[:, 0:1], 1.0 / D, B_st1, B_rx, st1b[:, 0:1], B_st1b)
              ts("dve", h_bf[:], xs[:], rstdx[:, 0:1], ALU.mult, [Bx, B_rx], [B_h])
              ph = bf(bk0)
              for kc in range(8):
                  tr(ph[:, kc * 128:(kc + 1) * 128], h_bf[:, kc * 128:(kc + 1) * 128], [B_h], [B_bk[id(bk0)]])
              cp("act", hTk[:, :, 2:130], ph.rearrange("p (a b) -> p a b", a=8), [B_bk[id(bk0)]], [BhT])
              if own:
                  cp("pool", hT["O"][:, :, 0:2], hT["P"][:, :, 128:130], [B_hT["P"]], [B_hT["O"]])
              kvp = bk1
              Bkvp = B_bk[id(bk1)]
              for kc in range(8):
                  mm(kvp[:, 0:416], hTk[:, kc, 2:130], w_in[:, kc, 0:416], kc == 0, kc == 7, [BhT, B_W], [Bkvp])
              act(junk[:, 0:128], kvp[:, 0:128], AF.Square, [Bkvp], [B_junk, B_st1], accum=st1[:, 0:1])
              rstd(st1[:, 0:1], rstdx[:, 0:1], 1.0 / 128, B_st1, B_rx, st1b[:, 0:1], B_st1b)
              ts("dve", kvn[:], kvp[:, 0:128], rstdx[:, 0:1], ALU.mult, [Bkvp, B_rx], [B_kvn])
              psm = bf(bk2)
              Bsm = B_bk[id(bk2)]
              tr(psm[:, 0:128], kvn[:], [B_kvn], [Bsm])
              cp("act", kvnT[:], psm[:, 0:128], [Bsm], [B_kvnT])
              for hf in range(2):
                  mm(big[:, hf * 512:(hf + 1) * 512], kvnT[:], w_kvb[:, hf * 512:(hf + 1) * 512], True, True,
                     [B_kvnT, B_W], [B_big])
              kv3 = big[:].rearrange("p (h d) -> p h d", h=8)
              cp("act", vst[:, :, 0:64], kv3[:, :, 64:128], [B_big], [B_vst])
              cp("pool", vst[:, :, 64:128], vvalid[:, n:n + 1].unsqueeze(2).to_broadcast([128, 8, 64]), [B_C], [B_vst])
              P.dma(v_scr[n].rearrange("p h d -> p (h d)"), vst[:].rearrange("p h d -> p (h d)"), reads=[B_vst], slot="vst")
              act(sqk[:, :, 0:64], kv3[:, :, 0:64], AF.Square, [B_big], [B_sqk])
              red("dve", s8a[:], sqk[:, :, 0:64], [B_sqk], [B_s8a])
              act(junk[:, 0:32], kvp[:, 128:160], AF.Square, [Bkvp], [B_junk, B_st1], accum=st1[:, 0:1])
              ts("dve", s8a[:], s8a[:], st1[:, 0:1], ALU.add, [B_s8a, B_st1], [B_s8a])
              rstd(s8a[:], s8c[:], 1.0 / 96, B_s8a, B_s8c, s8b[:], B_s8b)
              tt("dve", ktmp[:, :, 0:64], kv3[:, :, 0:64], s8c[:].unsqueeze(2).to_broadcast([128, 8, 64]), ALU.mult,
                 [B_big, B_s8c], [B_ktmp])
              tt("pool", kt[:, :, 0:64], ktmp[:, :, 0:64], gk_b[:, 0:64].unsqueeze(1).to_broadcast([128, 8, 64]),
                 ALU.mult, [B_ktmp, B_W], [B_kt])
              tt("dve", kr[:, 0, :], kvp[:, 128:160], gk_b[:, 64:96], ALU.mult, [Bkvp, B_W], [B_kr])
              tt("pool", rt1[:, 0, :], kr[:, 0, :], cs[:], ALU.mult, [B_kr, Bcs], [B_rt1])
              tt("pool", rt2[:, 0, 0:16], kr[:, 0, 16:32], sn[:, 0:16], ALU.mult, [B_kr, Bcs], [B_rt2])
              tt("pool", rt2[:, 0, 16:32], kr[:, 0, 0:16], sn[:, 16:32], ALU.mult, [B_kr, Bcs], [B_rt2])
              tt("pool", krr[:], rt1[:, 0, :], rt2[:, 0, :], ALU.add, [B_rt1, B_rt2], [B_krr])
              tt("dve", kt[:, :, 64:96], krr[:].unsqueeze(1).to_broadcast([128, 8, 32]),
                 s8c[:].unsqueeze(2).to_broadcast([128, 8, 32]), ALU.mult, [B_krr, B_s8c], [B_kt])
              ptr = bf(bk5).rearrange("p (h t) -> p h t", h=8)
              Btr = B_bk[id(bk5)]
              for h in range(8):
                  tr(ptr[0:96, h, :], kt[:, h, :], [B_kt], [Btr])
              cp("act", kTst[0:96], ptr[0:96], [Btr], [B_kTst])
              P.dma(kT_scr.rearrange("h d t -> d h t")[:, :, n * 128:(n + 1) * 128], kTst[0:96], reads=[B_kTst], slot="kTst")
              skp = kvp[:, 160:288].rearrange("p (g d) -> p g d", g=2)
              act(sqk[:, 0:2, 0:64], skp, AF.Square, [Bkvp], [B_sqk])
              red("dve", s8a[:, 0:2], sqk[:, 0:2, 0:64], [B_sqk], [B_s8a])
              rstd(s8a[:, 0:2], s8c[:, 0:2], 1.0 / 64, B_s8a, B_s8c, s8b[:, 0:2], B_s8b)
              tt("dve", ks1[:], skp, s8c[:, 0:2].unsqueeze(2).to_broadcast([128, 2, 64]), ALU.mult, [Bkvp, B_s8c], [B_ks1])
              tt("pool", ksd[:, :, 0, :], ks1[:], gsk_b[:].unsqueeze(1).to_broadcast([128, 2, 64]), ALU.mult,
                 [B_ks1, B_W], [B_ksd])
              for g in range(2):
                  tr(psm[0:64, 128 + g * 128:256 + g * 128], ksd[:, g, 0, :], [B_ksd], [Bsm])
              cp("act", skT[kind][0:64], psm[0:64, 128:384].rearrange("p (g t) -> p g t", g=2), [Bsm], [B_skT[kind]])
              cp("act", sv[kind][:, :, 0:64], kvp[:, 288:416].rearrange("p (g d) -> p g d", g=2), [Bkvp], [B_sv[kind]])
              cp("pool", sv[kind][:, :, 64:128], vvalid[:, n:n + 1].unsqueeze(2).to_broadcast([128, 2, 64]), [B_C],
                 [B_sv[kind]])
              if not own:
                  continue
              hTo = hT["O"]; BhTo = B_hT["O"]
              qp = bk6; Bqp = B_bk[id(bk6)]
              for kc in range(8):
                  mm(qp[:, 0:256], hTo[:, kc, 2:130], w_in[:, kc, C_QL:C_QL + 256], kc == 0, kc == 7, [BhTo, B_W], [Bqp])
              act(junk[:, 0:256], qp[:, 0:256], AF.Square, [Bqp], [B_junk, B_st1], accum=st1[:, 0:1])
              rstd(st1[:, 0:1], rstdx[:, 0:1], 1.0 / 256, B_st1, B_rx, st1b[:, 0:1], B_st1b)
              ts("dve", qln[:], qp[:, 0:256], rstdx[:, 0:1], ALU.mult, [Bqp, B_rx], [B_qln])
              for c in range(2):
                  tr(psm[:, 384 + c * 128:512 + c * 128], qln[:, c * 128:(c + 1) * 128], [B_qln], [Bsm])
              cp("act", qlnT[:], psm[:, 384:640].rearrange("p (c t) -> p c t", c=2), [Bsm], [B_qlnT])
              for (c0, c1) in ((0, 512), (512, 768)):
                  for c in range(2):
                      mm(big[:, c0:c1], qlnT[:, c, :], w_qb[:, c, c0:c1], c == 0, c == 1, [B_qlnT, B_W], [B_big])
              q3 = big[:, 0:768].rearrange("p (h d) -> p h d", h=8)
              act(sqk[:], q3, AF.Square, [B_big], [B_sqk])
              red("dve", s8a[:], sqk[:], [B_sqk], [B_s8a])
              rstd(s8a[:], s8c[:], 1.0 / 96, B_s8a, B_s8c, s8b[:], B_s8b)
              tt("dve", ktmp[:], q3, s8c[:].unsqueeze(2).to_broadcast([128, 8, 96]), ALU.mult, [B_big, B_s8c], [B_ktmp])
              tt("pool", qt[:, :, 0:64], ktmp[:, :, 0:64], gq_b[:, 0:64].unsqueeze(1).to_broadcast([128, 8, 64]),
                 ALU.mult, [B_ktmp, B_W], [B_qt])
              tt("pool", kr[:], ktmp[:, :, 64:96], gq_b[:, 64:96].unsqueeze(1).to_broadcast([128, 8, 32]), ALU.mult,
                 [B_ktmp, B_W], [B_kr])
              tt("dve", rt1[:], kr[:], cs[:].unsqueeze(1).to_broadcast([128, 8, 32]), ALU.mult, [B_kr, Bcs], [B_rt1])
              tt("pool", rt2[:, :, 0:16], kr[:, :, 16:32], sn[:, 0:16].unsqueeze(1).to_broadcast([128, 8, 16]), ALU.mult,
                 [B_kr, Bcs], [B_rt2])
              tt("pool", rt2[:, :, 16:32], kr[:, :, 0:16], sn[:, 16:32].unsqueeze(1).to_broadcast([128, 8, 16]), ALU.mult,
                 [B_kr, Bcs], [B_rt2])
              tt("dve", qt[:, :, 64:96], rt1[:], rt2[:], ALU.add, [B_rt1, B_rt2], [B_qt])
              for h in range(8):
                  tr(ptr[0:96, h, :], qt[:, h, :], [B_qt], [Btr])
              cp("act", qTst[0:96], ptr[0:96], [Btr], [B_qTst])
              P.dma(qT_scr.rearrange("h d t -> d h t")[:, :, i * 128:(i + 1) * 128], qTst[0:96], reads=[B_qTst], slot="qTst")
              sqp = bk7; Bsqp = B_bk[id(bk7)]
              for kc in range(8):
                  mm(sqp[:], hTo[:, kc, 2:130], w_in[:, kc, C_SQ:C_SQ + 512], kc == 0, kc == 7, [BhTo, B_W], [Bsqp])
              sq3 = sqp[:].rearrange("p (h d) -> p h d", h=8)
              act(sqk[:, :, 0:64], sq3, AF.Square, [Bsqp], [B_sqk])
              red("dve", s8a[:], sqk[:, :, 0:64], [B_sqk], [B_s8a])
              rstd(s8a[:], s8c[:], 1.0 / 64, B_s8a, B_s8c, s8b[:], B_s8b)
              tt("dve", sqn[:], sq3, s8c[:].unsqueeze(2).to_broadcast([128, 8, 64]), ALU.mult, [Bsqp, B_s8c], [B_sqn])
              tt("pool", sqb[:], sqn[:], gsq_b[:].unsqueeze(1).to_broadcast([128, 8, 64]), ALU.mult, [B_sqn, B_W], [B_sqb])
              for h in range(8):
                  tr(ph[0:64, h * 128:(h + 1) * 128], sqb[:, h, :], [B_sqb], [B_bk[id(bk0)]])
              cp("act", sqT[0:64], ph[0:64, :].rearrange("p (c t) -> p c t", c=8), [B_bk[id(bk0)]], [B_sqT])
              Obk = [bk1, bk5]
              Sbk = [bk6, bk7]
              ns = 0
              for g in range(2):
                  Ops = Obk[g]; BO = B_bk[id(Ops)]
                  for kb, kk in enumerate(("P", "O")):
                      Sps = Sbk[ns % 2]; BS = B_bk[id(Sps)]
                      for e4 in range(4):
                          h = 4 * g + e4
                          hb = (h % 2) * 64
                          mm(Sps[:, e4 * 128:(e4 + 1) * 128], skT[kk][0:64, g, :], sqT[0:64, h, :],
                             True, True, [B_skT[kk], B_sqT], [BS])
                      E = Ebuf[ns % 2]; BE = B_E[ns % 2]
                      act(E[:], Sps[:], AF.Exp, [BS], [BE], scale=0.125)
                      pm = Pm[ns % 2]; BP = B_Pm[ns % 2]
                      tt("dve", pm[:], E[:], swaM[:, kb, 4 * g:4 * g + 4, :].rearrange("p h q -> p (h q)"), ALU.mult,
                         [BE, B_C], [BP])
                      mm(Ops[:], sv[kk][:, g, :], pm[:], kb == 0, kb == 1, [B_sv[kk], BP], [BO])
                      ns += 1
                  tt("dve", dtot[64:128, :].rearrange("p (h q) -> p h q", h=4),
                     Ops[64:128, :].rearrange("p (h q) -> p h q", h=4),
                     esink[64:128, 4 * g:4 * g + 4].unsqueeze(2).to_broadcast([64, 4, 128]), ALU.add, [BO, B_W], [B_dtot])
                  rcp(rden[64:128, :], dtot[64:128, :], [B_dtot], [B_rden])
                  for e4 in range(4):
                      h = 4 * g + e4
                      hb = (h % 2) * 64
                      tt("dve", tsw[hb:hb + 64, h // 2, :], Ops[0:64, e4 * 128:(e4 + 1) * 128],
                         rden[64:128, e4 * 128:(e4 + 1) * 128], ALU.mult, [BO, B_rden], [B_tsw])
              fmb = [bk6, bk7, bk0, bk2]
              nf = 0

              def fm_mm(ps_ap, col0, lo, hi, Bps):
                  for kc in range(8):
                      mm(ps_ap, w_in[:, kc, col0:col0 + 128], hTo[:, kc, lo:hi], kc == 0, kc == 7, [BhTo, B_W], [Bps])

              gs_ps = fmb[nf % 4]; nf += 1
              for c in range(4):
                  fm_mm(gs_ps[:, c * 128:(c + 1) * 128], C_GS + c * 128, 2, 130, B_bk[id(gs_ps)])
              act(sgs[:].rearrange("p c t -> p (c t)"), gs_ps[:], AF.Silu, [B_bk[id(gs_ps)]], [B_sgs])
              tt("pool", yT[:, 4:8, :], tsw[:], sgs[:], ALU.mult, [B_tsw, B_sgs], [B_yT])
              gm_ps = fmb[nf % 4]; nf += 1
              for c in range(4):
                  fm_mm(gm_ps[:, c * 128:(c + 1) * 128], C_GM + c * 128, 2, 130, B_bk[id(gm_ps)])
              act(sgm[:].rearrange("p c t -> p (c t)"), gm_ps[:], AF.Silu, [B_bk[id(gm_ps)]], [B_sgm])
              P.dma(sg_scr.rearrange("c p t -> p c t")[:, :, i * 128:(i + 1) * 128], sgm[:], reads=[B_sgm], slot="sgm")
              for cc in range(4):
                  X = fmb[nf % 4]; nf += 1
                  Y = fmb[nf % 4]; nf += 1
                  BX = B_bk[id(X)]; BY = B_bk[id(Y)]
                  fm_mm(X[:, 0:130], C_CH + cc * 128, 0, 130, BX)
                  fm_mm(X[:, 130:260], C_CC + cc * 128, 0, 130, BX)
                  fm_mm(Y[:, 0:128], C_CB + cc * 128, 2, 130, BY)
                  fm_mm(Y[:, 128:256], C_GC + cc * 128, 2, 130, BY)
                  cp("act", ch_sb[:], X[:, 0:130], [BX], [B_ch])
                  tt("dve", u_sb[:], ch_sb[:], X[:, 130:260], ALU.mult, [B_ch, BX], [B_u])
                  ts("pool", cy[0][:], u_sb[:, 2:130], convw[:, 2, cc:cc + 1], ALU.mult, [B_u, B_W], [B_cy[0]])
                  stt("dve", cy[1][:], u_sb[:, 1:129], convw[:, 1, cc:cc + 1], cy[0][:], ALU.mult, ALU.add,
                      [B_u, B_W, B_cy[0]], [B_cy[1]])
                  stt("dve", cy[0][:], u_sb[:, 0:128], convw[:, 0, cc:cc + 1], cy[1][:], ALU.mult, ALU.add,
                      [B_u, B_W, B_cy[1]], [B_cy[0]])
                  act(sgc[:], Y[:, 128:256], AF.Silu, [BY], [B_sgc])
                  tt("dve", tcb[:], Y[:, 0:128], sgc[:], ALU.mult, [BY, B_sgc], [B_tcb])
                  tt("dve", yT[:, cc, :], cy[0][:], tcb[:], ALU.mult, [B_cy[0], B_tcb], [B_yT])
              for hf in range(2):
                  for kc in range(8):
                      mm(big[:, hf * 512:(hf + 1) * 512], yT[:, kc, :], w_out[:, 4 + kc, hf * 512:(hf + 1) * 512],
                         kc == 0, kc == 7, [B_yT, B_W], [B_big])
              tt("dve", osb[:], big[:], xs[:], ALU.add, [B_big, Bx], [B_osb])
              P.dma(part_scr[i], osb[:], reads=[B_osb], slot="part")

          if dbg == "p1":
              raise _Stop()
          P.barrier()
          ar.off = 0
          kTp = A([128, 2, NS * 128], BF16)
          vp = A([128, NS, 2, 128], BF16)
          qTp = A([128, 2, NB * 128], BF16)
          maskM = A([128, 8, 512], BF16); B_mask = P.buf("mask")
          PT = [A([128, 512], BF16) for _ in range(4)]; B_PT = [P.buf() for _ in range(4)]
          gbuf = [A([128, 512], BF16) for _ in range(2)]; B_g = [P.buf(), P.buf()]
          ymla = A([128, 4, NB * 128], BF16); B_ym = [P.buf(f"ym{j}") for j in range(8)]
          rden2 = A([128, 512], F32); B_rd2 = P.buf()
          tn = A([128, 512], F32); B_tn = P.buf()
          ptile = [A([128, D], F32) for _ in range(2)]; B_pt = [P.buf(), P.buf()]
          osb2 = [A([128, D], F32) for _ in range(2)]; B_o2 = [P.buf(), P.buf()]
          B_kc = [P.buf(f"kTc{c}") for c in range(8)]
          B_vc = [P.buf(f"vc{c}") for c in range(8)]
          B_q = P.buf("qTp")
          P.dma(maskM[:], maskM_d, writes=[B_mask], slot="c0")
          SC = 96 ** -0.5
          Sb = [bk0, bk1, bk2]
          Ob = [bk5, bk6]
          nO = 0
          nS = 0
          for hp in range(4):
              for c in range(8):
                  for hh in range(2):
                      P.dma(kTp[0:96, hh, c * 1024:(c + 1) * 1024], kT_scr[2 * hp + hh, :, c * 1024:(c + 1) * 1024],
                            writes=[B_kc[c]], slot=f"kl{hh}")
                  P.dma(vp[:, 8 * c:8 * c + 8, :, :], v_scr[8 * c:8 * c + 8, :, 2 * hp:2 * hp + 2, :].rearrange("s p h d -> p s h d"),
                        writes=[B_vc[c]], slot="vl")
                  if c == 0:
                      for hh in range(2):
                          P.dma(qTp[0:96, hh, :], qT_scr[2 * hp + hh], writes=[B_q], slot=f"ql{hh}")
              for j in range(8):
                  gb = gbuf[j % 2]; Bg = B_g[j % 2]
                  P.dma(gb[:], sg_scr[hp, :, j * 512:(j + 1) * 512], writes=[Bg], slot=f"gl{j % 2}")
                  for hh in range(2):
                      Ops = Ob[nO % 2]; BO = B_bk[id(Ops)]
                      nO += 1
                      nsl = 8 * j + 8
                      LA = 2
                      pend = {}
                      for t in range(nsl + LA):
                          if t < nsl:
                              Sps = Sb[nS % 3]; BS = B_bk[id(Sps)]
                              nS += 1
                              mm(Sps[:], kTp[0:96, hh, t * 128:(t + 1) * 128], qTp[0:96, hh, j * 512:(j + 1) * 512],
                                 True, True, [B_kc[t // 8], B_q], [BS])
                              pend[t] = (Sps, BS)
                          if t >= LA:
                              sl = t - LA
                              Sps, BS = pend.pop(sl)
                              pt = PT[sl % 4]; BP = B_PT[sl % 4]
                              act(pt[:], Sps[:], AF.Exp, [BS], [BP], scale=SC)
                              if sl >= 8 * j:
                                  tt("pool", pt[:], pt[:], maskM[:, sl - 8 * j, :], ALU.mult, [BP, B_mask], [BP])
                              mm(Ops[:], vp[:, sl, hh, :], pt[:], sl == 0, sl == nsl - 1, [B_vc[sl // 8], BP], [BO])
                      hb = hh * 64
                      rcp(rden2[64:128, :], Ops[64:128, :], [BO], [B_rd2])
                      tt("dve", tn[hb:hb + 64, :], Ops[0:64, :], rden2[64:128, :], ALU.mult, [BO, B_rd2], [B_tn])
                      tt("pool", ymla[hb:hb + 64, hp, j * 512:(j + 1) * 512], tn[hb:hb + 64, :], gb[hb:hb + 64, :], ALU.mult,
                         [B_tn, Bg], [B_ym[j]])

          if dbg == "p2":
              raise _Stop()
          def load_pt(i):
              P.dma(ptile[i % 2][:], part_scr[i], writes=[B_pt[i % 2]], slot=f"pl{i % 2}")

          load_pt(0)
          for i in range(NB):
              if i + 1 < NB:
                  load_pt(i + 1)
              for hf in range(2):
                  for c in range(4):
                      mm(big[:, hf * 512:(hf + 1) * 512], ymla[:, c, i * 128:(i + 1) * 128],
                         w_out[:, c, hf * 512:(hf + 1) * 512], c == 0, c == 3, [B_ym[i // 4], B_W], [B_big])
              o2 = osb2[i % 2]; Bo2 = B_o2[i % 2]
              tt("dve", o2[:], big[:], ptile[i % 2][:], ALU.add, [B_big, B_pt[i % 2]], [Bo2])
              if last:
                  P.dma(y_out[i], o2[:], reads=[Bo2], slot=f"yo{i % 2}")
              else:
                  raise NotImplementedError

      except _Stop:
        pass
    print("total ops recorded", P.total, {e: len(v) for e, v in P.ops.items()})
    P.emit(final_slots=[s_ for s_ in P.slot_counts])
    P.close()
    return nc


def _consts(r):
    half = 16
    inv_freq = np.power(np.float32(10000.0), -np.arange(half, dtype=np.float32) / half).astype(np.float32)
    pidx = np.arange(128, dtype=np.float32)
    cosT = np.zeros((128, NS, 32), np.float32)
    sinT = np.zeros((128, NS, 32), np.float32)
    for s in range(NS):
        gb = s - 1 + r
        pos = (gb * 128 + pidx).astype(np.float32)
        ang = pos[:, None] * inv_freq[None, :]
        c = np.cos(ang).astype(np.float32)
        sn = np.sin(ang).astype(np.float32)
        cosT[:, s, :16] = c
        cosT[:, s, 16:] = c
        sinT[:, s, :16] = -sn
        sinT[:, s, 16:] = sn
    maskM = np.zeros((128, 8, 512), np.float32)
    ki = np.arange(128)[:, None]
    qi = np.arange(128)[None, :]
    tri = (ki <= qi).astype(np.float32)
    for so in range(8):
        for qb in range(4):
            d = 2 * qb + 1
            if so < d:
                maskM[:, so, qb * 128:(qb + 1) * 128] = 1.0
            elif so == d:
                maskM[:, so, qb * 128:(qb + 1) * 128] = tri
    slopes = np.exp2(-8.0 * np.arange(1, 9, dtype=np.float32) / 8).astype(np.float32)
    swaM = np.zeros((128, 2, 8, 128), np.float32)
    for h in range(8):
        d0 = (128 + qi - ki).astype(np.float32)
        swaM[:, 0, h, :] = np.where(d0 < 128, np.exp(-slopes[h] * d0), 0.0)
        d1 = (qi - ki).astype(np.float32)
        swaM[:, 1, h, :] = np.where(d1 >= 0, np.exp(-slopes[h] * np.maximum(d1, 0)), 0.0)
    vvalid = np.ones((128, NS), np.float32)
    if r == 0:
        vvalid[:, 0] = 0.0
    return {"cosT": cosT, "sinT": sinT, "maskM": maskM.astype(ml_dtypes.bfloat16), "swaM": swaM, "vvalid": vvalid}


def _layer_weights(inp, l, suffix):
    f = lambda a: np.ascontiguousarray(a, dtype=np.float32)
    return {
        f"w_in_{suffix}": f(inp["w_in"][l][:, PERM]), f"norm_g_{suffix}": f(inp["norm_g"][l].reshape(8, 128).T),
        f"w_kvb_{suffix}": f(inp["mla_w_kvb"][l]), f"kva_g_{suffix}": f(inp["mla_kv_a_norm"][l].reshape(128, 1)),
        f"w_qb_{suffix}": f(inp["mla_w_qb"][l]), f"qa_g_{suffix}": f(inp["mla_q_a_norm"][l].reshape(2, 128).T),
        f"q_g_{suffix}": f(inp["mla_q_norm"][l]), f"k_g_{suffix}": f(inp["mla_k_norm"][l]),
        f"conv_w_{suffix}": f(inp["conv_w"][l].reshape(3, 4, 128).transpose(2, 0, 1)), f"sq_g_{suffix}": f(inp["swa_q_norm"][l]),
        f"sk_g_{suffix}": f(inp["swa_k_norm"][l]), f"sinks_{suffix}": f(inp["swa_sinks"][l]),
        f"w_out_{suffix}": f(inp["w_out"][l]),
    }


def _shard_x(x):
    xb = x.reshape(4, 64, 128, D)
    outs = []
    for c in range(8):
        b, r = c // 2, c % 2
        own = xb[b, r::2]
        prev = np.zeros_like(own)
        if r == 0:
            prev[1:] = xb[b, 1:63:2]
        else:
            prev[:] = xb[b, 0::2]
        outs.append((np.ascontiguousarray(own), np.ascontiguousarray(prev)))
    return outs


def _gather(res):
    out = np.zeros((4, 64, 128, D), np.float32)
    for c in range(8):
        b, r = c // 2, c % 2
        out[b, r::2] = res[c]["y_out"]
    return out.reshape(4, 8192, D)


_CACHE = {}


def kernel(**inp):
    inp = {k: np.asarray(v) for k, v in inp.items()}
    x = np.ascontiguousarray(inp["x"], dtype=np.float32)
    depth = inp["w_in"].shape[0]
    if "nc1" not in _CACHE:
        _CACHE["nc1"] = build_program(1)
    consts = [_consts(c % 2) for c in range(8)]
    for l in range(depth):
        nc = _CACHE["nc1"]
        w = _layer_weights(inp, l, "0")
        sh = _shard_x(x)
        in_maps = []
        for c in range(8):
            m = {"x_own": sh[c][0], "x_prev": sh[c][1]}
            m.update(w)
            m.update(consts[c])
            in_maps.append(m)
        res = run_bass_kernel_spmd(nc, in_maps, core_ids=list(range(8)))
        x = _gather(res.results)
    return x.astype(np.float32)
```

```python
import numpy as np
import ml_dtypes
from contextlib import ExitStack
import concourse.bass as bass
import concourse.mybir as mybir
from concourse.bass_utils import run_bass_kernel_spmd

F32 = mybir.dt.float32
BF16 = mybir.dt.bfloat16
ALU = mybir.AluOpType
AF = mybir.ActivationFunctionType
AX = mybir.AxisListType

NB = 32
NS = 64
D = 1024
NCOL = 4256
EPS = 1e-6
C_KVL, C_KR, C_SK, C_SV, C_QL, C_SQ, C_GM, C_GS, C_CH, C_CC, C_CB, C_GC = (
    0, 128, 160, 288, 416, 672, 1184, 1696, 2208, 2720, 3232, 3744)
PERM = np.concatenate([np.arange(256, 384), np.arange(384, 416), np.arange(3488, 3616), np.arange(3616, 3744),
                       np.arange(0, 256), np.arange(2976, 3488), np.arange(416, 928), np.arange(3744, 4256),
                       np.arange(928, 1440), np.arange(1952, 2464), np.arange(1440, 1952), np.arange(2464, 2976)])

COMPUTE = ("pe", "act", "dve", "pool")
ALLENG = COMPUTE + ("sp",)


class Buf:
    __slots__ = ("name", "lw", "rd")

    def __init__(self, name):
        self.name = name
        self.lw = None
        self.rd = []


class Op:
    __slots__ = ("eng", "fn", "waits", "signal", "dma", "slot", "slot_cnt")

    def __init__(self, eng, fn, dma=False, slot=None):
        self.eng = eng
        self.fn = fn
        self.waits = []
        self.signal = False
        self.dma = dma
        self.slot = slot
        self.slot_cnt = 0


class Prog:
    def __init__(self, nc):
        self.nc = nc
        self.ops = {e: [] for e in ALLENG}
        self.seen = {e: {} for e in ALLENG}
        self.pending = {e: [] for e in ALLENG}
        self.slot_counts = {}
        self.stack = ExitStack()
        self.nbuf = 0

    def sbuf(self, name, shape, dtype):
        return self.stack.enter_context(self.nc.sbuf_tensor("sb_" + name, list(shape), dtype))

    def psum(self, name, shape, dtype):
        return self.stack.enter_context(self.nc.psum_tensor("ps_" + name, list(shape), dtype))

    def buf(self, name=None):
        self.nbuf += 1
        return Buf(name or f"b{self.nbuf}")

    def _dep(self, op, eng, key):
        kind, k, v = key
        if kind == "eng" and k == "pe" and eng == "pe":
            return
        seen = self.seen[eng]
        if seen.get((kind, k), -1) >= v:
            return
        seen[(kind, k)] = v
        op.waits.append(key)
        if kind == "eng":
            self.ops[k][v].signal = True

    limit = None
    total = 0
    trace = None

    def add(self, eng, fn, reads=(), writes=(), dma=False, slot=None):
        self.total += 1
        if self.limit is not None and self.total > self.limit:
            return None
        op = Op(eng, fn, dma=dma, slot=slot)
        if self.trace is not None:
            import sys as _s
            f = _s._getframe(1)
            while f is not None and f.f_code.co_name != "build_program":
                f = f.f_back
            self.trace.append((self.total, eng, f.f_lineno if f else -1))
        idx = len(self.ops[eng])
        for key in self.pending[eng]:
            self._dep(op, eng, key)
        self.pending[eng] = []
        if dma:
            cnt = self.slot_counts.get(slot, 0)
            if cnt > 0:
                self._dep(op, eng, ("slot", slot, cnt))
            self.slot_counts[slot] = cnt + 1
            op.slot_cnt = cnt + 1
            me = ("slot", slot, cnt + 1)
        else:
            me = ("eng", eng, idx)
        for b in reads:
            if b.lw is not None:
                self._dep(op, eng, b.lw)
        for b in writes:
            if b.lw is not None:
                self._dep(op, eng, b.lw)
            for r in b.rd:
                self._dep(op, eng, r)
        for b in reads:
            b.rd.append(me)
        for b in writes:
            b.lw = me
            b.rd = []
        self.ops[eng].append(op)
        return op

    def barrier(self):
        keys = []
        for e in ALLENG:
            for i in range(len(self.ops[e]) - 1, -1, -1):
                if not self.ops[e][i].dma:
                    keys.append(("eng", e, i))
                    break
        for s, c in self.slot_counts.items():
            keys.append(("slot", s, c))
        for e in ALLENG:
            self.pending[e] = self.pending[e] + [k for k in keys if not (k[0] == "eng" and k[1] == e)]

    def dma(self, out, in_, reads=(), writes=(), slot="d0", eng="sp"):
        return self.add(eng, lambda e: e.dma_start(out=out, in_=in_), reads, writes, dma=True, slot=slot)

    def emit(self, final_slots=()):
        nc = self.nc
        st = self.stack
        esem = {e: st.enter_context(nc.semaphore(f"s_{e}")) for e in ALLENG}
        ssem = {s: st.enter_context(nc.semaphore(f"d_{s}")) for s in self.slot_counts}
        cnt = {}
        for e, lst in self.ops.items():
            c = 0
            for i, op in enumerate(lst):
                if op.signal and not op.dma:
                    c += 1
                cnt[(e, i)] = c

        def run(engname, handle):
            for op in self.ops[engname]:
                for kind, k, v in op.waits:
                    if kind == "eng":
                        handle.wait_ge(esem[k], cnt[(k, v)])
                    else:
                        handle.wait_ge(ssem[k], 16 * v)
                ins = op.fn(handle)
                if op.dma:
                    ins.then_inc(ssem[op.slot], 16)
                elif op.signal:
                    ins.then_inc(esem[engname], 1)
            if engname == "sp":
                for s in final_slots:
                    handle.wait_ge(ssem[s], 16 * self.slot_counts[s])

        block = st.enter_context(nc.Block())

        @block.sync
        def _(e):
            run("sp", e)

        @block.tensor
        def _(e):
            run("pe", e)

        @block.scalar
        def _(e):
            run("act", e)

        @block.vector
        def _(e):
            run("dve", e)

        @block.gpsimd
        def _(e):
            run("pool", e)

    def close(self):
        self.stack.close()


WNAMES = ["w_in", "norm_g", "w_kvb", "kva_g", "w_qb", "qa_g", "q_g", "k_g", "conv_w", "sq_g", "sk_g",
          "sinks", "w_out"]
WSHAPES = {"w_in": [D, NCOL], "norm_g": [128, 8], "w_kvb": [128, 1024], "kva_g": [128, 1], "w_qb": [256, 768],
           "qa_g": [128, 2], "q_g": [96], "k_g": [96], "conv_w": [128, 3, 4], "sq_g": [64], "sk_g": [64],
           "sinks": [8], "w_out": [1536, 1024]}


class _Stop(Exception):
    pass


def build_program(nlayers=1, dbg=None):
    nc = bass.Bass("TRN2", target_bir_lowering=False)
    P = Prog(nc)
    import os
    if os.environ.get("K_LIMIT"):
        P.limit = int(os.environ["K_LIMIT"])

    def dram_in(name, shape, dt=F32):
        return nc.dram_tensor(name, list(shape), dt, kind="ExternalInput").ap()

    x_own = dram_in("x_own", [NB, 128, D])
    x_prev = dram_in("x_prev", [NB, 128, D])
    Wd = [{n: dram_in(f"{n}_{l}", WSHAPES[n]) for n in WNAMES} for l in range(nlayers)]
    cosT_d = dram_in("cosT", [128, NS, 32])
    sinT_d = dram_in("sinT", [128, NS, 32])
    maskM_d = dram_in("maskM", [128, 8, 512], BF16)
    swaM_d = dram_in("swaM", [128, 2, 8, 128])
    vvalid_d = dram_in("vvalid", [128, NS])
    y_out = nc.dram_tensor("y_out", [NB, 128, D], F32, kind="ExternalOutput").ap()
    fused = nlayers == 2
    if not fused:
        passes = [dict(W=Wd[0], own=lambda i: x_own[i], prev=lambda i: x_prev[i], cos=cosT_d, sin=sinT_d,
                       vv=vvalid_d, out=lambda i: y_out[i], blend=False)]
    else:
        x_own2 = dram_in("x_own2", [NB, 128, D])
        x_prev2 = dram_in("x_prev2", [NB, 128, D])
        cosT2_d = dram_in("cosT2", [128, NS, 32])
        sinT2_d = dram_in("sinT2", [128, NS, 32])
        vvalid2_d = dram_in("vvalid2", [128, NS])
        blendw_d = dram_in("blendw", [128, 2])
        x1_mine = nc.dram_tensor("x1_mine", [NB, 128, D], F32, kind="ExternalOutput").ap()
        Zb = nc.dram_tensor("x1_other", [NB + 1, 128, D], F32, kind="ExternalOutput").ap()
        passes = [
            dict(W=Wd[0], own=lambda i: x_own[i], prev=lambda i: x_prev[i], cos=cosT_d, sin=sinT_d, vv=vvalid_d,
                 out=lambda i: x1_mine[i], blend=False),
            dict(W=Wd[0], own=lambda i: x_own2[i], prev=lambda i: x_prev2[i], cos=cosT2_d, sin=sinT2_d, vv=vvalid2_d,
                 out=lambda i: Zb[i + 1], blend=False),
            dict(W=Wd[1], own=lambda i: x1_mine[i], prev=lambda i: Zb[i], prev2=lambda i: Zb[i + 1], cos=cosT_d,
                 sin=sinT_d, vv=vvalid_d, out=lambda i: y_out[i], blend=True),
        ]

    skind = "ExternalOutput"
    kT_scr = nc.dram_tensor("kT_scr", [8, 96, NS * 128], BF16, kind=skind).ap()
    v_scr = nc.dram_tensor("v_scr", [NS, 128, 8, 128], BF16, kind=skind).ap()
    qT_scr = nc.dram_tensor("qT_scr", [8, 96, NB * 128], BF16, kind=skind).ap()
    sg_scr = nc.dram_tensor("sg_scr", [4, 128, NB * 128], BF16, kind=skind).ap()
    part_scr = nc.dram_tensor("part_scr", [NB, 128, D], F32, kind=skind).ap()

    ident = P.sbuf("ident", [128, 128], BF16); B_ident = P.buf("ident")
    idf = P.sbuf("idf", [128, 128], F32)
    w_kvb = P.sbuf("w_kvb", [128, 1024], BF16)
    w_qb = P.sbuf("w_qb", [128, 2, 768], BF16)
    w_out = P.sbuf("w_out", [128, 12, 1024], BF16)
    swaM = P.sbuf("swaM", [128, 2, 8, 128], F32)
    vvalid = P.sbuf("vvalid", [128, NS], F32)
    gq_b = P.sbuf("gq_b", [128, 96], F32)
    gk_b = P.sbuf("gk_b", [128, 96], F32)
    gsq_b = P.sbuf("gsq_b", [128, 64], F32)
    gsk_b = P.sbuf("gsk_b", [128, 64], F32)
    esink = P.sbuf("esink", [128, 8], F32)
    normg = P.sbuf("normg", [128, 8], F32)
    convw = P.sbuf("convw", [128, 3, 4], F32)
    kvag = P.sbuf("kvag", [128, 1], F32)
    qag = P.sbuf("qag", [128, 2], F32)
    eps_t = P.sbuf("eps_t", [128, 1], F32)
    B_W = P.buf("weights")
    B_C = P.buf("consts")

    ARENA_B = 150 * 1024
    arena = P.sbuf("arena", [128, ARENA_B // 2], BF16)

    class Arena:
        def __init__(self):
            self.off = 0

        def alloc(self, shape, dt, parts=128):
            n = int(np.prod(shape[1:]))
            nbytes = n * (4 if dt == F32 else 2)
            nbytes = (nbytes + 31) // 32 * 32
            o = self.off
            self.off += nbytes
            assert self.off <= ARENA_B, f"arena overflow {self.off}"
            v = arena[:, o // 2:(o + nbytes) // 2]
            if dt == F32:
                v = v.bitcast(F32)
            v = v[:, 0:n]
            if len(shape) == 3:
                v = v.rearrange("p (a b) -> p a b", a=shape[1])
            elif len(shape) == 4:
                v = v.rearrange("p (a b c) -> p a b c", a=shape[1], b=shape[2])
            return v

    banks = [P.psum(f"bank{i}", [128, 512], F32) for i in range(3)]
    big = P.psum("big", [128, 1024], F32)
    banks += [P.psum(f"bank{i}", [128, 512], F32) for i in range(5, 8)]
    bk0, bk1, bk2, bk5, bk6, bk7 = banks
    B_bk = {id(b): P.buf(f"bk{i}") for i, b in enumerate(banks)}
    B_big = P.buf("big")

    def bf(ps):
        return ps[:].bitcast(BF16)

    def tt(eng, out, in0, in1, op, R, W):
        return P.add(eng, lambda e: e.tensor_tensor(out=out, in0=in0, in1=in1, op=op), R, W)

    def ts(eng, out, in0, s1, op0, R, W, s2=None, op1=None):
        if op1 is None:
            return P.add(eng, lambda e: e.tensor_scalar(out=out, in0=in0, scalar1=s1, scalar2=None, op0=op0), R, W)
        return P.add(eng, lambda e: e.tensor_scalar(out=out, in0=in0, scalar1=s1, scalar2=s2, op0=op0, op1=op1), R, W)

    def stt(eng, out, in0, scalar, in1, op0, op1, R, W):
        return P.add(eng, lambda e: e.scalar_tensor_tensor(out=out, in0=in0, scalar=scalar, in1=in1, op0=op0, op1=op1), R, W)

    def act(out, in_, func, R, W, scale=None, bias=None, accum=None):
        kw = {}
        if scale is not None:
            kw["scale"] = scale
        if bias is not None:
            kw["bias"] = bias
        if accum is not None:
            kw["accum_out"] = accum
        return P.add("act", lambda e: e.activation(out=out, in_=in_, func=func, **kw), R, W)

    def cp(eng, out, in_, R, W):
        if eng == "act":
            return P.add("act", lambda e: e.activation(out=out, in_=in_, func=AF.Copy), R, W)
        return P.add(eng, lambda e: e.tensor_copy(out=out, in_=in_), R, W)

    def red(eng, out, in_, R, W):
        return P.add(eng, lambda e: e.tensor_reduce(out=out, in_=in_, axis=AX.X, op=ALU.add), R, W)

    def rcp(out, in_, R, W):
        return P.add("dve", lambda e: e.reciprocal(out=out, in_=in_), R, W)

    def mm(out, lhsT, rhs, start, stop, R, W):
        return P.add("pe", lambda e: e.matmul(out, lhsT=lhsT, rhs=rhs, start=start, stop=stop), R, W)

    def tr(out, in_, R, W):
        return P.add("pe", lambda e: e.transpose(out=out, in_=in_, identity=ident[:]), list(R) + [B_ident], W)

    def rstd(ss_ap, out_ap, scale, R_ss, B_out, tmp_ap, B_tmp):
        act(tmp_ap, ss_ap, AF.Sqrt, [R_ss, B_C], [B_tmp], scale=scale, bias=eps_t[:, 0:1])
        rcp(out_ap, tmp_ap, [B_tmp], [B_out])

    P.add("pool", lambda e: e.memset(idf[:], 0.0), [], [B_ident])
    P.add("pool", lambda e: e.affine_select(out=idf[:], in_=idf[:], pattern=[[-1, 128]], compare_op=ALU.not_equal,
                                            fill=1.0, base=0, channel_multiplier=1), [B_ident], [B_ident])
    P.add("pool", lambda e: e.tensor_copy(out=ident[:], in_=idf[:]), [B_ident], [B_ident])
    P.add("pool", lambda e: e.memset(eps_t[:], EPS), [], [B_C])
    P.dma(swaM[:], swaM_d, writes=[B_C], slot="c0")
    blendw = P.sbuf("blendw", [128, 2], F32)
    zero_t = P.sbuf("zero_t", [128, D], F32)
    if fused:
        P.dma(blendw[:], blendw_d, writes=[B_C], slot="c1")
        P.add("pool", lambda e: e.memset(zero_t[:], 0.0), [], [B_C])
        P.dma(Zb[0], zero_t[:], reads=[B_C], slot="c1")

    for ps in passes:
      try:
          W = ps["W"]
          cosT_d = ps["cos"]; sinT_d = ps["sin"]
          P.barrier()
          P.dma(vvalid[:], ps["vv"], writes=[B_C], slot="c1")
          ar = Arena()
          w_in = ar.alloc([128, 8, NCOL], BF16)
          stage = [ar.alloc([128, 2128], F32) for _ in range(2)]
          B_st = [P.buf("st0"), P.buf("st1")]
          mark = ar.off
          P.dma(normg[:], W["norm_g"], writes=[B_W], slot="c0")
          P.dma(convw[:], W["conv_w"], writes=[B_W], slot="c1")
          P.dma(kvag[:], W["kva_g"], writes=[B_W], slot="c0")
          P.dma(qag[:], W["qa_g"], writes=[B_W], slot="c1")
          P.dma(gq_b[:], W["q_g"].partition_broadcast(128), writes=[B_W], slot="c0")
          P.dma(gk_b[:], W["k_g"].partition_broadcast(128), writes=[B_W], slot="c1")
          P.dma(gsq_b[:], W["sq_g"].partition_broadcast(128), writes=[B_W], slot="c0")
          P.dma(gsk_b[:], W["sk_g"].partition_broadcast(128), writes=[B_W], slot="c1")
          P.dma(esink[:], W["sinks"].partition_broadcast(128), writes=[B_W], slot="c0")
          act(esink[:], esink[:], AF.Exp, [B_W], [B_W])
          n = 0
          engs = ["dve", "pool"]
          win_d = W["w_in"].rearrange("(kc p) c -> p kc c", p=128)
          for kc in range(8):
              for hf in range(2):
                  s = n % 2
                  P.dma(stage[s][:], win_d[:, kc, hf * 2128:(hf + 1) * 2128], writes=[B_st[s]], slot=f"st{s}")
                  ts(engs[n % 2], w_in[:, kc, hf * 2128:(hf + 1) * 2128], stage[s][:], normg[:, kc:kc + 1], ALU.mult,
                     [B_st[s], B_W], [B_W])
                  n += 1
          wout_d = W["w_out"].rearrange("(kc p) c -> p kc c", p=128)
          for kc in range(12):
              s = n % 2
              P.dma(stage[s][:, 0:1024], wout_d[:, kc, :], writes=[B_st[s]], slot=f"st{s}")
              cp(engs[n % 2], w_out[:, kc, :], stage[s][:, 0:1024], [B_st[s]], [B_W])
              n += 1
          s = n % 2
          P.dma(stage[s][:, 0:1024], W["w_kvb"], writes=[B_st[s]], slot=f"st{s}")
          ts(engs[n % 2], w_kvb[:], stage[s][:, 0:1024], kvag[:, 0:1], ALU.mult, [B_st[s], B_W], [B_W])
          n += 1
          wqb_d = W["w_qb"].rearrange("(c p) n -> p c n", p=128)
          for c in range(2):
              s = n % 2
              P.dma(stage[s][:, 0:768], wqb_d[:, c, :], writes=[B_st[s]], slot=f"st{s}")
              ts(engs[n % 2], w_qb[:, c, :], stage[s][:, 0:768], qag[:, c:c + 1], ALU.mult, [B_st[s], B_W], [B_W])
              n += 1

          if dbg == "w":
              raise _Stop()
          P.barrier()
          ar.off = mark - 2 * ((2128 * 4 + 31) // 32 * 32)
          A = ar.alloc
          x_sb = [A([128, D], F32) for _ in range(2)]; B_x = [P.buf("x0"), P.buf("x1")]
          cs_t = [A([128, 32], F32) for _ in range(2)]; sn_t = [A([128, 32], F32) for _ in range(2)]
          B_cs = [P.buf("cs0"), P.buf("cs1")]
          junk = A([128, D], BF16); B_junk = P.buf("junk")
          xb2 = A([128, D], F32); B_xb2 = P.buf("xb2")
          st1 = A([128, 1], F32); st1b = A([128, 1], F32); B_st1 = P.buf(); B_st1b = P.buf()
          rstdx = A([128, 1], F32); B_rx = P.buf()
          h_bf = A([128, D], BF16); B_h = P.buf("h")
          hT = {"P": A([128, 8, 130], BF16), "O": A([128, 8, 130], BF16)}
          B_hT = {"P": P.buf("hTP"), "O": P.buf("hTO")}
          kvn = A([128, 128], BF16); B_kvn = P.buf()
          kvnT = A([128, 128], BF16); B_kvnT = P.buf()
          vst = A([128, 8, 128], BF16); B_vst = P.buf("vst")
          sqk = A([128, 8, 96], F32); B_sqk = P.buf("sqk")
          s8a = A([128, 8], F32); s8b = A([128, 8], F32); s8c = A([128, 8], F32)
          B_s8a = P.buf(); B_s8b = P.buf(); B_s8c = P.buf()
          ktmp = A([128, 8, 96], F32); B_ktmp = P.buf("ktmp")
          kt = A([128, 8, 96], BF16); B_kt = P.buf("kt")
          kr = A([128, 8, 32], F32); B_kr = P.buf("kr")
          rt1 = A([128, 8, 32], F32); rt2 = A([128, 8, 32], F32); B_rt1 = P.buf(); B_rt2 = P.buf()
          krr = A([128, 32], F32); B_krr = P.buf()
          kTst = A([128, 8, 128], BF16); B_kTst = P.buf("kTst")
          ks1 = A([128, 2, 64], F32); B_ks1 = P.buf()
          ksd = A([128, 2, 2, 64], BF16); B_ksd = P.buf()
          skT = {"P": A([128, 2, 128], BF16), "O": A([128, 2, 128], BF16)}
          B_skT = {"P": P.buf("skTP"), "O": P.buf("skTO")}
          sv = {"P": A([128, 2, 128], BF16), "O": A([128, 2, 128], BF16)}
          B_sv = {"P": P.buf("svP"), "O": P.buf("svO")}
          qln = A([128, 256], BF16); B_qln = P.buf()
          qlnT = A([128, 2, 128], BF16); B_qlnT = P.buf()
          qt = A([128, 8, 96], BF16); B_qt = P.buf("qt")
          qTst = A([128, 8, 128], BF16); B_qTst = P.buf("qTst")
          sqn = A([128, 8, 64], F32); B_sqn = P.buf()
          sqb = A([128, 8, 64], BF16); B_sqb = P.buf()
          sqT = A([128, 8, 128], BF16); B_sqT = P.buf()
          Ebuf = [A([128, 512], F32) for _ in range(2)]; B_E = [P.buf(), P.buf()]
          Pm = [A([128, 512], BF16) for _ in range(2)]; B_Pm = [P.buf(), P.buf()]
          dtot = A([128, 512], F32); B_dtot = P.buf()
          rden = A([128, 512], F32); B_rden = P.buf()
          tsw = A([128, 4, 128], F32); B_tsw = P.buf()
          sgs = A([128, 4, 128], F32); B_sgs = P.buf()
          sgm = A([128, 4, 128], BF16); B_sgm = P.buf()
          ch_sb = A([128, 130], F32); B_ch = P.buf()
          u_sb = A([128, 130], F32); B_u = P.buf()
          cy = [A([128, 128], F32) for _ in range(2)]; B_cy = [P.buf(), P.buf()]
          sgc = A([128, 128], F32); B_sgc = P.buf()
          tcb = A([128, 128], F32); B_tcb = P.buf()
          yT = A([128, 8, 128], BF16); B_yT = P.buf("yT")
          osb = A([128, D], F32); B_osb = P.buf("osb")
          p1_end = ar.off

          blocks = []
          for i in range(NB):
              blocks.append(("P", i))
              blocks.append(("O", i))

          def x_src(kind, i):
              return ps["prev"](i) if kind == "P" else ps["own"](i)

          def load_x(n):
              kind, i = blocks[n]
              s = n % 2
              P.dma(x_sb[s][:], x_src(kind, i), writes=[B_x[s]], slot=f"x{s}")
              if ps["blend"] and kind == "P":
                  P.dma(xb2[:], ps["prev2"](i), writes=[B_xb2], slot="xb2")
              P.dma(cs_t[s][:], cosT_d[:, n, :], writes=[B_cs[s]], slot=f"cs{s}")
              P.dma(sn_t[s][:], sinT_d[:, n, :], writes=[B_cs[s]], slot=f"sn{s}")

          load_x(0)
          for n, (kind, i) in enumerate(blocks):
              if dbg is not None and dbg.startswith("b") and n >= int(dbg[1:]):
                  raise _Stop()
              s = n % 2
              own = kind == "O"
              if n + 1 < len(blocks):
                  load_x(n + 1)
              xs = x_sb[s]; Bx = B_x[s]
              if ps["blend"] and kind == "P":
                  ts("dve", xs[:], xs[:], blendw[:, 0:1], ALU.mult, [Bx, B_C], [Bx])
                  stt("dve", xs[:], xb2[:], blendw[:, 1:2], xs[:], ALU.mult, ALU.add, [B_xb2, B_C, Bx], [Bx])
              cs = cs_t[s]; sn = sn_t[s]; Bcs = B_cs[s]
              hTk = hT[kind]; BhT = B_hT[kind]
              act(junk[:], xs[:], AF.Square, [Bx], [B_junk, B_st1], accum=st1[:, 0:1])
              rstd(st1[:, 0:1], rstdx[:, 0:1], 1.0 / D, B_st1, B_rx, st1b[:, 0:1], B_st1b)
              ts("dve", h_bf[:], xs[:], rstdx[:, 0:1], ALU.mult, [Bx, B_rx], [B_h])
              ph = bf(bk0)
              for kc in range(8):
                  tr(ph[:, kc * 128:(kc + 1) * 128], h_bf[:, kc * 128:(kc + 1) * 128], [B_h], [B_bk[id(bk0)]])
              cp("act", hTk[:, :, 2:130], ph.rearrange("p (a b) -> p a b", a=8), [B_bk[id(bk0)]], [BhT])
              if own:
                  cp("pool", hT["O"][:, :, 0:2], hT["P"][:, :, 128:130], [B_hT["P"]], [B_hT["O"]])
              kvp = bk1
              Bkvp = B_bk[id(bk1)]
              for kc in range(8):
                  mm(kvp[:, 0:416], hTk[:, kc, 2:130], w_in[:, kc, 0:416], kc == 0, kc == 7, [BhT, B_W], [Bkvp])
              act(junk[:, 0:128], kvp[:, 0:128], AF.Square, [Bkvp], [B_junk, B_st1], accum=st1[:, 0:1])
              rstd(st1[:, 0:1], rstdx[:, 0:1], 1.0 / 128, B_st1, B_rx, st1b[:, 0:1], B_st1b)
              ts("dve", kvn[:], kvp[:, 0:128], rstdx[:, 0:1], ALU.mult, [Bkvp, B_rx], [B_kvn])
              psm = bf(bk2)
              Bsm = B_bk[id(bk2)]
              tr(psm[:, 0:128], kvn[:], [B_kvn], [Bsm])
              cp("act", kvnT[:], psm[:, 0:128], [Bsm], [B_kvnT])
              for hf in range(2):
                  mm(big[:, hf * 512:(hf + 1) * 512], kvnT[:], w_kvb[:, hf * 512:(hf + 1) * 512], True, True,
                     [B_kvnT, B_W], [B_big])
              kv3 = big[:].rearrange("p (h d) -> p h d", h=8)
              cp("act", vst[:, :, 0:64], kv3[:, :, 64:128], [B_big], [B_vst])
              cp("pool", vst[:, :, 64:128], vvalid[:, n:n + 1].unsqueeze(2).to_broadcast([128, 8, 64]), [B_C], [B_vst])
              P.dma(v_scr[n].rearrange("p h d -> p (h d)"), vst[:].rearrange("p h d -> p (h d)"), reads=[B_vst], slot="vst")
              act(sqk[:, :, 0:64], kv3[:, :, 0:64], AF.Square, [B_big], [B_sqk])
              red("dve", s8a[:], sqk[:, :, 0:64], [B_sqk], [B_s8a])
              act(junk[:, 0:32], kvp[:, 128:160], AF.Square, [Bkvp], [B_junk, B_st1], accum=st1[:, 0:1])
              ts("dve", s8a[:], s8a[:], st1[:, 0:1], ALU.add, [B_s8a, B_st1], [B_s8a])
              rstd(s8a[:], s8c[:], 1.0 / 96, B_s8a, B_s8c, s8b[:], B_s8b)
              tt("dve", ktmp[:, :, 0:64], kv3[:, :, 0:64], s8c[:].unsqueeze(2).to_broadcast([128, 8, 64]), ALU.mult,
                 [B_big, B_s8c], [B_ktmp])
              tt("pool", kt[:, :, 0:64], ktmp[:, :, 0:64], gk_b[:, 0:64].unsqueeze(1).to_broadcast([128, 8, 64]),
                 ALU.mult, [B_ktmp, B_W], [B_kt])
              tt("dve", kr[:, 0, :], kvp[:, 128:160], gk_b[:, 64:96], ALU.mult, [Bkvp, B_W], [B_kr])
              tt("pool", rt1[:, 0, :], kr[:, 0, :], cs[:], ALU.mult, [B_kr, Bcs], [B_rt1])
              tt("pool", rt2[:, 0, 0:16], kr[:, 0, 16:32], sn[:, 0:16], ALU.mult, [B_kr, Bcs], [B_rt2])
              tt("pool", rt2[:, 0, 16:32], kr[:, 0, 0:16], sn[:, 16:32], ALU.mult, [B_kr, Bcs], [B_rt2])
              tt("pool", krr[:], rt1[:, 0, :], rt2[:, 0, :], ALU.add, [B_rt1, B_rt2], [B_krr])
              tt("dve", kt[:, :, 64:96], krr[:].unsqueeze(1).to_broadcast([128, 8, 32]),
                 s8c[:].unsqueeze(2).to_broadcast([128, 8, 32]), ALU.mult, [B_krr, B_s8c], [B_kt])
              ptr = bf(bk5).rearrange("p (h t) -> p h t", h=8)
              Btr = B_bk[id(bk5)]
              for h in range(8):
                  tr(ptr[0:96, h, :], kt[:, h, :], [B_kt], [Btr])
              cp("act", kTst[0:96], ptr[0:96], [Btr], [B_kTst])
              P.dma(kT_scr.rearrange("h d t -> d h t")[:, :, n * 128:(n + 1) * 128], kTst[0:96], reads=[B_kTst], slot="kTst")
              skp = kvp[:, 160:288].rearrange("p (g d) -> p g d", g=2)
              act(sqk[:, 0:2, 0:64], skp, AF.Square, [Bkvp], [B_sqk])
              red("dve", s8a[:, 0:2], sqk[:, 0:2, 0:64], [B_sqk], [B_s8a])
              rstd(s8a[:, 0:2], s8c[:, 0:2], 1.0 / 64, B_s8a, B_s8c, s8b[:, 0:2], B_s8b)
              tt("dve", ks1[:], skp, s8c[:, 0:2].unsqueeze(2).to_broadcast([128, 2, 64]), ALU.mult, [Bkvp, B_s8c], [B_ks1])
              tt("pool", ksd[:, :, 0, :], ks1[:], gsk_b[:].unsqueeze(1).to_broadcast([128, 2, 64]), ALU.mult,
                 [B_ks1, B_W], [B_ksd])
              for g in range(2):
                  tr(psm[0:64, 128 + g * 128:256 + g * 128], ksd[:, g, 0, :], [B_ksd], [Bsm])
              cp("act", skT[kind][0:64], psm[0:64, 128:384].rearrange("p (g t) -> p g t", g=2), [Bsm], [B_skT[kind]])
              cp("act", sv[kind][:, :, 0:64], kvp[:, 288:416].rearrange("p (g d) -> p g d", g=2), [Bkvp], [B_sv[kind]])
              cp("pool", sv[kind][:, :, 64:128], vvalid[:, n:n + 1].unsqueeze(2).to_broadcast([128, 2, 64]), [B_C],
                 [B_sv[kind]])
              if not own:
                  continue
              hTo = hT["O"]; BhTo = B_hT["O"]
              qp = bk6; Bqp = B_bk[id(bk6)]
              for kc in range(8):
                  mm(qp[:, 0:256], hTo[:, kc, 2:130], w_in[:, kc, C_QL:C_QL + 256], kc == 0, kc == 7, [BhTo, B_W], [Bqp])
              act(junk[:, 0:256], qp[:, 0:256], AF.Square, [Bqp], [B_junk, B_st1], accum=st1[:, 0:1])
              rstd(st1[:, 0:1], rstdx[:, 0:1], 1.0 / 256, B_st1, B_rx, st1b[:, 0:1], B_st1b)
              ts("dve", qln[:], qp[:, 0:256], rstdx[:, 0:1], ALU.mult, [Bqp, B_rx], [B_qln])
              for c in range(2):
                  tr(psm[:, 384 + c * 128:512 + c * 128], qln[:, c * 128:(c + 1) * 128], [B_qln], [Bsm])
              cp("act", qlnT[:], psm[:, 384:640].rearrange("p (c t) -> p c t", c=2), [Bsm], [B_qlnT])
              for (c0, c1) in ((0, 512), (512, 768)):
                  for c in range(2):
                      mm(big[:, c0:c1], qlnT[:, c, :], w_qb[:, c, c0:c1], c == 0, c == 1, [B_qlnT, B_W], [B_big])
              q3 = big[:, 0:768].rearrange("p (h d) -> p h d", h=8)
              act(sqk[:], q3, AF.Square, [B_big], [B_sqk])
              red("dve", s8a[:], sqk[:], [B_sqk], [B_s8a])
              rstd(s8a[:], s8c[:], 1.0 / 96, B_s8a, B_s8c, s8b[:], B_s8b)
              tt("dve", ktmp[:], q3, s8c[:].unsqueeze(2).to_broadcast([128, 8, 96]), ALU.mult, [B_big, B_s8c], [B_ktmp])
              tt("pool", qt[:, :, 0:64], ktmp[:, :, 0:64], gq_b[:, 0:64].unsqueeze(1).to_broadcast([128, 8, 64]),
                 ALU.mult, [B_ktmp, B_W], [B_qt])
              tt("pool", kr[:], ktmp[:, :, 64:96], gq_b[:, 64:96].unsqueeze(1).to_broadcast([128, 8, 32]), ALU.mult,
                 [B_ktmp, B_W], [B_kr])
              tt("dve", rt1[:], kr[:], cs[:].unsqueeze(1).to_broadcast([128, 8, 32]), ALU.mult, [B_kr, Bcs], [B_rt1])
              tt("pool", rt2[:, :, 0:16], kr[:, :, 16:32], sn[:, 0:16].unsqueeze(1).to_broadcast([128, 8, 16]), ALU.mult,
                 [B_kr, Bcs], [B_rt2])
              tt("pool", rt2[:, :, 16:32], kr[:, :, 0:16], sn[:, 16:32].unsqueeze(1).to_broadcast([128, 8, 16]), ALU.mult,
                 [B_kr, Bcs], [B_rt2])
              tt("dve", qt[:, :, 64:96], rt1[:], rt2[:], ALU.add, [B_rt1, B_rt2], [B_qt])
              for h in range(8):
                  tr(ptr[0:96, h, :], qt[:, h, :], [B_qt], [Btr])
              cp("act", qTst[0:96], ptr[0:96], [Btr], [B_qTst])
              P.dma(qT_scr.rearrange("h d t -> d h t")[:, :, i * 128:(i + 1) * 128], qTst[0:96], reads=[B_qTst], slot="qTst")
              sqp = bk7; Bsqp = B_bk[id(bk7)]
              for kc in range(8):
                  mm(sqp[:], hTo[:, kc, 2:130], w_in[:, kc, C_SQ:C_SQ + 512], kc == 0, kc == 7, [BhTo, B_W], [Bsqp])
              sq3 = sqp[:].rearrange("p (h d) -> p h d", h=8)
              act(sqk[:, :, 0:64], sq3, AF.Square, [Bsqp], [B_sqk])
              red("dve", s8a[:], sqk[:, :, 0:64], [B_sqk], [B_s8a])
              rstd(s8a[:], s8c[:], 1.0 / 64, B_s8a, B_s8c, s8b[:], B_s8b)
              tt("dve", sqn[:], sq3, s8c[:].unsqueeze(2).to_broadcast([128, 8, 64]), ALU.mult, [Bsqp, B_s8c], [B_sqn])
              tt("pool", sqb[:], sqn[:], gsq_b[:].unsqueeze(1).to_broadcast([128, 8, 64]), ALU.mult, [B_sqn, B_W], [B_sqb])
              for h in range(8):
                  tr(ph[0:64, h * 128:(h + 1) * 128], sqb[:, h, :], [B_sqb], [B_bk[id(bk0)]])
              cp("act", sqT[0:64], ph[0:64, :].rearrange("p (c t) -> p c t", c=8), [B_bk[id(bk0)]], [B_sqT])
              Obk = [bk1, bk5]
              Sbk = [bk6, bk7]
              ns = 0
              for g in range(2):
                  Ops = Obk[g]; BO = B_bk[id(Ops)]
                  for kb, kk in enumerate(("P", "O")):
                      Sps = Sbk[ns % 2]; BS = B_bk[id(Sps)]
                      for e4 in range(4):
                          h = 4 * g + e4
                          hb = (h % 2) * 64
                          mm(Sps[:, e4 * 128:(e4 + 1) * 128], skT[kk][0:64, g, :], sqT[0:64, h, :],
                             True, True, [B_skT[kk], B_sqT], [BS])
                      E = Ebuf[ns % 2]; BE = B_E[ns % 2]
                      act(E[:], Sps[:], AF.Exp, [BS], [BE], scale=0.125)
                      pm = Pm[ns % 2]; BP = B_Pm[ns % 2]
                      tt("dve", pm[:], E[:], swaM[:, kb, 4 * g:4 * g + 4, :].rearrange("p h q -> p (h q)"), ALU.mult,
                         [BE, B_C], [BP])
                      mm(Ops[:], sv[kk][:, g, :], pm[:], kb == 0, kb == 1, [B_sv[kk], BP], [BO])
                      ns += 1
                  tt("dve", dtot[64:128, :].rearrange("p (h q) -> p h q", h=4),
                     Ops[64:128, :].rearrange("p (h q) -> p h q", h=4),
                     esink[64:128, 4 * g:4 * g + 4].unsqueeze(2).to_broadcast([64, 4, 128]), ALU.add, [BO, B_W], [B_dtot])
                  rcp(rden[64:128, :], dtot[64:128, :], [B_dtot], [B_rden])
                  for e4 in range(4):
                      h = 4 * g + e4
                      hb = (h % 2) * 64
                      tt("dve", tsw[hb:hb + 64, h // 2, :], Ops[0:64, e4 * 128:(e4 + 1) * 128],
                         rden[64:128, e4 * 128:(e4 + 1) * 128], ALU.mult, [BO, B_rden], [B_tsw])
              fmb = [bk6, bk7, bk0, bk2]
              nf = 0

              def fm_mm(ps_ap, col0, lo, hi, Bps):
                  for kc in range(8):
                      mm(ps_ap, w_in[:, kc, col0:col0 + 128], hTo[:, kc, lo:hi], kc == 0, kc == 7, [BhTo, B_W], [Bps])

              gs_ps = fmb[nf % 4]; nf += 1
              for c in range(4):
                  fm_mm(gs_ps[:, c * 128:(c + 1) * 128], C_GS + c * 128, 2, 130, B_bk[id(gs_ps)])
              act(sgs[:].rearrange("p c t -> p (c t)"), gs_ps[:], AF.Silu, [B_bk[id(gs_ps)]], [B_sgs])
              tt("pool", yT[:, 4:8, :], tsw[:], sgs[:], ALU.mult, [B_tsw, B_sgs], [B_yT])
              gm_ps = fmb[nf % 4]; nf += 1
              for c in range(4):
                  fm_mm(gm_ps[:, c * 128:(c + 1) * 128], C_GM + c * 128, 2, 130, B_bk[id(gm_ps)])
              act(sgm[:].rearrange("p c t -> p (c t)"), gm_ps[:], AF.Silu, [B_bk[id(gm_ps)]], [B_sgm])
              P.dma(sg_scr.rearrange("c p t -> p c t")[:, :, i * 128:(i + 1) * 128], sgm[:], reads=[B_sgm], slot="sgm")
              for cc in range(4):
                  X = fmb[nf % 4]; nf += 1
                  Y = fmb[nf % 4]; nf += 1
                  BX = B_bk[id(X)]; BY = B_bk[id(Y)]
                  fm_mm(X[:, 0:130], C_CH + cc * 128, 0, 130, BX)
                  fm_mm(X[:, 130:260], C_CC + cc * 128, 0, 130, BX)
                  fm_mm(Y[:, 0:128], C_CB + cc * 128, 2, 130, BY)
                  fm_mm(Y[:, 128:256], C_GC + cc * 128, 2, 130, BY)
                  cp("act", ch_sb[:], X[:, 0:130], [BX], [B_ch])
                  tt("dve", u_sb[:], ch_sb[:], X[:, 130:260], ALU.mult, [B_ch, BX], [B_u])
                  ts("pool", cy[0][:], u_sb[:, 2:130], convw[:, 2, cc:cc + 1], ALU.mult, [B_u, B_W], [B_cy[0]])
                  stt("dve", cy[1][:], u_sb[:, 1:129], convw[:, 1, cc:cc + 1], cy[0][:], ALU.mult, ALU.add,
                      [B_u, B_W, B_cy[0]], [B_cy[1]])
                  stt("dve", cy[0][:], u_sb[:, 0:128], convw[:, 0, cc:cc + 1], cy[1][:], ALU.mult, ALU.add,
                      [B_u, B_W, B_cy[1]], [B_cy[0]])
                  act(sgc[:], Y[:, 128:256], AF.Silu, [BY], [B_sgc])
                  tt("dve", tcb[:], Y[:, 0:128], sgc[:], ALU.mult, [BY, B_sgc], [B_tcb])
                  tt("dve", yT[:, cc, :], cy[0][:], tcb[:], ALU.mult, [B_cy[0], B_tcb], [B_yT])
              for hf in range(2):
                  for kc in range(8):
                      mm(big[:, hf * 512:(hf + 1) * 512], yT[:, kc, :], w_out[:, 4 + kc, hf * 512:(hf + 1) * 512],
                         kc == 0, kc == 7, [B_yT, B_W], [B_big])
              tt("dve", osb[:], big[:], xs[:], ALU.add, [B_big, Bx], [B_osb])
              P.dma(part_scr[i], osb[:], reads=[B_osb], slot="part")

          if dbg == "p1":
              raise _Stop()
          P.barrier()
          ar.off = 0
          kTp = A([128, 2, NS * 128], BF16)
          vp = A([128, NS, 2, 128], BF16)
          qTp = A([128, 2, NB * 128], BF16)
          maskM = A([128, 8, 512], BF16); B_mask = P.buf("mask")
          PT = [A([128, 512], BF16) for _ in range(4)]; B_PT = [P.buf() for _ in range(4)]
          gbuf = [A([128, 512], BF16) for _ in range(2)]; B_g = [P.buf(), P.buf()]
          ymla = A([128, 4, NB * 128], BF16); B_ym = [P.buf(f"ym{j}") for j in range(8)]
          rden2 = A([128, 512], F32); B_rd2 = P.buf()
          tn = A([128, 512], F32); B_tn = P.buf()
          ptile = [A([128, D], F32) for _ in range(2)]; B_pt = [P.buf(), P.buf()]
          osb2 = [A([128, D], F32) for _ in range(2)]; B_o2 = [P.buf(), P.buf()]
          B_kc = [P.buf(f"kTc{c}") for c in range(8)]
          B_vc = [P.buf(f"vc{c}") for c in range(8)]
          B_q = P.buf("qTp")
          P.dma(maskM[:], maskM_d, writes=[B_mask], slot="c0")
          SC = 96 ** -0.5
          Sb = [bk0, bk1, bk2]
          Ob = [bk5, bk6]
          nO = 0
          nS = 0
          for hp in range(4):
              for c in range(8):
                  for hh in range(2):
                      P.dma(kTp[0:96, hh, c * 1024:(c + 1) * 1024], kT_scr[2 * hp + hh, :, c * 1024:(c + 1) * 1024],
                            writes=[B_kc[c]], slot=f"kl{hh}")
                  P.dma(vp[:, 8 * c:8 * c + 8, :, :], v_scr[8 * c:8 * c + 8, :, 2 * hp:2 * hp + 2, :].rearrange("s p h d -> p s h d"),
                        writes=[B_vc[c]], slot="vl")
                  if c == 0:
                      for hh in range(2):
                          P.dma(qTp[0:96, hh, :], qT_scr[2 * hp + hh], writes=[B_q], slot=f"ql{hh}")
              for j in range(8):
                  gb = gbuf[j % 2]; Bg = B_g[j % 2]
                  P.dma(gb[:], sg_scr[hp, :, j * 512:(j + 1) * 512], writes=[Bg], slot=f"gl{j % 2}")
                  for hh in range(2):
                      Ops = Ob[nO % 2]; BO = B_bk[id(Ops)]
                      nO += 1
                      nsl = 8 * j + 8
                      LA = 2
                      pend = {}
                      for t in range(nsl + LA):
                          if t < nsl:
                              Sps = Sb[nS % 3]; BS = B_bk[id(Sps)]
                              nS += 1
                              mm(Sps[:], kTp[0:96, hh, t * 128:(t + 1) * 128], qTp[0:96, hh, j * 512:(j + 1) * 512],
                                 True, True, [B_kc[t // 8], B_q], [BS])
                              pend[t] = (Sps, BS)
                          if t >= LA:
                              sl = t - LA
                              Sps, BS = pend.pop(sl)
                              pt = PT[sl % 4]; BP = B_PT[sl % 4]
                              act(pt[:], Sps[:], AF.Exp, [BS], [BP], scale=SC)
                              if sl >= 8 * j:
                                  tt("pool", pt[:], pt[:], maskM[:, sl - 8 * j, :], ALU.mult, [BP, B_mask], [BP])
                              mm(Ops[:], vp[:, sl, hh, :], pt[:], sl == 0, sl == nsl - 1, [B_vc[sl // 8], BP], [BO])
                      hb = hh * 64
                      rcp(rden2[64:128, :], Ops[64:128, :], [BO], [B_rd2])
                      tt("dve", tn[hb:hb + 64, :], Ops[0:64, :], rden2[64:128, :], ALU.mult, [BO, B_rd2], [B_tn])
                      tt("pool", ymla[hb:hb + 64, hp, j * 512:(j + 1) * 512], tn[hb:hb + 64, :], gb[hb:hb + 64, :], ALU.mult,
                         [B_tn, Bg], [B_ym[j]])

          if dbg == "p2":
              raise _Stop()
          def load_pt(i):
              P.dma(ptile[i % 2][:], part_scr[i], writes=[B_pt[i % 2]], slot=f"pl{i % 2}")

          load_pt(0)
          for i in range(NB):
              if i + 1 < NB:
                  load_pt(i + 1)
              for hf in range(2):
                  for c in range(4):
                      mm(big[:, hf * 512:(hf + 1) * 512], ymla[:, c, i * 128:(i + 1) * 128],
                         w_out[:, c, hf * 512:(hf + 1) * 512], c == 0, c == 3, [B_ym[i // 4], B_W], [B_big])
              o2 = osb2[i % 2]; Bo2 = B_o2[i % 2]
              tt("dve", o2[:], big[:], ptile[i % 2][:], ALU.add, [B_big, B_pt[i % 2]], [Bo2])
              P.dma(ps["out"](i), o2[:], reads=[Bo2], slot=f"yo{i % 2}")
      except _Stop:
        pass
    print("arena p1 end", p1_end, "total ops recorded", P.total, {e: len(v) for e, v in P.ops.items()})
    P.emit(final_slots=[s_ for s_ in P.slot_counts])
    P.close()
    return nc


def _consts(r):
    half = 16
    inv_freq = np.power(np.float32(10000.0), -np.arange(half, dtype=np.float32) / half).astype(np.float32)
    pidx = np.arange(128, dtype=np.float32)
    cosT = np.zeros((128, NS, 32), np.float32)
    sinT = np.zeros((128, NS, 32), np.float32)
    for s in range(NS):
        gb = s - 1 + r
        pos = (gb * 128 + pidx).astype(np.float32)
        ang = pos[:, None] * inv_freq[None, :]
        c = np.cos(ang).astype(np.float32)
        sn = np.sin(ang).astype(np.float32)
        cosT[:, s, :16] = c
        cosT[:, s, 16:] = c
        sinT[:, s, :16] = -sn
        sinT[:, s, 16:] = sn
    maskM = np.zeros((128, 8, 512), np.float32)
    ki = np.arange(128)[:, None]
    qi = np.arange(128)[None, :]
    tri = (ki <= qi).astype(np.float32)
    for so in range(8):
        for qb in range(4):
            d = 2 * qb + 1
            if so < d:
                maskM[:, so, qb * 128:(qb + 1) * 128] = 1.0
            elif so == d:
                maskM[:, so, qb * 128:(qb + 1) * 128] = tri
    slopes = np.exp2(-8.0 * np.arange(1, 9, dtype=np.float32) / 8).astype(np.float32)
    swaM = np.zeros((128, 2, 8, 128), np.float32)
    for h in range(8):
        d0 = (128 + qi - ki).astype(np.float32)
        swaM[:, 0, h, :] = np.where(d0 < 128, np.exp(-slopes[h] * d0), 0.0)
        d1 = (qi - ki).astype(np.float32)
        swaM[:, 1, h, :] = np.where(d1 >= 0, np.exp(-slopes[h] * np.maximum(d1, 0)), 0.0)
    vvalid = np.ones((128, NS), np.float32)
    if r == 0:
        vvalid[:, 0] = 0.0
    return {"cosT": cosT, "sinT": sinT, "maskM": maskM.astype(ml_dtypes.bfloat16), "swaM": swaM, "vvalid": vvalid}


def _layer_weights(inp, l, suffix):
    f = lambda a: np.ascontiguousarray(a, dtype=np.float32)
    return {
        f"w_in_{suffix}": f(inp["w_in"][l][:, PERM]), f"norm_g_{suffix}": f(inp["norm_g"][l].reshape(8, 128).T),
        f"w_kvb_{suffix}": f(inp["mla_w_kvb"][l]), f"kva_g_{suffix}": f(inp["mla_kv_a_norm"][l].reshape(128, 1)),
        f"w_qb_{suffix}": f(inp["mla_w_qb"][l]), f"qa_g_{suffix}": f(inp["mla_q_a_norm"][l].reshape(2, 128).T),
        f"q_g_{suffix}": f(inp["mla_q_norm"][l]), f"k_g_{suffix}": f(inp["mla_k_norm"][l]),
        f"conv_w_{suffix}": f(inp["conv_w"][l].reshape(3, 4, 128).transpose(2, 0, 1)), f"sq_g_{suffix}": f(inp["swa_q_norm"][l]),
        f"sk_g_{suffix}": f(inp["swa_k_norm"][l]), f"sinks_{suffix}": f(inp["swa_sinks"][l]),
        f"w_out_{suffix}": f(inp["w_out"][l]),
    }


def _shard_x(x):
    xb = x.reshape(4, 64, 128, D)
    outs = []
    for c in range(8):
        b, r = c // 2, c % 2
        own = xb[b, r::2]
        prev = np.zeros_like(own)
        if r == 0:
            prev[1:] = xb[b, 1:63:2]
        else:
            prev[:] = xb[b, 0::2]
        outs.append((np.ascontiguousarray(own), np.ascontiguousarray(prev)))
    return outs


def _gather(res):
    out = np.zeros((4, 64, 128, D), np.float32)
    for c in range(8):
        b, r = c // 2, c % 2
        out[b, r::2] = res[c]["y_out"]
    return out.reshape(4, 8192, D)


_CACHE = {}
FUSED = True


def kernel(**inp):
    inp = {k: np.asarray(v) for k, v in inp.items()}
    x = np.ascontiguousarray(inp["x"], dtype=np.float32)
    depth = inp["w_in"].shape[0]
    consts = [_consts(r) for r in range(2)]
    if FUSED and depth == 2:
        if "nc2" not in _CACHE:
            _CACHE["nc2"] = build_program(2)
        nc = _CACHE["nc2"]
        w = {}
        for l in range(2):
            w.update(_layer_weights(inp, l, str(l)))
        sh = _shard_x(x)
        in_maps = []
        for c in range(8):
            r = c % 2
            o = c + 1 - 2 * r
            m = {"x_own": sh[c][0], "x_prev": sh[c][1], "x_own2": sh[o][0], "x_prev2": sh[o][1]}
            m.update(w)
            m.update(consts[r])
            co = consts[1 - r]
            m["cosT2"] = co["cosT"]; m["sinT2"] = co["sinT"]; m["vvalid2"] = co["vvalid"]
            bw = np.zeros((128, 2), np.float32)
            bw[:, r] = 1.0
            m["blendw"] = bw
            in_maps.append(m)
        res = run_bass_kernel_spmd(nc, in_maps, core_ids=list(range(8)))
        return _gather(res.results).astype(np.float32)
    if "nc1" not in _CACHE:
        _CACHE["nc1"] = build_program(1)
    for l in range(depth):
        nc = _CACHE["nc1"]
        w = _layer_weights(inp, l, "0")
        sh = _shard_x(x)
        in_maps = []
        for c in range(8):
            m = {"x_own": sh[c][0], "x_prev": sh[c][1]}
            m.update(w)
            m.update(consts[c % 2])
            in_maps.append(m)
        res = run_bass_kernel_spmd(nc, in_maps, core_ids=list(range(8)))
        x = _gather(res.results)
    return x.astype(np.float32)
```

```python
import numpy as np
import ml_dtypes
from contextlib import ExitStack
import concourse.bass as bass
import concourse.mybir as mybir
from concourse.bass_utils import run_bass_kernel_spmd

F32 = mybir.dt.float32
BF16 = mybir.dt.bfloat16
ALU = mybir.AluOpType
AF = mybir.ActivationFunctionType
AX = mybir.AxisListType

NB = 32
NS = 64
D = 1024
NCOL = 4256
EPS = 1e-6
C_KVL, C_KR, C_SK, C_SV, C_QL, C_SQ, C_GM, C_GS, C_CH, C_CC, C_CB, C_GC = (
    0, 128, 160, 288, 416, 672, 1184, 1696, 2208, 2720, 3232, 3744)
PERM = np.concatenate([np.arange(256, 384), np.arange(384, 416), np.arange(3488, 3616), np.arange(3616, 3744),
                       np.arange(0, 256), np.arange(2976, 3488), np.arange(416, 928), np.arange(3744, 4256),
                       np.arange(928, 1440), np.arange(1952, 2464), np.arange(1440, 1952), np.arange(2464, 2976)])

COMPUTE = ("pe", "act", "dve", "pool")
ALLENG = COMPUTE + ("sp",)


class Buf:
    __slots__ = ("name", "lw", "rd")

    def __init__(self, name):
        self.name = name
        self.lw = None
        self.rd = []


class Op:
    __slots__ = ("eng", "fn", "waits", "signal", "dma", "slot", "slot_cnt")

    def __init__(self, eng, fn, dma=False, slot=None):
        self.eng = eng
        self.fn = fn
        self.waits = []
        self.signal = False
        self.dma = dma
        self.slot = slot
        self.slot_cnt = 0


class Prog:
    def __init__(self, nc):
        self.nc = nc
        self.ops = {e: [] for e in ALLENG}
        self.seen = {e: {} for e in ALLENG}
        self.pending = {e: [] for e in ALLENG}
        self.slot_counts = {}
        self.stack = ExitStack()
        self.nbuf = 0

    def sbuf(self, name, shape, dtype):
        return self.stack.enter_context(self.nc.sbuf_tensor("sb_" + name, list(shape), dtype))

    def psum(self, name, shape, dtype):
        return self.stack.enter_context(self.nc.psum_tensor("ps_" + name, list(shape), dtype))

    def buf(self, name=None):
        self.nbuf += 1
        return Buf(name or f"b{self.nbuf}")

    def _dep(self, op, eng, key):
        kind, k, v = key
        if kind == "eng" and k == "pe" and eng == "pe":
            return
        seen = self.seen[eng]
        if seen.get((kind, k), -1) >= v:
            return
        seen[(kind, k)] = v
        op.waits.append(key)
        if kind == "eng":
            self.ops[k][v].signal = True

    limit = None
    total = 0
    trace = None

    def add(self, eng, fn, reads=(), writes=(), dma=False, slot=None):
        self.total += 1
        if self.limit is not None and self.total > self.limit:
            return None
        op = Op(eng, fn, dma=dma, slot=slot)
        if self.trace is not None:
            import sys as _s
            f = _s._getframe(1)
            while f is not None and f.f_code.co_name != "build_program":
                f = f.f_back
            self.trace.append((self.total, eng, f.f_lineno if f else -1))
        idx = len(self.ops[eng])
        for key in self.pending[eng]:
            self._dep(op, eng, key)
        self.pending[eng] = []
        if dma:
            cnt = self.slot_counts.get(slot, 0)
            if cnt > 0:
                self._dep(op, eng, ("slot", slot, cnt))
            self.slot_counts[slot] = cnt + 1
            op.slot_cnt = cnt + 1
            me = ("slot", slot, cnt + 1)
        else:
            me = ("eng", eng, idx)
        for b in reads:
            if b.lw is not None:
                self._dep(op, eng, b.lw)
        for b in writes:
            if b.lw is not None:
                self._dep(op, eng, b.lw)
            for r in b.rd:
                self._dep(op, eng, r)
        for b in reads:
            b.rd.append(me)
        for b in writes:
            b.lw = me
            b.rd = []
        self.ops[eng].append(op)
        return op

    def barrier(self):
        keys = []
        for e in ALLENG:
            for i in range(len(self.ops[e]) - 1, -1, -1):
                if not self.ops[e][i].dma:
                    keys.append(("eng", e, i))
                    break
        for s, c in self.slot_counts.items():
            keys.append(("slot", s, c))
        for e in ALLENG:
            self.pending[e] = self.pending[e] + [k for k in keys if not (k[0] == "eng" and k[1] == e)]

    def dma(self, out, in_, reads=(), writes=(), slot="d0", eng="sp"):
        return self.add(eng, lambda e: e.dma_start(out=out, in_=in_), reads, writes, dma=True, slot=slot)

    def emit(self, final_slots=()):
        nc = self.nc
        st = self.stack
        esem = {e: st.enter_context(nc.semaphore(f"s_{e}")) for e in ALLENG}
        ssem = {s: st.enter_context(nc.semaphore(f"d_{s}")) for s in self.slot_counts}
        cnt = {}
        for e, lst in self.ops.items():
            c = 0
            for i, op in enumerate(lst):
                if op.signal and not op.dma:
                    c += 1
                cnt[(e, i)] = c

        def run(engname, handle):
            for op in self.ops[engname]:
                for kind, k, v in op.waits:
                    if kind == "eng":
                        handle.wait_ge(esem[k], cnt[(k, v)])
                    else:
                        handle.wait_ge(ssem[k], 16 * v)
                ins = op.fn(handle)
                if op.dma:
                    ins.then_inc(ssem[op.slot], 16)
                elif op.signal:
                    ins.then_inc(esem[engname], 1)
            if engname == "sp":
                for s in final_slots:
                    handle.wait_ge(ssem[s], 16 * self.slot_counts[s])

        block = st.enter_context(nc.Block())

        @block.sync
        def _(e):
            run("sp", e)

        @block.tensor
        def _(e):
            run("pe", e)

        @block.scalar
        def _(e):
            run("act", e)

        @block.vector
        def _(e):
            run("dve", e)

        @block.gpsimd
        def _(e):
            run("pool", e)

    def close(self):
        self.stack.close()


WNAMES = ["w_in", "norm_g", "w_kvb", "kva_g", "w_qb", "qa_g", "q_g", "k_g", "conv_w", "sq_g", "sk_g",
          "sinks", "w_out"]
WSHAPES = {"w_in": [D, NCOL], "norm_g": [128, 8], "w_kvb": [128, 1024], "kva_g": [128, 1], "w_qb": [256, 768],
           "qa_g": [128, 2], "q_g": [96], "k_g": [96], "conv_w": [128, 3, 4], "sq_g": [64], "sk_g": [64],
           "sinks": [8], "w_out": [1536, 1024]}


class _Stop(Exception):
    pass


def build_program(nlayers=1, dbg=None):
    nc = bass.Bass("TRN2", target_bir_lowering=False)
    P = Prog(nc)
    import os
    if os.environ.get("K_LIMIT"):
        P.limit = int(os.environ["K_LIMIT"])

    def dram_in(name, shape, dt=F32):
        return nc.dram_tensor(name, list(shape), dt, kind="ExternalInput").ap()

    x_own = dram_in("x_own", [NB, 128, D])
    x_prev = dram_in("x_prev", [NB, 128, D])
    Wd = [{n: dram_in(f"{n}_{l}", WSHAPES[n]) for n in WNAMES} for l in range(nlayers)]
    cosT_d = dram_in("cosT", [128, NS, 32])
    sinT_d = dram_in("sinT", [128, NS, 32])
    maskM_d = dram_in("maskM", [128, 8, 512], BF16)
    swaM_d = dram_in("swaM", [128, 2, 8, 128])
    vvalid_d = dram_in("vvalid", [128, NS])
    y_out = nc.dram_tensor("y_out", [NB, 128, D], F32, kind="ExternalOutput").ap()
    fused = nlayers == 2
    if not fused:
        passes = [dict(W=Wd[0], own=lambda i: x_own[i], prev=lambda i: x_prev[i], cos=cosT_d, sin=sinT_d,
                       vv=vvalid_d, out=lambda i: y_out[i], blend=False)]
    else:
        x_own2 = dram_in("x_own2", [NB, 128, D])
        x_prev2 = dram_in("x_prev2", [NB, 128, D])
        cosT2_d = dram_in("cosT2", [128, NS, 32])
        sinT2_d = dram_in("sinT2", [128, NS, 32])
        vvalid2_d = dram_in("vvalid2", [128, NS])
        blendw_d = dram_in("blendw", [128, 2])
        x1_mine = nc.dram_tensor("x1_mine", [NB, 128, D], F32, kind="ExternalOutput").ap()
        Zb = nc.dram_tensor("x1_other", [NB + 1, 128, D], F32, kind="ExternalOutput").ap()
        passes = [
            dict(W=Wd[0], own=lambda i: x_own[i], prev=lambda i: x_prev[i], cos=cosT_d, sin=sinT_d, vv=vvalid_d,
                 out=lambda i: x1_mine[i], blend=False),
            dict(W=Wd[0], own=lambda i: x_own2[i], prev=lambda i: x_prev2[i], cos=cosT2_d, sin=sinT2_d, vv=vvalid2_d,
                 out=lambda i: Zb[i + 1], blend=False),
            dict(W=Wd[1], own=lambda i: x1_mine[i], prev=lambda i: Zb[i], prev2=lambda i: Zb[i + 1], cos=cosT_d,
                 sin=sinT_d, vv=vvalid_d, out=lambda i: y_out[i], blend=True),
        ]

    skind = "ExternalOutput"
    kT_scr = nc.dram_tensor("kT_scr", [8, 96, NS * 128], BF16, kind=skind).ap()
    v_scr = nc.dram_tensor("v_scr", [NS, 128, 8, 128], BF16, kind=skind).ap()
    qT_scr = nc.dram_tensor("qT_scr", [8, 96, NB * 128], BF16, kind=skind).ap()
    sg_scr = nc.dram_tensor("sg_scr", [4, 128, NB * 128], BF16, kind=skind).ap()
    part_scr = nc.dram_tensor("part_scr", [NB, 128, D], F32, kind=skind).ap()

    ident = P.sbuf("ident", [128, 128], BF16); B_ident = P.buf("ident")
    idf = P.sbuf("idf", [128, 128], F32)
    w_kvb = P.sbuf("w_kvb", [128, 1024], BF16)
    w_qb = P.sbuf("w_qb", [128, 2, 768], BF16)
    w_out = P.sbuf("w_out", [128, 12, 1024], BF16)
    swaM = P.sbuf("swaM", [128, 2, 8, 128], F32)
    vvalid = P.sbuf("vvalid", [128, NS], F32)
    gq_b = P.sbuf("gq_b", [128, 96], F32)
    gk_b = P.sbuf("gk_b", [128, 96], F32)
    gsq_b = P.sbuf("gsq_b", [128, 64], F32)
    gsk_b = P.sbuf("gsk_b", [128, 64], F32)
    esink = P.sbuf("esink", [128, 8], F32)
    normg = P.sbuf("normg", [128, 8], F32)
    convw = P.sbuf("convw", [128, 3, 4], F32)
    kvag = P.sbuf("kvag", [128, 1], F32)
    qag = P.sbuf("qag", [128, 2], F32)
    eps_t = P.sbuf("eps_t", [128, 1], F32)
    B_W = P.buf("weights")
    B_C = P.buf("consts")

    ARENA_B = 167 * 1024
    arena = P.sbuf("arena", [128, ARENA_B // 2], BF16)

    class Arena:
        def __init__(self):
            self.off = 0

        def alloc(self, shape, dt, parts=128):
            n = int(np.prod(shape[1:]))
            nbytes = n * (4 if dt == F32 else 2)
            nbytes = (nbytes + 31) // 32 * 32
            o = self.off
            self.off += nbytes
            assert self.off <= ARENA_B, f"arena overflow {self.off}"
            v = arena[:, o // 2:(o + nbytes) // 2]
            if dt == F32:
                v = v.bitcast(F32)
            v = v[:, 0:n]
            if len(shape) == 3:
                v = v.rearrange("p (a b) -> p a b", a=shape[1])
            elif len(shape) == 4:
                v = v.rearrange("p (a b c) -> p a b c", a=shape[1], b=shape[2])
            return v

    banks = [P.psum(f"bank{i}", [128, 512], F32) for i in range(3)]
    big = P.psum("big", [128, 1024], F32)
    banks += [P.psum(f"bank{i}", [128, 512], F32) for i in range(5, 8)]
    bk0, bk1, bk2, bk5, bk6, bk7 = banks
    B_bk = {id(b): P.buf(f"bk{i}") for i, b in enumerate(banks)}
    B_big = P.buf("big")

    def bf(ps):
        return ps[:].bitcast(BF16)

    def tt(eng, out, in0, in1, op, R, W):
        return P.add(eng, lambda e: e.tensor_tensor(out=out, in0=in0, in1=in1, op=op), R, W)

    def ts(eng, out, in0, s1, op0, R, W, s2=None, op1=None):
        if op1 is None:
            return P.add(eng, lambda e: e.tensor_scalar(out=out, in0=in0, scalar1=s1, scalar2=None, op0=op0), R, W)
        return P.add(eng, lambda e: e.tensor_scalar(out=out, in0=in0, scalar1=s1, scalar2=s2, op0=op0, op1=op1), R, W)

    def stt(eng, out, in0, scalar, in1, op0, op1, R, W):
        return P.add(eng, lambda e: e.scalar_tensor_tensor(out=out, in0=in0, scalar=scalar, in1=in1, op0=op0, op1=op1), R, W)

    def act(out, in_, func, R, W, scale=None, bias=None, accum=None):
        kw = {}
        if scale is not None:
            kw["scale"] = scale
        if bias is not None:
            kw["bias"] = bias
        if accum is not None:
            kw["accum_out"] = accum
        return P.add("act", lambda e: e.activation(out=out, in_=in_, func=func, **kw), R, W)

    def cp(eng, out, in_, R, W):
        if eng == "act":
            return P.add("act", lambda e: e.activation(out=out, in_=in_, func=AF.Copy), R, W)
        return P.add(eng, lambda e: e.tensor_copy(out=out, in_=in_), R, W)

    def red(eng, out, in_, R, W):
        return P.add(eng, lambda e: e.tensor_reduce(out=out, in_=in_, axis=AX.X, op=ALU.add), R, W)

    def rcp(out, in_, R, W):
        return P.add("dve", lambda e: e.reciprocal(out=out, in_=in_), R, W)

    def mm(out, lhsT, rhs, start, stop, R, W):
        return P.add("pe", lambda e: e.matmul(out, lhsT=lhsT, rhs=rhs, start=start, stop=stop), R, W)

    def tr(out, in_, R, W):
        return P.add("pe", lambda e: e.transpose(out=out, in_=in_, identity=ident[:]), list(R) + [B_ident], W)

    def rstd(ss_ap, out_ap, scale, R_ss, B_out, tmp_ap, B_tmp):
        act(tmp_ap, ss_ap, AF.Sqrt, [R_ss, B_C], [B_tmp], scale=scale, bias=eps_t[:, 0:1])
        rcp(out_ap, tmp_ap, [B_tmp], [B_out])

    P.add("pool", lambda e: e.memset(idf[:], 0.0), [], [B_ident])
    P.add("pool", lambda e: e.affine_select(out=idf[:], in_=idf[:], pattern=[[-1, 128]], compare_op=ALU.not_equal,
                                            fill=1.0, base=0, channel_multiplier=1), [B_ident], [B_ident])
    P.add("pool", lambda e: e.tensor_copy(out=ident[:], in_=idf[:]), [B_ident], [B_ident])
    P.add("pool", lambda e: e.memset(eps_t[:], EPS), [], [B_C])
    P.dma(swaM[:], swaM_d, writes=[B_C], slot="c0")
    blendw = P.sbuf("blendw", [128, 2], F32)
    if fused:
        P.dma(blendw[:], blendw_d, writes=[B_C], slot="c1")
        zero_t = arena[:, 0:2 * D].bitcast(F32)
        P.add("pool", lambda e: e.memset(zero_t, 0.0), [], [B_C])
        P.dma(Zb[0], zero_t, reads=[B_C], slot="c1")

    for ps in passes:
      try:
          W = ps["W"]
          cosT_d = ps["cos"]; sinT_d = ps["sin"]
          P.barrier()
          P.dma(vvalid[:], ps["vv"], writes=[B_C], slot="c1")
          ar = Arena()
          w_in = ar.alloc([128, 8, NCOL], BF16)
          stage = [ar.alloc([128, 2128], F32) for _ in range(2)]
          B_st = [P.buf("st0"), P.buf("st1")]
          mark = ar.off
          P.dma(normg[:], W["norm_g"], writes=[B_W], slot="c0")
          P.dma(convw[:], W["conv_w"], writes=[B_W], slot="c1")
          P.dma(kvag[:], W["kva_g"], writes=[B_W], slot="c0")
          P.dma(qag[:], W["qa_g"], writes=[B_W], slot="c1")
          P.dma(gq_b[:], W["q_g"].partition_broadcast(128), writes=[B_W], slot="c0")
          P.dma(gk_b[:], W["k_g"].partition_broadcast(128), writes=[B_W], slot="c1")
          P.dma(gsq_b[:], W["sq_g"].partition_broadcast(128), writes=[B_W], slot="c0")
          P.dma(gsk_b[:], W["sk_g"].partition_broadcast(128), writes=[B_W], slot="c1")
          P.dma(esink[:], W["sinks"].partition_broadcast(128), writes=[B_W], slot="c0")
          act(esink[:], esink[:], AF.Exp, [B_W], [B_W])
          n = 0
          engs = ["dve", "pool"]
          win_d = W["w_in"].rearrange("(kc p) c -> p kc c", p=128)
          for kc in range(8):
              for hf in range(2):
                  s = n % 2
                  P.dma(stage[s][:], win_d[:, kc, hf * 2128:(hf + 1) * 2128], writes=[B_st[s]], slot=f"st{s}")
                  ts(engs[n % 2], w_in[:, kc, hf * 2128:(hf + 1) * 2128], stage[s][:], normg[:, kc:kc + 1], ALU.mult,
                     [B_st[s], B_W], [B_W])
                  n += 1
          wout_d = W["w_out"].rearrange("(kc p) c -> p kc c", p=128)
          for kc in range(12):
              s = n % 2
              P.dma(stage[s][:, 0:1024], wout_d[:, kc, :], writes=[B_st[s]], slot=f"st{s}")
              cp(engs[n % 2], w_out[:, kc, :], stage[s][:, 0:1024], [B_st[s]], [B_W])
              n += 1
          s = n % 2
          P.dma(stage[s][:, 0:1024], W["w_kvb"], writes=[B_st[s]], slot=f"st{s}")
          ts(engs[n % 2], w_kvb[:], stage[s][:, 0:1024], kvag[:, 0:1], ALU.mult, [B_st[s], B_W], [B_W])
          n += 1
          wqb_d = W["w_qb"].rearrange("(c p) n -> p c n", p=128)
          for c in range(2):
              s = n % 2
              P.dma(stage[s][:, 0:768], wqb_d[:, c, :], writes=[B_st[s]], slot=f"st{s}")
              ts(engs[n % 2], w_qb[:, c, :], stage[s][:, 0:768], qag[:, c:c + 1], ALU.mult, [B_st[s], B_W], [B_W])
              n += 1

          if dbg == "w":
              raise _Stop()
          P.barrier()
          ar.off = mark - 2 * ((2128 * 4 + 31) // 32 * 32)
          A = ar.alloc

          def AB(shape, dt, name):
            return A(shape, dt), P.buf(name)

          x_P = [AB([128, D], F32, f"xP{k}") for k in range(2)]
          x_O = [AB([128, D], F32, f"xO{k}") for k in range(2)]
          xb2, B_xb2 = AB([128, D], F32, "xb2")
          tabs = {(kd, k): (A([128, 32], F32), A([128, 32], F32), P.buf(f"tab{kd}{k}")) for kd in "PO" for k in range(2)}
          junk = A([128, D], BF16)
          h_bf = [AB([128, D], BF16, "h0")] * 2
          hT = {(kd, k): AB([128, 8, 130], BF16, f"hT{kd}{k}") for kd in "PO" for k in range(2)}
          kvps = {(kd, k): AB([128, 416], F32, f"kvps{kd}{k}") for kd in "PO" for k in range(2)}
          skT = {(kd, k): AB([128, 2, 128], BF16, f"skT{kd}{k}") for kd in "PO" for k in range(2)}
          sv = {(kd, k): AB([128, 2, 128], BF16, f"sv{kd}{k}") for kd in "PO" for k in range(2)}

          def kset(tag):
            d = {}
            d["sm"] = [AB([128, 8], F32, f"sm{tag}{k}") for k in range(8)]
            d["kvn"] = AB([128, 128], BF16, "kvn" + tag)
            d["kvnT"] = AB([128, 128], BF16, "kvnT" + tag)
            d["vst"] = AB([128, 8, 128], BF16, "vst" + tag)
            d["sqk"] = AB([128, 8, 64], F32, "sqk" + tag)
            d["kt"] = AB([128, 8, 96], BF16, "kt" + tag)
            d["kr"] = AB([128, 32], F32, "kr" + tag)
            d["rt1"] = AB([128, 32], F32, "rt1" + tag)
            d["rt2"] = AB([128, 32], F32, "rt2" + tag)
            d["krr"] = AB([128, 32], F32, "krr" + tag)
            d["kTst"] = AB([128, 8, 128], BF16, "kTst" + tag)
            d["sq2"] = AB([128, 2, 64], F32, "sq2" + tag)
            d["ks1"] = AB([128, 2, 64], F32, "ks1" + tag)
            d["ksd"] = AB([128, 2, 64], BF16, "ksd" + tag)
            return d

          KS = {"P": kset("A"), "O": kset("B")}
          FS = [AB([128, 8], F32, f"fs{k}") for k in range(4)]
          QS = [AB([128, 8], F32, f"qs{k}") for k in range(6)]
          qln, B_qln = AB([128, 256], BF16, "qln")
          qlnT, B_qlnT = AB([128, 2, 128], BF16, "qlnT")
          sqq, B_sqq = AB([128, 8, 96], F32, "sqq")
          qt, B_qt = AB([128, 8, 96], BF16, "qt")
          qr, B_qr = AB([128, 8, 32], F32, "qr")
          qr1, B_qr1 = AB([128, 8, 32], F32, "qr1")
          qr2, B_qr2 = AB([128, 8, 32], F32, "qr2")
          qTst, B_qTst = AB([128, 8, 128], BF16, "qTst")
          SS = [AB([128, 8], F32, f"ss{k}") for k in range(4)]
          sqs, B_sqs = AB([128, 8, 64], F32, "sqs")
          sqb, B_sqb = AB([128, 8, 64], BF16, "sqb")
          sqT, B_sqT = AB([128, 8, 128], BF16, "sqT")
          Ebuf, B_E = AB([128, 512], F32, "E")
          Pm = [AB([128, 512], BF16, "Pm0")] * 2
          rden, B_rden = AB([128, 512], F32, "rden")
          tsw, B_tsw = AB([128, 4, 128], F32, "tsw")
          sgs, B_sgs = AB([128, 4, 128], F32, "sgs")
          sgm, B_sgm = AB([128, 4, 128], BF16, "sgm")
          sgc, B_sgc = AB([128, 4, 128], F32, "sgc")
          ch_sb, B_ch = AB([128, 130], F32, "ch")
          u_sb, B_u = AB([128, 130], F32, "u")
          cy = [AB([128, 128], F32, f"cy{k}") for k in range(2)]
          tcb, B_tcb = AB([128, 128], F32, "tcb")
          yT, B_yT = AB([128, 8, 128], BF16, "yT")
          osb, B_osb = AB([128, D], F32, "osb")
          p1_end = ar.off

          ph = bf(bk0); B_b0 = B_bk[id(bk0)]
          B_b1 = B_bk[id(bk1)]
          psm = bf(bk2)
          B_sm = {k: P.buf("psm" + k) for k in ("kvnP", "kvnO", "kTP", "kTO", "qln")}
          SMR = {"kvnP": (0, 128), "kvnO": (128, 256), "kTP": (256, 512), "kTO": (512, 768), "qln": (768, 1024)}
          ptr = bf(bk5).rearrange("p (h t) -> p h t", h=8); B_b5 = B_bk[id(bk5)]
          B_b6 = B_bk[id(bk6)]; B_b7 = B_bk[id(bk7)]
          L0, L1, LBIG, L5 = "L0", "L1", "LBIG", "L5"

          def ACQ(l):
            return ("acq", l)

          def REL(l):
            return ("rel", l)

          def drive(chains):
            st = [dict(g=g, want=None) for g in chains]
            while st:
                progressed = False
                for c in list(st):
                    while True:
                        if c["want"] is not None:
                            kind, obj = c["want"]
                            if kind == "acq":
                                if locks.get(obj) is None:
                                    locks[obj] = c["g"]
                                elif locks[obj] is not c["g"]:
                                    break
                            elif kind == "wait":
                                if obj not in events:
                                    break
                            c["want"] = None
                        try:
                            r = next(c["g"])
                        except StopIteration:
                            st.remove(c)
                            progressed = True
                            break
                        progressed = True
                        if r is None:
                            break
                        kind, obj = r
                        if kind == "rel":
                            assert locks.get(obj) is c["g"], ("release of unowned lock", obj)
                            locks[obj] = None
                        elif kind == "set":
                            events.add(obj)
                        else:
                            c["want"] = r
                if not progressed:
                    raise RuntimeError("chain deadlock at build time")

          locks = {}
          events = set()

          def load_xP(i):
            xs, Bx = x_P[i % 2]
            P.dma(xs[:], ps["prev"](i), writes=[Bx], slot=f"xP{i % 2}")

          def load_rest(i):
            xs, Bx = x_O[i % 2]
            P.dma(xs[:], ps["own"](i), writes=[Bx], slot=f"xO{i % 2}")
            for kd, n in (("P", 2 * i), ("O", 2 * i + 1)):
                cs, sn, Bt = tabs[(kd, i % 2)]
                P.dma(cs[:], cosT_d[:, n, :], writes=[Bt], slot=f"cs{kd}{i % 2}")
                P.dma(sn[:], sinT_d[:, n, :], writes=[Bt], slot=f"sn{kd}{i % 2}")
            if ps["blend"]:
                P.dma(xb2[:], ps["prev2"](i), writes=[B_xb2], slot="xb2")

          def front(kd, i):
            par = i % 2
            xs, Bx = (x_P if kd == "P" else x_O)[par]
            hb, Bh = h_bf[0 if kd == "P" else 1]
            hTk, BhT = hT[(kd, par)]
            (s0, Bs0), (s1, Bs1), (s2, Bs2) = FS[0], FS[1], FS[2]
            if ps["blend"] and kd == "P":
                ts("dve", xs[:], xs[:], blendw[:, 0:1], ALU.mult, [Bx, B_C], [Bx])
                yield
                stt("dve", xs[:], xb2[:], blendw[:, 1:2], xs[:], ALU.mult, ALU.add, [B_xb2, B_C, Bx], [Bx])
                yield
            act(junk[:], xs[:], AF.Square, [Bx], [Bs0], accum=s0[:, 0:1])
            yield
            act(s1[:, 0:1], s0[:, 0:1], AF.Sqrt, [Bs0, B_C], [Bs1], scale=1.0 / D, bias=eps_t[:, 0:1])
            yield
            rcp(s2[:, 0:1], s1[:, 0:1], [Bs1], [Bs2])
            yield
            ts("dve", hb[:], xs[:], s2[:, 0:1], ALU.mult, [Bx, Bs2], [Bh])
            yield
            yield ACQ(L0)
            for kc in range(8):
                tr(ph[:, kc * 128:(kc + 1) * 128], hb[:, kc * 128:(kc + 1) * 128], [Bh], [B_b0])
            yield
            cp("act", hTk[:, :, 2:130], ph.rearrange("p (a b) -> p a b", a=8), [B_b0], [BhT])
            yield REL(L0)
            if kd == "O":
                hTp, BhTp = hT[("P", par)]
                cp("pool", hTk[:, :, 0:2], hTp[:, :, 128:130], [BhTp], [BhT])
                yield
            yield ACQ(L1)
            for kc in range(8):
                mm(bk1[:, 0:416], hTk[:, kc, 2:130], w_in[:, kc, 0:416], kc == 0, kc == 7, [BhT, B_W], [B_b1])
            yield
            kv_sb, Bkv = kvps[(kd, par)]
            cp("act", kv_sb[:], bk1[:, 0:416], [B_b1], [Bkv])
            yield REL(L1)

          def kmla(kd, i):
            par = i % 2
            n = 2 * i + (0 if kd == "P" else 1)
            K = KS[kd]
            kvp, Bkvp = kvps[(kd, par)]
            cs, sn, Bcs = tabs[(kd, par)]
            sm = K["sm"]
            (kvn, Bkvn), (kvnT, BkvnT), (vst, Bvst), (sqk, Bsqk), (kt, Bkt) = K["kvn"], K["kvnT"], K["vst"], K["sqk"], K["kt"]
            (kr, Bkr), (rt1, Brt1), (rt2, Brt2), (krr, Bkrr), (kTst, BkTst) = K["kr"], K["rt1"], K["rt2"], K["krr"], K["kTst"]
            act(junk[:, 0:128], kvp[:, 0:128], AF.Square, [Bkvp], [sm[0][1]], accum=sm[0][0][:, 0:1])
            yield
            act(sm[1][0][:, 0:1], sm[0][0][:, 0:1], AF.Sqrt, [sm[0][1], B_C], [sm[1][1]], scale=1.0 / 128, bias=eps_t[:, 0:1])
            yield
            rcp(sm[2][0][:, 0:1], sm[1][0][:, 0:1], [sm[1][1]], [sm[2][1]])
            yield
            ts("dve", kvn[:], kvp[:, 0:128], sm[2][0][:, 0:1], ALU.mult, [Bkvp, sm[2][1]], [Bkvn])
            yield
            act(junk[:, 128:160], kvp[:, 128:160], AF.Square, [Bkvp], [sm[3][1]], accum=sm[3][0][:, 0:1])
            yield
            tt("pool", kr[:], kvp[:, 128:160], gk_b[:, 64:96], ALU.mult, [Bkvp, B_W], [Bkr])
            yield
            tt("pool", rt1[:], kr[:], cs[:], ALU.mult, [Bkr, Bcs], [Brt1])
            yield
            tt("pool", rt2[:, 0:16], kr[:, 16:32], sn[:, 0:16], ALU.mult, [Bkr, Bcs], [Brt2])
            yield
            tt("pool", rt2[:, 16:32], kr[:, 0:16], sn[:, 16:32], ALU.mult, [Bkr, Bcs], [Brt2])
            yield
            tt("pool", krr[:], rt1[:], rt2[:], ALU.add, [Brt1, Brt2], [Bkrr])
            yield
            r0, r1 = SMR["kvn" + kd]
            Bsmr = B_sm["kvn" + kd]
            tr(psm[:, r0:r1], kvn[:], [Bkvn], [Bsmr])
            yield
            cp("act", kvnT[:], psm[:, r0:r1], [Bsmr], [BkvnT])
            yield
            yield ACQ(LBIG)
            for hf in range(2):
                mm(big[:, hf * 512:(hf + 1) * 512], kvnT[:], w_kvb[:, hf * 512:(hf + 1) * 512], True, True,
                   [BkvnT, B_W], [B_big])
            yield
            kv3 = big[:].rearrange("p (h d) -> p h d", h=8)
            cp("act", vst[:, :, 0:64], kv3[:, :, 64:128], [B_big], [Bvst])
            yield
            cp("pool", vst[:, :, 64:128], vvalid[:, n:n + 1].unsqueeze(2).to_broadcast([128, 8, 64]), [B_C], [Bvst])
            yield
            P.dma(v_scr[n].rearrange("p h d -> p (h d)"), vst[:].rearrange("p h d -> p (h d)"), reads=[Bvst], slot="vst" + kd)
            act(sqk[:], kv3[:, :, 0:64], AF.Square, [B_big], [Bsqk])
            yield
            red("dve", sm[4][0][:], sqk[:], [Bsqk], [sm[4][1]])
            yield
            ts("dve", sm[4][0][:], sm[4][0][:], sm[3][0][:, 0:1], ALU.add, [sm[4][1], sm[3][1]], [sm[4][1]])
            yield
            act(sm[5][0][:], sm[4][0][:], AF.Sqrt, [sm[4][1], B_C], [sm[5][1]], scale=1.0 / 96, bias=eps_t[:, 0:1])
            yield
            rcp(sm[6][0][:], sm[5][0][:], [sm[5][1]], [sm[6][1]])
            yield
            tt("dve", sqk[:], kv3[:, :, 0:64], sm[6][0][:].unsqueeze(2).to_broadcast([128, 8, 64]), ALU.mult,
               [B_big, sm[6][1], Bsqk], [Bsqk])
            yield REL(LBIG)
            tt("pool", kt[:, :, 0:64], sqk[:], gk_b[:, 0:64].unsqueeze(1).to_broadcast([128, 8, 64]), ALU.mult,
               [Bsqk, B_W], [Bkt])
            yield
            tt("dve", kt[:, :, 64:96], krr[:].unsqueeze(1).to_broadcast([128, 8, 32]),
               sm[6][0][:].unsqueeze(2).to_broadcast([128, 8, 32]), ALU.mult, [Bkrr, sm[6][1]], [Bkt])
            yield
            yield ACQ(L5)
            for h in range(8):
                tr(ptr[0:96, h, :], kt[:, h, :], [Bkt], [B_b5])
            yield
            cp("act", kTst[0:96], ptr[0:96], [B_b5], [BkTst])
            yield REL(L5)
            P.dma(kT_scr.rearrange("h d t -> d h t")[:, :, n * 128:(n + 1) * 128], kTst[0:96], reads=[BkTst], slot="kTst" + kd)
            yield

          def swakv(kd, i):
            par = i % 2
            n = 2 * i + (0 if kd == "P" else 1)
            K = KS[kd]
            kvp, Bkvp = kvps[(kd, par)]
            sm = K["sm"]
            (sq2, Bsq2), (ks1, Bks1), (ksd, Bksd) = K["sq2"], K["ks1"], K["ksd"]
            skTk, BskT = skT[(kd, par)]
            svk, Bsv = sv[(kd, par)]
            skp = kvp[:, 160:288].rearrange("p (g d) -> p g d", g=2)
            act(sq2[:], skp, AF.Square, [Bkvp], [Bsq2])
            yield
            red("dve", sm[7][0][:, 0:2], sq2[:], [Bsq2], [sm[7][1]])
            yield
            act(sm[7][0][:, 2:4], sm[7][0][:, 0:2], AF.Sqrt, [sm[7][1], B_C], [sm[7][1]], scale=1.0 / 64, bias=eps_t[:, 0:1])
            yield
            rcp(sm[7][0][:, 4:6], sm[7][0][:, 2:4], [sm[7][1]], [sm[7][1]])
            yield
            tt("dve", ks1[:], skp, sm[7][0][:, 4:6].unsqueeze(2).to_broadcast([128, 2, 64]), ALU.mult, [Bkvp, sm[7][1]], [Bks1])
            yield
            tt("pool", ksd[:], ks1[:], gsk_b[:].unsqueeze(1).to_broadcast([128, 2, 64]), ALU.mult, [Bks1, B_W], [Bksd])
            yield
            r0, r1 = SMR["kT" + kd]
            Bsmr = B_sm["kT" + kd]
            for g in range(2):
                tr(psm[0:64, r0 + g * 128:r0 + (g + 1) * 128], ksd[:, g, :], [Bksd], [Bsmr])
            yield
            cp("act", skTk[0:64], psm[0:64, r0:r1].rearrange("p (g t) -> p g t", g=2), [Bsmr], [BskT])
            yield
            cp("act", svk[:, :, 0:64], kvp[:, 288:416].rearrange("p (g d) -> p g d", g=2), [Bkvp], [Bsv])
            yield
            cp("pool", svk[:, :, 64:128], vvalid[:, n:n + 1].unsqueeze(2).to_broadcast([128, 2, 64]), [B_C], [Bsv])
            yield

          def chain_next(i):
            yield from front("P", i)
            yield from kmla("P", i)
            yield from swakv("P", i)
            yield from front("O", i)

          def chain_ok(i):
            yield from swakv("O", i)
            yield ("set", ("swakv", i))
            yield from kmla("O", i)

          def chain_q(i):
            par = i % 2
            hTo, BhTo = hT[("O", par)]
            cs, sn, Bcs = tabs[("O", par)]
            yield ACQ(L0)
            qp = bk0
            for kc in range(8):
                mm(qp[:, 0:256], hTo[:, kc, 2:130], w_in[:, kc, C_QL:C_QL + 256], kc == 0, kc == 7, [BhTo, B_W], [B_b0])
            yield
            act(junk[:, 256:512], qp[:, 0:256], AF.Square, [B_b0], [QS[0][1]], accum=QS[0][0][:, 0:1])
            yield
            act(QS[1][0][:, 0:1], QS[0][0][:, 0:1], AF.Sqrt, [QS[0][1], B_C], [QS[1][1]], scale=1.0 / 256, bias=eps_t[:, 0:1])
            yield
            rcp(QS[2][0][:, 0:1], QS[1][0][:, 0:1], [QS[1][1]], [QS[2][1]])
            yield
            ts("dve", qln[:], qp[:, 0:256], QS[2][0][:, 0:1], ALU.mult, [B_b0, QS[2][1]], [B_qln])
            yield REL(L0)
            r0, r1 = SMR["qln"]
            for c in range(2):
                tr(psm[:, r0 + c * 128:r0 + (c + 1) * 128], qln[:, c * 128:(c + 1) * 128], [B_qln], [B_sm["qln"]])
            yield
            cp("act", qlnT[:], psm[:, r0:r1].rearrange("p (c t) -> p c t", c=2), [B_sm["qln"]], [B_qlnT])
            yield
            yield ACQ(LBIG)
            for (c0, c1) in ((0, 512), (512, 768)):
                for c in range(2):
                    mm(big[:, c0:c1], qlnT[:, c, :], w_qb[:, c, c0:c1], c == 0, c == 1, [B_qlnT, B_W], [B_big])
            yield
            q3 = big[:, 0:768].rearrange("p (h d) -> p h d", h=8)
            act(sqq[:], q3, AF.Square, [B_big], [B_sqq])
            yield
            red("dve", QS[3][0][:], sqq[:], [B_sqq], [QS[3][1]])
            yield
            act(QS[4][0][:], QS[3][0][:], AF.Sqrt, [QS[3][1], B_C], [QS[4][1]], scale=1.0 / 96, bias=eps_t[:, 0:1])
            yield
            rcp(QS[5][0][:], QS[4][0][:], [QS[4][1]], [QS[5][1]])
            yield
            tt("dve", sqq[:], q3, QS[5][0][:].unsqueeze(2).to_broadcast([128, 8, 96]), ALU.mult, [B_big, QS[5][1], B_sqq], [B_sqq])
            yield REL(LBIG)
            tt("pool", qt[:, :, 0:64], sqq[:, :, 0:64], gq_b[:, 0:64].unsqueeze(1).to_broadcast([128, 8, 64]),
               ALU.mult, [B_sqq, B_W], [B_qt])
            yield
            tt("pool", qr[:], sqq[:, :, 64:96], gq_b[:, 64:96].unsqueeze(1).to_broadcast([128, 8, 32]), ALU.mult,
               [B_sqq, B_W], [B_qr])
            yield
            tt("dve", qr1[:], qr[:], cs[:].unsqueeze(1).to_broadcast([128, 8, 32]), ALU.mult, [B_qr, Bcs], [B_qr1])
            yield
            tt("pool", qr2[:, :, 0:16], qr[:, :, 16:32], sn[:, 0:16].unsqueeze(1).to_broadcast([128, 8, 16]), ALU.mult,
               [B_qr, Bcs], [B_qr2])
            yield
            tt("pool", qr2[:, :, 16:32], qr[:, :, 0:16], sn[:, 16:32].unsqueeze(1).to_broadcast([128, 8, 16]), ALU.mult,
               [B_qr, Bcs], [B_qr2])
            yield
            tt("dve", qt[:, :, 64:96], qr1[:], qr2[:], ALU.add, [B_qr1, B_qr2], [B_qt])
            yield
            yield ACQ(L5)
            for h in range(8):
                tr(ptr[0:96, h, :], qt[:, h, :], [B_qt], [B_b5])
            yield
            cp("act", qTst[0:96], ptr[0:96], [B_b5], [B_qTst])
            yield REL(L5)
            P.dma(qT_scr.rearrange("h d t -> d h t")[:, :, i * 128:(i + 1) * 128], qTst[0:96], reads=[B_qTst], slot="qTst")
            yield

          def chain_swa(i):
            par = i % 2
            hTo, BhTo = hT[("O", par)]
            sqp = bk7
            for kc in range(8):
                mm(sqp[:], hTo[:, kc, 2:130], w_in[:, kc, C_SQ:C_SQ + 512], kc == 0, kc == 7, [BhTo, B_W], [B_b7])
            yield
            sq3 = sqp[:].rearrange("p (h d) -> p h d", h=8)
            act(sqs[:], sq3, AF.Square, [B_b7], [B_sqs])
            yield
            red("dve", SS[0][0][:], sqs[:], [B_sqs], [SS[0][1]])
            yield
            act(SS[1][0][:], SS[0][0][:], AF.Sqrt, [SS[0][1], B_C], [SS[1][1]], scale=1.0 / 64, bias=eps_t[:, 0:1])
            yield
            rcp(SS[2][0][:], SS[1][0][:], [SS[1][1]], [SS[2][1]])
            yield
            tt("dve", sqs[:], sq3, SS[2][0][:].unsqueeze(2).to_broadcast([128, 8, 64]), ALU.mult, [B_b7, SS[2][1], B_sqs], [B_sqs])
            yield
            tt("pool", sqb[:], sqs[:], gsq_b[:].unsqueeze(1).to_broadcast([128, 8, 64]), ALU.mult, [B_sqs, B_W], [B_sqb])
            yield
            yield ACQ(L0)
            for h in range(8):
                tr(ph[0:64, h * 128:(h + 1) * 128], sqb[:, h, :], [B_sqb], [B_b0])
            yield
            cp("act", sqT[0:64], ph[0:64, :].rearrange("p (c t) -> p c t", c=8), [B_b0], [B_sqT])
            yield REL(L0)
            yield ("wait", ("swakv", i))
            ns = 0
            for g in range(2):
                yield ACQ(L5)
                Ops = bk5
                for kb, kk in enumerate(("P", "O")):
                    skTk, BskT = skT[(kk, par)]
                    svk, Bsv = sv[(kk, par)]
                    for e4 in range(4):
                        h = 4 * g + e4
                        mm(bk7[:, e4 * 128:(e4 + 1) * 128], skTk[0:64, g, :], sqT[0:64, h, :], True, True,
                           [BskT, B_sqT], [B_b7])
                    yield
                    act(Ebuf[:], bk7[:], AF.Exp, [B_b7], [B_E], scale=0.125)
                    yield
                    pm, BP = Pm[ns % 2]
                    tt("dve", pm[:], Ebuf[:], swaM[:, kb, 4 * g:4 * g + 4, :].rearrange("p h q -> p (h q)"), ALU.mult,
                       [B_E, B_C], [BP])
                    yield
                    mm(Ops[:], svk[:, g, :], pm[:], kb == 0, kb == 1, [Bsv, BP], [B_b5])
                    yield
                    ns += 1
                tt("dve", rden[64:128, :].rearrange("p (h q) -> p h q", h=4),
                   Ops[64:128, :].rearrange("p (h q) -> p h q", h=4),
                   esink[64:128, 4 * g:4 * g + 4].unsqueeze(2).to_broadcast([64, 4, 128]), ALU.add, [B_b5, B_W], [B_rden])
                yield
                rcp(rden[64:128, :], rden[64:128, :], [B_rden], [B_rden])
                yield
                for e4 in range(4):
                    h = 4 * g + e4
                    hb_ = (h % 2) * 64
                    tt("dve", tsw[hb_:hb_ + 64, h // 2, :], Ops[0:64, e4 * 128:(e4 + 1) * 128],
                       rden[64:128, e4 * 128:(e4 + 1) * 128], ALU.mult, [B_b5, B_rden], [B_tsw])
                    yield
                yield REL(L5)

          def chain_gc(i):
            par = i % 2
            hTo, BhTo = hT[("O", par)]

            def fm_mm(ps_ap, col0, lo, hi, Bps):
                for kc in range(8):
                    mm(ps_ap, w_in[:, kc, col0:col0 + 128], hTo[:, kc, lo:hi], kc == 0, kc == 7, [BhTo, B_W], [Bps])

            G = bk6
            for c in range(4):
                fm_mm(G[:, c * 128:(c + 1) * 128], C_GM + c * 128, 2, 130, B_b6)
                yield
            act(sgm[:].rearrange("p c t -> p (c t)"), G[:], AF.Silu, [B_b6], [B_sgm])
            yield
            P.dma(sg_scr.rearrange("c p t -> p c t")[:, :, i * 128:(i + 1) * 128], sgm[:], reads=[B_sgm], slot="sgm")
            for c in range(4):
                fm_mm(G[:, c * 128:(c + 1) * 128], C_GS + c * 128, 2, 130, B_b6)
                yield
            act(sgs[:].rearrange("p c t -> p (c t)"), G[:], AF.Silu, [B_b6], [B_sgs])
            yield
            for c in range(4):
                fm_mm(G[:, c * 128:(c + 1) * 128], C_GC + c * 128, 2, 130, B_b6)
                yield
            act(sgc[:].rearrange("p c t -> p (c t)"), G[:], AF.Silu, [B_b6], [B_sgc])
            yield
            for cc in range(4):
                yield ACQ(L1)
                X = bk1
                fm_mm(X[:, 0:130], C_CH + cc * 128, 0, 130, B_b1)
                yield
                fm_mm(X[:, 130:260], C_CC + cc * 128, 0, 130, B_b1)
                yield
                fm_mm(X[:, 260:388], C_CB + cc * 128, 2, 130, B_b1)
                yield
                cp("act", ch_sb[:], X[:, 0:130], [B_b1], [B_ch])
                yield
                tt("dve", u_sb[:], ch_sb[:], X[:, 130:260], ALU.mult, [B_ch, B_b1], [B_u])
                yield
                tt("dve", tcb[:], X[:, 260:388], sgc[:, cc, :], ALU.mult, [B_b1, B_sgc], [B_tcb])
                yield REL(L1)
                ts("pool", cy[0][0][:], u_sb[:, 2:130], convw[:, 2, cc:cc + 1], ALU.mult, [B_u, B_W], [cy[0][1]])
                yield
                stt("dve", cy[1][0][:], u_sb[:, 1:129], convw[:, 1, cc:cc + 1], cy[0][0][:], ALU.mult, ALU.add,
                    [B_u, B_W, cy[0][1]], [cy[1][1]])
                yield
                stt("dve", cy[0][0][:], u_sb[:, 0:128], convw[:, 0, cc:cc + 1], cy[1][0][:], ALU.mult, ALU.add,
                    [B_u, B_W, cy[1][1]], [cy[0][1]])
                yield
                tt("pool", yT[:, cc, :], cy[0][0][:], tcb[:], ALU.mult, [cy[0][1], B_tcb], [B_yT])
                yield

          def tail(i):
            xs, Bx = x_O[i % 2]
            tt("pool", yT[:, 4:8, :], tsw[:], sgs[:], ALU.mult, [B_tsw, B_sgs], [B_yT])
            for hf in range(2):
                for kc in range(8):
                    mm(big[:, hf * 512:(hf + 1) * 512], yT[:, kc, :], w_out[:, 4 + kc, hf * 512:(hf + 1) * 512],
                       kc == 0, kc == 7, [B_yT, B_W], [B_big])
            tt("dve", osb[:], big[:], xs[:], ALU.add, [B_big, Bx], [B_osb])
            P.dma(part_scr[i], osb[:], reads=[B_osb], slot="part")

          nb_run = NB if not (dbg is not None and dbg.startswith("b")) else int(dbg[1:])
          load_xP(0)
          load_rest(0)
          if nb_run > 1:
              load_xP(1)
          drive([chain_next(0)])
          for i in range(nb_run):
              chains = [chain_ok(i), chain_q(i), chain_swa(i), chain_gc(i)]
              if i + 1 < nb_run:
                  load_rest(i + 1)
                  if i + 2 < nb_run:
                      load_xP(i + 2)
                  chains = [chain_next(i + 1)] + chains
              drive(chains)
              assert all(v is None for v in locks.values()), locks
              tail(i)

          if dbg == "p1":
              raise _Stop()
          P.barrier()
          ar.off = 0
          kTp = A([128, 2, NS * 128], BF16)
          vp = A([128, NS, 2, 128], BF16)
          qTp = A([128, 2, NB * 128], BF16)
          maskM = A([128, 8, 512], BF16); B_mask = P.buf("mask")
          PT = [A([128, 512], BF16) for _ in range(4)]; B_PT = [P.buf() for _ in range(4)]
          gbuf = [A([128, 512], BF16) for _ in range(2)]; B_g = [P.buf(), P.buf()]
          ymla = A([128, 4, NB * 128], BF16); B_ym = [P.buf(f"ym{j}") for j in range(8)]
          rden2 = A([128, 512], F32); B_rd2 = P.buf()
          tn = A([128, 512], F32); B_tn = P.buf()
          ptile = [A([128, D], F32) for _ in range(2)]; B_pt = [P.buf(), P.buf()]
          osb2 = [A([128, D], F32) for _ in range(2)]; B_o2 = [P.buf(), P.buf()]
          B_kc = [P.buf(f"kTc{c}") for c in range(8)]
          B_vc = [P.buf(f"vc{c}") for c in range(8)]
          B_q = P.buf("qTp")
          P.dma(maskM[:], maskM_d, writes=[B_mask], slot="c0")
          SC = 96 ** -0.5
          Sb = [bk0, bk1, bk2]
          Ob = [bk5, bk6]
          nO = 0
          nS = 0
          for hp in range(4):
              for c in range(8):
                  for hh in range(2):
                      P.dma(kTp[0:96, hh, c * 1024:(c + 1) * 1024], kT_scr[2 * hp + hh, :, c * 1024:(c + 1) * 1024],
                            writes=[B_kc[c]], slot=f"kl{hh}")
                  P.dma(vp[:, 8 * c:8 * c + 8, :, :], v_scr[8 * c:8 * c + 8, :, 2 * hp:2 * hp + 2, :].rearrange("s p h d -> p s h d"),
                        writes=[B_vc[c]], slot="vl")
                  if c == 0:
                      for hh in range(2):
                          P.dma(qTp[0:96, hh, :], qT_scr[2 * hp + hh], writes=[B_q], slot=f"ql{hh}")
              for j in range(8):
                  gb = gbuf[j % 2]; Bg = B_g[j % 2]
                  P.dma(gb[:], sg_scr[hp, :, j * 512:(j + 1) * 512], writes=[Bg], slot=f"gl{j % 2}")
                  for hh in range(2):
                      Ops = Ob[nO % 2]; BO = B_bk[id(Ops)]
                      nO += 1
                      nsl = 8 * j + 8
                      LA = 2
                      pend = {}
                      for t in range(nsl + LA):
                          if t < nsl:
                              Sps = Sb[nS % 3]; BS = B_bk[id(Sps)]
                              nS += 1
                              mm(Sps[:], kTp[0:96, hh, t * 128:(t + 1) * 128], qTp[0:96, hh, j * 512:(j + 1) * 512],
                                 True, True, [B_kc[t // 8], B_q], [BS])
                              pend[t] = (Sps, BS)
                          if t >= LA:
                              sl = t - LA
                              Sps, BS = pend.pop(sl)
                              pt = PT[sl % 4]; BP = B_PT[sl % 4]
                              act(pt[:], Sps[:], AF.Exp, [BS], [BP], scale=SC)
                              if sl >= 8 * j:
                                  tt("pool", pt[:], pt[:], maskM[:, sl - 8 * j, :], ALU.mult, [BP, B_mask], [BP])
                              mm(Ops[:], vp[:, sl, hh, :], pt[:], sl == 0, sl == nsl - 1, [B_vc[sl // 8], BP], [BO])
                      hb = hh * 64
                      rcp(rden2[64:128, :], Ops[64:128, :], [BO], [B_rd2])
                      tt("dve", tn[hb:hb + 64, :], Ops[0:64, :], rden2[64:128, :], ALU.mult, [BO, B_rd2], [B_tn])
                      tt("pool", ymla[hb:hb + 64, hp, j * 512:(j + 1) * 512], tn[hb:hb + 64, :], gb[hb:hb + 64, :], ALU.mult,
                         [B_tn, Bg], [B_ym[j]])

          if dbg == "p2":
              raise _Stop()
          def load_pt(i):
              P.dma(ptile[i % 2][:], part_scr[i], writes=[B_pt[i % 2]], slot=f"pl{i % 2}")

          load_pt(0)
          for i in range(NB):
              if i + 1 < NB:
                  load_pt(i + 1)
              for hf in range(2):
                  for c in range(4):
                      mm(big[:, hf * 512:(hf + 1) * 512], ymla[:, c, i * 128:(i + 1) * 128],
                         w_out[:, c, hf * 512:(hf + 1) * 512], c == 0, c == 3, [B_ym[i // 4], B_W], [B_big])
              o2 = osb2[i % 2]; Bo2 = B_o2[i % 2]
              tt("dve", o2[:], big[:], ptile[i % 2][:], ALU.add, [B_big, B_pt[i % 2]], [Bo2])
              P.dma(ps["out"](i), o2[:], reads=[Bo2], slot=f"yo{i % 2}")
      except _Stop:
        pass
    print("arena p1 end", p1_end, "total ops recorded", P.total, {e: len(v) for e, v in P.ops.items()})
    P.emit(final_slots=[s_ for s_ in P.slot_counts])
    P.close()
    return nc


def _consts(r):
    half = 16
    inv_freq = np.power(np.float32(10000.0), -np.arange(half, dtype=np.float32) / half).astype(np.float32)
    pidx = np.arange(128, dtype=np.float32)
    cosT = np.zeros((128, NS, 32), np.float32)
    sinT = np.zeros((128, NS, 32), np.float32)
    for s in range(NS):
        gb = s - 1 + r
        pos = (gb * 128 + pidx).astype(np.float32)
        ang = pos[:, None] * inv_freq[None, :]
        c = np.cos(ang).astype(np.float32)
        sn = np.sin(ang).astype(np.float32)
        cosT[:, s, :16] = c
        cosT[:, s, 16:] = c
        sinT[:, s, :16] = -sn
        sinT[:, s, 16:] = sn
    maskM = np.zeros((128, 8, 512), np.float32)
    ki = np.arange(128)[:, None]
    qi = np.arange(128)[None, :]
    tri = (ki <= qi).astype(np.float32)
    for so in range(8):
        for qb in range(4):
            d = 2 * qb + 1
            if so < d:
                maskM[:, so, qb * 128:(qb + 1) * 128] = 1.0
            elif so == d:
                maskM[:, so, qb * 128:(qb + 1) * 128] = tri
    slopes = np.exp2(-8.0 * np.arange(1, 9, dtype=np.float32) / 8).astype(np.float32)
    swaM = np.zeros((128, 2, 8, 128), np.float32)
    for h in range(8):
        d0 = (128 + qi - ki).astype(np.float32)
        swaM[:, 0, h, :] = np.where(d0 < 128, np.exp(-slopes[h] * d0), 0.0)
        d1 = (qi - ki).astype(np.float32)
        swaM[:, 1, h, :] = np.where(d1 >= 0, np.exp(-slopes[h] * np.maximum(d1, 0)), 0.0)
    vvalid = np.ones((128, NS), np.float32)
    if r == 0:
        vvalid[:, 0] = 0.0
    return {"cosT": cosT, "sinT": sinT, "maskM": maskM.astype(ml_dtypes.bfloat16), "swaM": swaM, "vvalid": vvalid}


def _layer_weights(inp, l, suffix):
    f = lambda a: np.ascontiguousarray(a, dtype=np.float32)
    return {
        f"w_in_{suffix}": f(inp["w_in"][l][:, PERM]), f"norm_g_{suffix}": f(inp["norm_g"][l].reshape(8, 128).T),
        f"w_kvb_{suffix}": f(inp["mla_w_kvb"][l]), f"kva_g_{suffix}": f(inp["mla_kv_a_norm"][l].reshape(128, 1)),
        f"w_qb_{suffix}": f(inp["mla_w_qb"][l]), f"qa_g_{suffix}": f(inp["mla_q_a_norm"][l].reshape(2, 128).T),
        f"q_g_{suffix}": f(inp["mla_q_norm"][l]), f"k_g_{suffix}": f(inp["mla_k_norm"][l]),
        f"conv_w_{suffix}": f(inp["conv_w"][l].reshape(3, 4, 128).transpose(2, 0, 1)), f"sq_g_{suffix}": f(inp["swa_q_norm"][l]),
        f"sk_g_{suffix}": f(inp["swa_k_norm"][l]), f"sinks_{suffix}": f(inp["swa_sinks"][l]),
        f"w_out_{suffix}": f(inp["w_out"][l]),
    }


def _shard_x(x):
    xb = x.reshape(4, 64, 128, D)
    outs = []
    for c in range(8):
        b, r = c // 2, c % 2
        own = xb[b, r::2]
        prev = np.zeros_like(own)
        if r == 0:
            prev[1:] = xb[b, 1:63:2]
        else:
            prev[:] = xb[b, 0::2]
        outs.append((np.ascontiguousarray(own), np.ascontiguousarray(prev)))
    return outs


def _gather(res):
    out = np.zeros((4, 64, 128, D), np.float32)
    for c in range(8):
        b, r = c // 2, c % 2
        out[b, r::2] = res[c]["y_out"]
    return out.reshape(4, 8192, D)


_CACHE = {}
FUSED = True


def kernel(**inp):
    inp = {k: np.asarray(v) for k, v in inp.items()}
    x = np.ascontiguousarray(inp["x"], dtype=np.float32)
    depth = inp["w_in"].shape[0]
    consts = [_consts(r) for r in range(2)]
    if FUSED and depth == 2:
        if "nc2" not in _CACHE:
            _CACHE["nc2"] = build_program(2)
        nc = _CACHE["nc2"]
        w = {}
        for l in range(2):
            w.update(_layer_weights(inp, l, str(l)))
        sh = _shard_x(x)
        in_maps = []
        for c in range(8):
            r = c % 2
            o = c + 1 - 2 * r
            m = {"x_own": sh[c][0], "x_prev": sh[c][1], "x_own2": sh[o][0], "x_prev2": sh[o][1]}
            m.update(w)
            m.update(consts[r])
            co = consts[1 - r]
            m["cosT2"] = co["cosT"]; m["sinT2"] = co["sinT"]; m["vvalid2"] = co["vvalid"]
            bw = np.zeros((128, 2), np.float32)
            bw[:, r] = 1.0
            m["blendw"] = bw
            in_maps.append(m)
        res = run_bass_kernel_spmd(nc, in_maps, core_ids=list(range(8)))
        return _gather(res.results).astype(np.float32)
    if "nc1" not in _CACHE:
        _CACHE["nc1"] = build_program(1)
    for l in range(depth):
        nc = _CACHE["nc1"]
        w = _layer_weights(inp, l, "0")
        sh = _shard_x(x)
        in_maps = []
        for c in range(8):
            m = {"x_own": sh[c][0], "x_prev": sh[c][1]}
            m.update(w)
            m.update(consts[c % 2])
            in_maps.append(m)
        res = run_bass_kernel_spmd(nc, in_maps, core_ids=list(range(8)))
        x = _gather(res.results)
    return x.astype(np.float32)
```

```python
import numpy as np
import ml_dtypes
from contextlib import ExitStack
import concourse.bass as bass
import concourse.mybir as mybir
from concourse.bass_utils import run_bass_kernel_spmd

F32 = mybir.dt.float32
BF16 = mybir.dt.bfloat16
ALU = mybir.AluOpType
AF = mybir.ActivationFunctionType
AX = mybir.AxisListType

NB = 32
NS = 64
D = 1024
NCOL = 4256
EPS = 1e-6
C_KVL, C_KR, C_SK, C_SV, C_QL, C_SQ, C_GM, C_GS, C_CH, C_CC, C_CB, C_GC = (
    0, 128, 160, 288, 416, 672, 1184, 1696, 2208, 2720, 3232, 3744)
PERM = np.concatenate([np.arange(256, 384), np.arange(384, 416), np.arange(3488, 3616), np.arange(3616, 3744),
                       np.arange(0, 256), np.arange(2976, 3488), np.arange(416, 928), np.arange(3744, 4256),
                       np.arange(928, 1440), np.arange(1952, 2464), np.arange(1440, 1952), np.arange(2464, 2976)])

COMPUTE = ("pe", "act", "dve", "pool")
ALLENG = COMPUTE + ("sp",)


class Buf:
    __slots__ = ("name", "lw", "rd")

    def __init__(self, name):
        self.name = name
        self.lw = None
        self.rd = []


class Op:
    __slots__ = ("eng", "fn", "waits", "signal", "dma", "slot", "slot_cnt")

    def __init__(self, eng, fn, dma=False, slot=None):
        self.eng = eng
        self.fn = fn
        self.waits = []
        self.signal = False
        self.dma = dma
        self.slot = slot
        self.slot_cnt = 0


class Prog:
    def __init__(self, nc):
        self.nc = nc
        self.ops = {e: [] for e in ALLENG}
        self.seen = {e: {} for e in ALLENG}
        self.pending = {e: [] for e in ALLENG}
        self.slot_counts = {}
        self.stack = ExitStack()
        self.nbuf = 0

    def sbuf(self, name, shape, dtype):
        return self.stack.enter_context(self.nc.sbuf_tensor("sb_" + name, list(shape), dtype))

    def psum(self, name, shape, dtype):
        return self.stack.enter_context(self.nc.psum_tensor("ps_" + name, list(shape), dtype))

    def buf(self, name=None):
        self.nbuf += 1
        return Buf(name or f"b{self.nbuf}")

    def _dep(self, op, eng, key):
        kind, k, v = key
        if kind == "eng" and k == "pe" and eng == "pe":
            return
        seen = self.seen[eng]
        if seen.get((kind, k), -1) >= v:
            return
        seen[(kind, k)] = v
        op.waits.append(key)
        if kind == "eng":
            self.ops[k][v].signal = True

    limit = None
    total = 0
    trace = None

    def add(self, eng, fn, reads=(), writes=(), dma=False, slot=None):
        self.total += 1
        if self.limit is not None and self.total > self.limit:
            return None
        op = Op(eng, fn, dma=dma, slot=slot)
        if self.trace is not None:
            import sys as _s
            f = _s._getframe(1)
            while f is not None and f.f_code.co_name != "build_program":
                f = f.f_back
            self.trace.append((self.total, eng, f.f_lineno if f else -1))
        idx = len(self.ops[eng])
        for key in self.pending[eng]:
            self._dep(op, eng, key)
        self.pending[eng] = []
        if dma:
            cnt = self.slot_counts.get(slot, 0)
            if cnt > 0:
                self._dep(op, eng, ("slot", slot, cnt))
            self.slot_counts[slot] = cnt + 1
            op.slot_cnt = cnt + 1
            me = ("slot", slot, cnt + 1)
        else:
            me = ("eng", eng, idx)
        for b in reads:
            if b.lw is not None:
                self._dep(op, eng, b.lw)
        for b in writes:
            if b.lw is not None:
                self._dep(op, eng, b.lw)
            for r in b.rd:
                self._dep(op, eng, r)
        for b in reads:
            b.rd.append(me)
        for b in writes:
            b.lw = me
            b.rd = []
        self.ops[eng].append(op)
        return op

    def barrier(self):
        keys = []
        for e in ALLENG:
            for i in range(len(self.ops[e]) - 1, -1, -1):
                if not self.ops[e][i].dma:
                    keys.append(("eng", e, i))
                    break
        for s, c in self.slot_counts.items():
            keys.append(("slot", s, c))
        for e in ALLENG:
            self.pending[e] = self.pending[e] + [k for k in keys if not (k[0] == "eng" and k[1] == e)]

    def dma(self, out, in_, reads=(), writes=(), slot="d0", eng="sp"):
        return self.add(eng, lambda e: e.dma_start(out=out, in_=in_), reads, writes, dma=True, slot=slot)

    def emit(self, final_slots=()):
        nc = self.nc
        st = self.stack
        esem = {e: st.enter_context(nc.semaphore(f"s_{e}")) for e in ALLENG}
        ssem = {s: st.enter_context(nc.semaphore(f"d_{s}")) for s in self.slot_counts}
        cnt = {}
        for e, lst in self.ops.items():
            c = 0
            for i, op in enumerate(lst):
                if op.signal and not op.dma:
                    c += 1
                cnt[(e, i)] = c

        def run(engname, handle):
            for op in self.ops[engname]:
                for kind, k, v in op.waits:
                    if kind == "eng":
                        handle.wait_ge(esem[k], cnt[(k, v)])
                    else:
                        handle.wait_ge(ssem[k], 16 * v)
                ins = op.fn(handle)
                if op.dma:
                    ins.then_inc(ssem[op.slot], 16)
                elif op.signal:
                    ins.then_inc(esem[engname], 1)
            if engname == "sp":
                for s in final_slots:
                    handle.wait_ge(ssem[s], 16 * self.slot_counts[s])

        block = st.enter_context(nc.Block())

        @block.sync
        def _(e):
            run("sp", e)

        @block.tensor
        def _(e):
            run("pe", e)

        @block.scalar
        def _(e):
            run("act", e)

        @block.vector
        def _(e):
            run("dve", e)

        @block.gpsimd
        def _(e):
            run("pool", e)

    def close(self):
        self.stack.close()


WNAMES = ["w_in", "norm_g", "w_kvb", "kva_g", "w_qb", "qa_g", "q_g", "k_g", "conv_w", "sq_g", "sk_g",
          "sinks", "w_out"]
WSHAPES = {"w_in": [D, NCOL], "norm_g": [128, 8], "w_kvb": [128, 1024], "kva_g": [128, 1], "w_qb": [256, 768],
           "qa_g": [128, 2], "q_g": [96], "k_g": [96], "conv_w": [128, 3, 4], "sq_g": [64], "sk_g": [64],
           "sinks": [8], "w_out": [1536, 1024]}


class _Stop(Exception):
    pass


def build_program(nlayers=1, dbg=None):
    nc = bass.Bass("TRN2", target_bir_lowering=False)
    P = Prog(nc)
    import os
    if os.environ.get("K_LIMIT"):
        P.limit = int(os.environ["K_LIMIT"])

    def dram_in(name, shape, dt=F32):
        return nc.dram_tensor(name, list(shape), dt, kind="ExternalInput").ap()

    x_own = dram_in("x_own", [NB, 128, D])
    x_prev = dram_in("x_prev", [NB, 128, D])
    Wd = [{n: dram_in(f"{n}_{l}", WSHAPES[n]) for n in WNAMES} for l in range(nlayers)]
    cosT_d = dram_in("cosT", [128, NS, 32])
    sinT_d = dram_in("sinT", [128, NS, 32])
    maskM_d = dram_in("maskM", [128, 8, 512], BF16)
    swaM_d = dram_in("swaM", [128, 2, 8, 128])
    vvalid_d = dram_in("vvalid", [128, NS])
    y_out = nc.dram_tensor("y_out", [NB, 128, D], F32, kind="ExternalOutput").ap()
    fused = nlayers == 2
    if not fused:
        passes = [dict(W=Wd[0], own=lambda i: x_own[i], prev=lambda i: x_prev[i], cos=cosT_d, sin=sinT_d,
                       vv=vvalid_d, out=lambda i: y_out[i], blend=False)]
    else:
        x_own2 = dram_in("x_own2", [NB, 128, D])
        x_prev2 = dram_in("x_prev2", [NB, 128, D])
        cosT2_d = dram_in("cosT2", [128, NS, 32])
        sinT2_d = dram_in("sinT2", [128, NS, 32])
        vvalid2_d = dram_in("vvalid2", [128, NS])
        blendw_d = dram_in("blendw", [128, 2])
        x1_mine = nc.dram_tensor("x1_mine", [NB, 128, D], F32, kind="ExternalOutput").ap()
        Zb = nc.dram_tensor("x1_other", [NB + 1, 128, D], F32, kind="ExternalOutput").ap()
        passes = [
            dict(W=Wd[0], own=lambda i: x_own[i], prev=lambda i: x_prev[i], cos=cosT_d, sin=sinT_d, vv=vvalid_d,
                 out=lambda i: x1_mine[i], blend=False),
            dict(W=Wd[0], own=lambda i: x_own2[i], prev=lambda i: x_prev2[i], cos=cosT2_d, sin=sinT2_d, vv=vvalid2_d,
                 out=lambda i: Zb[i + 1], blend=False),
            dict(W=Wd[1], own=lambda i: x1_mine[i], prev=lambda i: Zb[i], prev2=lambda i: Zb[i + 1], cos=cosT_d,
                 sin=sinT_d, vv=vvalid_d, out=lambda i: y_out[i], blend=True),
        ]

    skind = "ExternalOutput"
    kT_scr = nc.dram_tensor("kT_scr", [8, 96, NS * 128], BF16, kind=skind).ap()
    v_scr = nc.dram_tensor("v_scr", [NS, 128, 8, 128], BF16, kind=skind).ap()
    qT_scr = nc.dram_tensor("qT_scr", [8, 96, NB * 128], BF16, kind=skind).ap()
    sg_scr = nc.dram_tensor("sg_scr", [4, 128, NB * 128], BF16, kind=skind).ap()
    part_scr = nc.dram_tensor("part_scr", [NB, 128, D], F32, kind=skind).ap()

    ident = P.sbuf("ident", [128, 128], BF16); B_ident = P.buf("ident")
    idf = P.sbuf("idf", [128, 128], F32)
    w_kvb = P.sbuf("w_kvb", [128, 1024], BF16)
    w_qb = P.sbuf("w_qb", [128, 2, 768], BF16)
    w_out = P.sbuf("w_out", [128, 12, 1024], BF16)
    swaM = P.sbuf("swaM", [128, 2, 8, 128], F32)
    vvalid = P.sbuf("vvalid", [128, NS], F32)
    gq_b = P.sbuf("gq_b", [128, 96], F32)
    gk_b = P.sbuf("gk_b", [128, 96], F32)
    gsq_b = P.sbuf("gsq_b", [128, 64], F32)
    gsk_b = P.sbuf("gsk_b", [128, 64], F32)
    esink = P.sbuf("esink", [128, 8], F32)
    normg = P.sbuf("normg", [128, 8], F32)
    convw = P.sbuf("convw", [128, 3, 4], F32)
    kvag = P.sbuf("kvag", [128, 1], F32)
    qag = P.sbuf("qag", [128, 2], F32)
    eps_t = P.sbuf("eps_t", [128, 1], F32)
    mhalf = P.sbuf("mhalf", [128, 8], F32)

    def mhalf_like(ap):
        return mhalf[:, 0:ap.shape[-1]]
    B_W = P.buf("weights")
    B_C = P.buf("consts")

    ARENA_B = 167 * 1024
    arena = P.sbuf("arena", [128, ARENA_B // 2], BF16)

    class Arena:
        def __init__(self):
            self.off = 0

        def alloc(self, shape, dt, parts=128):
            n = int(np.prod(shape[1:]))
            nbytes = n * (4 if dt == F32 else 2)
            nbytes = (nbytes + 31) // 32 * 32
            o = self.off
            self.off += nbytes
            assert self.off <= ARENA_B, f"arena overflow {self.off}"
            v = arena[:, o // 2:(o + nbytes) // 2]
            if dt == F32:
                v = v.bitcast(F32)
            v = v[:, 0:n]
            if len(shape) == 3:
                v = v.rearrange("p (a b) -> p a b", a=shape[1])
            elif len(shape) == 4:
                v = v.rearrange("p (a b c) -> p a b c", a=shape[1], b=shape[2])
            return v

    banks = [P.psum(f"bank{i}", [128, 512], F32) for i in range(3)]
    big = P.psum("big", [128, 1024], F32)
    banks += [P.psum(f"bank{i}", [128, 512], F32) for i in range(5, 8)]
    bk0, bk1, bk2, bk5, bk6, bk7 = banks
    B_bk = {id(b): P.buf(f"bk{i}") for i, b in enumerate(banks)}
    B_big = P.buf("big")

    def bf(ps):
        return ps[:].bitcast(BF16)

    def tt(eng, out, in0, in1, op, R, W):
        return P.add(eng, lambda e: e.tensor_tensor(out=out, in0=in0, in1=in1, op=op), R, W)

    def ts(eng, out, in0, s1, op0, R, W, s2=None, op1=None):
        if op1 is None:
            return P.add(eng, lambda e: e.tensor_scalar(out=out, in0=in0, scalar1=s1, scalar2=None, op0=op0), R, W)
        return P.add(eng, lambda e: e.tensor_scalar(out=out, in0=in0, scalar1=s1, scalar2=s2, op0=op0, op1=op1), R, W)

    def stt(eng, out, in0, scalar, in1, op0, op1, R, W):
        return P.add(eng, lambda e: e.scalar_tensor_tensor(out=out, in0=in0, scalar=scalar, in1=in1, op0=op0, op1=op1), R, W)

    def act(out, in_, func, R, W, scale=None, bias=None, accum=None):
        kw = {}
        if scale is not None:
            kw["scale"] = scale
        if bias is not None:
            kw["bias"] = bias
        if accum is not None:
            kw["accum_out"] = accum
        return P.add("act", lambda e: e.activation(out=out, in_=in_, func=func, **kw), R, W)

    def cp(eng, out, in_, R, W):
        if eng == "act":
            return P.add("act", lambda e: e.activation(out=out, in_=in_, func=AF.Copy), R, W)
        return P.add(eng, lambda e: e.tensor_copy(out=out, in_=in_), R, W)

    def red(eng, out, in_, R, W):
        return P.add(eng, lambda e: e.tensor_reduce(out=out, in_=in_, axis=AX.X, op=ALU.add), R, W)

    def rcp(out, in_, R, W):
        return P.add("dve", lambda e: e.reciprocal(out=out, in_=in_), R, W)

    def mm(out, lhsT, rhs, start, stop, R, W):
        return P.add("pe", lambda e: e.matmul(out, lhsT=lhsT, rhs=rhs, start=start, stop=stop), R, W)

    def tr(out, in_, R, W):
        return P.add("pe", lambda e: e.transpose(out=out, in_=in_, identity=ident[:]), list(R) + [B_ident], W)

    def rstd(ss_ap, out_ap, scale, R_ss, B_out, tmp_ap, B_tmp):
        act(tmp_ap, ss_ap, AF.Sqrt, [R_ss, B_C], [B_tmp], scale=scale, bias=eps_t[:, 0:1])
        rcp(out_ap, tmp_ap, [B_tmp], [B_out])

    P.add("pool", lambda e: e.memset(idf[:], 0.0), [], [B_ident])
    P.add("pool", lambda e: e.affine_select(out=idf[:], in_=idf[:], pattern=[[-1, 128]], compare_op=ALU.not_equal,
                                            fill=1.0, base=0, channel_multiplier=1), [B_ident], [B_ident])
    P.add("pool", lambda e: e.tensor_copy(out=ident[:], in_=idf[:]), [B_ident], [B_ident])
    P.add("pool", lambda e: e.memset(eps_t[:], EPS), [], [B_C])
    P.add("pool", lambda e: e.memset(mhalf[:], -0.5), [], [B_C])
    P.dma(swaM[:], swaM_d, writes=[B_C], slot="c0")
    blendw = P.sbuf("blendw", [128, 2], F32)
    if fused:
        P.dma(blendw[:], blendw_d, writes=[B_C], slot="c1")
        zero_t = arena[:, 0:2 * D].bitcast(F32)
        P.add("pool", lambda e: e.memset(zero_t, 0.0), [], [B_C])
        P.dma(Zb[0], zero_t, reads=[B_C], slot="c1")

    for ps in passes:
      try:
          W = ps["W"]
          cosT_d = ps["cos"]; sinT_d = ps["sin"]
          P.barrier()
          P.dma(vvalid[:], ps["vv"], writes=[B_C], slot="c1")
          ar = Arena()
          w_in = ar.alloc([128, 8, NCOL], BF16)
          stage = [ar.alloc([128, 2128], F32) for _ in range(2)]
          B_st = [P.buf("st0"), P.buf("st1")]
          mark = ar.off
          P.dma(normg[:], W["norm_g"], writes=[B_W], slot="c0")
          P.dma(convw[:], W["conv_w"], writes=[B_W], slot="c1")
          P.dma(kvag[:], W["kva_g"], writes=[B_W], slot="c0")
          P.dma(qag[:], W["qa_g"], writes=[B_W], slot="c1")
          P.dma(gq_b[:], W["q_g"].partition_broadcast(128), writes=[B_W], slot="c0")
          P.dma(gk_b[:], W["k_g"].partition_broadcast(128), writes=[B_W], slot="c1")
          P.dma(gsq_b[:], W["sq_g"].partition_broadcast(128), writes=[B_W], slot="c0")
          P.dma(gsk_b[:], W["sk_g"].partition_broadcast(128), writes=[B_W], slot="c1")
          P.dma(esink[:], W["sinks"].partition_broadcast(128), writes=[B_W], slot="c0")
          act(esink[:], esink[:], AF.Exp, [B_W], [B_W])
          n = 0
          engs = ["dve", "dve"]
          win_d = W["w_in"].rearrange("(kc p) c -> p kc c", p=128)
          for kc in range(8):
              for hf in range(2):
                  s = n % 2
                  P.dma(stage[s][:], win_d[:, kc, hf * 2128:(hf + 1) * 2128], writes=[B_st[s]], slot=f"st{s}")
                  ts(engs[n % 2], w_in[:, kc, hf * 2128:(hf + 1) * 2128], stage[s][:], normg[:, kc:kc + 1], ALU.mult,
                     [B_st[s], B_W], [B_W])
                  n += 1
          wout_d = W["w_out"].rearrange("(kc p) c -> p kc c", p=128)
          for kc in range(12):
              s = n % 2
              P.dma(stage[s][:, 0:1024], wout_d[:, kc, :], writes=[B_st[s]], slot=f"st{s}")
              ts(engs[n % 2], w_out[:, kc, :], stage[s][:, 0:1024], 0.5, ALU.mult, [B_st[s]], [B_W])
              n += 1
          s = n % 2
          P.dma(stage[s][:, 0:1024], W["w_kvb"], writes=[B_st[s]], slot=f"st{s}")
          ts(engs[n % 2], w_kvb[:], stage[s][:, 0:1024], kvag[:, 0:1], ALU.mult, [B_st[s], B_W], [B_W])
          n += 1
          wqb_d = W["w_qb"].rearrange("(c p) n -> p c n", p=128)
          for c in range(2):
              s = n % 2
              P.dma(stage[s][:, 0:768], wqb_d[:, c, :], writes=[B_st[s]], slot=f"st{s}")
              ts(engs[n % 2], w_qb[:, c, :], stage[s][:, 0:768], qag[:, c:c + 1], ALU.mult, [B_st[s], B_W], [B_W])
              n += 1

          if dbg == "w":
              raise _Stop()
          P.barrier()
          ar.off = mark - 2 * ((2128 * 4 + 31) // 32 * 32)
          A = ar.alloc

          def AB(shape, dt, name):
            return A(shape, dt), P.buf(name)

          x_P = [AB([128, D], F32, f"xP{k}") for k in range(2)]
          x_O = [AB([128, D], F32, f"xO{k}") for k in range(2)]
          xb2, B_xb2 = AB([128, D], F32, "xb2")
          tabs = {(kd, k): (A([128, 32], F32), A([128, 32], F32), P.buf(f"tab{kd}{k}")) for kd in "PO" for k in range(2)}
          h_bf = [AB([128, D], BF16, f"h{k}") for k in range(2)]
          hT = {(kd, k): AB([128, 8, 130], BF16, f"hT{kd}{k}") for kd in "PO" for k in range(2)}
          kvps = {(kd, k): AB([128, 416], F32, f"kvps{kd}{k}") for kd in "PO" for k in range(2)}
          skT = {(kd, k): AB([128, 2, 128], BF16, f"skT{kd}{k}") for kd in "PO" for k in range(2)}
          sv = {(kd, k): AB([128, 2, 128], BF16, f"sv{kd}{k}") for kd in "PO" for k in range(2)}

          def kset(tag):
            d = {}
            d["sm"] = [AB([128, 8], F32, f"sm{tag}{k}") for k in range(8)]
            d["kvn"] = AB([128, 128], BF16, "kvn" + tag)
            d["kvnT"] = AB([128, 128], BF16, "kvnT" + tag)
            d["vst"] = AB([128, 8, 128], BF16, "vst" + tag)
            d["sqk"] = AB([128, 8, 64], F32, "sqk" + tag)
            d["kt"] = AB([128, 8, 96], BF16, "kt" + tag)
            d["kr"] = AB([128, 32], F32, "kr" + tag)
            d["rt1"] = AB([128, 32], F32, "rt1" + tag)
            d["rt2"] = AB([128, 32], F32, "rt2" + tag)
            d["krr"] = AB([128, 32], F32, "krr" + tag)
            d["kTst"] = AB([128, 8, 128], BF16, "kTst" + tag)
            d["sq2"] = AB([128, 2, 64], F32, "sq2" + tag)
            d["ks1"] = AB([128, 2, 64], F32, "ks1" + tag)
            d["ksd"] = AB([128, 2, 64], BF16, "ksd" + tag)
            return d

          KS = {"P": kset("A"), "O": kset("B")}
          FS = {kd: [AB([128, 8], F32, f"fs{kd}{k}") for k in range(3)] for kd in "PO"}
          QS = [AB([128, 8], F32, f"qs{k}") for k in range(6)]
          qln, B_qln = AB([128, 256], BF16, "qln")
          qlnT, B_qlnT = AB([128, 2, 128], BF16, "qlnT")
          sqq, B_sqq = AB([128, 8, 96], F32, "sqq")
          qt, B_qt = AB([128, 8, 96], BF16, "qt")
          qr, B_qr = AB([128, 8, 32], F32, "qr")
          qr1, B_qr1 = AB([128, 8, 32], F32, "qr1")
          qr2, B_qr2 = AB([128, 8, 32], F32, "qr2")
          qTst, B_qTst = AB([128, 8, 128], BF16, "qTst")
          SS = [AB([128, 8], F32, f"ss{k}") for k in range(4)]
          sqs, B_sqs = AB([128, 8, 64], F32, "sqs")
          sqb, B_sqb = AB([128, 8, 64], BF16, "sqb")
          sqT, B_sqT = AB([128, 8, 128], BF16, "sqT")
          Ebuf, B_E = AB([128, 512], F32, "E")
          Pm = [AB([128, 512], BF16, "Pm0")] * 2
          rden, B_rden = AB([128, 512], F32, "rden")
          tsw, B_tsw = AB([128, 4, 128], F32, "tsw")
          sgs, B_sgs = AB([128, 4, 128], F32, "sgs")
          sgm, B_sgm = AB([128, 4, 128], BF16, "sgm")
          sgc, B_sgc = AB([128, 4, 128], F32, "sgc")
          ch_sb, B_ch = AB([128, 130], F32, "ch")
          u_sb, B_u = AB([128, 130], F32, "u")
          cy = [AB([128, 128], F32, f"cy{k}") for k in range(2)]
          tcb, B_tcb = AB([128, 128], F32, "tcb")
          yT, B_yT = AB([128, 8, 128], BF16, "yT")
          osb, B_osb = AB([128, D], F32, "osb")
          sge, B_sge = osb[:, 0:512], B_osb
          p1_end = ar.off

          ph = bf(bk0); B_b0 = B_bk[id(bk0)]
          B_b1 = B_bk[id(bk1)]
          psm = bf(bk2)
          B_b2 = B_bk[id(bk2)]
          B_sm = {k: B_b2 for k in ("kvnP", "kvnO", "kTP", "kTO", "qln")}
          L2 = "L2"
          SMR = {"kvnP": (0, 128), "kvnO": (128, 256), "kTP": (256, 512), "kTO": (512, 768), "qln": (768, 1024)}
          ptr = bf(bk5).rearrange("p (h t) -> p h t", h=8); B_b5 = B_bk[id(bk5)]
          B_b6 = B_bk[id(bk6)]; B_b7 = B_bk[id(bk7)]
          L0, L1, LBIG, L5 = "L0", "L1", "LBIG", "L5"

          def ACQ(l):
            return ("acq", l)

          def REL(l):
            return ("rel", l)

          def drive(chains):
            st = [dict(g=g, want=None) for g in chains]
            while st:
                progressed = False
                for c in list(st):
                    while True:
                        if c["want"] is not None:
                            kind, obj = c["want"]
                            if kind == "acq":
                                if locks.get(obj) is None:
                                    locks[obj] = c["g"]
                                elif locks[obj] is not c["g"]:
                                    break
                            elif kind == "wait":
                                if obj not in events:
                                    break
                            c["want"] = None
                        try:
                            r = next(c["g"])
                        except StopIteration:
                            st.remove(c)
                            progressed = True
                            break
                        progressed = True
                        if r is None:
                            break
                        kind, obj = r
                        if kind == "rel":
                            assert locks.get(obj) is c["g"], ("release of unowned lock", obj)
                            locks[obj] = None
                        elif kind == "set":
                            events.add(obj)
                        else:
                            c["want"] = r
                if not progressed:
                    raise RuntimeError("chain deadlock at build time")

          locks = {}
          events = set()

          def load_xP(i):
            xs, Bx = x_P[i % 2]
            P.dma(xs[:], ps["prev"](i), writes=[Bx], slot=f"xP{i % 2}")

          def load_rest(i):
            xs, Bx = x_O[i % 2]
            P.dma(xs[:], ps["own"](i), writes=[Bx], slot=f"xO{i % 2}")
            for kd, n in (("P", 2 * i), ("O", 2 * i + 1)):
                cs, sn, Bt = tabs[(kd, i % 2)]
                P.dma(cs[:], cosT_d[:, n, :], writes=[Bt], slot=f"cs{kd}{i % 2}")
                P.dma(sn[:], sinT_d[:, n, :], writes=[Bt], slot=f"sn{kd}{i % 2}")
            if ps["blend"]:
                P.dma(xb2[:], ps["prev2"](i), writes=[B_xb2], slot="xb2")

          def front(kd, i):
            par = i % 2
            xs, Bx = (x_P if kd == "P" else x_O)[par]
            hb, Bh = h_bf[0 if kd == "P" else 1]
            hTk, BhT = hT[(kd, par)]
            (s0, Bs0), (s1, Bs1), (s2, Bs2) = FS[kd]
            if ps["blend"] and kd == "P":
                ts("dve", xs[:], xs[:], blendw[:, 0:1], ALU.mult, [Bx, B_C], [Bx])
                yield
                stt("dve", xs[:], xb2[:], blendw[:, 1:2], xs[:], ALU.mult, ALU.add, [B_xb2, B_C, Bx], [Bx])
                yield
            act(hb[:], xs[:], AF.Square, [Bx], [Bs0, Bh], accum=s0[:, 0:1])
            yield
            ts("pool", s1[:, 0:1], s0[:, 0:1], 1.0 / D, ALU.mult, [Bs0], [Bs1], s2=EPS, op1=ALU.add)
            yield
            tt("pool", s2[:, 0:1], s1[:, 0:1], mhalf[:, 0:1], ALU.pow, [Bs1, B_C], [Bs2])
            yield
            ts("dve", hb[:], xs[:], s2[:, 0:1], ALU.mult, [Bx, Bs2], [Bh])
            yield
            yield ACQ(L0)
            for kc in range(8):
                tr(ph[:, kc * 128:(kc + 1) * 128], hb[:, kc * 128:(kc + 1) * 128], [Bh], [B_b0])
            yield
            cp("act", hTk[:, :, 2:130], ph.rearrange("p (a b) -> p a b", a=8), [B_b0], [BhT])
            yield REL(L0)
            if kd == "P":
                yield ("set", ("hTP", i))
            if kd == "O":
                yield ("wait", ("hTP", i))
                hTp, BhTp = hT[("P", par)]
                cp("pool", hTk[:, :, 0:2], hTp[:, :, 128:130], [BhTp], [BhT])
                yield
            yield ACQ(L1)
            for kc in range(8):
                mm(bk1[:, 0:416], hTk[:, kc, 2:130], w_in[:, kc, 0:416], kc == 0, kc == 7, [BhT, B_W], [B_b1])
            yield
            kv_sb, Bkv = kvps[(kd, par)]
            cp("act", kv_sb[:], bk1[:, 0:416], [B_b1], [Bkv])
            yield REL(L1)
            yield ("set", ("kv" + kd, i))

          def kmla(kd, i):
            par = i % 2
            n = 2 * i + (0 if kd == "P" else 1)
            K = KS[kd]
            kvp, Bkvp = kvps[(kd, par)]
            cs, sn, Bcs = tabs[(kd, par)]
            sm = K["sm"]
            (kvn, Bkvn), (kvnT, BkvnT), (vst, Bvst), (sqk, Bsqk), (kt, Bkt) = K["kvn"], K["kvnT"], K["vst"], K["sqk"], K["kt"]
            (kr, Bkr), (rt1, Brt1), (rt2, Brt2), (krr, Bkrr), (kTst, BkTst) = K["kr"], K["rt1"], K["rt2"], K["krr"], K["kTst"]
            act(kvn[:], kvp[:, 0:128], AF.Square, [Bkvp], [sm[0][1], Bkvn], accum=sm[0][0][:, 0:1])
            yield
            ts("pool", sm[1][0][:, 0:1], sm[0][0][:, 0:1], 1.0 / 128, ALU.mult, [sm[0][1]], [sm[1][1]], s2=EPS, op1=ALU.add)
            yield
            tt("pool", sm[2][0][:, 0:1], sm[1][0][:, 0:1], mhalf[:, 0:1], ALU.pow, [sm[1][1], B_C], [sm[2][1]])
            yield
            ts("dve", kvn[:], kvp[:, 0:128], sm[2][0][:, 0:1], ALU.mult, [Bkvp, sm[2][1]], [Bkvn])
            yield
            act(rt1[:], kvp[:, 128:160], AF.Square, [Bkvp], [sm[3][1], Brt1], accum=sm[3][0][:, 0:1])
            yield
            tt("pool", kr[:], kvp[:, 128:160], gk_b[:, 64:96], ALU.mult, [Bkvp, B_W], [Bkr])
            yield
            tt("pool", rt1[:], kr[:], cs[:], ALU.mult, [Bkr, Bcs], [Brt1])
            yield
            tt("pool", rt2[:, 0:16], kr[:, 16:32], sn[:, 0:16], ALU.mult, [Bkr, Bcs], [Brt2])
            yield
            tt("pool", rt2[:, 16:32], kr[:, 0:16], sn[:, 16:32], ALU.mult, [Bkr, Bcs], [Brt2])
            yield
            tt("pool", krr[:], rt1[:], rt2[:], ALU.add, [Brt1, Brt2], [Bkrr])
            yield
            r0, r1 = SMR["kvn" + kd]
            Bsmr = B_sm["kvn" + kd]
            yield ACQ(L2)
            tr(psm[:, r0:r1], kvn[:], [Bkvn], [Bsmr])
            yield
            cp("act", kvnT[:], psm[:, r0:r1], [Bsmr], [BkvnT])
            yield REL(L2)
            yield ACQ(LBIG)
            for hf in range(2):
                mm(big[:, hf * 512:(hf + 1) * 512], kvnT[:], w_kvb[:, hf * 512:(hf + 1) * 512], True, True,
                   [BkvnT, B_W], [B_big])
            yield
            kv3 = big[:].rearrange("p (h d) -> p h d", h=8)
            cp("act", vst[:, :, 0:64], kv3[:, :, 64:128], [B_big], [Bvst])
            yield
            cp("pool", vst[:, :, 64:128], vvalid[:, n:n + 1].unsqueeze(2).to_broadcast([128, 8, 64]), [B_C], [Bvst])
            yield
            P.dma(v_scr[n].rearrange("p h d -> p (h d)"), vst[:].rearrange("p h d -> p (h d)"), reads=[Bvst], slot="vst" + kd)
            act(sqk[:], kv3[:, :, 0:64], AF.Square, [B_big], [Bsqk])
            yield
            red("dve", sm[4][0][:], sqk[:], [Bsqk], [sm[4][1]])
            yield
            ts("dve", sm[4][0][:], sm[4][0][:], sm[3][0][:, 0:1], ALU.add, [sm[4][1], sm[3][1]], [sm[4][1]])
            yield
            ts("pool", sm[5][0][:], sm[4][0][:], 1.0 / 96, ALU.mult, [sm[4][1]], [sm[5][1]], s2=EPS, op1=ALU.add)
            yield
            tt("pool", sm[6][0][:], sm[5][0][:], mhalf[:, 0:8], ALU.pow, [sm[5][1], B_C], [sm[6][1]])
            yield
            tt("dve", sqk[:], kv3[:, :, 0:64], sm[6][0][:].unsqueeze(2).to_broadcast([128, 8, 64]), ALU.mult,
               [B_big, sm[6][1], Bsqk], [Bsqk])
            yield REL(LBIG)
            tt("pool", kt[:, :, 0:64], sqk[:], gk_b[:, 0:64].unsqueeze(1).to_broadcast([128, 8, 64]), ALU.mult,
               [Bsqk, B_W], [Bkt])
            yield
            tt("dve", kt[:, :, 64:96], krr[:].unsqueeze(1).to_broadcast([128, 8, 32]),
               sm[6][0][:].unsqueeze(2).to_broadcast([128, 8, 32]), ALU.mult, [Bkrr, sm[6][1]], [Bkt])
            yield
            yield ACQ(L5)
            for h in range(8):
                tr(ptr[0:96, h, :], kt[:, h, :], [Bkt], [B_b5])
            yield
            cp("act", kTst[0:96], ptr[0:96], [B_b5], [BkTst])
            yield REL(L5)
            P.dma(kT_scr.rearrange("h d t -> d h t")[:, :, n * 128:(n + 1) * 128], kTst[0:96], reads=[BkTst], slot="kTst" + kd)
            yield

          def swakv(kd, i):
            par = i % 2
            n = 2 * i + (0 if kd == "P" else 1)
            K = KS[kd]
            kvp, Bkvp = kvps[(kd, par)]
            sm = K["sm"]
            (sq2, Bsq2), (ks1, Bks1), (ksd, Bksd) = K["sq2"], K["ks1"], K["ksd"]
            skTk, BskT = skT[(kd, par)]
            svk, Bsv = sv[(kd, par)]
            skp = kvp[:, 160:288].rearrange("p (g d) -> p g d", g=2)
            act(sq2[:], skp, AF.Square, [Bkvp], [Bsq2])
            yield
            red("dve", sm[7][0][:, 0:2], sq2[:], [Bsq2], [sm[7][1]])
            yield
            ts("pool", sm[7][0][:, 2:4], sm[7][0][:, 0:2], 1.0 / 64, ALU.mult, [sm[7][1]], [sm[7][1]], s2=EPS, op1=ALU.add)
            yield
            tt("pool", sm[7][0][:, 4:6], sm[7][0][:, 2:4], mhalf[:, 0:2], ALU.pow, [sm[7][1], B_C], [sm[7][1]])
            yield
            tt("dve", ks1[:], skp, sm[7][0][:, 4:6].unsqueeze(2).to_broadcast([128, 2, 64]), ALU.mult, [Bkvp, sm[7][1]], [Bks1])
            yield
            tt("pool", ksd[:], ks1[:], gsk_b[:].unsqueeze(1).to_broadcast([128, 2, 64]), ALU.mult, [Bks1, B_W], [Bksd])
            yield
            r0, r1 = SMR["kT" + kd]
            Bsmr = B_sm["kT" + kd]
            yield ACQ(L2)
            for g in range(2):
                tr(psm[0:64, r0 + g * 128:r0 + (g + 1) * 128], ksd[:, g, :], [Bksd], [Bsmr])
            yield
            cp("act", skTk[0:64], psm[0:64, r0:r1].rearrange("p (g t) -> p g t", g=2), [Bsmr], [BskT])
            yield REL(L2)
            cp("act", svk[:, :, 0:64], kvp[:, 288:416].rearrange("p (g d) -> p g d", g=2), [Bkvp], [Bsv])
            yield
            cp("pool", svk[:, :, 64:128], vvalid[:, n:n + 1].unsqueeze(2).to_broadcast([128, 2, 64]), [B_C], [Bsv])
            yield

          def chain_pa(i):
            yield from front("P", i)
            yield from kmla("P", i)

          def chain_pb(i):
            yield from front("O", i)

          def chain_pc(i):
            yield ("wait", ("kvP", i))
            yield from swakv("P", i)

          def chain_next(i):
            return [chain_pa(i), chain_pb(i), chain_pc(i)]

          def chain_ok(i):
            yield from swakv("O", i)
            yield ("set", ("swakv", i))

          def chain_ok2(i):
            yield from kmla("O", i)

          def chain_q(i):
            par = i % 2
            hTo, BhTo = hT[("O", par)]
            cs, sn, Bcs = tabs[("O", par)]
            yield ACQ(L0)
            qp = bk0
            for kc in range(8):
                mm(qp[:, 0:256], hTo[:, kc, 2:130], w_in[:, kc, C_QL:C_QL + 256], kc == 0, kc == 7, [BhTo, B_W], [B_b0])
            yield
            act(qln[:], qp[:, 0:256], AF.Square, [B_b0], [QS[0][1], B_qln], accum=QS[0][0][:, 0:1])
            yield
            ts("pool", QS[1][0][:, 0:1], QS[0][0][:, 0:1], 1.0 / 256, ALU.mult, [QS[0][1]], [QS[1][1]], s2=EPS, op1=ALU.add)
            yield
            tt("pool", QS[2][0][:, 0:1], QS[1][0][:, 0:1], mhalf[:, 0:1], ALU.pow, [QS[1][1], B_C], [QS[2][1]])
            yield
            ts("dve", qln[:], qp[:, 0:256], QS[2][0][:, 0:1], ALU.mult, [B_b0, QS[2][1]], [B_qln])
            yield REL(L0)
            r0, r1 = SMR["qln"]
            yield ACQ(L2)
            for c in range(2):
                tr(psm[:, r0 + c * 128:r0 + (c + 1) * 128], qln[:, c * 128:(c + 1) * 128], [B_qln], [B_sm["qln"]])
            yield
            cp("act", qlnT[:], psm[:, r0:r1].rearrange("p (c t) -> p c t", c=2), [B_sm["qln"]], [B_qlnT])
            yield REL(L2)
            yield ACQ(LBIG)
            for (c0, c1) in ((0, 512), (512, 768)):
                for c in range(2):
                    mm(big[:, c0:c1], qlnT[:, c, :], w_qb[:, c, c0:c1], c == 0, c == 1, [B_qlnT, B_W], [B_big])
            yield
            q3 = big[:, 0:768].rearrange("p (h d) -> p h d", h=8)
            act(sqq[:], q3, AF.Square, [B_big], [B_sqq])
            yield
            red("dve", QS[3][0][:], sqq[:], [B_sqq], [QS[3][1]])
            yield
            ts("pool", QS[4][0][:], QS[3][0][:], 1.0 / 96, ALU.mult, [QS[3][1]], [QS[4][1]], s2=EPS, op1=ALU.add)
            yield
            tt("pool", QS[5][0][:], QS[4][0][:], mhalf[:, 0:8], ALU.pow, [QS[4][1], B_C], [QS[5][1]])
            yield
            tt("dve", sqq[:], q3, QS[5][0][:].unsqueeze(2).to_broadcast([128, 8, 96]), ALU.mult, [B_big, QS[5][1], B_sqq], [B_sqq])
            yield REL(LBIG)
            tt("pool", qt[:, :, 0:64], sqq[:, :, 0:64], gq_b[:, 0:64].unsqueeze(1).to_broadcast([128, 8, 64]),
               ALU.mult, [B_sqq, B_W], [B_qt])
            yield
            tt("pool", qr[:], sqq[:, :, 64:96], gq_b[:, 64:96].unsqueeze(1).to_broadcast([128, 8, 32]), ALU.mult,
               [B_sqq, B_W], [B_qr])
            yield
            tt("dve", qr1[:], qr[:], cs[:].unsqueeze(1).to_broadcast([128, 8, 32]), ALU.mult, [B_qr, Bcs], [B_qr1])
            yield
            tt("pool", qr2[:, :, 0:16], qr[:, :, 16:32], sn[:, 0:16].unsqueeze(1).to_broadcast([128, 8, 16]), ALU.mult,
               [B_qr, Bcs], [B_qr2])
            yield
            tt("pool", qr2[:, :, 16:32], qr[:, :, 0:16], sn[:, 16:32].unsqueeze(1).to_broadcast([128, 8, 16]), ALU.mult,
               [B_qr, Bcs], [B_qr2])
            yield
            tt("dve", qt[:, :, 64:96], qr1[:], qr2[:], ALU.add, [B_qr1, B_qr2], [B_qt])
            yield
            yield ACQ(L5)
            for h in range(8):
                tr(ptr[0:96, h, :], qt[:, h, :], [B_qt], [B_b5])
            yield
            cp("act", qTst[0:96], ptr[0:96], [B_b5], [B_qTst])
            yield REL(L5)
            P.dma(qT_scr.rearrange("h d t -> d h t")[:, :, i * 128:(i + 1) * 128], qTst[0:96], reads=[B_qTst], slot="qTst")
            yield

          def chain_swa(i):
            par = i % 2
            hTo, BhTo = hT[("O", par)]
            sqp = bk7
            for kc in range(8):
                mm(sqp[:], hTo[:, kc, 2:130], w_in[:, kc, C_SQ:C_SQ + 512], kc == 0, kc == 7, [BhTo, B_W], [B_b7])
            yield
            sq3 = sqp[:].rearrange("p (h d) -> p h d", h=8)
            act(sqs[:], sq3, AF.Square, [B_b7], [B_sqs])
            yield
            red("dve", SS[0][0][:], sqs[:], [B_sqs], [SS[0][1]])
            yield
            ts("pool", SS[1][0][:], SS[0][0][:], 1.0 / 64, ALU.mult, [SS[0][1]], [SS[1][1]], s2=EPS, op1=ALU.add)
            yield
            tt("pool", SS[2][0][:], SS[1][0][:], mhalf[:, 0:8], ALU.pow, [SS[1][1], B_C], [SS[2][1]])
            yield
            tt("dve", sqs[:], sq3, SS[2][0][:].unsqueeze(2).to_broadcast([128, 8, 64]), ALU.mult, [B_b7, SS[2][1], B_sqs], [B_sqs])
            yield
            tt("pool", sqb[:], sqs[:], gsq_b[:].unsqueeze(1).to_broadcast([128, 8, 64]), ALU.mult, [B_sqs, B_W], [B_sqb])
            yield
            yield ACQ(L0)
            for h in range(8):
                tr(ph[0:64, h * 128:(h + 1) * 128], sqb[:, h, :], [B_sqb], [B_b0])
            yield
            cp("act", sqT[0:64], ph[0:64, :].rearrange("p (c t) -> p c t", c=8), [B_b0], [B_sqT])
            yield REL(L0)
            yield ("wait", ("swakv", i))
            ns = 0
            for g in range(2):
                yield ACQ(L5)
                Ops = bk5
                for kb, kk in enumerate(("P", "O")):
                    skTk, BskT = skT[(kk, par)]
                    svk, Bsv = sv[(kk, par)]
                    for e4 in range(4):
                        h = 4 * g + e4
                        mm(bk7[:, e4 * 128:(e4 + 1) * 128], skTk[0:64, g, :], sqT[0:64, h, :], True, True,
                           [BskT, B_sqT], [B_b7])
                    yield
                    act(Ebuf[:], bk7[:], AF.Exp, [B_b7], [B_E], scale=0.125)
                    yield
                    pm, BP = Pm[ns % 2]
                    tt("dve", pm[:], Ebuf[:], swaM[:, kb, 4 * g:4 * g + 4, :].rearrange("p h q -> p (h q)"), ALU.mult,
                       [B_E, B_C], [BP])
                    yield
                    mm(Ops[:], svk[:, g, :], pm[:], kb == 0, kb == 1, [Bsv, BP], [B_b5])
                    yield
                    ns += 1
                tt("dve", rden[64:128, :].rearrange("p (h q) -> p h q", h=4),
                   Ops[64:128, :].rearrange("p (h q) -> p h q", h=4),
                   esink[64:128, 4 * g:4 * g + 4].unsqueeze(2).to_broadcast([64, 4, 128]), ALU.add, [B_b5, B_W], [B_rden])
                yield
                rcp(rden[64:128, :], rden[64:128, :], [B_rden], [B_rden])
                yield
                for e4 in range(4):
                    h = 4 * g + e4
                    hb_ = (h % 2) * 64
                    tt("dve", tsw[hb_:hb_ + 64, h // 2, :], Ops[0:64, e4 * 128:(e4 + 1) * 128],
                       rden[64:128, e4 * 128:(e4 + 1) * 128], ALU.mult, [B_b5, B_rden], [B_tsw])
                    yield
                yield REL(L5)

          def silu_from(G, out_ap, B_out):
            act(sge[:], G[:], AF.Tanh, [B_b6], [B_sge], scale=0.5)
            yield
            stt("dve", out_ap, sge[:], 1.0, G[:], ALU.add, ALU.mult, [B_b6, B_sge], [B_out])
            yield

          def chain_gc(i):
            par = i % 2
            hTo, BhTo = hT[("O", par)]

            def fm_mm(ps_ap, col0, lo, hi, Bps):
                for kc in range(8):
                    mm(ps_ap, w_in[:, kc, col0:col0 + 128], hTo[:, kc, lo:hi], kc == 0, kc == 7, [BhTo, B_W], [Bps])

            G = bk6
            for c in range(4):
                fm_mm(G[:, c * 128:(c + 1) * 128], C_GC + c * 128, 2, 130, B_b6)
                yield
            yield from silu_from(G, sgc[:].rearrange("p c t -> p (c t)"), B_sgc)
            yield ("set", ("sgc", i))
            for c in range(4):
                fm_mm(G[:, c * 128:(c + 1) * 128], C_GM + c * 128, 2, 130, B_b6)
                yield
            yield from silu_from(G, sgm[:].rearrange("p c t -> p (c t)"), B_sgm)
            P.dma(sg_scr.rearrange("c p t -> p c t")[:, :, i * 128:(i + 1) * 128], sgm[:], reads=[B_sgm], slot="sgm")
            for c in range(4):
                fm_mm(G[:, c * 128:(c + 1) * 128], C_GS + c * 128, 2, 130, B_b6)
                yield
            yield from silu_from(G, sgs[:].rearrange("p c t -> p (c t)"), B_sgs)

          def chain_conv(i):
            par = i % 2
            hTo, BhTo = hT[("O", par)]

            def fm_mm(ps_ap, col0, lo, hi, Bps):
                for kc in range(8):
                    mm(ps_ap, w_in[:, kc, col0:col0 + 128], hTo[:, kc, lo:hi], kc == 0, kc == 7, [BhTo, B_W], [Bps])

            for cc in range(4):
                yield ACQ(L1)
                X = bk1
                fm_mm(X[:, 0:130], C_CH + cc * 128, 0, 130, B_b1)
                yield
                fm_mm(X[:, 130:260], C_CC + cc * 128, 0, 130, B_b1)
                yield
                fm_mm(X[:, 260:388], C_CB + cc * 128, 2, 130, B_b1)
                yield
                cp("act", ch_sb[:], X[:, 0:130], [B_b1], [B_ch])
                yield
                tt("dve", u_sb[:], ch_sb[:], X[:, 130:260], ALU.mult, [B_ch, B_b1], [B_u])
                yield
                if cc == 0:
                    yield ("wait", ("sgc", i))
                tt("dve", tcb[:], X[:, 260:388], sgc[:, cc, :], ALU.mult, [B_b1, B_sgc], [B_tcb])
                yield REL(L1)
                ts("pool", cy[0][0][:], u_sb[:, 2:130], convw[:, 2, cc:cc + 1], ALU.mult, [B_u, B_W], [cy[0][1]])
                yield
                stt("dve", cy[1][0][:], u_sb[:, 1:129], convw[:, 1, cc:cc + 1], cy[0][0][:], ALU.mult, ALU.add,
                    [B_u, B_W, cy[0][1]], [cy[1][1]])
                yield
                stt("dve", cy[0][0][:], u_sb[:, 0:128], convw[:, 0, cc:cc + 1], cy[1][0][:], ALU.mult, ALU.add,
                    [B_u, B_W, cy[1][1]], [cy[0][1]])
                yield
                tt("pool", yT[:, cc, :], cy[0][0][:], tcb[:], ALU.mult, [cy[0][1], B_tcb], [B_yT])
                yield

          def tail(i):
            xs, Bx = x_O[i % 2]
            tt("pool", yT[:, 4:8, :], tsw[:], sgs[:], ALU.mult, [B_tsw, B_sgs], [B_yT])
            for hf in range(2):
                for kc in range(8):
                    mm(big[:, hf * 512:(hf + 1) * 512], yT[:, kc, :], w_out[:, 4 + kc, hf * 512:(hf + 1) * 512],
                       kc == 0, kc == 7, [B_yT, B_W], [B_big])
            tt("dve", osb[:], big[:], xs[:], ALU.add, [B_big, Bx], [B_osb])
            P.dma(part_scr[i], osb[:], reads=[B_osb], slot="part")

          nb_run = NB if not (dbg is not None and dbg.startswith("b")) else int(dbg[1:])
          load_xP(0)
          load_rest(0)
          if nb_run > 1:
              load_xP(1)
          drive(chain_next(0))
          for i in range(nb_run):
              chains = [chain_ok(i), chain_ok2(i), chain_q(i), chain_swa(i), chain_gc(i), chain_conv(i)]
              if i + 1 < nb_run:
                  load_rest(i + 1)
                  if i + 2 < nb_run:
                      load_xP(i + 2)
                  chains = chain_next(i + 1) + chains
              drive(chains)
              assert all(v is None for v in locks.values()), locks
              tail(i)

          if dbg == "p1":
              raise _Stop()
          P.barrier()
          ar.off = 0
          kTp = A([128, 2, NS * 128], BF16)
          vp = A([128, NS, 2, 128], BF16)
          qTp = A([128, 2, NB * 128], BF16)
          maskM = A([128, 8, 512], BF16); B_mask = P.buf("mask")
          PT = [A([128, 512], BF16) for _ in range(4)]; B_PT = [P.buf() for _ in range(4)]
          gbuf = [A([128, 512], BF16) for _ in range(2)]; B_g = [P.buf(), P.buf()]
          ymla = A([128, 4, NB * 128], BF16); B_ym = [P.buf(f"ym{j}") for j in range(8)]
          rden2 = A([128, 512], F32); B_rd2 = P.buf()
          tn = A([128, 512], F32); B_tn = P.buf()
          ptile = [A([128, D], F32) for _ in range(2)]; B_pt = [P.buf(), P.buf()]
          osb2 = [A([128, D], F32) for _ in range(2)]; B_o2 = [P.buf(), P.buf()]
          B_kc = [P.buf(f"kTc{c}") for c in range(8)]
          B_vc = [P.buf(f"vc{c}") for c in range(8)]
          B_q = P.buf("qTp")
          P.dma(maskM[:], maskM_d, writes=[B_mask], slot="c0")
          SC = 96 ** -0.5
          Sb = [bk0, bk1, bk2]
          Ob = [bk5, bk6]
          nO = 0
          nS = 0
          for hp in range(4):
              for c in range(8):
                  for hh in range(2):
                      P.dma(kTp[0:96, hh, c * 1024:(c + 1) * 1024], kT_scr[2 * hp + hh, :, c * 1024:(c + 1) * 1024],
                            writes=[B_kc[c]], slot=f"kl{hh}")
                  P.dma(vp[:, 8 * c:8 * c + 8, :, :], v_scr[8 * c:8 * c + 8, :, 2 * hp:2 * hp + 2, :].rearrange("s p h d -> p s h d"),
                        writes=[B_vc[c]], slot="vl")
                  if c == 0:
                      for hh in range(2):
                          P.dma(qTp[0:96, hh, :], qT_scr[2 * hp + hh], writes=[B_q], slot=f"ql{hh}")
              for j in range(8):
                  gb = gbuf[j % 2]; Bg = B_g[j % 2]
                  P.dma(gb[:], sg_scr[hp, :, j * 512:(j + 1) * 512], writes=[Bg], slot=f"gl{j % 2}")
                  for hh in range(2):
                      Ops = Ob[nO % 2]; BO = B_bk[id(Ops)]
                      nO += 1
                      nsl = 8 * j + 8
                      LA = 2
                      pend = {}
                      for t in range(nsl + LA):
                          if t < nsl:
                              Sps = Sb[nS % 3]; BS = B_bk[id(Sps)]
                              nS += 1
                              mm(Sps[:], kTp[0:96, hh, t * 128:(t + 1) * 128], qTp[0:96, hh, j * 512:(j + 1) * 512],
                                 True, True, [B_kc[t // 8], B_q], [BS])
                              pend[t] = (Sps, BS)
                          if t >= LA:
                              sl = t - LA
                              Sps, BS = pend.pop(sl)
                              pt = PT[sl % 4]; BP = B_PT[sl % 4]
                              act(pt[:], Sps[:], AF.Exp, [BS], [BP], scale=SC)
                              if sl >= 8 * j:
                                  tt("pool", pt[:], pt[:], maskM[:, sl - 8 * j, :], ALU.mult, [BP, B_mask], [BP])
                              mm(Ops[:], vp[:, sl, hh, :], pt[:], sl == 0, sl == nsl - 1, [B_vc[sl // 8], BP], [BO])
                      hb = hh * 64
                      rcp(rden2[64:128, :], Ops[64:128, :], [BO], [B_rd2])
                      tt("dve", tn[hb:hb + 64, :], Ops[0:64, :], rden2[64:128, :], ALU.mult, [BO, B_rd2], [B_tn])
                      tt("pool", ymla[hb:hb + 64, hp, j * 512:(j + 1) * 512], tn[hb:hb + 64, :], gb[hb:hb + 64, :], ALU.mult,
                         [B_tn, Bg], [B_ym[j]])

          if dbg == "p2":
              raise _Stop()
          def load_pt(i):
              P.dma(ptile[i % 2][:], part_scr[i], writes=[B_pt[i % 2]], slot=f"pl{i % 2}")

          load_pt(0)
          for i in range(NB):
              if i + 1 < NB:
                  load_pt(i + 1)
              for hf in range(2):
                  for c in range(4):
                      mm(big[:, hf * 512:(hf + 1) * 512], ymla[:, c, i * 128:(i + 1) * 128],
                         w_out[:, c, hf * 512:(hf + 1) * 512], c == 0, c == 3, [B_ym[i // 4], B_W], [B_big])
              o2 = osb2[i % 2]; Bo2 = B_o2[i % 2]
              tt("dve", o2[:], big[:], ptile[i % 2][:], ALU.add, [B_big, B_pt[i % 2]], [Bo2])
              P.dma(ps["out"](i), o2[:], reads=[Bo2], slot=f"yo{i % 2}")
      except _Stop:
        pass
    print("arena p1 end", p1_end, "total ops recorded", P.total, {e: len(v) for e, v in P.ops.items()})
    P.emit(final_slots=[s_ for s_ in P.slot_counts])
    P.close()
    return nc


def _consts(r):
    half = 16
    inv_freq = np.power(np.float32(10000.0), -np.arange(half, dtype=np.float32) / half).astype(np.float32)
    pidx = np.arange(128, dtype=np.float32)
    cosT = np.zeros((128, NS, 32), np.float32)
    sinT = np.zeros((128, NS, 32), np.float32)
    for s in range(NS):
        gb = s - 1 + r
        pos = (gb * 128 + pidx).astype(np.float32)
        ang = pos[:, None] * inv_freq[None, :]
        c = np.cos(ang).astype(np.float32)
        sn = np.sin(ang).astype(np.float32)
        cosT[:, s, :16] = c
        cosT[:, s, 16:] = c
        sinT[:, s, :16] = -sn
        sinT[:, s, 16:] = sn
    maskM = np.zeros((128, 8, 512), np.float32)
    ki = np.arange(128)[:, None]
    qi = np.arange(128)[None, :]
    tri = (ki <= qi).astype(np.float32)
    for so in range(8):
        for qb in range(4):
            d = 2 * qb + 1
            if so < d:
                maskM[:, so, qb * 128:(qb + 1) * 128] = 1.0
            elif so == d:
                maskM[:, so, qb * 128:(qb + 1) * 128] = tri
    slopes = np.exp2(-8.0 * np.arange(1, 9, dtype=np.float32) / 8).astype(np.float32)
    swaM = np.zeros((128, 2, 8, 128), np.float32)
    for h in range(8):
        d0 = (128 + qi - ki).astype(np.float32)
        swaM[:, 0, h, :] = np.where(d0 < 128, np.exp(-slopes[h] * d0), 0.0)
        d1 = (qi - ki).astype(np.float32)
        swaM[:, 1, h, :] = np.where(d1 >= 0, np.exp(-slopes[h] * np.maximum(d1, 0)), 0.0)
    vvalid = np.ones((128, NS), np.float32)
    if r == 0:
        vvalid[:, 0] = 0.0
    return {"cosT": cosT, "sinT": sinT, "maskM": maskM.astype(ml_dtypes.bfloat16), "swaM": swaM, "vvalid": vvalid}


def _layer_weights(inp, l, suffix):
    f = lambda a: np.ascontiguousarray(a, dtype=np.float32)
    return {
        f"w_in_{suffix}": f(inp["w_in"][l][:, PERM]), f"norm_g_{suffix}": f(inp["norm_g"][l].reshape(8, 128).T),
        f"w_kvb_{suffix}": f(inp["mla_w_kvb"][l]), f"kva_g_{suffix}": f(inp["mla_kv_a_norm"][l].reshape(128, 1)),
        f"w_qb_{suffix}": f(inp["mla_w_qb"][l]), f"qa_g_{suffix}": f(inp["mla_q_a_norm"][l].reshape(2, 128).T),
        f"q_g_{suffix}": f(inp["mla_q_norm"][l]), f"k_g_{suffix}": f(inp["mla_k_norm"][l]),
        f"conv_w_{suffix}": f(inp["conv_w"][l].reshape(3, 4, 128).transpose(2, 0, 1)), f"sq_g_{suffix}": f(inp["swa_q_norm"][l]),
        f"sk_g_{suffix}": f(inp["swa_k_norm"][l]), f"sinks_{suffix}": f(inp["swa_sinks"][l]),
        f"w_out_{suffix}": f(inp["w_out"][l]),
    }


def _shard_x(x):
    xb = x.reshape(4, 64, 128, D)
    outs = []
    for c in range(8):
        b, r = c // 2, c % 2
        own = xb[b, r::2]
        prev = np.zeros_like(own)
        if r == 0:
            prev[1:] = xb[b, 1:63:2]
        else:
            prev[:] = xb[b, 0::2]
        outs.append((np.ascontiguousarray(own), np.ascontiguousarray(prev)))
    return outs


def _gather(res):
    out = np.zeros((4, 64, 128, D), np.float32)
    for c in range(8):
        b, r = c // 2, c % 2
        out[b, r::2] = res[c]["y_out"]
    return out.reshape(4, 8192, D)


_CACHE = {}
FUSED = True


def kernel(**inp):
    inp = {k: np.asarray(v) for k, v in inp.items()}
    x = np.ascontiguousarray(inp["x"], dtype=np.float32)
    depth = inp["w_in"].shape[0]
    consts = [_consts(r) for r in range(2)]
    if FUSED and depth == 2:
        if "nc2" not in _CACHE:
            _CACHE["nc2"] = build_program(2)
        nc = _CACHE["nc2"]
        w = {}
        for l in range(2):
            w.update(_layer_weights(inp, l, str(l)))
        sh = _shard_x(x)
        in_maps = []
        for c in range(8):
            r = c % 2
            o = c + 1 - 2 * r
            m = {"x_own": sh[c][0], "x_prev": sh[c][1], "x_own2": sh[o][0], "x_prev2": sh[o][1]}
            m.update(w)
            m.update(consts[r])
            co = consts[1 - r]
            m["cosT2"] = co["cosT"]; m["sinT2"] = co["sinT"]; m["vvalid2"] = co["vvalid"]
            bw = np.zeros((128, 2), np.float32)
            bw[:, r] = 1.0
            m["blendw"] = bw
            in_maps.append(m)
        res = run_bass_kernel_spmd(nc, in_maps, core_ids=list(range(8)))
        return _gather(res.results).astype(np.float32)
    if "nc1" not in _CACHE:
        _CACHE["nc1"] = build_program(1)
    for l in range(depth):
        nc = _CACHE["nc1"]
        w = _layer_weights(inp, l, "0")
        sh = _shard_x(x)
        in_maps = []
        for c in range(8):
            m = {"x_own": sh[c][0], "x_prev": sh[c][1]}
            m.update(w)
            m.update(consts[c % 2])
            in_maps.append(m)
        res = run_bass_kernel_spmd(nc, in_maps, core_ids=list(range(8)))
        x = _gather(res.results)
    return x.astype(np.float32)
```

```python
import numpy as np
import ml_dtypes
from contextlib import ExitStack
import concourse.bass as bass
import concourse.mybir as mybir
from concourse.bass_utils import run_bass_kernel_spmd

F32 = mybir.dt.float32
BF16 = mybir.dt.bfloat16
ALU = mybir.AluOpType
AF = mybir.ActivationFunctionType
AX = mybir.AxisListType

NB = 32
NS = 64
D = 1024
NCOL = 4256
EPS = 1e-6
C_KVL, C_KR, C_SK, C_SV, C_QL, C_SQ, C_GM, C_GS, C_CH, C_CC, C_CB, C_GC = (
    0, 128, 160, 288, 416, 672, 1184, 1696, 2208, 2720, 3232, 3744)
PERM = np.concatenate([np.arange(256, 384), np.arange(384, 416), np.arange(3488, 3616), np.arange(3616, 3744),
                       np.arange(0, 256), np.arange(2976, 3488), np.arange(416, 928), np.arange(3744, 4256),
                       np.arange(928, 1440), np.arange(1952, 2464), np.arange(1440, 1952), np.arange(2464, 2976)])

COMPUTE = ("pe", "act", "dve", "pool")
ALLENG = COMPUTE + ("sp",)


class Buf:
    __slots__ = ("name", "lw", "rd")

    def __init__(self, name):
        self.name = name
        self.lw = None
        self.rd = []


class Op:
    __slots__ = ("eng", "fn", "waits", "signal", "dma", "slot", "slot_cnt")

    def __init__(self, eng, fn, dma=False, slot=None):
        self.eng = eng
        self.fn = fn
        self.waits = []
        self.signal = False
        self.dma = dma
        self.slot = slot
        self.slot_cnt = 0


class Prog:
    def __init__(self, nc):
        self.nc = nc
        self.ops = {e: [] for e in ALLENG}
        self.seen = {e: {} for e in ALLENG}
        self.pending = {e: [] for e in ALLENG}
        self.slot_counts = {}
        self.stack = ExitStack()
        self.nbuf = 0

    def sbuf(self, name, shape, dtype):
        return self.stack.enter_context(self.nc.sbuf_tensor("sb_" + name, list(shape), dtype))

    def psum(self, name, shape, dtype):
        return self.stack.enter_context(self.nc.psum_tensor("ps_" + name, list(shape), dtype))

    def buf(self, name=None):
        self.nbuf += 1
        return Buf(name or f"b{self.nbuf}")

    def _dep(self, op, eng, key):
        kind, k, v = key
        if kind == "eng" and k == "pe" and eng == "pe":
            return
        seen = self.seen[eng]
        if seen.get((kind, k), -1) >= v:
            return
        seen[(kind, k)] = v
        op.waits.append(key)
        if kind == "eng":
            self.ops[k][v].signal = True

    limit = None
    total = 0
    trace = None

    def add(self, eng, fn, reads=(), writes=(), dma=False, slot=None):
        self.total += 1
        if self.limit is not None and self.total > self.limit:
            return None
        op = Op(eng, fn, dma=dma, slot=slot)
        if self.trace is not None:
            import sys as _s
            f = _s._getframe(1)
            while f is not None and f.f_code.co_name != "build_program":
                f = f.f_back
            self.trace.append((self.total, eng, f.f_lineno if f else -1))
        idx = len(self.ops[eng])
        for key in self.pending[eng]:
            self._dep(op, eng, key)
        self.pending[eng] = []
        if dma:
            cnt = self.slot_counts.get(slot, 0)
            if cnt > 0:
                self._dep(op, eng, ("slot", slot, cnt))
            self.slot_counts[slot] = cnt + 1
            op.slot_cnt = cnt + 1
            me = ("slot", slot, cnt + 1)
        else:
            me = ("eng", eng, idx)
        for b in reads:
            if b.lw is not None:
                self._dep(op, eng, b.lw)
        for b in writes:
            if b.lw is not None:
                self._dep(op, eng, b.lw)
            for r in b.rd:
                self._dep(op, eng, r)
        for b in reads:
            b.rd.append(me)
        for b in writes:
            b.lw = me
            b.rd = []
        self.ops[eng].append(op)
        return op

    def barrier(self):
        keys = []
        for e in ALLENG:
            for i in range(len(self.ops[e]) - 1, -1, -1):
                if not self.ops[e][i].dma:
                    keys.append(("eng", e, i))
                    break
        for s, c in self.slot_counts.items():
            keys.append(("slot", s, c))
        for e in ALLENG:
            self.pending[e] = self.pending[e] + [k for k in keys if not (k[0] == "eng" and k[1] == e)]

    def dma(self, out, in_, reads=(), writes=(), slot="d0", eng="sp"):
        return self.add(eng, lambda e: e.dma_start(out=out, in_=in_), reads, writes, dma=True, slot=slot)

    def emit(self, final_slots=()):
        nc = self.nc
        st = self.stack
        esem = {e: st.enter_context(nc.semaphore(f"s_{e}")) for e in ALLENG}
        ssem = {s: st.enter_context(nc.semaphore(f"d_{s}")) for s in self.slot_counts}
        cnt = {}
        for e, lst in self.ops.items():
            c = 0
            for i, op in enumerate(lst):
                if op.signal and not op.dma:
                    c += 1
                cnt[(e, i)] = c

        def run(engname, handle):
            for op in self.ops[engname]:
                for kind, k, v in op.waits:
                    if kind == "eng":
                        handle.wait_ge(esem[k], cnt[(k, v)])
                    else:
                        handle.wait_ge(ssem[k], 16 * v)
                ins = op.fn(handle)
                if op.dma:
                    ins.then_inc(ssem[op.slot], 16)
                elif op.signal:
                    ins.then_inc(esem[engname], 1)
            if engname == "sp":
                for s in final_slots:
                    handle.wait_ge(ssem[s], 16 * self.slot_counts[s])

        block = st.enter_context(nc.Block())

        @block.sync
        def _(e):
            run("sp", e)

        @block.tensor
        def _(e):
            run("pe", e)

        @block.scalar
        def _(e):
            run("act", e)

        @block.vector
        def _(e):
            run("dve", e)

        @block.gpsimd
        def _(e):
            run("pool", e)

    def close(self):
        self.stack.close()


WNAMES = ["w_in", "norm_g", "w_kvb", "kva_g", "w_qb", "qa_g", "q_g", "k_g", "conv_w", "sq_g", "sk_g",
          "sinks", "w_out"]
WSHAPES = {"w_in": [D, NCOL], "norm_g": [128, 8], "w_kvb": [128, 1024], "kva_g": [128, 1], "w_qb": [256, 768],
           "qa_g": [128, 2], "q_g": [96], "k_g": [96], "conv_w": [128, 3, 4], "sq_g": [64], "sk_g": [64],
           "sinks": [8], "w_out": [1536, 1024]}


class _Stop(Exception):
    pass


def build_program(nlayers=1, dbg=None):
    nc = bass.Bass("TRN2", target_bir_lowering=False)
    P = Prog(nc)
    import os
    if os.environ.get("K_LIMIT"):
        P.limit = int(os.environ["K_LIMIT"])

    def dram_in(name, shape, dt=F32):
        return nc.dram_tensor(name, list(shape), dt, kind="ExternalInput").ap()

    x_own = dram_in("x_own", [NB, 128, D])
    x_prev = dram_in("x_prev", [NB, 128, D])
    Wd = [{n: dram_in(f"{n}_{l}", WSHAPES[n]) for n in WNAMES} for l in range(nlayers)]
    cosT_d = dram_in("cosT", [128, NS, 32])
    sinT_d = dram_in("sinT", [128, NS, 32])
    maskM_d = dram_in("maskM", [128, 8, 512], BF16)
    swaM_d = dram_in("swaM", [128, 2, 8, 128])
    vvalid_d = dram_in("vvalid", [128, NS])
    y_out = nc.dram_tensor("y_out", [NB, 128, D], F32, kind="ExternalOutput").ap()
    fused = nlayers == 2
    if not fused:
        passes = [dict(W=Wd[0], own=lambda i: x_own[i], prev=lambda i: x_prev[i], cos=cosT_d, sin=sinT_d,
                       vv=vvalid_d, out=lambda i: y_out[i], blend=False)]
    else:
        x_own2 = dram_in("x_own2", [NB, 128, D])
        x_prev2 = dram_in("x_prev2", [NB, 128, D])
        cosT2_d = dram_in("cosT2", [128, NS, 32])
        sinT2_d = dram_in("sinT2", [128, NS, 32])
        vvalid2_d = dram_in("vvalid2", [128, NS])
        blendw_d = dram_in("blendw", [128, 2])
        x1_mine = nc.dram_tensor("x1_mine", [NB, 128, D], F32, kind="ExternalOutput").ap()
        Zb = nc.dram_tensor("x1_other", [NB + 1, 128, D], F32, kind="ExternalOutput").ap()
        passes = [
            dict(W=Wd[0], own=lambda i: x_own[i], prev=lambda i: x_prev[i], cos=cosT_d, sin=sinT_d, vv=vvalid_d,
                 out=lambda i: x1_mine[i], blend=False),
            dict(W=Wd[0], own=lambda i: x_own2[i], prev=lambda i: x_prev2[i], cos=cosT2_d, sin=sinT2_d, vv=vvalid2_d,
                 out=lambda i: Zb[i + 1], blend=False),
            dict(W=Wd[1], own=lambda i: x1_mine[i], prev=lambda i: Zb[i], prev2=lambda i: Zb[i + 1], cos=cosT_d,
                 sin=sinT_d, vv=vvalid_d, out=lambda i: y_out[i], blend=True),
        ]

    skind = "ExternalOutput"
    kT_scr = nc.dram_tensor("kT_scr", [8, 96, NS * 128], BF16, kind=skind).ap()
    v_scr = nc.dram_tensor("v_scr", [NS, 128, 8, 128], BF16, kind=skind).ap()
    qT_scr = nc.dram_tensor("qT_scr", [8, 96, NB * 128], BF16, kind=skind).ap()
    sg_scr = nc.dram_tensor("sg_scr", [4, 128, NB * 128], BF16, kind=skind).ap()
    part_scr = nc.dram_tensor("part_scr", [NB, 128, D], F32, kind=skind).ap()

    ident = P.sbuf("ident", [128, 128], BF16); B_ident = P.buf("ident")
    idf = P.sbuf("idf", [128, 128], F32)
    w_kvb = P.sbuf("w_kvb", [128, 1024], BF16)
    w_qb = P.sbuf("w_qb", [128, 2, 768], BF16)
    w_out = P.sbuf("w_out", [128, 12, 1024], BF16)
    swaM = P.sbuf("swaM", [128, 2, 8, 128], F32)
    vvalid = P.sbuf("vvalid", [128, NS], F32)
    gq_b = P.sbuf("gq_b", [128, 96], F32)
    gk_b = P.sbuf("gk_b", [128, 96], F32)
    gsq_b = P.sbuf("gsq_b", [128, 64], F32)
    gsk_b = P.sbuf("gsk_b", [128, 64], F32)
    esink = P.sbuf("esink", [128, 8], F32)
    normg = P.sbuf("normg", [128, 8], F32)
    convw = P.sbuf("convw", [128, 3, 4], F32)
    kvag = P.sbuf("kvag", [128, 1], F32)
    qag = P.sbuf("qag", [128, 2], F32)
    eps_t = P.sbuf("eps_t", [128, 1], F32)
    mhalf = P.sbuf("mhalf", [128, 8], F32)

    def mhalf_like(ap):
        return mhalf[:, 0:ap.shape[-1]]
    B_W = P.buf("weights")
    B_C = P.buf("consts")

    ARENA_B = 167 * 1024
    arena = P.sbuf("arena", [128, ARENA_B // 2], BF16)

    class Arena:
        def __init__(self):
            self.off = 0

        def alloc(self, shape, dt, parts=128):
            n = int(np.prod(shape[1:]))
            nbytes = n * (4 if dt == F32 else 2)
            nbytes = (nbytes + 31) // 32 * 32
            o = self.off
            self.off += nbytes
            assert self.off <= ARENA_B, f"arena overflow {self.off}"
            v = arena[:, o // 2:(o + nbytes) // 2]
            if dt == F32:
                v = v.bitcast(F32)
            v = v[:, 0:n]
            if len(shape) == 3:
                v = v.rearrange("p (a b) -> p a b", a=shape[1])
            elif len(shape) == 4:
                v = v.rearrange("p (a b c) -> p a b c", a=shape[1], b=shape[2])
            return v

    banks = [P.psum(f"bank{i}", [128, 512], F32) for i in range(3)]
    big = P.psum("big", [128, 1024], F32)
    banks += [P.psum(f"bank{i}", [128, 512], F32) for i in range(5, 8)]
    bk0, bk1, bk2, bk5, bk6, bk7 = banks
    B_bk = {id(b): P.buf(f"bk{i}") for i, b in enumerate(banks)}
    B_big = P.buf("big")

    def bf(ps):
        return ps[:].bitcast(BF16)

    def tt(eng, out, in0, in1, op, R, W):
        return P.add(eng, lambda e: e.tensor_tensor(out=out, in0=in0, in1=in1, op=op), R, W)

    def ts(eng, out, in0, s1, op0, R, W, s2=None, op1=None):
        if op1 is None:
            return P.add(eng, lambda e: e.tensor_scalar(out=out, in0=in0, scalar1=s1, scalar2=None, op0=op0), R, W)
        return P.add(eng, lambda e: e.tensor_scalar(out=out, in0=in0, scalar1=s1, scalar2=s2, op0=op0, op1=op1), R, W)

    def stt(eng, out, in0, scalar, in1, op0, op1, R, W):
        return P.add(eng, lambda e: e.scalar_tensor_tensor(out=out, in0=in0, scalar=scalar, in1=in1, op0=op0, op1=op1), R, W)

    def act(out, in_, func, R, W, scale=None, bias=None, accum=None):
        kw = {}
        if scale is not None:
            kw["scale"] = scale
        if bias is not None:
            kw["bias"] = bias
        if accum is not None:
            kw["accum_out"] = accum
        return P.add("act", lambda e: e.activation(out=out, in_=in_, func=func, **kw), R, W)

    def cp(eng, out, in_, R, W):
        if eng == "act":
            return P.add("act", lambda e: e.activation(out=out, in_=in_, func=AF.Copy), R, W)
        return P.add(eng, lambda e: e.tensor_copy(out=out, in_=in_), R, W)

    def red(eng, out, in_, R, W):
        return P.add(eng, lambda e: e.tensor_reduce(out=out, in_=in_, axis=AX.X, op=ALU.add), R, W)

    def rcp(out, in_, R, W):
        return P.add("dve", lambda e: e.reciprocal(out=out, in_=in_), R, W)

    def mm(out, lhsT, rhs, start, stop, R, W):
        return P.add("pe", lambda e: e.matmul(out, lhsT=lhsT, rhs=rhs, start=start, stop=stop), R, W)

    def tr(out, in_, R, W):
        return P.add("pe", lambda e: e.transpose(out=out, in_=in_, identity=ident[:]), list(R) + [B_ident], W)

    def rstd(ss_ap, out_ap, scale, R_ss, B_out, tmp_ap, B_tmp):
        act(tmp_ap, ss_ap, AF.Sqrt, [R_ss, B_C], [B_tmp], scale=scale, bias=eps_t[:, 0:1])
        rcp(out_ap, tmp_ap, [B_tmp], [B_out])

    P.add("pool", lambda e: e.memset(idf[:], 0.0), [], [B_ident])
    P.add("pool", lambda e: e.affine_select(out=idf[:], in_=idf[:], pattern=[[-1, 128]], compare_op=ALU.not_equal,
                                            fill=1.0, base=0, channel_multiplier=1), [B_ident], [B_ident])
    P.add("pool", lambda e: e.tensor_copy(out=ident[:], in_=idf[:]), [B_ident], [B_ident])
    P.add("pool", lambda e: e.memset(eps_t[:], EPS), [], [B_C])
    P.add("pool", lambda e: e.memset(mhalf[:], -0.5), [], [B_C])
    P.dma(swaM[:], swaM_d, writes=[B_C], slot="c0")
    blendw = P.sbuf("blendw", [128, 2], F32)
    if fused:
        P.dma(blendw[:], blendw_d, writes=[B_C], slot="c1")
        zero_t = arena[:, 0:2 * D].bitcast(F32)
        P.add("pool", lambda e: e.memset(zero_t, 0.0), [], [B_C])
        P.dma(Zb[0], zero_t, reads=[B_C], slot="c1")

    for ps in passes:
      try:
          W = ps["W"]
          cosT_d = ps["cos"]; sinT_d = ps["sin"]
          P.barrier()
          P.dma(vvalid[:], ps["vv"], writes=[B_C], slot="c1")
          ar = Arena()
          w_in = ar.alloc([128, 8, NCOL], BF16)
          stage = [ar.alloc([128, 2128], F32) for _ in range(2)]
          B_st = [P.buf("st0"), P.buf("st1")]
          mark = ar.off
          P.dma(normg[:], W["norm_g"], writes=[B_W], slot="c0")
          P.dma(convw[:], W["conv_w"], writes=[B_W], slot="c1")
          P.dma(kvag[:], W["kva_g"], writes=[B_W], slot="c0")
          P.dma(qag[:], W["qa_g"], writes=[B_W], slot="c1")
          P.dma(gq_b[:], W["q_g"].partition_broadcast(128), writes=[B_W], slot="c0")
          P.dma(gk_b[:], W["k_g"].partition_broadcast(128), writes=[B_W], slot="c1")
          P.dma(gsq_b[:], W["sq_g"].partition_broadcast(128), writes=[B_W], slot="c0")
          P.dma(gsk_b[:], W["sk_g"].partition_broadcast(128), writes=[B_W], slot="c1")
          P.dma(esink[:], W["sinks"].partition_broadcast(128), writes=[B_W], slot="c0")
          act(esink[:], esink[:], AF.Exp, [B_W], [B_W])
          n = 0
          engs = ["dve", "dve"]
          win_d = W["w_in"].rearrange("(kc p) c -> p kc c", p=128)
          for kc in range(8):
              for hf in range(2):
                  s = n % 2
                  P.dma(stage[s][:], win_d[:, kc, hf * 2128:(hf + 1) * 2128], writes=[B_st[s]], slot=f"st{s}")
                  ts(engs[n % 2], w_in[:, kc, hf * 2128:(hf + 1) * 2128], stage[s][:], normg[:, kc:kc + 1], ALU.mult,
                     [B_st[s], B_W], [B_W])
                  n += 1
          wout_d = W["w_out"].rearrange("(kc p) c -> p kc c", p=128)
          for kc in range(12):
              s = n % 2
              P.dma(stage[s][:, 0:1024], wout_d[:, kc, :], writes=[B_st[s]], slot=f"st{s}")
              ts(engs[n % 2], w_out[:, kc, :], stage[s][:, 0:1024], 0.5, ALU.mult, [B_st[s]], [B_W])
              n += 1
          s = n % 2
          P.dma(stage[s][:, 0:1024], W["w_kvb"], writes=[B_st[s]], slot=f"st{s}")
          ts(engs[n % 2], w_kvb[:], stage[s][:, 0:1024], kvag[:, 0:1], ALU.mult, [B_st[s], B_W], [B_W])
          n += 1
          wqb_d = W["w_qb"].rearrange("(c p) n -> p c n", p=128)
          for c in range(2):
              s = n % 2
              P.dma(stage[s][:, 0:768], wqb_d[:, c, :], writes=[B_st[s]], slot=f"st{s}")
              ts(engs[n % 2], w_qb[:, c, :], stage[s][:, 0:768], qag[:, c:c + 1], ALU.mult, [B_st[s], B_W], [B_W])
              n += 1

          if dbg == "w":
              raise _Stop()
          P.barrier()
          ar.off = mark - 2 * ((2128 * 4 + 31) // 32 * 32)
          A = ar.alloc

          def AB(shape, dt, name):
            return A(shape, dt), P.buf(name)

          x_P = [AB([128, D], F32, f"xP{k}") for k in range(2)]
          x_O = [AB([128, D], F32, f"xO{k}") for k in range(2)]
          xb2, B_xb2 = AB([128, D], F32, "xb2")
          tabs = {(kd, k): (A([128, 32], F32), A([128, 32], F32), P.buf(f"tab{kd}{k}")) for kd in "PO" for k in range(2)}
          h_bf = [AB([128, D], BF16, f"h{k}") for k in range(2)]
          hT = {(kd, k): AB([128, 8, 130], BF16, f"hT{kd}{k}") for kd in "PO" for k in range(2)}
          kvps = {(kd, k): AB([128, 416], F32, f"kvps{kd}{k}") for kd in "PO" for k in range(2)}
          skT = {(kd, k): AB([128, 2, 128], BF16, f"skT{kd}{k}") for kd in "PO" for k in range(2)}
          sv = {(kd, k): AB([128, 2, 128], BF16, f"sv{kd}{k}") for kd in "PO" for k in range(2)}

          def kset(tag):
            d = {}
            d["sm"] = [AB([128, 8], F32, f"sm{tag}{k}") for k in range(8)]
            d["kvn"] = AB([128, 128], BF16, "kvn" + tag)
            d["kvnT"] = AB([128, 128], BF16, "kvnT" + tag)
            d["vst"] = AB([128, 8, 128], BF16, "vst" + tag)
            d["sqk"] = AB([128, 8, 64], F32, "sqk" + tag)
            d["kt"] = AB([128, 8, 96], BF16, "kt" + tag)
            d["kr"] = AB([128, 32], F32, "kr" + tag)
            d["rt1"] = AB([128, 32], F32, "rt1" + tag)
            d["rt2"] = AB([128, 32], F32, "rt2" + tag)
            d["krr"] = AB([128, 32], F32, "krr" + tag)
            d["kTst"] = AB([128, 8, 128], BF16, "kTst" + tag)
            d["sq2"] = AB([128, 2, 64], F32, "sq2" + tag)
            d["ks1"] = AB([128, 2, 64], F32, "ks1" + tag)
            d["ksd"] = AB([128, 2, 64], BF16, "ksd" + tag)
            return d

          KS = {"P": kset("A"), "O": kset("B")}
          FS = {kd: [AB([128, 8], F32, f"fs{kd}{k}") for k in range(3)] for kd in "PO"}
          QS = [AB([128, 8], F32, f"qs{k}") for k in range(6)]
          qln, B_qln = AB([128, 256], BF16, "qln")
          qlnT, B_qlnT = AB([128, 2, 128], BF16, "qlnT")
          sqq, B_sqq = AB([128, 8, 96], F32, "sqq")
          qt, B_qt = AB([128, 8, 96], BF16, "qt")
          qr, B_qr = AB([128, 8, 32], F32, "qr")
          qr1, B_qr1 = AB([128, 8, 32], F32, "qr1")
          qr2, B_qr2 = AB([128, 8, 32], F32, "qr2")
          qTst, B_qTst = AB([128, 8, 128], BF16, "qTst")
          SS = [AB([128, 8], F32, f"ss{k}") for k in range(4)]
          sqs, B_sqs = AB([128, 8, 64], F32, "sqs")
          sqb, B_sqb = AB([128, 8, 64], BF16, "sqb")
          sqT, B_sqT = AB([128, 8, 128], BF16, "sqT")
          Ebuf, B_E = AB([128, 512], F32, "E")
          Pm = [AB([128, 512], BF16, "Pm0")] * 2
          rden, B_rden = AB([128, 512], F32, "rden")
          tsw, B_tsw = AB([128, 4, 128], F32, "tsw")
          sgs, B_sgs = AB([128, 4, 128], F32, "sgs")
          sgm, B_sgm = AB([128, 4, 128], BF16, "sgm")
          sgc, B_sgc = AB([128, 4, 128], F32, "sgc")
          ch_sb, B_ch = AB([128, 130], F32, "ch")
          u_sb, B_u = AB([128, 130], F32, "u")
          cy = [AB([128, 128], F32, f"cy{k}") for k in range(2)]
          tcb, B_tcb = AB([128, 128], F32, "tcb")
          yT, B_yT = AB([128, 8, 128], BF16, "yT")
          osb, B_osb = AB([128, D], F32, "osb")
          sge, B_sge = osb[:, 0:512], B_osb
          p1_end = ar.off

          ph = bf(bk0); B_b0 = B_bk[id(bk0)]
          B_b1 = B_bk[id(bk1)]
          psm = bf(bk2)
          B_b2 = B_bk[id(bk2)]
          B_sm = {k: B_b2 for k in ("kvnP", "kvnO", "kTP", "kTO", "qln")}
          L2 = "L2"
          SMR = {"kvnP": (0, 128), "kvnO": (128, 256), "kTP": (256, 512), "kTO": (512, 768), "qln": (768, 1024)}
          ptr = bf(bk5).rearrange("p (h t) -> p h t", h=8); B_b5 = B_bk[id(bk5)]
          B_b6 = B_bk[id(bk6)]; B_b7 = B_bk[id(bk7)]
          L0, L1, LBIG, L5 = "L0", "L1", "LBIG", "L5"

          def ACQ(l):
            return ("acq", l)

          def REL(l):
            return ("rel", l)

          def drive(chains):
            st = [dict(g=g, want=None) for g in chains]
            while st:
                progressed = False
                for c in list(st):
                    while True:
                        if c["want"] is not None:
                            kind, obj = c["want"]
                            if kind == "acq":
                                if locks.get(obj) is None:
                                    locks[obj] = c["g"]
                                elif locks[obj] is not c["g"]:
                                    break
                            elif kind == "wait":
                                if obj not in events:
                                    break
                            c["want"] = None
                        try:
                            r = next(c["g"])
                        except StopIteration:
                            st.remove(c)
                            progressed = True
                            break
                        progressed = True
                        if r is None:
                            break
                        kind, obj = r
                        if kind == "rel":
                            assert locks.get(obj) is c["g"], ("release of unowned lock", obj)
                            locks[obj] = None
                        elif kind == "set":
                            events.add(obj)
                        else:
                            c["want"] = r
                if not progressed:
                    raise RuntimeError("chain deadlock at build time")

          locks = {}
          events = set()

          def load_xP(i):
            xs, Bx = x_P[i % 2]
            P.dma(xs[:], ps["prev"](i), writes=[Bx], slot=f"xP{i % 2}")

          def load_rest(i):
            xs, Bx = x_O[i % 2]
            P.dma(xs[:], ps["own"](i), writes=[Bx], slot=f"xO{i % 2}")
            for kd, n in (("P", 2 * i), ("O", 2 * i + 1)):
                cs, sn, Bt = tabs[(kd, i % 2)]
                P.dma(cs[:], cosT_d[:, n, :], writes=[Bt], slot=f"cs{kd}{i % 2}")
                P.dma(sn[:], sinT_d[:, n, :], writes=[Bt], slot=f"sn{kd}{i % 2}")
            if ps["blend"]:
                P.dma(xb2[:], ps["prev2"](i), writes=[B_xb2], slot="xb2")

          def front(kd, i):
            par = i % 2
            xs, Bx = (x_P if kd == "P" else x_O)[par]
            hb, Bh = h_bf[0 if kd == "P" else 1]
            hTk, BhT = hT[(kd, par)]
            (s0, Bs0), (s1, Bs1), (s2, Bs2) = FS[kd]
            if ps["blend"] and kd == "P":
                ts("dve", xs[:], xs[:], blendw[:, 0:1], ALU.mult, [Bx, B_C], [Bx])
                yield
                stt("dve", xs[:], xb2[:], blendw[:, 1:2], xs[:], ALU.mult, ALU.add, [B_xb2, B_C, Bx], [Bx])
                yield
            act(hb[:], xs[:], AF.Square, [Bx], [Bs0, Bh], accum=s0[:, 0:1])
            yield
            ts("pool", s1[:, 0:1], s0[:, 0:1], 1.0 / D, ALU.mult, [Bs0], [Bs1], s2=EPS, op1=ALU.add)
            yield
            tt("pool", s2[:, 0:1], s1[:, 0:1], mhalf[:, 0:1], ALU.pow, [Bs1, B_C], [Bs2])
            yield
            ts("dve", hb[:], xs[:], s2[:, 0:1], ALU.mult, [Bx, Bs2], [Bh])
            yield
            yield ACQ(L0)
            for kc in range(8):
                tr(ph[:, kc * 128:(kc + 1) * 128], hb[:, kc * 128:(kc + 1) * 128], [Bh], [B_b0])
            yield
            cp("act", hTk[:, :, 2:130], ph.rearrange("p (a b) -> p a b", a=8), [B_b0], [BhT])
            yield REL(L0)
            if kd == "P":
                yield ("set", ("hTP", i))
            if kd == "O":
                yield ("wait", ("hTP", i))
                hTp, BhTp = hT[("P", par)]
                cp("pool", hTk[:, :, 0:2], hTp[:, :, 128:130], [BhTp], [BhT])
                yield
            yield ACQ(L1)
            for kc in range(8):
                mm(bk1[:, 0:416], hTk[:, kc, 2:130], w_in[:, kc, 0:416], kc == 0, kc == 7, [BhT, B_W], [B_b1])
            yield
            kv_sb, Bkv = kvps[(kd, par)]
            cp("act", kv_sb[:], bk1[:, 0:416], [B_b1], [Bkv])
            yield REL(L1)
            yield ("set", ("kv" + kd, i))

          def kmla(kd, i):
            par = i % 2
            n = 2 * i + (0 if kd == "P" else 1)
            K = KS[kd]
            kvp, Bkvp = kvps[(kd, par)]
            cs, sn, Bcs = tabs[(kd, par)]
            sm = K["sm"]
            (kvn, Bkvn), (kvnT, BkvnT), (vst, Bvst), (sqk, Bsqk), (kt, Bkt) = K["kvn"], K["kvnT"], K["vst"], K["sqk"], K["kt"]
            (kr, Bkr), (rt1, Brt1), (rt2, Brt2), (krr, Bkrr), (kTst, BkTst) = K["kr"], K["rt1"], K["rt2"], K["krr"], K["kTst"]
            act(kvn[:], kvp[:, 0:128], AF.Square, [Bkvp], [sm[0][1], Bkvn], accum=sm[0][0][:, 0:1])
            yield
            ts("pool", sm[1][0][:, 0:1], sm[0][0][:, 0:1], 1.0 / 128, ALU.mult, [sm[0][1]], [sm[1][1]], s2=EPS, op1=ALU.add)
            yield
            tt("pool", sm[2][0][:, 0:1], sm[1][0][:, 0:1], mhalf[:, 0:1], ALU.pow, [sm[1][1], B_C], [sm[2][1]])
            yield
            ts("dve", kvn[:], kvp[:, 0:128], sm[2][0][:, 0:1], ALU.mult, [Bkvp, sm[2][1]], [Bkvn])
            yield
            act(rt1[:], kvp[:, 128:160], AF.Square, [Bkvp], [sm[3][1], Brt1], accum=sm[3][0][:, 0:1])
            yield
            tt("pool", kr[:], kvp[:, 128:160], gk_b[:, 64:96], ALU.mult, [Bkvp, B_W], [Bkr])
            yield
            tt("pool", rt1[:], kr[:], cs[:], ALU.mult, [Bkr, Bcs], [Brt1])
            yield
            tt("pool", rt2[:, 0:16], kr[:, 16:32], sn[:, 0:16], ALU.mult, [Bkr, Bcs], [Brt2])
            yield
            tt("pool", rt2[:, 16:32], kr[:, 0:16], sn[:, 16:32], ALU.mult, [Bkr, Bcs], [Brt2])
            yield
            tt("pool", krr[:], rt1[:], rt2[:], ALU.add, [Brt1, Brt2], [Bkrr])
            yield
            r0, r1 = SMR["kvn" + kd]
            Bsmr = B_sm["kvn" + kd]
            yield ACQ(L2)
            tr(psm[:, r0:r1], kvn[:], [Bkvn], [Bsmr])
            yield
            cp("act", kvnT[:], psm[:, r0:r1], [Bsmr], [BkvnT])
            yield REL(L2)
            yield ACQ(LBIG)
            for hf in range(2):
                mm(big[:, hf * 512:(hf + 1) * 512], kvnT[:], w_kvb[:, hf * 512:(hf + 1) * 512], True, True,
                   [BkvnT, B_W], [B_big])
            yield
            kv3 = big[:].rearrange("p (h d) -> p h d", h=8)
            cp("act", vst[:, :, 0:64], kv3[:, :, 64:128], [B_big], [Bvst])
            yield
            cp("pool", vst[:, :, 64:128], vvalid[:, n:n + 1].unsqueeze(2).to_broadcast([128, 8, 64]), [B_C], [Bvst])
            yield
            P.dma(v_scr[n].rearrange("p h d -> p (h d)"), vst[:].rearrange("p h d -> p (h d)"), reads=[Bvst], slot="vst" + kd)
            act(sqk[:], kv3[:, :, 0:64], AF.Square, [B_big], [Bsqk])
            yield
            red("dve", sm[4][0][:], sqk[:], [Bsqk], [sm[4][1]])
            yield
            ts("dve", sm[4][0][:], sm[4][0][:], sm[3][0][:, 0:1], ALU.add, [sm[4][1], sm[3][1]], [sm[4][1]])
            yield
            ts("pool", sm[5][0][:], sm[4][0][:], 1.0 / 96, ALU.mult, [sm[4][1]], [sm[5][1]], s2=EPS, op1=ALU.add)
            yield
            tt("pool", sm[6][0][:], sm[5][0][:], mhalf[:, 0:8], ALU.pow, [sm[5][1], B_C], [sm[6][1]])
            yield
            tt("dve", sqk[:], kv3[:, :, 0:64], sm[6][0][:].unsqueeze(2).to_broadcast([128, 8, 64]), ALU.mult,
               [B_big, sm[6][1], Bsqk], [Bsqk])
            yield REL(LBIG)
            tt("pool", kt[:, :, 0:64], sqk[:], gk_b[:, 0:64].unsqueeze(1).to_broadcast([128, 8, 64]), ALU.mult,
               [Bsqk, B_W], [Bkt])
            yield
            tt("dve", kt[:, :, 64:96], krr[:].unsqueeze(1).to_broadcast([128, 8, 32]),
               sm[6][0][:].unsqueeze(2).to_broadcast([128, 8, 32]), ALU.mult, [Bkrr, sm[6][1]], [Bkt])
            yield
            yield ACQ(L5)
            for h in range(8):
                tr(ptr[0:96, h, :], kt[:, h, :], [Bkt], [B_b5])
            yield
            cp("act", kTst[0:96], ptr[0:96], [B_b5], [BkTst])
            yield REL(L5)
            P.dma(kT_scr.rearrange("h d t -> d h t")[:, :, n * 128:(n + 1) * 128], kTst[0:96], reads=[BkTst], slot="kTst" + kd)
            yield

          def swakv(kd, i):
            par = i % 2
            n = 2 * i + (0 if kd == "P" else 1)
            K = KS[kd]
            kvp, Bkvp = kvps[(kd, par)]
            sm = K["sm"]
            (sq2, Bsq2), (ks1, Bks1), (ksd, Bksd) = K["sq2"], K["ks1"], K["ksd"]
            skTk, BskT = skT[(kd, par)]
            svk, Bsv = sv[(kd, par)]
            skp = kvp[:, 160:288].rearrange("p (g d) -> p g d", g=2)
            act(sq2[:], skp, AF.Square, [Bkvp], [Bsq2])
            yield
            red("dve", sm[7][0][:, 0:2], sq2[:], [Bsq2], [sm[7][1]])
            yield
            ts("pool", sm[7][0][:, 2:4], sm[7][0][:, 0:2], 1.0 / 64, ALU.mult, [sm[7][1]], [sm[7][1]], s2=EPS, op1=ALU.add)
            yield
            tt("pool", sm[7][0][:, 4:6], sm[7][0][:, 2:4], mhalf[:, 0:2], ALU.pow, [sm[7][1], B_C], [sm[7][1]])
            yield
            tt("dve", ks1[:], skp, sm[7][0][:, 4:6].unsqueeze(2).to_broadcast([128, 2, 64]), ALU.mult, [Bkvp, sm[7][1]], [Bks1])
            yield
            tt("pool", ksd[:], ks1[:], gsk_b[:].unsqueeze(1).to_broadcast([128, 2, 64]), ALU.mult, [Bks1, B_W], [Bksd])
            yield
            r0, r1 = SMR["kT" + kd]
            Bsmr = B_sm["kT" + kd]
            yield ACQ(L2)
            for g in range(2):
                tr(psm[0:64, r0 + g * 128:r0 + (g + 1) * 128], ksd[:, g, :], [Bksd], [Bsmr])
            yield
            cp("act", skTk[0:64], psm[0:64, r0:r1].rearrange("p (g t) -> p g t", g=2), [Bsmr], [BskT])
            yield REL(L2)
            cp("act", svk[:, :, 0:64], kvp[:, 288:416].rearrange("p (g d) -> p g d", g=2), [Bkvp], [Bsv])
            yield
            cp("pool", svk[:, :, 64:128], vvalid[:, n:n + 1].unsqueeze(2).to_broadcast([128, 2, 64]), [B_C], [Bsv])
            yield

          def chain_pa(i):
            yield from front("P", i)
            yield from kmla("P", i)

          def chain_pb(i):
            yield from front("O", i)

          def chain_pc(i):
            yield ("wait", ("kvP", i))
            yield from swakv("P", i)

          def chain_next(i):
            return [chain_pa(i), chain_pb(i), chain_pc(i)]

          def chain_ok(i):
            yield from swakv("O", i)
            yield ("set", ("swakv", i))

          def chain_ok2(i):
            yield from kmla("O", i)

          def chain_q(i):
            par = i % 2
            hTo, BhTo = hT[("O", par)]
            cs, sn, Bcs = tabs[("O", par)]
            yield ACQ(L0)
            qp = bk0
            for kc in range(8):
                mm(qp[:, 0:256], hTo[:, kc, 2:130], w_in[:, kc, C_QL:C_QL + 256], kc == 0, kc == 7, [BhTo, B_W], [B_b0])
            yield
            act(qln[:], qp[:, 0:256], AF.Square, [B_b0], [QS[0][1], B_qln], accum=QS[0][0][:, 0:1])
            yield
            ts("pool", QS[1][0][:, 0:1], QS[0][0][:, 0:1], 1.0 / 256, ALU.mult, [QS[0][1]], [QS[1][1]], s2=EPS, op1=ALU.add)
            yield
            tt("pool", QS[2][0][:, 0:1], QS[1][0][:, 0:1], mhalf[:, 0:1], ALU.pow, [QS[1][1], B_C], [QS[2][1]])
            yield
            ts("dve", qln[:], qp[:, 0:256], QS[2][0][:, 0:1], ALU.mult, [B_b0, QS[2][1]], [B_qln])
            yield REL(L0)
            r0, r1 = SMR["qln"]
            yield ACQ(L2)
            for c in range(2):
                tr(psm[:, r0 + c * 128:r0 + (c + 1) * 128], qln[:, c * 128:(c + 1) * 128], [B_qln], [B_sm["qln"]])
            yield
            cp("act", qlnT[:], psm[:, r0:r1].rearrange("p (c t) -> p c t", c=2), [B_sm["qln"]], [B_qlnT])
            yield REL(L2)
            yield ACQ(LBIG)
            for (c0, c1) in ((0, 512), (512, 768)):
                for c in range(2):
                    mm(big[:, c0:c1], qlnT[:, c, :], w_qb[:, c, c0:c1], c == 0, c == 1, [B_qlnT, B_W], [B_big])
            yield
            q3 = big[:, 0:768].rearrange("p (h d) -> p h d", h=8)
            act(sqq[:], q3, AF.Square, [B_big], [B_sqq])
            yield
            red("dve", QS[3][0][:], sqq[:], [B_sqq], [QS[3][1]])
            yield
            ts("pool", QS[4][0][:], QS[3][0][:], 1.0 / 96, ALU.mult, [QS[3][1]], [QS[4][1]], s2=EPS, op1=ALU.add)
            yield
            tt("pool", QS[5][0][:], QS[4][0][:], mhalf[:, 0:8], ALU.pow, [QS[4][1], B_C], [QS[5][1]])
            yield
            tt("dve", sqq[:], q3, QS[5][0][:].unsqueeze(2).to_broadcast([128, 8, 96]), ALU.mult, [B_big, QS[5][1], B_sqq], [B_sqq])
            yield REL(LBIG)
            tt("pool", qt[:, :, 0:64], sqq[:, :, 0:64], gq_b[:, 0:64].unsqueeze(1).to_broadcast([128, 8, 64]),
               ALU.mult, [B_sqq, B_W], [B_qt])
            yield
            tt("pool", qr[:], sqq[:, :, 64:96], gq_b[:, 64:96].unsqueeze(1).to_broadcast([128, 8, 32]), ALU.mult,
               [B_sqq, B_W], [B_qr])
            yield
            tt("dve", qr1[:], qr[:], cs[:].unsqueeze(1).to_broadcast([128, 8, 32]), ALU.mult, [B_qr, Bcs], [B_qr1])
            yield
            tt("pool", qr2[:, :, 0:16], qr[:, :, 16:32], sn[:, 0:16].unsqueeze(1).to_broadcast([128, 8, 16]), ALU.mult,
               [B_qr, Bcs], [B_qr2])
            yield
            tt("pool", qr2[:, :, 16:32], qr[:, :, 0:16], sn[:, 16:32].unsqueeze(1).to_broadcast([128, 8, 16]), ALU.mult,
               [B_qr, Bcs], [B_qr2])
            yield
            tt("dve", qt[:, :, 64:96], qr1[:], qr2[:], ALU.add, [B_qr1, B_qr2], [B_qt])
            yield
            yield ACQ(L5)
            for h in range(8):
                tr(ptr[0:96, h, :], qt[:, h, :], [B_qt], [B_b5])
            yield
            cp("act", qTst[0:96], ptr[0:96], [B_b5], [B_qTst])
            yield REL(L5)
            P.dma(qT_scr.rearrange("h d t -> d h t")[:, :, i * 128:(i + 1) * 128], qTst[0:96], reads=[B_qTst], slot="qTst")
            yield

          def chain_swa(i):
            par = i % 2
            hTo, BhTo = hT[("O", par)]
            sqp = bk7
            for kc in range(8):
                mm(sqp[:], hTo[:, kc, 2:130], w_in[:, kc, C_SQ:C_SQ + 512], kc == 0, kc == 7, [BhTo, B_W], [B_b7])
            yield
            sq3 = sqp[:].rearrange("p (h d) -> p h d", h=8)
            act(sqs[:], sq3, AF.Square, [B_b7], [B_sqs])
            yield
            red("dve", SS[0][0][:], sqs[:], [B_sqs], [SS[0][1]])
            yield
            ts("pool", SS[1][0][:], SS[0][0][:], 1.0 / 64, ALU.mult, [SS[0][1]], [SS[1][1]], s2=EPS, op1=ALU.add)
            yield
            tt("pool", SS[2][0][:], SS[1][0][:], mhalf[:, 0:8], ALU.pow, [SS[1][1], B_C], [SS[2][1]])
            yield
            tt("dve", sqs[:], sq3, SS[2][0][:].unsqueeze(2).to_broadcast([128, 8, 64]), ALU.mult, [B_b7, SS[2][1], B_sqs], [B_sqs])
            yield
            tt("pool", sqb[:], sqs[:], gsq_b[:].unsqueeze(1).to_broadcast([128, 8, 64]), ALU.mult, [B_sqs, B_W], [B_sqb])
            yield
            yield ACQ(L0)
            for h in range(8):
                tr(ph[0:64, h * 128:(h + 1) * 128], sqb[:, h, :], [B_sqb], [B_b0])
            yield
            cp("act", sqT[0:64], ph[0:64, :].rearrange("p (c t) -> p c t", c=8), [B_b0], [B_sqT])
            yield REL(L0)
            yield ("wait", ("swakv", i))
            ns = 0
            for g in range(2):
                yield ACQ(L5)
                Ops = bk5
                for kb, kk in enumerate(("P", "O")):
                    skTk, BskT = skT[(kk, par)]
                    svk, Bsv = sv[(kk, par)]
                    for e4 in range(4):
                        h = 4 * g + e4
                        mm(bk7[:, e4 * 128:(e4 + 1) * 128], skTk[0:64, g, :], sqT[0:64, h, :], True, True,
                           [BskT, B_sqT], [B_b7])
                    yield
                    act(Ebuf[:], bk7[:], AF.Exp, [B_b7], [B_E], scale=0.125)
                    yield
                    pm, BP = Pm[ns % 2]
                    tt("dve", pm[:], Ebuf[:], swaM[:, kb, 4 * g:4 * g + 4, :].rearrange("p h q -> p (h q)"), ALU.mult,
                       [B_E, B_C], [BP])
                    yield
                    mm(Ops[:], svk[:, g, :], pm[:], kb == 0, kb == 1, [Bsv, BP], [B_b5])
                    yield
                    ns += 1
                tt("dve", rden[64:128, :].rearrange("p (h q) -> p h q", h=4),
                   Ops[64:128, :].rearrange("p (h q) -> p h q", h=4),
                   esink[64:128, 4 * g:4 * g + 4].unsqueeze(2).to_broadcast([64, 4, 128]), ALU.add, [B_b5, B_W], [B_rden])
                yield
                rcp(rden[64:128, :], rden[64:128, :], [B_rden], [B_rden])
                yield
                for e4 in range(4):
                    h = 4 * g + e4
                    hb_ = (h % 2) * 64
                    tt("dve", tsw[hb_:hb_ + 64, h // 2, :], Ops[0:64, e4 * 128:(e4 + 1) * 128],
                       rden[64:128, e4 * 128:(e4 + 1) * 128], ALU.mult, [B_b5, B_rden], [B_tsw])
                    yield
                yield REL(L5)

          def silu_from(G, out_ap, B_out):
            act(sge[:], G[:], AF.Tanh, [B_b6], [B_sge], scale=0.5)
            yield
            stt("dve", out_ap, sge[:], 1.0, G[:], ALU.add, ALU.mult, [B_b6, B_sge], [B_out])
            yield

          def chain_gc(i):
            par = i % 2
            hTo, BhTo = hT[("O", par)]

            def fm_mm(ps_ap, col0, lo, hi, Bps):
                for kc in range(8):
                    mm(ps_ap, w_in[:, kc, col0:col0 + 128], hTo[:, kc, lo:hi], kc == 0, kc == 7, [BhTo, B_W], [Bps])

            G = bk6
            for c in range(4):
                fm_mm(G[:, c * 128:(c + 1) * 128], C_GC + c * 128, 2, 130, B_b6)
                yield
            yield from silu_from(G, sgc[:].rearrange("p c t -> p (c t)"), B_sgc)
            yield ("set", ("sgc", i))
            for c in range(4):
                fm_mm(G[:, c * 128:(c + 1) * 128], C_GM + c * 128, 2, 130, B_b6)
                yield
            yield from silu_from(G, sgm[:].rearrange("p c t -> p (c t)"), B_sgm)
            P.dma(sg_scr.rearrange("c p t -> p c t")[:, :, i * 128:(i + 1) * 128], sgm[:], reads=[B_sgm], slot="sgm")
            for c in range(4):
                fm_mm(G[:, c * 128:(c + 1) * 128], C_GS + c * 128, 2, 130, B_b6)
                yield
            yield from silu_from(G, sgs[:].rearrange("p c t -> p (c t)"), B_sgs)

          def chain_conv(i):
            par = i % 2
            hTo, BhTo = hT[("O", par)]

            def fm_mm(ps_ap, col0, lo, hi, Bps):
                for kc in range(8):
                    mm(ps_ap, w_in[:, kc, col0:col0 + 128], hTo[:, kc, lo:hi], kc == 0, kc == 7, [BhTo, B_W], [Bps])

            for cc in range(4):
                yield ACQ(L1)
                X = bk1
                fm_mm(X[:, 0:130], C_CH + cc * 128, 0, 130, B_b1)
                yield
                fm_mm(X[:, 130:260], C_CC + cc * 128, 0, 130, B_b1)
                yield
                fm_mm(X[:, 260:388], C_CB + cc * 128, 2, 130, B_b1)
                yield
                cp("act", ch_sb[:], X[:, 0:130], [B_b1], [B_ch])
                yield
                tt("dve", u_sb[:], ch_sb[:], X[:, 130:260], ALU.mult, [B_ch, B_b1], [B_u])
                yield
                if cc == 0:
                    yield ("wait", ("sgc", i))
                tt("dve", tcb[:], X[:, 260:388], sgc[:, cc, :], ALU.mult, [B_b1, B_sgc], [B_tcb])
                yield REL(L1)
                ts("pool", cy[0][0][:], u_sb[:, 2:130], convw[:, 2, cc:cc + 1], ALU.mult, [B_u, B_W], [cy[0][1]])
                yield
                stt("dve", cy[1][0][:], u_sb[:, 1:129], convw[:, 1, cc:cc + 1], cy[0][0][:], ALU.mult, ALU.add,
                    [B_u, B_W, cy[0][1]], [cy[1][1]])
                yield
                stt("dve", cy[0][0][:], u_sb[:, 0:128], convw[:, 0, cc:cc + 1], cy[1][0][:], ALU.mult, ALU.add,
                    [B_u, B_W, cy[1][1]], [cy[0][1]])
                yield
                tt("pool", yT[:, cc, :], cy[0][0][:], tcb[:], ALU.mult, [cy[0][1], B_tcb], [B_yT])
                yield

          def tail(i):
            xs, Bx = x_O[i % 2]
            tt("pool", yT[:, 4:8, :], tsw[:], sgs[:], ALU.mult, [B_tsw, B_sgs], [B_yT])
            for hf in range(2):
                for kc in range(8):
                    mm(big[:, hf * 512:(hf + 1) * 512], yT[:, kc, :], w_out[:, 4 + kc, hf * 512:(hf + 1) * 512],
                       kc == 0, kc == 7, [B_yT, B_W], [B_big])
            tt("dve", osb[:], big[:], xs[:], ALU.add, [B_big, Bx], [B_osb])
            P.dma(part_scr[i], osb[:], reads=[B_osb], slot="part")

          nb_run = NB if not (dbg is not None and dbg.startswith("b")) else int(dbg[1:])
          load_xP(0)
          load_rest(0)
          if nb_run > 1:
              load_xP(1)
          drive(chain_next(0))
          for i in range(nb_run):
              chains = [chain_ok(i), chain_ok2(i), chain_q(i), chain_swa(i), chain_gc(i), chain_conv(i)]
              if i + 1 < nb_run:
                  load_rest(i + 1)
                  if i + 2 < nb_run:
                      load_xP(i + 2)
                  chains = chain_next(i + 1) + chains
              drive(chains)
              assert all(v is None for v in locks.values()), locks
              tail(i)

          if dbg == "p1":
              raise _Stop()
          P.barrier()
          ar.off = 0
          kTp = A([128, 2, NS * 128], BF16)
          vp = A([128, NS, 2, 128], BF16)
          qTp = A([128, 2, NB * 128], BF16)
          maskM = A([128, 8, 512], BF16); B_mask = P.buf("mask")
          PT = [A([128, 512], BF16) for _ in range(4)]; B_PT = [P.buf() for _ in range(4)]
          gbuf = [A([128, 512], BF16) for _ in range(2)]; B_g = [P.buf(), P.buf()]
          ymla = A([128, 4, NB * 128], BF16); B_ym = [P.buf(f"ym{j}") for j in range(8)]
          rden2 = A([128, 512], F32); B_rd2 = P.buf()
          tn = A([128, 512], F32); B_tn = P.buf()
          ptile = [A([128, D], F32) for _ in range(2)]; B_pt = [P.buf(), P.buf()]
          osb2 = [A([128, D], F32) for _ in range(2)]; B_o2 = [P.buf(), P.buf()]
          B_kc = [P.buf(f"kTc{c}") for c in range(8)]
          B_vc = [P.buf(f"vc{c}") for c in range(8)]
          B_q = P.buf("qTp")
          P.dma(maskM[:], maskM_d, writes=[B_mask], slot="c0")
          SC = 96 ** -0.5
          Sb = [bk0, bk1, bk2, bk7]
          Ob = [bk5, bk6]
          LA = 3
          PT6 = PT + [A([128, 512], BF16) for _ in range(2)]
          B_PT6 = B_PT + [P.buf(), P.buf()]
          nO = 0
          nS = 0
          for hp in range(4):
              for c in range(8):
                  for hh in range(2):
                      P.dma(kTp[0:96, hh, c * 1024:(c + 1) * 1024], kT_scr[2 * hp + hh, :, c * 1024:(c + 1) * 1024],
                            writes=[B_kc[c]], slot=f"kl{hh}")
                  P.dma(vp[:, 8 * c:8 * c + 8, :, :], v_scr[8 * c:8 * c + 8, :, 2 * hp:2 * hp + 2, :].rearrange("s p h d -> p s h d"),
                        writes=[B_vc[c]], slot="vl")
                  if c == 0:
                      for hh in range(2):
                          P.dma(qTp[0:96, hh, :], qT_scr[2 * hp + hh], writes=[B_q], slot=f"ql{hh}")
              units = []
              for j in range(8):
                  for hh in range(2):
                      nsl = 8 * j + 8
                      for sl in range(nsl):
                          units.append((j, hh, sl, nsl))
              pend = {}
              tile_bank = {}
              for t in range(len(units) + LA):
                  if t < len(units):
                      j, hh, sl, nsl = units[t]
                      if sl == 0 and hh == 0:
                          gb = gbuf[j % 2]; Bg = B_g[j % 2]
                          P.dma(gb[:], sg_scr[hp, :, j * 512:(j + 1) * 512], writes=[Bg], slot=f"gl{j % 2}")
                      Sps = Sb[nS % 4]; BS = B_bk[id(Sps)]
                      nS += 1
                      mm(Sps[:], kTp[0:96, hh, sl * 128:(sl + 1) * 128], qTp[0:96, hh, j * 512:(j + 1) * 512],
                         True, True, [B_kc[sl // 8], B_q], [BS])
                      pend[t] = (Sps, BS)
                  if t >= LA:
                      u = t - LA
                      j, hh, sl, nsl = units[u]
                      if sl == 0:
                          tile_bank[(j, hh)] = Ob[nO % 2]
                          nO += 1
                      Ops = tile_bank[(j, hh)]; BO = B_bk[id(Ops)]
                      Sps, BS = pend.pop(u)
                      pt = PT6[u % 6]; BP = B_PT6[u % 6]
                      act(pt[:], Sps[:], AF.Exp, [BS], [BP], scale=SC)
                      if sl >= 8 * j:
                          tt("dve", pt[:], pt[:], maskM[:, sl - 8 * j, :], ALU.mult, [BP, B_mask], [BP])
                      mm(Ops[:], vp[:, sl, hh, :], pt[:], sl == 0, sl == nsl - 1, [B_vc[sl // 8], BP], [BO])
                      if sl == nsl - 1:
                          gb = gbuf[j % 2]; Bg = B_g[j % 2]
                          hb = hh * 64
                          rcp(rden2[64:128, :], Ops[64:128, :], [BO], [B_rd2])
                          tt("dve", tn[hb:hb + 64, :], Ops[0:64, :], rden2[64:128, :], ALU.mult, [BO, B_rd2], [B_tn])
                          tt("pool", ymla[hb:hb + 64, hp, j * 512:(j + 1) * 512], tn[hb:hb + 64, :], gb[hb:hb + 64, :], ALU.mult,
                             [B_tn, Bg], [B_ym[j]])

          if dbg == "p2":
              raise _Stop()
          def load_pt(i):
              P.dma(ptile[i % 2][:], part_scr[i], writes=[B_pt[i % 2]], slot=f"pl{i % 2}")

          load_pt(0)
          for i in range(NB):
              if i + 1 < NB:
                  load_pt(i + 1)
              for hf in range(2):
                  for c in range(4):
                      mm(big[:, hf * 512:(hf + 1) * 512], ymla[:, c, i * 128:(i + 1) * 128],
                         w_out[:, c, hf * 512:(hf + 1) * 512], c == 0, c == 3, [B_ym[i // 4], B_W], [B_big])
              o2 = osb2[i % 2]; Bo2 = B_o2[i % 2]
              tt("dve", o2[:], big[:], ptile[i % 2][:], ALU.add, [B_big, B_pt[i % 2]], [Bo2])
              P.dma(ps["out"](i), o2[:], reads=[Bo2], slot=f"yo{i % 2}")
      except _Stop:
        pass
    print("arena p1 end", p1_end, "total ops recorded", P.total, {e: len(v) for e, v in P.ops.items()})
    P.emit(final_slots=[s_ for s_ in P.slot_counts])
    P.close()
    return nc


def _consts(r):
    half = 16
    inv_freq = np.power(np.float32(10000.0), -np.arange(half, dtype=np.float32) / half).astype(np.float32)
    pidx = np.arange(128, dtype=np.float32)
    cosT = np.zeros((128, NS, 32), np.float32)
    sinT = np.zeros((128, NS, 32), np.float32)
    for s in range(NS):
        gb = s - 1 + r
        pos = (gb * 128 + pidx).astype(np.float32)
        ang = pos[:, None] * inv_freq[None, :]
        c = np.cos(ang).astype(np.float32)
        sn = np.sin(ang).astype(np.float32)
        cosT[:, s, :16] = c
        cosT[:, s, 16:] = c
        sinT[:, s, :16] = -sn
        sinT[:, s, 16:] = sn
    maskM = np.zeros((128, 8, 512), np.float32)
    ki = np.arange(128)[:, None]
    qi = np.arange(128)[None, :]
    tri = (ki <= qi).astype(np.float32)
    for so in range(8):
        for qb in range(4):
            d = 2 * qb + 1
            if so < d:
                maskM[:, so, qb * 128:(qb + 1) * 128] = 1.0
            elif so == d:
                maskM[:, so, qb * 128:(qb + 1) * 128] = tri
    slopes = np.exp2(-8.0 * np.arange(1, 9, dtype=np.float32) / 8).astype(np.float32)
    swaM = np.zeros((128, 2, 8, 128), np.float32)
    for h in range(8):
        d0 = (128 + qi - ki).astype(np.float32)
        swaM[:, 0, h, :] = np.where(d0 < 128, np.exp(-slopes[h] * d0), 0.0)
        d1 = (qi - ki).astype(np.float32)
        swaM[:, 1, h, :] = np.where(d1 >= 0, np.exp(-slopes[h] * np.maximum(d1, 0)), 0.0)
    vvalid = np.ones((128, NS), np.float32)
    if r == 0:
        vvalid[:, 0] = 0.0
    return {"cosT": cosT, "sinT": sinT, "maskM": maskM.astype(ml_dtypes.bfloat16), "swaM": swaM, "vvalid": vvalid}


def _layer_weights(inp, l, suffix):
    f = lambda a: np.ascontiguousarray(a, dtype=np.float32)
    return {
        f"w_in_{suffix}": f(inp["w_in"][l][:, PERM]), f"norm_g_{suffix}": f(inp["norm_g"][l].reshape(8, 128).T),
        f"w_kvb_{suffix}": f(inp["mla_w_kvb"][l]), f"kva_g_{suffix}": f(inp["mla_kv_a_norm"][l].reshape(128, 1)),
        f"w_qb_{suffix}": f(inp["mla_w_qb"][l]), f"qa_g_{suffix}": f(inp["mla_q_a_norm"][l].reshape(2, 128).T),
        f"q_g_{suffix}": f(inp["mla_q_norm"][l]), f"k_g_{suffix}": f(inp["mla_k_norm"][l]),
        f"conv_w_{suffix}": f(inp["conv_w"][l].reshape(3, 4, 128).transpose(2, 0, 1)), f"sq_g_{suffix}": f(inp["swa_q_norm"][l]),
        f"sk_g_{suffix}": f(inp["swa_k_norm"][l]), f"sinks_{suffix}": f(inp["swa_sinks"][l]),
        f"w_out_{suffix}": f(inp["w_out"][l]),
    }


def _shard_x(x):
    xb = x.reshape(4, 64, 128, D)
    outs = []
    for c in range(8):
        b, r = c // 2, c % 2
        own = xb[b, r::2]
        prev = np.zeros_like(own)
        if r == 0:
            prev[1:] = xb[b, 1:63:2]
        else:
            prev[:] = xb[b, 0::2]
        outs.append((np.ascontiguousarray(own), np.ascontiguousarray(prev)))
    return outs


def _gather(res):
    out = np.zeros((4, 64, 128, D), np.float32)
    for c in range(8):
        b, r = c // 2, c % 2
        out[b, r::2] = res[c]["y_out"]
    return out.reshape(4, 8192, D)


_CACHE = {}
FUSED = True


def kernel(**inp):
    inp = {k: np.asarray(v) for k, v in inp.items()}
    x = np.ascontiguousarray(inp["x"], dtype=np.float32)
    depth = inp["w_in"].shape[0]
    consts = [_consts(r) for r in range(2)]
    if FUSED and depth == 2:
        if "nc2" not in _CACHE:
            _CACHE["nc2"] = build_program(2)
        nc = _CACHE["nc2"]
        w = {}
        for l in range(2):
            w.update(_layer_weights(inp, l, str(l)))
        sh = _shard_x(x)
        in_maps = []
        for c in range(8):
            r = c % 2
            o = c + 1 - 2 * r
            m = {"x_own": sh[c][0], "x_prev": sh[c][1], "x_own2": sh[o][0], "x_prev2": sh[o][1]}
            m.update(w)
            m.update(consts[r])
            co = consts[1 - r]
            m["cosT2"] = co["cosT"]; m["sinT2"] = co["sinT"]; m["vvalid2"] = co["vvalid"]
            bw = np.zeros((128, 2), np.float32)
            bw[:, r] = 1.0
            m["blendw"] = bw
            in_maps.append(m)
        res = run_bass_kernel_spmd(nc, in_maps, core_ids=list(range(8)))
        return _gather(res.results).astype(np.float32)
    if "nc1" not in _CACHE:
        _CACHE["nc1"] = build_program(1)
    for l in range(depth):
        nc = _CACHE["nc1"]
        w = _layer_weights(inp, l, "0")
        sh = _shard_x(x)
        in_maps = []
        for c in range(8):
            m = {"x_own": sh[c][0], "x_prev": sh[c][1]}
            m.update(w)
            m.update(consts[c % 2])
            in_maps.append(m)
        res = run_bass_kernel_spmd(nc, in_maps, core_ids=list(range(8)))
        x = _gather(res.results)
    return x.astype(np.float32)
```

```python
import numpy as np
import ml_dtypes
from contextlib import ExitStack
import concourse.bass as bass
import concourse.mybir as mybir
from concourse.bass_utils import run_bass_kernel_spmd

F32 = mybir.dt.float32
BF16 = mybir.dt.bfloat16
ALU = mybir.AluOpType
AF = mybir.ActivationFunctionType
AX = mybir.AxisListType

NB = 32
NS = 64
D = 1024
NCOL = 4256
EPS = 1e-6
C_KVL, C_KR, C_SK, C_SV, C_QL, C_SQ, C_GM, C_GS, C_CH, C_CC, C_CB, C_GC = (
    0, 128, 160, 288, 416, 672, 1184, 1696, 2208, 2720, 3232, 3744)
PERM = np.concatenate([np.arange(256, 384), np.arange(384, 416), np.arange(3488, 3616), np.arange(3616, 3744),
                       np.arange(0, 256), np.arange(2976, 3488), np.arange(416, 928), np.arange(3744, 4256),
                       np.arange(928, 1440), np.arange(1952, 2464), np.arange(1440, 1952), np.arange(2464, 2976)])

COMPUTE = ("pe", "act", "dve", "pool")
ALLENG = COMPUTE + ("sp",)


class Buf:
    __slots__ = ("name", "lw", "rd")

    def __init__(self, name):
        self.name = name
        self.lw = None
        self.rd = []


class Op:
    __slots__ = ("eng", "fn", "waits", "signal", "dma", "slot", "slot_cnt")

    def __init__(self, eng, fn, dma=False, slot=None):
        self.eng = eng
        self.fn = fn
        self.waits = []
        self.signal = False
        self.dma = dma
        self.slot = slot
        self.slot_cnt = 0


class Prog:
    def __init__(self, nc):
        self.nc = nc
        self.ops = {e: [] for e in ALLENG}
        self.seen = {e: {} for e in ALLENG}
        self.pending = {e: [] for e in ALLENG}
        self.slot_counts = {}
        self.stack = ExitStack()
        self.nbuf = 0

    def sbuf(self, name, shape, dtype):
        return self.stack.enter_context(self.nc.sbuf_tensor("sb_" + name, list(shape), dtype))

    def psum(self, name, shape, dtype):
        return self.stack.enter_context(self.nc.psum_tensor("ps_" + name, list(shape), dtype))

    def buf(self, name=None):
        self.nbuf += 1
        return Buf(name or f"b{self.nbuf}")

    def _dep(self, op, eng, key):
        kind, k, v = key
        if kind == "eng" and k == "pe" and eng == "pe":
            return
        seen = self.seen[eng]
        if seen.get((kind, k), -1) >= v:
            return
        seen[(kind, k)] = v
        op.waits.append(key)
        if kind == "eng":
            self.ops[k][v].signal = True

    limit = None
    total = 0
    trace = None

    def add(self, eng, fn, reads=(), writes=(), dma=False, slot=None):
        self.total += 1
        if self.limit is not None and self.total > self.limit:
            return None
        op = Op(eng, fn, dma=dma, slot=slot)
        if self.trace is not None:
            import sys as _s
            f = _s._getframe(1)
            while f is not None and f.f_code.co_name != "build_program":
                f = f.f_back
            self.trace.append((self.total, eng, f.f_lineno if f else -1))
        idx = len(self.ops[eng])
        for key in self.pending[eng]:
            self._dep(op, eng, key)
        self.pending[eng] = []
        if dma:
            cnt = self.slot_counts.get(slot, 0)
            if cnt > 0:
                self._dep(op, eng, ("slot", slot, cnt))
            self.slot_counts[slot] = cnt + 1
            op.slot_cnt = cnt + 1
            me = ("slot", slot, cnt + 1)
        else:
            me = ("eng", eng, idx)
        for b in reads:
            if b.lw is not None:
                self._dep(op, eng, b.lw)
        for b in writes:
            if b.lw is not None:
                self._dep(op, eng, b.lw)
            for r in b.rd:
                self._dep(op, eng, r)
        for b in reads:
            b.rd.append(me)
        for b in writes:
            b.lw = me
            b.rd = []
        self.ops[eng].append(op)
        return op

    def barrier(self):
        keys = []
        for e in ALLENG:
            for i in range(len(self.ops[e]) - 1, -1, -1):
                if not self.ops[e][i].dma:
                    keys.append(("eng", e, i))
                    break
        for s, c in self.slot_counts.items():
            keys.append(("slot", s, c))
        for e in ALLENG:
            self.pending[e] = self.pending[e] + [k for k in keys if not (k[0] == "eng" and k[1] == e)]

    def dma(self, out, in_, reads=(), writes=(), slot="d0", eng="sp"):
        return self.add(eng, lambda e: e.dma_start(out=out, in_=in_), reads, writes, dma=True, slot=slot)

    def emit(self, final_slots=()):
        nc = self.nc
        st = self.stack
        esem = {e: st.enter_context(nc.semaphore(f"s_{e}")) for e in ALLENG}
        ssem = {s: st.enter_context(nc.semaphore(f"d_{s}")) for s in self.slot_counts}
        cnt = {}
        for e, lst in self.ops.items():
            c = 0
            for i, op in enumerate(lst):
                if op.signal and not op.dma:
                    c += 1
                cnt[(e, i)] = c

        def run(engname, handle):
            for op in self.ops[engname]:
                for kind, k, v in op.waits:
                    if kind == "eng":
                        handle.wait_ge(esem[k], cnt[(k, v)])
                    else:
                        handle.wait_ge(ssem[k], 16 * v)
                ins = op.fn(handle)
                if op.dma:
                    ins.then_inc(ssem[op.slot], 16)
                elif op.signal:
                    ins.then_inc(esem[engname], 1)
            if engname == "sp":
                for s in final_slots:
                    handle.wait_ge(ssem[s], 16 * self.slot_counts[s])

        block = st.enter_context(nc.Block())

        @block.sync
        def _(e):
            run("sp", e)

        @block.tensor
        def _(e):
            run("pe", e)

        @block.scalar
        def _(e):
            run("act", e)

        @block.vector
        def _(e):
            run("dve", e)

        @block.gpsimd
        def _(e):
            run("pool", e)

    def close(self):
        self.stack.close()


WNAMES = ["w_in", "norm_g", "w_kvb", "kva_g", "w_qb", "qa_g", "q_g", "k_g", "conv_w", "sq_g", "sk_g",
          "sinks", "w_out"]
WSHAPES = {"w_in": [D, NCOL], "norm_g": [128, 8], "w_kvb": [128, 1024], "kva_g": [128, 1], "w_qb": [256, 768],
           "qa_g": [128, 2], "q_g": [96], "k_g": [96], "conv_w": [128, 3, 4], "sq_g": [64], "sk_g": [64],
           "sinks": [8], "w_out": [1536, 1024]}


class _Stop(Exception):
    pass


def build_program(nlayers=1, dbg=None):
    nc = bass.Bass("TRN2", target_bir_lowering=False)
    P = Prog(nc)
    import os
    if os.environ.get("K_LIMIT"):
        P.limit = int(os.environ["K_LIMIT"])

    def dram_in(name, shape, dt=F32):
        return nc.dram_tensor(name, list(shape), dt, kind="ExternalInput").ap()

    x_own = dram_in("x_own", [NB, 128, D])
    x_prev = dram_in("x_prev", [NB, 128, D])
    Wd = [{n: dram_in(f"{n}_{l}", WSHAPES[n]) for n in WNAMES} for l in range(nlayers)]
    cosT_d = dram_in("cosT", [128, NS, 32])
    sinT_d = dram_in("sinT", [128, NS, 32])
    maskM_d = dram_in("maskM", [128, 8, 512], BF16)
    swaM_d = dram_in("swaM", [128, 2, 8, 128])
    vvalid_d = dram_in("vvalid", [128, NS])
    y_out = nc.dram_tensor("y_out", [NB, 128, D], F32, kind="ExternalOutput").ap()
    fused = nlayers == 2
    if not fused:
        passes = [dict(W=Wd[0], own=lambda i: x_own[i], prev=lambda i: x_prev[i], cos=cosT_d, sin=sinT_d,
                       vv=vvalid_d, out=lambda i: y_out[i], blend=False)]
    else:
        x_own2 = dram_in("x_own2", [NB, 128, D])
        x_prev2 = dram_in("x_prev2", [NB, 128, D])
        cosT2_d = dram_in("cosT2", [128, NS, 32])
        sinT2_d = dram_in("sinT2", [128, NS, 32])
        vvalid2_d = dram_in("vvalid2", [128, NS])
        blendw_d = dram_in("blendw", [128, 2])
        x1_mine = nc.dram_tensor("x1_mine", [NB, 128, D], F32, kind="ExternalOutput").ap()
        Zb = nc.dram_tensor("x1_other", [NB + 1, 128, D], F32, kind="ExternalOutput").ap()
        passes = [
            dict(W=Wd[0], own=lambda i: x_own[i], prev=lambda i: x_prev[i], cos=cosT_d, sin=sinT_d, vv=vvalid_d,
                 out=lambda i: x1_mine[i], blend=False),
            dict(W=Wd[0], own=lambda i: x_own2[i], prev=lambda i: x_prev2[i], cos=cosT2_d, sin=sinT2_d, vv=vvalid2_d,
                 out=lambda i: Zb[i + 1], blend=False),
            dict(W=Wd[1], own=lambda i: x1_mine[i], prev=lambda i: Zb[i], prev2=lambda i: Zb[i + 1], cos=cosT_d,
                 sin=sinT_d, vv=vvalid_d, out=lambda i: y_out[i], blend=True),
        ]

    skind = "ExternalOutput"
    kT_scr = nc.dram_tensor("kT_scr", [8, 96, NS * 128], BF16, kind=skind).ap()
    v_scr = nc.dram_tensor("v_scr", [NS, 128, 8, 128], BF16, kind=skind).ap()
    qT_scr = nc.dram_tensor("qT_scr", [8, 96, NB * 128], BF16, kind=skind).ap()
    sg_scr = nc.dram_tensor("sg_scr", [4, 128, NB * 128], BF16, kind=skind).ap()
    part_scr = nc.dram_tensor("part_scr", [NB, 128, D], F32, kind=skind).ap()

    ident = P.sbuf("ident", [128, 128], BF16); B_ident = P.buf("ident")
    idf = P.sbuf("idf", [128, 128], F32)
    w_kvb = P.sbuf("w_kvb", [128, 1024], BF16)
    w_qb = P.sbuf("w_qb", [128, 2, 768], BF16)
    w_out = P.sbuf("w_out", [128, 12, 1024], BF16)
    swaM = P.sbuf("swaM", [128, 2, 8, 128], F32)
    vvalid = P.sbuf("vvalid", [128, NS], F32)
    gq_b = P.sbuf("gq_b", [128, 96], F32)
    gk_b = P.sbuf("gk_b", [128, 96], F32)
    gsq_b = P.sbuf("gsq_b", [128, 64], F32)
    gsk_b = P.sbuf("gsk_b", [128, 64], F32)
    esink = P.sbuf("esink", [128, 8], F32)
    normg = P.sbuf("normg", [128, 8], F32)
    convw = P.sbuf("convw", [128, 3, 4], F32)
    kvag = P.sbuf("kvag", [128, 1], F32)
    qag = P.sbuf("qag", [128, 2], F32)
    eps_t = P.sbuf("eps_t", [128, 1], F32)
    gqk = P.sbuf("gqk", [128, 64], F32)
    gsqk = P.sbuf("gsqk", [128, 64], F32)
    mhalf = P.sbuf("mhalf", [128, 8], F32)

    def mhalf_like(ap):
        return mhalf[:, 0:ap.shape[-1]]
    B_W = P.buf("weights")
    B_C = P.buf("consts")

    ARENA_B = 167 * 1024
    arena = P.sbuf("arena", [128, ARENA_B // 2], BF16)

    class Arena:
        def __init__(self):
            self.off = 0

        def alloc(self, shape, dt, parts=128):
            n = int(np.prod(shape[1:]))
            nbytes = n * (4 if dt == F32 else 2)
            nbytes = (nbytes + 31) // 32 * 32
            o = self.off
            self.off += nbytes
            assert self.off <= ARENA_B, f"arena overflow {self.off}"
            v = arena[:, o // 2:(o + nbytes) // 2]
            if dt == F32:
                v = v.bitcast(F32)
            v = v[:, 0:n]
            if len(shape) == 3:
                v = v.rearrange("p (a b) -> p a b", a=shape[1])
            elif len(shape) == 4:
                v = v.rearrange("p (a b c) -> p a b c", a=shape[1], b=shape[2])
            return v

    banks = [P.psum(f"bank{i}", [128, 512], F32) for i in range(3)]
    big = P.psum("big", [128, 1024], F32)
    banks += [P.psum(f"bank{i}", [128, 512], F32) for i in range(5, 8)]
    bk0, bk1, bk2, bk5, bk6, bk7 = banks
    B_bk = {id(b): P.buf(f"bk{i}") for i, b in enumerate(banks)}
    B_big = P.buf("big")

    def bf(ps):
        return ps[:].bitcast(BF16)

    def tt(eng, out, in0, in1, op, R, W):
        return P.add(eng, lambda e: e.tensor_tensor(out=out, in0=in0, in1=in1, op=op), R, W)

    def ts(eng, out, in0, s1, op0, R, W, s2=None, op1=None):
        if op1 is None:
            return P.add(eng, lambda e: e.tensor_scalar(out=out, in0=in0, scalar1=s1, scalar2=None, op0=op0), R, W)
        return P.add(eng, lambda e: e.tensor_scalar(out=out, in0=in0, scalar1=s1, scalar2=s2, op0=op0, op1=op1), R, W)

    def stt(eng, out, in0, scalar, in1, op0, op1, R, W):
        return P.add(eng, lambda e: e.scalar_tensor_tensor(out=out, in0=in0, scalar=scalar, in1=in1, op0=op0, op1=op1), R, W)

    def act(out, in_, func, R, W, scale=None, bias=None, accum=None):
        kw = {}
        if scale is not None:
            kw["scale"] = scale
        if bias is not None:
            kw["bias"] = bias
        if accum is not None:
            kw["accum_out"] = accum
        return P.add("act", lambda e: e.activation(out=out, in_=in_, func=func, **kw), R, W)

    def cp(eng, out, in_, R, W):
        if eng == "act":
            return P.add("act", lambda e: e.activation(out=out, in_=in_, func=AF.Copy), R, W)
        return P.add(eng, lambda e: e.tensor_copy(out=out, in_=in_), R, W)

    def red(eng, out, in_, R, W):
        return P.add(eng, lambda e: e.tensor_reduce(out=out, in_=in_, axis=AX.X, op=ALU.add), R, W)

    def rcp(out, in_, R, W):
        return P.add("dve", lambda e: e.reciprocal(out=out, in_=in_), R, W)

    def mm(out, lhsT, rhs, start, stop, R, W):
        return P.add("pe", lambda e: e.matmul(out, lhsT=lhsT, rhs=rhs, start=start, stop=stop), R, W)

    def tr(out, in_, R, W):
        return P.add("pe", lambda e: e.transpose(out=out, in_=in_, identity=ident[:]), list(R) + [B_ident], W)

    def rstd(ss_ap, out_ap, scale, R_ss, B_out, tmp_ap, B_tmp):
        act(tmp_ap, ss_ap, AF.Sqrt, [R_ss, B_C], [B_tmp], scale=scale, bias=eps_t[:, 0:1])
        rcp(out_ap, tmp_ap, [B_tmp], [B_out])

    P.add("pool", lambda e: e.memset(idf[:], 0.0), [], [B_ident])
    P.add("pool", lambda e: e.affine_select(out=idf[:], in_=idf[:], pattern=[[-1, 128]], compare_op=ALU.not_equal,
                                            fill=1.0, base=0, channel_multiplier=1), [B_ident], [B_ident])
    P.add("pool", lambda e: e.tensor_copy(out=ident[:], in_=idf[:]), [B_ident], [B_ident])
    P.add("pool", lambda e: e.memset(eps_t[:], EPS), [], [B_C])
    P.add("pool", lambda e: e.memset(mhalf[:], -0.5), [], [B_C])
    P.dma(swaM[:], swaM_d, writes=[B_C], slot="c0")
    blendw = P.sbuf("blendw", [128, 2], F32)
    if fused:
        P.dma(blendw[:], blendw_d, writes=[B_C], slot="c1")
        zero_t = arena[:, 0:2 * D].bitcast(F32)
        P.add("pool", lambda e: e.memset(zero_t, 0.0), [], [B_C])
        P.dma(Zb[0], zero_t, reads=[B_C], slot="c1")

    for ps in passes:
      try:
          W = ps["W"]
          cosT_d = ps["cos"]; sinT_d = ps["sin"]
          P.barrier()
          P.dma(vvalid[:], ps["vv"], writes=[B_C], slot="c1")
          ar = Arena()
          w_in = ar.alloc([128, 8, NCOL], BF16)
          stage = [ar.alloc([128, 2128], F32) for _ in range(2)]
          B_st = [P.buf("st0"), P.buf("st1")]
          mark = ar.off
          P.dma(normg[:], W["norm_g"], writes=[B_W], slot="c0")
          P.dma(convw[:], W["conv_w"], writes=[B_W], slot="c1")
          P.dma(kvag[:], W["kva_g"], writes=[B_W], slot="c0")
          P.dma(qag[:], W["qa_g"], writes=[B_W], slot="c1")
          P.dma(gq_b[:], W["q_g"].partition_broadcast(128), writes=[B_W], slot="c0")
          P.dma(gk_b[:], W["k_g"].partition_broadcast(128), writes=[B_W], slot="c1")
          P.dma(gsq_b[:], W["sq_g"].partition_broadcast(128), writes=[B_W], slot="c0")
          P.dma(gsk_b[:], W["sk_g"].partition_broadcast(128), writes=[B_W], slot="c1")
          P.dma(esink[:], W["sinks"].partition_broadcast(128), writes=[B_W], slot="c0")
          act(esink[:], esink[:], AF.Exp, [B_W], [B_W])
          tt("pool", gqk[:], gq_b[:, 0:64], gk_b[:, 0:64], ALU.mult, [B_W], [B_W])
          tt("pool", gsqk[:], gsq_b[:], gsk_b[:], ALU.mult, [B_W], [B_W])
          n = 0
          engs = ["dve", "dve"]
          win_d = W["w_in"].rearrange("(kc p) c -> p kc c", p=128)
          for kc in range(8):
              for hf in range(2):
                  s = n % 2
                  P.dma(stage[s][:], win_d[:, kc, hf * 2128:(hf + 1) * 2128], writes=[B_st[s]], slot=f"st{s}")
                  ts(engs[n % 2], w_in[:, kc, hf * 2128:(hf + 1) * 2128], stage[s][:], normg[:, kc:kc + 1], ALU.mult,
                     [B_st[s], B_W], [B_W])
                  n += 1
          wout_d = W["w_out"].rearrange("(kc p) c -> p kc c", p=128)
          for kc in range(12):
              s = n % 2
              P.dma(stage[s][:, 0:1024], wout_d[:, kc, :], writes=[B_st[s]], slot=f"st{s}")
              ts(engs[n % 2], w_out[:, kc, :], stage[s][:, 0:1024], 0.5, ALU.mult, [B_st[s]], [B_W])
              n += 1
          s = n % 2
          P.dma(stage[s][:, 0:1024], W["w_kvb"], writes=[B_st[s]], slot=f"st{s}")
          ts(engs[n % 2], w_kvb[:], stage[s][:, 0:1024], kvag[:, 0:1], ALU.mult, [B_st[s], B_W], [B_W])
          n += 1
          wqb_d = W["w_qb"].rearrange("(c p) n -> p c n", p=128)
          for c in range(2):
              s = n % 2
              P.dma(stage[s][:, 0:768], wqb_d[:, c, :], writes=[B_st[s]], slot=f"st{s}")
              ts(engs[n % 2], w_qb[:, c, :], stage[s][:, 0:768], qag[:, c:c + 1], ALU.mult, [B_st[s], B_W], [B_W])
              n += 1

          if dbg == "w":
              raise _Stop()
          P.barrier()
          ar.off = mark - 2 * ((2128 * 4 + 31) // 32 * 32)
          A = ar.alloc

          def AB(shape, dt, name):
            return A(shape, dt), P.buf(name)

          x_P = [AB([128, D], F32, f"xP{k}") for k in range(2)]
          x_O = [AB([128, D], F32, f"xO{k}") for k in range(2)]
          xb2, B_xb2 = AB([128, D], F32, "xb2")
          tabs = {(kd, k): (A([128, 32], F32), A([128, 32], F32), P.buf(f"tab{kd}{k}")) for kd in "PO" for k in range(2)}
          h_bf = [AB([128, D], BF16, f"h{k}") for k in range(2)]
          hT = {(kd, k): AB([128, 8, 130], BF16, f"hT{kd}{k}") for kd in "PO" for k in range(2)}
          kvps = {(kd, k): AB([128, 416], F32, f"kvps{kd}{k}") for kd in "PO" for k in range(2)}
          skT = {(kd, k): AB([128, 2, 128], BF16, f"skT{kd}{k}") for kd in "PO" for k in range(2)}
          sv = {(kd, k): AB([128, 2, 128], BF16, f"sv{kd}{k}") for kd in "PO" for k in range(2)}

          def kset(tag):
            d = {}
            d["sm"] = [AB([128, 8], F32, f"sm{tag}{k}") for k in range(8)]
            d["kvn"] = AB([128, 128], BF16, "kvn" + tag)
            d["kvnT"] = AB([128, 128], BF16, "kvnT" + tag)
            d["vst"] = AB([128, 8, 128], BF16, "vst" + tag)
            d["sqk"] = AB([128, 8, 64], F32, "sqk" + tag)
            d["kt"] = AB([128, 8, 96], BF16, "kt" + tag)
            d["kr"] = AB([128, 32], F32, "kr" + tag)
            d["rt1"] = AB([128, 32], F32, "rt1" + tag)
            d["rt2"] = AB([128, 32], F32, "rt2" + tag)
            d["krr"] = AB([128, 32], F32, "krr" + tag)
            d["kTst"] = AB([128, 8, 128], BF16, "kTst" + tag)
            d["sq2"] = AB([128, 2, 64], F32, "sq2" + tag)
            d["ks1"] = AB([128, 2, 64], F32, "ks1" + tag)
            d["ksd"] = AB([128, 2, 64], BF16, "ksd" + tag)
            return d

          KS = {"P": kset("A"), "O": kset("B")}
          FS = {kd: [AB([128, 8], F32, f"fs{kd}{k}") for k in range(3)] for kd in "PO"}
          QS = [AB([128, 8], F32, f"qs{k}") for k in range(6)]
          qln, B_qln = AB([128, 256], BF16, "qln")
          qlnT, B_qlnT = AB([128, 2, 128], BF16, "qlnT")
          sqq, B_sqq = AB([128, 8, 96], F32, "sqq")
          qt, B_qt = AB([128, 8, 96], BF16, "qt")
          qr, B_qr = AB([128, 8, 32], F32, "qr")
          qr1, B_qr1 = AB([128, 8, 32], F32, "qr1")
          qr2, B_qr2 = AB([128, 8, 32], F32, "qr2")
          qTst, B_qTst = AB([128, 8, 128], BF16, "qTst")
          SS = [AB([128, 8], F32, f"ss{k}") for k in range(4)]
          sqs, B_sqs = AB([128, 8, 64], F32, "sqs")
          sqb, B_sqb = AB([128, 8, 64], BF16, "sqb")
          sqT, B_sqT = AB([128, 8, 128], BF16, "sqT")
          Ebuf, B_E = AB([128, 512], F32, "E")
          Pm = [AB([128, 512], BF16, "Pm0")] * 2
          rden, B_rden = AB([128, 512], F32, "rden")
          tsw, B_tsw = AB([128, 4, 128], F32, "tsw")
          sgs, B_sgs = AB([128, 4, 128], F32, "sgs")
          sgm, B_sgm = AB([128, 4, 128], BF16, "sgm")
          sgc, B_sgc = AB([128, 4, 128], F32, "sgc")
          ch_sb, B_ch = AB([128, 130], F32, "ch")
          u_sb, B_u = AB([128, 130], F32, "u")
          cy = [AB([128, 128], F32, f"cy{k}") for k in range(2)]
          tcb, B_tcb = AB([128, 128], F32, "tcb")
          yT, B_yT = AB([128, 8, 128], BF16, "yT")
          osb, B_osb = AB([128, D], F32, "osb")
          sge, B_sge = osb[:, 0:512], B_osb
          p1_end = ar.off

          ph = bf(bk0); B_b0 = B_bk[id(bk0)]
          B_b1 = B_bk[id(bk1)]
          psm = bf(bk2)
          B_b2 = B_bk[id(bk2)]
          B_sm = {k: B_b2 for k in ("kvnP", "kvnO", "kTP", "kTO", "qln")}
          L2 = "L2"
          SMR = {"kvnP": (0, 128), "kvnO": (128, 256), "kTP": (256, 512), "kTO": (512, 768), "qln": (768, 1024)}
          ptr = bf(bk5).rearrange("p (h t) -> p h t", h=8); B_b5 = B_bk[id(bk5)]
          B_b6 = B_bk[id(bk6)]; B_b7 = B_bk[id(bk7)]
          L0, L1, LBIG, L5 = "L0", "L1", "LBIG", "L5"

          def ACQ(l):
            return ("acq", l)

          def REL(l):
            return ("rel", l)

          def drive(chains):
            st = [dict(g=g, want=None) for g in chains]
            while st:
                progressed = False
                for c in list(st):
                    while True:
                        if c["want"] is not None:
                            kind, obj = c["want"]
                            if kind == "acq":
                                if locks.get(obj) is None:
                                    locks[obj] = c["g"]
                                elif locks[obj] is not c["g"]:
                                    break
                            elif kind == "wait":
                                if obj not in events:
                                    break
                            c["want"] = None
                        try:
                            r = next(c["g"])
                        except StopIteration:
                            st.remove(c)
                            progressed = True
                            break
                        progressed = True
                        if r is None:
                            break
                        kind, obj = r
                        if kind == "rel":
                            assert locks.get(obj) is c["g"], ("release of unowned lock", obj)
                            locks[obj] = None
                        elif kind == "set":
                            events.add(obj)
                        else:
                            c["want"] = r
                if not progressed:
                    raise RuntimeError("chain deadlock at build time")

          locks = {}
          events = set()

          def load_xP(i):
            xs, Bx = x_P[i % 2]
            P.dma(xs[:], ps["prev"](i), writes=[Bx], slot=f"xP{i % 2}")

          def load_rest(i):
            xs, Bx = x_O[i % 2]
            P.dma(xs[:], ps["own"](i), writes=[Bx], slot=f"xO{i % 2}")
            for kd, n in (("P", 2 * i), ("O", 2 * i + 1)):
                cs, sn, Bt = tabs[(kd, i % 2)]
                P.dma(cs[:], cosT_d[:, n, :], writes=[Bt], slot=f"cs{kd}{i % 2}")
                P.dma(sn[:], sinT_d[:, n, :], writes=[Bt], slot=f"sn{kd}{i % 2}")
            if ps["blend"]:
                P.dma(xb2[:], ps["prev2"](i), writes=[B_xb2], slot="xb2")

          def front(kd, i):
            par = i % 2
            xs, Bx = (x_P if kd == "P" else x_O)[par]
            hb, Bh = h_bf[0 if kd == "P" else 1]
            hTk, BhT = hT[(kd, par)]
            (s0, Bs0), (s1, Bs1), (s2, Bs2) = FS[kd]
            if ps["blend"] and kd == "P":
                ts("dve", xs[:], xs[:], blendw[:, 0:1], ALU.mult, [Bx, B_C], [Bx])
                yield
                stt("dve", xs[:], xb2[:], blendw[:, 1:2], xs[:], ALU.mult, ALU.add, [B_xb2, B_C, Bx], [Bx])
                yield
            act(hb[:], xs[:], AF.Square, [Bx], [Bs0, Bh], accum=s0[:, 0:1])
            yield
            ts("pool", s1[:, 0:1], s0[:, 0:1], 1.0 / D, ALU.mult, [Bs0], [Bs1], s2=EPS, op1=ALU.add)
            yield
            tt("pool", s2[:, 0:1], s1[:, 0:1], mhalf[:, 0:1], ALU.pow, [Bs1, B_C], [Bs2])
            yield
            ts("dve", hb[:], xs[:], s2[:, 0:1], ALU.mult, [Bx, Bs2], [Bh])
            yield
            yield ACQ(L0)
            for kc in range(8):
                tr(ph[:, kc * 128:(kc + 1) * 128], hb[:, kc * 128:(kc + 1) * 128], [Bh], [B_b0])
            yield
            cp("act", hTk[:, :, 2:130], ph.rearrange("p (a b) -> p a b", a=8), [B_b0], [BhT])
            yield REL(L0)
            if kd == "P":
                yield ("set", ("hTP", i))
            if kd == "O":
                yield ("wait", ("hTP", i))
                hTp, BhTp = hT[("P", par)]
                cp("pool", hTk[:, :, 0:2], hTp[:, :, 128:130], [BhTp], [BhT])
                yield
            yield ACQ(L1)
            for kc in range(8):
                mm(bk1[:, 0:416], hTk[:, kc, 2:130], w_in[:, kc, 0:416], kc == 0, kc == 7, [BhT, B_W], [B_b1])
            yield
            kv_sb, Bkv = kvps[(kd, par)]
            cp("act", kv_sb[:], bk1[:, 0:416], [B_b1], [Bkv])
            yield REL(L1)
            yield ("set", ("kv" + kd, i))

          def kmla(kd, i):
            par = i % 2
            n = 2 * i + (0 if kd == "P" else 1)
            K = KS[kd]
            kvp, Bkvp = kvps[(kd, par)]
            cs, sn, Bcs = tabs[(kd, par)]
            sm = K["sm"]
            (kvn, Bkvn), (kvnT, BkvnT), (vst, Bvst), (sqk, Bsqk), (kt, Bkt) = K["kvn"], K["kvnT"], K["vst"], K["sqk"], K["kt"]
            (kr, Bkr), (rt1, Brt1), (rt2, Brt2), (krr, Bkrr), (kTst, BkTst) = K["kr"], K["rt1"], K["rt2"], K["krr"], K["kTst"]
            act(kvn[:], kvp[:, 0:128], AF.Square, [Bkvp], [sm[0][1], Bkvn], accum=sm[0][0][:, 0:1])
            yield
            ts("pool", sm[1][0][:, 0:1], sm[0][0][:, 0:1], 1.0 / 128, ALU.mult, [sm[0][1]], [sm[1][1]], s2=EPS, op1=ALU.add)
            yield
            tt("pool", sm[2][0][:, 0:1], sm[1][0][:, 0:1], mhalf[:, 0:1], ALU.pow, [sm[1][1], B_C], [sm[2][1]])
            yield
            ts("dve", kvn[:], kvp[:, 0:128], sm[2][0][:, 0:1], ALU.mult, [Bkvp, sm[2][1]], [Bkvn])
            yield
            act(rt1[:], kvp[:, 128:160], AF.Square, [Bkvp], [sm[3][1], Brt1], accum=sm[3][0][:, 0:1])
            yield
            tt("pool", kr[:], kvp[:, 128:160], gk_b[:, 64:96], ALU.mult, [Bkvp, B_W], [Bkr])
            yield
            tt("pool", rt1[:], kr[:], cs[:], ALU.mult, [Bkr, Bcs], [Brt1])
            yield
            tt("pool", rt2[:, 0:16], kr[:, 16:32], sn[:, 0:16], ALU.mult, [Bkr, Bcs], [Brt2])
            yield
            tt("pool", rt2[:, 16:32], kr[:, 0:16], sn[:, 16:32], ALU.mult, [Bkr, Bcs], [Brt2])
            yield
            tt("pool", krr[:], rt1[:], rt2[:], ALU.add, [Brt1, Brt2], [Bkrr])
            yield
            r0, r1 = SMR["kvn" + kd]
            Bsmr = B_sm["kvn" + kd]
            yield ACQ(L2)
            tr(psm[:, r0:r1], kvn[:], [Bkvn], [Bsmr])
            yield
            cp("act", kvnT[:], psm[:, r0:r1], [Bsmr], [BkvnT])
            yield REL(L2)
            yield ACQ(LBIG)
            for hf in range(2):
                mm(big[:, hf * 512:(hf + 1) * 512], kvnT[:], w_kvb[:, hf * 512:(hf + 1) * 512], True, True,
                   [BkvnT, B_W], [B_big])
            yield
            kv3 = big[:].rearrange("p (h d) -> p h d", h=8)
            cp("act", vst[:, :, 0:64], kv3[:, :, 64:128], [B_big], [Bvst])
            yield
            if i <= 2:
                cp("pool", vst[:, :, 64:128], vvalid[:, n:n + 1].unsqueeze(2).to_broadcast([128, 8, 64]), [B_C], [Bvst])
                yield
            P.dma(v_scr[n].rearrange("p h d -> p (h d)"), vst[:].rearrange("p h d -> p (h d)"), reads=[Bvst], slot="vst" + kd)
            act(sqk[:], kv3[:, :, 0:64], AF.Square, [B_big], [Bsqk])
            yield
            red("dve", sm[4][0][:], sqk[:], [Bsqk], [sm[4][1]])
            yield
            ts("dve", sm[4][0][:], sm[4][0][:], sm[3][0][:, 0:1], ALU.add, [sm[4][1], sm[3][1]], [sm[4][1]])
            yield
            ts("pool", sm[5][0][:], sm[4][0][:], 1.0 / 96, ALU.mult, [sm[4][1]], [sm[5][1]], s2=EPS, op1=ALU.add)
            yield
            tt("pool", sm[6][0][:], sm[5][0][:], mhalf[:, 0:8], ALU.pow, [sm[5][1], B_C], [sm[6][1]])
            yield
            tt("dve", kt[:, :, 0:64], kv3[:, :, 0:64], sm[6][0][:].unsqueeze(2).to_broadcast([128, 8, 64]), ALU.mult,
               [B_big, sm[6][1]], [Bkt])
            yield REL(LBIG)
            tt("dve", kt[:, :, 64:96], krr[:].unsqueeze(1).to_broadcast([128, 8, 32]),
               sm[6][0][:].unsqueeze(2).to_broadcast([128, 8, 32]), ALU.mult, [Bkrr, sm[6][1]], [Bkt])
            yield
            yield ACQ(L5)
            for h in range(8):
                tr(ptr[0:96, h, :], kt[:, h, :], [Bkt], [B_b5])
            yield
            cp("act", kTst[0:96], ptr[0:96], [B_b5], [BkTst])
            yield REL(L5)
            P.dma(kT_scr.rearrange("h d t -> d h t")[:, :, n * 128:(n + 1) * 128], kTst[0:96], reads=[BkTst], slot="kTst" + kd)
            yield

          def swakv(kd, i):
            par = i % 2
            n = 2 * i + (0 if kd == "P" else 1)
            K = KS[kd]
            kvp, Bkvp = kvps[(kd, par)]
            sm = K["sm"]
            (sq2, Bsq2), (ks1, Bks1), (ksd, Bksd) = K["sq2"], K["ks1"], K["ksd"]
            skTk, BskT = skT[(kd, par)]
            svk, Bsv = sv[(kd, par)]
            skp = kvp[:, 160:288].rearrange("p (g d) -> p g d", g=2)
            act(sq2[:], skp, AF.Square, [Bkvp], [Bsq2])
            yield
            red("dve", sm[7][0][:, 0:2], sq2[:], [Bsq2], [sm[7][1]])
            yield
            ts("pool", sm[7][0][:, 2:4], sm[7][0][:, 0:2], 1.0 / 64, ALU.mult, [sm[7][1]], [sm[7][1]], s2=EPS, op1=ALU.add)
            yield
            tt("pool", sm[7][0][:, 4:6], sm[7][0][:, 2:4], mhalf[:, 0:2], ALU.pow, [sm[7][1], B_C], [sm[7][1]])
            yield
            tt("dve", ksd[:], skp, sm[7][0][:, 4:6].unsqueeze(2).to_broadcast([128, 2, 64]), ALU.mult, [Bkvp, sm[7][1]], [Bksd])
            yield
            r0, r1 = SMR["kT" + kd]
            Bsmr = B_sm["kT" + kd]
            yield ACQ(L2)
            for g in range(2):
                tr(psm[0:64, r0 + g * 128:r0 + (g + 1) * 128], ksd[:, g, :], [Bksd], [Bsmr])
            yield
            cp("act", skTk[0:64], psm[0:64, r0:r1].rearrange("p (g t) -> p g t", g=2), [Bsmr], [BskT])
            yield REL(L2)
            cp("act", svk[:, :, 0:64], kvp[:, 288:416].rearrange("p (g d) -> p g d", g=2), [Bkvp], [Bsv])
            yield
            if i <= 2:
                cp("pool", svk[:, :, 64:128], vvalid[:, n:n + 1].unsqueeze(2).to_broadcast([128, 2, 64]), [B_C], [Bsv])
                yield

          def chain_pa(i):
            yield from front("P", i)
            yield from kmla("P", i)

          def chain_pb(i):
            yield from front("O", i)

          def chain_pc(i):
            yield ("wait", ("kvP", i))
            yield from swakv("P", i)

          def chain_next(i):
            return [chain_pa(i), chain_pb(i), chain_pc(i)]

          def chain_ok(i):
            yield from swakv("O", i)
            yield ("set", ("swakv", i))

          def chain_ok2(i):
            yield from kmla("O", i)

          def chain_q(i):
            par = i % 2
            hTo, BhTo = hT[("O", par)]
            cs, sn, Bcs = tabs[("O", par)]
            yield ACQ(L0)
            qp = bk0
            for kc in range(8):
                mm(qp[:, 0:256], hTo[:, kc, 2:130], w_in[:, kc, C_QL:C_QL + 256], kc == 0, kc == 7, [BhTo, B_W], [B_b0])
            yield
            act(qln[:], qp[:, 0:256], AF.Square, [B_b0], [QS[0][1], B_qln], accum=QS[0][0][:, 0:1])
            yield
            ts("pool", QS[1][0][:, 0:1], QS[0][0][:, 0:1], 1.0 / 256, ALU.mult, [QS[0][1]], [QS[1][1]], s2=EPS, op1=ALU.add)
            yield
            tt("pool", QS[2][0][:, 0:1], QS[1][0][:, 0:1], mhalf[:, 0:1], ALU.pow, [QS[1][1], B_C], [QS[2][1]])
            yield
            ts("dve", qln[:], qp[:, 0:256], QS[2][0][:, 0:1], ALU.mult, [B_b0, QS[2][1]], [B_qln])
            yield REL(L0)
            r0, r1 = SMR["qln"]
            yield ACQ(L2)
            for c in range(2):
                tr(psm[:, r0 + c * 128:r0 + (c + 1) * 128], qln[:, c * 128:(c + 1) * 128], [B_qln], [B_sm["qln"]])
            yield
            cp("act", qlnT[:], psm[:, r0:r1].rearrange("p (c t) -> p c t", c=2), [B_sm["qln"]], [B_qlnT])
            yield REL(L2)
            yield ACQ(LBIG)
            for (c0, c1) in ((0, 512), (512, 768)):
                for c in range(2):
                    mm(big[:, c0:c1], qlnT[:, c, :], w_qb[:, c, c0:c1], c == 0, c == 1, [B_qlnT, B_W], [B_big])
            yield
            q3 = big[:, 0:768].rearrange("p (h d) -> p h d", h=8)
            act(sqq[:], q3, AF.Square, [B_big], [B_sqq])
            yield
            red("dve", QS[3][0][:], sqq[:], [B_sqq], [QS[3][1]])
            yield
            ts("pool", QS[4][0][:], QS[3][0][:], 1.0 / 96, ALU.mult, [QS[3][1]], [QS[4][1]], s2=EPS, op1=ALU.add)
            yield
            tt("pool", QS[5][0][:], QS[4][0][:], mhalf[:, 0:8], ALU.pow, [QS[4][1], B_C], [QS[5][1]])
            yield
            tt("dve", sqq[:], q3, QS[5][0][:].unsqueeze(2).to_broadcast([128, 8, 96]), ALU.mult, [B_big, QS[5][1], B_sqq], [B_sqq])
            yield REL(LBIG)
            tt("dve", qt[:, :, 0:64], sqq[:, :, 0:64], gqk[:].unsqueeze(1).to_broadcast([128, 8, 64]),
               ALU.mult, [B_sqq, B_W], [B_qt])
            yield
            tt("pool", qr[:], sqq[:, :, 64:96], gq_b[:, 64:96].unsqueeze(1).to_broadcast([128, 8, 32]), ALU.mult,
               [B_sqq, B_W], [B_qr])
            yield
            tt("dve", qr1[:], qr[:], cs[:].unsqueeze(1).to_broadcast([128, 8, 32]), ALU.mult, [B_qr, Bcs], [B_qr1])
            yield
            tt("pool", qr2[:, :, 0:16], qr[:, :, 16:32], sn[:, 0:16].unsqueeze(1).to_broadcast([128, 8, 16]), ALU.mult,
               [B_qr, Bcs], [B_qr2])
            yield
            tt("pool", qr2[:, :, 16:32], qr[:, :, 0:16], sn[:, 16:32].unsqueeze(1).to_broadcast([128, 8, 16]), ALU.mult,
               [B_qr, Bcs], [B_qr2])
            yield
            tt("dve", qt[:, :, 64:96], qr1[:], qr2[:], ALU.add, [B_qr1, B_qr2], [B_qt])
            yield
            yield ACQ(L5)
            for h in range(8):
                tr(ptr[0:96, h, :], qt[:, h, :], [B_qt], [B_b5])
            yield
            cp("act", qTst[0:96], ptr[0:96], [B_b5], [B_qTst])
            yield REL(L5)
            P.dma(qT_scr.rearrange("h d t -> d h t")[:, :, i * 128:(i + 1) * 128], qTst[0:96], reads=[B_qTst], slot="qTst")
            yield

          def chain_swa(i):
            par = i % 2
            hTo, BhTo = hT[("O", par)]
            sqp = bk7
            for kc in range(8):
                mm(sqp[:], hTo[:, kc, 2:130], w_in[:, kc, C_SQ:C_SQ + 512], kc == 0, kc == 7, [BhTo, B_W], [B_b7])
            yield
            sq3 = sqp[:].rearrange("p (h d) -> p h d", h=8)
            act(sqs[:], sq3, AF.Square, [B_b7], [B_sqs])
            yield
            red("dve", SS[0][0][:], sqs[:], [B_sqs], [SS[0][1]])
            yield
            ts("pool", SS[1][0][:], SS[0][0][:], 1.0 / 64, ALU.mult, [SS[0][1]], [SS[1][1]], s2=EPS, op1=ALU.add)
            yield
            tt("pool", SS[2][0][:], SS[1][0][:], mhalf[:, 0:8], ALU.pow, [SS[1][1], B_C], [SS[2][1]])
            yield
            tt("dve", sqs[:], sq3, SS[2][0][:].unsqueeze(2).to_broadcast([128, 8, 64]), ALU.mult, [B_b7, SS[2][1], B_sqs], [B_sqs])
            yield
            tt("dve", sqb[:], sqs[:], gsqk[:].unsqueeze(1).to_broadcast([128, 8, 64]), ALU.mult, [B_sqs, B_W], [B_sqb])
            yield
            yield ACQ(L0)
            for h in range(8):
                tr(ph[0:64, h * 128:(h + 1) * 128], sqb[:, h, :], [B_sqb], [B_b0])
            yield
            cp("act", sqT[0:64], ph[0:64, :].rearrange("p (c t) -> p c t", c=8), [B_b0], [B_sqT])
            yield REL(L0)
            yield ("wait", ("swakv", i))
            ns = 0
            for g in range(2):
                yield ACQ(L5)
                Ops = bk5
                for kb, kk in enumerate(("P", "O")):
                    skTk, BskT = skT[(kk, par)]
                    svk, Bsv = sv[(kk, par)]
                    for e4 in range(4):
                        h = 4 * g + e4
                        mm(bk7[:, e4 * 128:(e4 + 1) * 128], skTk[0:64, g, :], sqT[0:64, h, :], True, True,
                           [BskT, B_sqT], [B_b7])
                    yield
                    act(Ebuf[:], bk7[:], AF.Exp, [B_b7], [B_E], scale=0.125)
                    yield
                    pm, BP = Pm[ns % 2]
                    tt("dve", pm[:], Ebuf[:], swaM[:, kb, 4 * g:4 * g + 4, :].rearrange("p h q -> p (h q)"), ALU.mult,
                       [B_E, B_C], [BP])
                    yield
                    mm(Ops[:], svk[:, g, :], pm[:], kb == 0, kb == 1, [Bsv, BP], [B_b5])
                    yield
                    ns += 1
                tt("dve", rden[64:128, :].rearrange("p (h q) -> p h q", h=4),
                   Ops[64:128, :].rearrange("p (h q) -> p h q", h=4),
                   esink[64:128, 4 * g:4 * g + 4].unsqueeze(2).to_broadcast([64, 4, 128]), ALU.add, [B_b5, B_W], [B_rden])
                yield
                rcp(rden[64:128, :], rden[64:128, :], [B_rden], [B_rden])
                yield
                for e4 in range(4):
                    h = 4 * g + e4
                    hb_ = (h % 2) * 64
                    tt("dve", tsw[hb_:hb_ + 64, h // 2, :], Ops[0:64, e4 * 128:(e4 + 1) * 128],
                       rden[64:128, e4 * 128:(e4 + 1) * 128], ALU.mult, [B_b5, B_rden], [B_tsw])
                    yield
                yield REL(L5)

          def silu_from(G, out_ap, B_out):
            act(sge[:], G[:], AF.Tanh, [B_b6], [B_sge], scale=0.5)
            yield
            stt("dve", out_ap, sge[:], 1.0, G[:], ALU.add, ALU.mult, [B_b6, B_sge], [B_out])
            yield

          def chain_gc(i):
            par = i % 2
            hTo, BhTo = hT[("O", par)]

            def fm_mm(ps_ap, col0, lo, hi, Bps):
                for kc in range(8):
                    mm(ps_ap, w_in[:, kc, col0:col0 + 128], hTo[:, kc, lo:hi], kc == 0, kc == 7, [BhTo, B_W], [Bps])

            G = bk6
            for c in range(4):
                fm_mm(G[:, c * 128:(c + 1) * 128], C_GC + c * 128, 2, 130, B_b6)
                yield
            yield from silu_from(G, sgc[:].rearrange("p c t -> p (c t)"), B_sgc)
            yield ("set", ("sgc", i))
            for c in range(4):
                fm_mm(G[:, c * 128:(c + 1) * 128], C_GM + c * 128, 2, 130, B_b6)
                yield
            yield from silu_from(G, sgm[:].rearrange("p c t -> p (c t)"), B_sgm)
            P.dma(sg_scr.rearrange("c p t -> p c t")[:, :, i * 128:(i + 1) * 128], sgm[:], reads=[B_sgm], slot="sgm")
            for c in range(4):
                fm_mm(G[:, c * 128:(c + 1) * 128], C_GS + c * 128, 2, 130, B_b6)
                yield
            yield from silu_from(G, sgs[:].rearrange("p c t -> p (c t)"), B_sgs)

          def chain_conv(i):
            par = i % 2
            hTo, BhTo = hT[("O", par)]

            def fm_mm(ps_ap, col0, lo, hi, Bps):
                for kc in range(8):
                    mm(ps_ap, w_in[:, kc, col0:col0 + 128], hTo[:, kc, lo:hi], kc == 0, kc == 7, [BhTo, B_W], [Bps])

            for cc in range(4):
                yield ACQ(L1)
                X = bk1
                fm_mm(X[:, 0:130], C_CH + cc * 128, 0, 130, B_b1)
                yield
                fm_mm(X[:, 130:260], C_CC + cc * 128, 0, 130, B_b1)
                yield
                fm_mm(X[:, 260:388], C_CB + cc * 128, 2, 130, B_b1)
                yield
                cp("act", ch_sb[:], X[:, 0:130], [B_b1], [B_ch])
                yield
                tt("dve", u_sb[:], ch_sb[:], X[:, 130:260], ALU.mult, [B_ch, B_b1], [B_u])
                yield
                if cc == 0:
                    yield ("wait", ("sgc", i))
                tt("dve", tcb[:], X[:, 260:388], sgc[:, cc, :], ALU.mult, [B_b1, B_sgc], [B_tcb])
                yield REL(L1)
                ts("pool", cy[0][0][:], u_sb[:, 2:130], convw[:, 2, cc:cc + 1], ALU.mult, [B_u, B_W], [cy[0][1]])
                yield
                stt("dve", cy[1][0][:], u_sb[:, 1:129], convw[:, 1, cc:cc + 1], cy[0][0][:], ALU.mult, ALU.add,
                    [B_u, B_W, cy[0][1]], [cy[1][1]])
                yield
                stt("dve", cy[0][0][:], u_sb[:, 0:128], convw[:, 0, cc:cc + 1], cy[1][0][:], ALU.mult, ALU.add,
                    [B_u, B_W, cy[1][1]], [cy[0][1]])
                yield
                tt("pool", yT[:, cc, :], cy[0][0][:], tcb[:], ALU.mult, [cy[0][1], B_tcb], [B_yT])
                yield

          def tail(i):
            xs, Bx = x_O[i % 2]
            tt("dve", yT[:, 4:8, :], tsw[:], sgs[:], ALU.mult, [B_tsw, B_sgs], [B_yT])
            for hf in range(2):
                for kc in range(8):
                    mm(big[:, hf * 512:(hf + 1) * 512], yT[:, kc, :], w_out[:, 4 + kc, hf * 512:(hf + 1) * 512],
                       kc == 0, kc == 7, [B_yT, B_W], [B_big])
            tt("dve", osb[:], big[:], xs[:], ALU.add, [B_big, Bx], [B_osb])
            P.dma(part_scr[i], osb[:], reads=[B_osb], slot="part")

          nb_run = NB if not (dbg is not None and dbg.startswith("b")) else int(dbg[1:])
          load_xP(0)
          load_rest(0)
          if nb_run > 1:
              load_xP(1)
          drive(chain_next(0))
          for i in range(nb_run):
              chains = [chain_ok(i), chain_ok2(i), chain_q(i), chain_swa(i), chain_gc(i), chain_conv(i)]
              if i + 1 < nb_run:
                  load_rest(i + 1)
                  if i + 2 < nb_run:
                      load_xP(i + 2)
                  chains = chain_next(i + 1) + chains
              drive(chains)
              assert all(v is None for v in locks.values()), locks
              tail(i)

          if dbg == "p1":
              raise _Stop()
          P.barrier()
          ar.off = 0
          kTp = A([128, 2, NS * 128], BF16)
          vp = A([128, NS, 2, 128], BF16)
          qTp = A([128, 2, NB * 128], BF16)
          maskM = A([128, 8, 512], BF16); B_mask = P.buf("mask")
          PT = [A([128, 512], BF16) for _ in range(4)]; B_PT = [P.buf() for _ in range(4)]
          gbuf = [A([128, 512], BF16) for _ in range(2)]; B_g = [P.buf(), P.buf()]
          ymla = A([128, 4, NB * 128], BF16); B_ym = [P.buf(f"ym{j}") for j in range(8)]
          rden2 = A([128, 512], F32); B_rd2 = P.buf()
          tn = A([128, 512], F32); B_tn = P.buf()
          ptile = [A([128, D], F32) for _ in range(2)]; B_pt = [P.buf(), P.buf()]
          osb2 = [A([128, D], F32) for _ in range(2)]; B_o2 = [P.buf(), P.buf()]
          B_kc = [P.buf(f"kTc{c}") for c in range(8)]
          B_vc = [P.buf(f"vc{c}") for c in range(8)]
          B_q = P.buf("qTp")
          P.dma(maskM[:], maskM_d, writes=[B_mask], slot="c0")
          SC = 96 ** -0.5
          Sb = [bk0, bk1, bk2, bk7]
          Ob = [bk5, bk6]
          LA = 3
          PT6 = PT + [A([128, 512], BF16) for _ in range(2)]
          B_PT6 = B_PT + [P.buf(), P.buf()]
          nO = 0
          nS = 0
          for hp in range(4):
              for c in range(8):
                  for hh in range(2):
                      P.dma(kTp[0:96, hh, c * 1024:(c + 1) * 1024], kT_scr[2 * hp + hh, :, c * 1024:(c + 1) * 1024],
                            writes=[B_kc[c]], slot=f"kl{hh}")
                  P.dma(vp[:, 8 * c:8 * c + 8, :, :], v_scr[8 * c:8 * c + 8, :, 2 * hp:2 * hp + 2, :].rearrange("s p h d -> p s h d"),
                        writes=[B_vc[c]], slot="vl")
                  if c == 0:
                      for hh in range(2):
                          P.dma(qTp[0:96, hh, :], qT_scr[2 * hp + hh], writes=[B_q], slot=f"ql{hh}")
              units = []
              for j in range(8):
                  for hh in range(2):
                      nsl = 8 * j + 8
                      for sl in range(nsl):
                          units.append((j, hh, sl, nsl))
              pend = {}
              tile_bank = {}
              for t in range(len(units) + LA):
                  if t < len(units):
                      j, hh, sl, nsl = units[t]
                      if sl == 0 and hh == 0:
                          gb = gbuf[j % 2]; Bg = B_g[j % 2]
                          P.dma(gb[:], sg_scr[hp, :, j * 512:(j + 1) * 512], writes=[Bg], slot=f"gl{j % 2}")
                      Sps = Sb[nS % 4]; BS = B_bk[id(Sps)]
                      nS += 1
                      mm(Sps[:], kTp[0:96, hh, sl * 128:(sl + 1) * 128], qTp[0:96, hh, j * 512:(j + 1) * 512],
                         True, True, [B_kc[sl // 8], B_q], [BS])
                      pend[t] = (Sps, BS)
                  if t >= LA:
                      u = t - LA
                      j, hh, sl, nsl = units[u]
                      if sl == 0:
                          tile_bank[(j, hh)] = Ob[nO % 2]
                          nO += 1
                      Ops = tile_bank[(j, hh)]; BO = B_bk[id(Ops)]
                      Sps, BS = pend.pop(u)
                      pt = PT6[u % 6]; BP = B_PT6[u % 6]
                      act(pt[:], Sps[:], AF.Exp, [BS], [BP], scale=SC)
                      if sl >= 8 * j:
                          tt("dve", pt[:], pt[:], maskM[:, sl - 8 * j, :], ALU.mult, [BP, B_mask], [BP])
                      mm(Ops[:], vp[:, sl, hh, :], pt[:], sl == 0, sl == nsl - 1, [B_vc[sl // 8], BP], [BO])
                      if sl == nsl - 1:
                          gb = gbuf[j % 2]; Bg = B_g[j % 2]
                          hb = hh * 64
                          rcp(rden2[64:128, :], Ops[64:128, :], [BO], [B_rd2])
                          tt("dve", tn[hb:hb + 64, :], Ops[0:64, :], rden2[64:128, :], ALU.mult, [BO, B_rd2], [B_tn])
                          tt("pool", ymla[hb:hb + 64, hp, j * 512:(j + 1) * 512], tn[hb:hb + 64, :], gb[hb:hb + 64, :], ALU.mult,
                             [B_tn, Bg], [B_ym[j]])

          if dbg == "p2":
              raise _Stop()
          def load_pt(i):
              P.dma(ptile[i % 2][:], part_scr[i], writes=[B_pt[i % 2]], slot=f"pl{i % 2}")

          load_pt(0)
          for i in range(NB):
              if i + 1 < NB:
                  load_pt(i + 1)
              for hf in range(2):
                  for c in range(4):
                      mm(big[:, hf * 512:(hf + 1) * 512], ymla[:, c, i * 128:(i + 1) * 128],
                         w_out[:, c, hf * 512:(hf + 1) * 512], c == 0, c == 3, [B_ym[i // 4], B_W], [B_big])
              o2 = osb2[i % 2]; Bo2 = B_o2[i % 2]
              tt("dve", o2[:], big[:], ptile[i % 2][:], ALU.add, [B_big, B_pt[i % 2]], [Bo2])
              P.dma(ps["out"](i), o2[:], reads=[Bo2], slot=f"yo{i % 2}")
      except _Stop:
        pass
    print("arena p1 end", p1_end, "total ops recorded", P.total, {e: len(v) for e, v in P.ops.items()})
    P.emit(final_slots=[s_ for s_ in P.slot_counts])
    P.close()
    return nc


def _consts(r):
    half = 16
    inv_freq = np.power(np.float32(10000.0), -np.arange(half, dtype=np.float32) / half).astype(np.float32)
    pidx = np.arange(128, dtype=np.float32)
    cosT = np.zeros((128, NS, 32), np.float32)
    sinT = np.zeros((128, NS, 32), np.float32)
    for s in range(NS):
        gb = s - 1 + r
        pos = (gb * 128 + pidx).astype(np.float32)
        ang = pos[:, None] * inv_freq[None, :]
        c = np.cos(ang).astype(np.float32)
        sn = np.sin(ang).astype(np.float32)
        cosT[:, s, :16] = c
        cosT[:, s, 16:] = c
        sinT[:, s, :16] = -sn
        sinT[:, s, 16:] = sn
    maskM = np.zeros((128, 8, 512), np.float32)
    ki = np.arange(128)[:, None]
    qi = np.arange(128)[None, :]
    tri = (ki <= qi).astype(np.float32)
    for so in range(8):
        for qb in range(4):
            d = 2 * qb + 1
            if so < d:
                maskM[:, so, qb * 128:(qb + 1) * 128] = 1.0
            elif so == d:
                maskM[:, so, qb * 128:(qb + 1) * 128] = tri
    slopes = np.exp2(-8.0 * np.arange(1, 9, dtype=np.float32) / 8).astype(np.float32)
    swaM = np.zeros((128, 2, 8, 128), np.float32)
    for h in range(8):
        d0 = (128 + qi - ki).astype(np.float32)
        swaM[:, 0, h, :] = np.where(d0 < 128, np.exp(-slopes[h] * d0), 0.0)
        d1 = (qi - ki).astype(np.float32)
        swaM[:, 1, h, :] = np.where(d1 >= 0, np.exp(-slopes[h] * np.maximum(d1, 0)), 0.0)
    vvalid = np.ones((128, NS), np.float32)
    if r == 0:
        vvalid[:, 0] = 0.0
    return {"cosT": cosT, "sinT": sinT, "maskM": maskM.astype(ml_dtypes.bfloat16), "swaM": swaM, "vvalid": vvalid}


def _layer_weights(inp, l, suffix):
    f = lambda a: np.ascontiguousarray(a, dtype=np.float32)
    return {
        f"w_in_{suffix}": f(inp["w_in"][l][:, PERM]), f"norm_g_{suffix}": f(inp["norm_g"][l].reshape(8, 128).T),
        f"w_kvb_{suffix}": f(inp["mla_w_kvb"][l]), f"kva_g_{suffix}": f(inp["mla_kv_a_norm"][l].reshape(128, 1)),
        f"w_qb_{suffix}": f(inp["mla_w_qb"][l]), f"qa_g_{suffix}": f(inp["mla_q_a_norm"][l].reshape(2, 128).T),
        f"q_g_{suffix}": f(inp["mla_q_norm"][l]), f"k_g_{suffix}": f(inp["mla_k_norm"][l]),
        f"conv_w_{suffix}": f(inp["conv_w"][l].reshape(3, 4, 128).transpose(2, 0, 1)), f"sq_g_{suffix}": f(inp["swa_q_norm"][l]),
        f"sk_g_{suffix}": f(inp["swa_k_norm"][l]), f"sinks_{suffix}": f(inp["swa_sinks"][l]),
        f"w_out_{suffix}": f(inp["w_out"][l]),
    }


def _shard_x(x):
    xb = x.reshape(4, 64, 128, D)
    outs = []
    for c in range(8):
        b, r = c // 2, c % 2
        own = xb[b, r::2]
        prev = np.zeros_like(own)
        if r == 0:
            prev[1:] = xb[b, 1:63:2]
        else:
            prev[:] = xb[b, 0::2]
        outs.append((np.ascontiguousarray(own), np.ascontiguousarray(prev)))
    return outs


def _gather(res):
    out = np.zeros((4, 64, 128, D), np.float32)
    for c in range(8):
        b, r = c // 2, c % 2
        out[b, r::2] = res[c]["y_out"]
    return out.reshape(4, 8192, D)


_CACHE = {}
FUSED = True


def kernel(**inp):
    inp = {k: np.asarray(v) for k, v in inp.items()}
    x = np.ascontiguousarray(inp["x"], dtype=np.float32)
    depth = inp["w_in"].shape[0]
    consts = [_consts(r) for r in range(2)]
    if FUSED and depth == 2:
        if "nc2" not in _CACHE:
            _CACHE["nc2"] = build_program(2)
        nc = _CACHE["nc2"]
        w = {}
        for l in range(2):
            w.update(_layer_weights(inp, l, str(l)))
        sh = _shard_x(x)
        in_maps = []
        for c in range(8):
            r = c % 2
            o = c + 1 - 2 * r
            m = {"x_own": sh[c][0], "x_prev": sh[c][1], "x_own2": sh[o][0], "x_prev2": sh[o][1]}
            m.update(w)
            m.update(consts[r])
            co = consts[1 - r]
            m["cosT2"] = co["cosT"]; m["sinT2"] = co["sinT"]; m["vvalid2"] = co["vvalid"]
            bw = np.zeros((128, 2), np.float32)
            bw[:, r] = 1.0
            m["blendw"] = bw
            in_maps.append(m)
        res = run_bass_kernel_spmd(nc, in_maps, core_ids=list(range(8)))
        return _gather(res.results).astype(np.float32)
    if "nc1" not in _CACHE:
        _CACHE["nc1"] = build_program(1)
    for l in range(depth):
        nc = _CACHE["nc1"]
        w = _layer_weights(inp, l, "0")
        sh = _shard_x(x)
        in_maps = []
        for c in range(8):
            m = {"x_own": sh[c][0], "x_prev": sh[c][1]}
            m.update(w)
            m.update(consts[c % 2])
            in_maps.append(m)
        res = run_bass_kernel_spmd(nc, in_maps, core_ids=list(range(8)))
        x = _gather(res.results)
    return x.astype(np.float32)
```

```python
import numpy as np
import ml_dtypes
from contextlib import ExitStack
import concourse.bass as bass
import concourse.mybir as mybir
from concourse.bass_utils import run_bass_kernel_spmd

F32 = mybir.dt.float32
BF16 = mybir.dt.bfloat16
ALU = mybir.AluOpType
AF = mybir.ActivationFunctionType
AX = mybir.AxisListType

NB = 32
NS = 64
D = 1024
NCOL = 4256
EPS = 1e-6
C_KVL, C_KR, C_SK, C_SV, C_QL, C_SQ, C_GM, C_GS, C_CH, C_CC, C_CB, C_GC = (
    0, 128, 160, 288, 416, 672, 1184, 1696, 2208, 2720, 3232, 3744)
PERM = np.concatenate([np.arange(256, 384), np.arange(384, 416), np.arange(3488, 3616), np.arange(3616, 3744),
                       np.arange(0, 256), np.arange(2976, 3488), np.arange(416, 928), np.arange(3744, 4256),
                       np.arange(928, 1440), np.arange(1952, 2464), np.arange(1440, 1952), np.arange(2464, 2976)])

COMPUTE = ("pe", "act", "dve", "pool")
ALLENG = COMPUTE + ("sp",)


class Buf:
    __slots__ = ("name", "lw", "rd")

    def __init__(self, name):
        self.name = name
        self.lw = None
        self.rd = []


class Op:
    __slots__ = ("eng", "fn", "waits", "signal", "dma", "slot", "slot_cnt")

    def __init__(self, eng, fn, dma=False, slot=None):
        self.eng = eng
        self.fn = fn
        self.waits = []
        self.signal = False
        self.dma = dma
        self.slot = slot
        self.slot_cnt = 0


class Prog:
    def __init__(self, nc):
        self.nc = nc
        self.ops = {e: [] for e in ALLENG}
        self.seen = {e: {} for e in ALLENG}
        self.pending = {e: [] for e in ALLENG}
        self.slot_counts = {}
        self.stack = ExitStack()
        self.nbuf = 0

    def sbuf(self, name, shape, dtype):
        return self.stack.enter_context(self.nc.sbuf_tensor("sb_" + name, list(shape), dtype))

    def psum(self, name, shape, dtype):
        return self.stack.enter_context(self.nc.psum_tensor("ps_" + name, list(shape), dtype))

    def buf(self, name=None):
        self.nbuf += 1
        return Buf(name or f"b{self.nbuf}")

    def _dep(self, op, eng, key):
        kind, k, v = key
        if kind == "eng" and k == "pe" and eng == "pe":
            return
        seen = self.seen[eng]
        if seen.get((kind, k), -1) >= v:
            return
        seen[(kind, k)] = v
        op.waits.append(key)
        if kind == "eng":
            self.ops[k][v].signal = True

    limit = None
    total = 0
    trace = None

    def add(self, eng, fn, reads=(), writes=(), dma=False, slot=None):
        self.total += 1
        if self.limit is not None and self.total > self.limit:
            return None
        op = Op(eng, fn, dma=dma, slot=slot)
        if self.trace is not None:
            import sys as _s
            f = _s._getframe(1)
            while f is not None and f.f_code.co_name != "build_program":
                f = f.f_back
            self.trace.append((self.total, eng, f.f_lineno if f else -1))
        idx = len(self.ops[eng])
        for key in self.pending[eng]:
            self._dep(op, eng, key)
        self.pending[eng] = []
        if dma:
            cnt = self.slot_counts.get(slot, 0)
            if cnt > 0:
                self._dep(op, eng, ("slot", slot, cnt))
            self.slot_counts[slot] = cnt + 1
            op.slot_cnt = cnt + 1
            me = ("slot", slot, cnt + 1)
        else:
            me = ("eng", eng, idx)
        for b in reads:
            if b.lw is not None:
                self._dep(op, eng, b.lw)
        for b in writes:
            if b.lw is not None:
                self._dep(op, eng, b.lw)
            for r in b.rd:
                self._dep(op, eng, r)
        for b in reads:
            b.rd.append(me)
        for b in writes:
            b.lw = me
            b.rd = []
        self.ops[eng].append(op)
        return op

    def barrier(self):
        keys = []
        for e in ALLENG:
            for i in range(len(self.ops[e]) - 1, -1, -1):
                if not self.ops[e][i].dma:
                    keys.append(("eng", e, i))
                    break
        for s, c in self.slot_counts.items():
            keys.append(("slot", s, c))
        for e in ALLENG:
            self.pending[e] = self.pending[e] + [k for k in keys if not (k[0] == "eng" and k[1] == e)]

    def dma(self, out, in_, reads=(), writes=(), slot="d0", eng="sp"):
        return self.add(eng, lambda e: e.dma_start(out=out, in_=in_), reads, writes, dma=True, slot=slot)

    def emit(self, final_slots=()):
        nc = self.nc
        st = self.stack
        esem = {e: st.enter_context(nc.semaphore(f"s_{e}")) for e in ALLENG}
        ssem = {s: st.enter_context(nc.semaphore(f"d_{s}")) for s in self.slot_counts}
        cnt = {}
        for e, lst in self.ops.items():
            c = 0
            for i, op in enumerate(lst):
                if op.signal and not op.dma:
                    c += 1
                cnt[(e, i)] = c

        def run(engname, handle):
            for op in self.ops[engname]:
                for kind, k, v in op.waits:
                    if kind == "eng":
                        handle.wait_ge(esem[k], cnt[(k, v)])
                    else:
                        handle.wait_ge(ssem[k], 16 * v)
                ins = op.fn(handle)
                if op.dma:
                    ins.then_inc(ssem[op.slot], 16)
                elif op.signal:
                    ins.then_inc(esem[engname], 1)
            if engname == "sp":
                for s in final_slots:
                    handle.wait_ge(ssem[s], 16 * self.slot_counts[s])

        block = st.enter_context(nc.Block())

        @block.sync
        def _(e):
            run("sp", e)

        @block.tensor
        def _(e):
            run("pe", e)

        @block.scalar
        def _(e):
            run("act", e)

        @block.vector
        def _(e):
            run("dve", e)

        @block.gpsimd
        def _(e):
            run("pool", e)

    def close(self):
        self.stack.close()


WNAMES = ["w_in", "norm_g", "w_kvb", "kva_g", "w_qb", "qa_g", "q_g", "k_g", "conv_w", "sq_g", "sk_g",
          "sinks", "w_out"]
WSHAPES = {"w_in": [D, NCOL], "norm_g": [128, 8], "w_kvb": [128, 1024], "kva_g": [128, 1], "w_qb": [256, 768],
           "qa_g": [128, 2], "q_g": [96], "k_g": [96], "conv_w": [128, 3, 4], "sq_g": [64], "sk_g": [64],
           "sinks": [8], "w_out": [1536, 1024]}


class _Stop(Exception):
    pass


def build_program(nlayers=1, dbg=None):
    nc = bass.Bass("TRN2", target_bir_lowering=False)
    P = Prog(nc)
    import os
    if os.environ.get("K_LIMIT"):
        P.limit = int(os.environ["K_LIMIT"])

    def dram_in(name, shape, dt=F32):
        return nc.dram_tensor(name, list(shape), dt, kind="ExternalInput").ap()

    x_own = dram_in("x_own", [NB, 128, D])
    x_prev = dram_in("x_prev", [NB, 128, D])
    Wd = [{n: dram_in(f"{n}_{l}", WSHAPES[n]) for n in WNAMES} for l in range(nlayers)]
    cosT_d = dram_in("cosT", [128, NS, 32])
    sinT_d = dram_in("sinT", [128, NS, 32])
    maskM_d = dram_in("maskM", [128, 8, 512], BF16)
    swaM_d = dram_in("swaM", [128, 2, 8, 128])
    vvalid_d = dram_in("vvalid", [128, NS])
    y_out = nc.dram_tensor("y_out", [NB, 128, D], F32, kind="ExternalOutput").ap()
    fused = nlayers == 2
    if not fused:
        passes = [dict(W=Wd[0], own=lambda i: x_own[i], prev=lambda i: x_prev[i], cos=cosT_d, sin=sinT_d,
                       vv=vvalid_d, out=lambda i: y_out[i], blend=False)]
    else:
        x_own2 = dram_in("x_own2", [NB, 128, D])
        x_prev2 = dram_in("x_prev2", [NB, 128, D])
        cosT2_d = dram_in("cosT2", [128, NS, 32])
        sinT2_d = dram_in("sinT2", [128, NS, 32])
        vvalid2_d = dram_in("vvalid2", [128, NS])
        blendw_d = dram_in("blendw", [128, 2])
        x1_mine = nc.dram_tensor("x1_mine", [NB, 128, D], F32, kind="ExternalOutput").ap()
        Zb = nc.dram_tensor("x1_other", [NB + 1, 128, D], F32, kind="ExternalOutput").ap()
        passes = [
            dict(W=Wd[0], own=lambda i: x_own[i], prev=lambda i: x_prev[i], cos=cosT_d, sin=sinT_d, vv=vvalid_d,
                 out=lambda i: x1_mine[i], blend=False),
            dict(W=Wd[0], own=lambda i: x_own2[i], prev=lambda i: x_prev2[i], cos=cosT2_d, sin=sinT2_d, vv=vvalid2_d,
                 out=lambda i: Zb[i + 1], blend=False, same_w=True),
            dict(W=Wd[1], own=lambda i: x1_mine[i], prev=lambda i: Zb[i], prev2=lambda i: Zb[i + 1], cos=cosT_d,
                 sin=sinT_d, vv=vvalid_d, out=lambda i: y_out[i], blend=True),
        ]

    skind = "ExternalOutput"
    kT_scr = nc.dram_tensor("kT_scr", [8, 96, NS * 128], BF16, kind=skind).ap()
    v_scr = nc.dram_tensor("v_scr", [NS, 128, 8, 128], BF16, kind=skind).ap()
    qT_scr = nc.dram_tensor("qT_scr", [8, 96, NB * 128], BF16, kind=skind).ap()
    sg_scr = nc.dram_tensor("sg_scr", [4, 128, NB * 128], BF16, kind=skind).ap()
    part_scr = nc.dram_tensor("part_scr", [NB, 128, D], F32, kind=skind).ap()

    ident = P.sbuf("ident", [128, 128], BF16); B_ident = P.buf("ident")
    idf = P.sbuf("idf", [128, 128], F32)
    w_kvb = P.sbuf("w_kvb", [128, 1024], BF16)
    w_qb = P.sbuf("w_qb", [128, 2, 768], BF16)
    w_out = P.sbuf("w_out", [128, 12, 1024], BF16)
    swaM = P.sbuf("swaM", [128, 2, 8, 128], F32)
    vvalid = P.sbuf("vvalid", [128, NS], F32)
    gq_b = P.sbuf("gq_b", [128, 96], F32)
    gk_b = P.sbuf("gk_b", [128, 96], F32)
    gsq_b = P.sbuf("gsq_b", [128, 64], F32)
    gsk_b = P.sbuf("gsk_b", [128, 64], F32)
    esink = P.sbuf("esink", [128, 8], F32)
    normg = P.sbuf("normg", [128, 8], F32)
    convw = P.sbuf("convw", [128, 3, 4], F32)
    kvag = P.sbuf("kvag", [128, 1], F32)
    qag = P.sbuf("qag", [128, 2], F32)
    eps_t = P.sbuf("eps_t", [128, 1], F32)
    gqk = P.sbuf("gqk", [128, 64], F32)
    gsqk = P.sbuf("gsqk", [128, 64], F32)
    mhalf = P.sbuf("mhalf", [128, 8], F32)

    def mhalf_like(ap):
        return mhalf[:, 0:ap.shape[-1]]
    B_W = P.buf("weights")
    B_C = P.buf("consts")

    ARENA_B = 167 * 1024
    arena = P.sbuf("arena", [128, ARENA_B // 2], BF16)

    class Arena:
        def __init__(self):
            self.off = 0

        def alloc(self, shape, dt, parts=128):
            n = int(np.prod(shape[1:]))
            nbytes = n * (4 if dt == F32 else 2)
            nbytes = (nbytes + 31) // 32 * 32
            o = self.off
            self.off += nbytes
            assert self.off <= ARENA_B, f"arena overflow {self.off}"
            v = arena[:, o // 2:(o + nbytes) // 2]
            if dt == F32:
                v = v.bitcast(F32)
            v = v[:, 0:n]
            if len(shape) == 3:
                v = v.rearrange("p (a b) -> p a b", a=shape[1])
            elif len(shape) == 4:
                v = v.rearrange("p (a b c) -> p a b c", a=shape[1], b=shape[2])
            return v

    banks = [P.psum(f"bank{i}", [128, 512], F32) for i in range(3)]
    big = P.psum("big", [128, 1024], F32)
    banks += [P.psum(f"bank{i}", [128, 512], F32) for i in range(5, 8)]
    bk0, bk1, bk2, bk5, bk6, bk7 = banks
    B_bk = {id(b): P.buf(f"bk{i}") for i, b in enumerate(banks)}
    B_big = P.buf("big")

    def bf(ps):
        return ps[:].bitcast(BF16)

    def tt(eng, out, in0, in1, op, R, W):
        return P.add(eng, lambda e: e.tensor_tensor(out=out, in0=in0, in1=in1, op=op), R, W)

    def ts(eng, out, in0, s1, op0, R, W, s2=None, op1=None):
        if op1 is None:
            return P.add(eng, lambda e: e.tensor_scalar(out=out, in0=in0, scalar1=s1, scalar2=None, op0=op0), R, W)
        return P.add(eng, lambda e: e.tensor_scalar(out=out, in0=in0, scalar1=s1, scalar2=s2, op0=op0, op1=op1), R, W)

    def stt(eng, out, in0, scalar, in1, op0, op1, R, W):
        return P.add(eng, lambda e: e.scalar_tensor_tensor(out=out, in0=in0, scalar=scalar, in1=in1, op0=op0, op1=op1), R, W)

    def act(out, in_, func, R, W, scale=None, bias=None, accum=None):
        kw = {}
        if scale is not None:
            kw["scale"] = scale
        if bias is not None:
            kw["bias"] = bias
        if accum is not None:
            kw["accum_out"] = accum
        return P.add("act", lambda e: e.activation(out=out, in_=in_, func=func, **kw), R, W)

    def cp(eng, out, in_, R, W):
        if eng == "act":
            return P.add("act", lambda e: e.activation(out=out, in_=in_, func=AF.Copy), R, W)
        return P.add(eng, lambda e: e.tensor_copy(out=out, in_=in_), R, W)

    def red(eng, out, in_, R, W):
        return P.add(eng, lambda e: e.tensor_reduce(out=out, in_=in_, axis=AX.X, op=ALU.add), R, W)

    def rcp(out, in_, R, W):
        return P.add("dve", lambda e: e.reciprocal(out=out, in_=in_), R, W)

    def mm(out, lhsT, rhs, start, stop, R, W):
        return P.add("pe", lambda e: e.matmul(out, lhsT=lhsT, rhs=rhs, start=start, stop=stop), R, W)

    def tr(out, in_, R, W):
        return P.add("pe", lambda e: e.transpose(out=out, in_=in_, identity=ident[:]), list(R) + [B_ident], W)

    def rstd(ss_ap, out_ap, scale, R_ss, B_out, tmp_ap, B_tmp):
        act(tmp_ap, ss_ap, AF.Sqrt, [R_ss, B_C], [B_tmp], scale=scale, bias=eps_t[:, 0:1])
        rcp(out_ap, tmp_ap, [B_tmp], [B_out])

    P.add("pool", lambda e: e.memset(idf[:], 0.0), [], [B_ident])
    P.add("pool", lambda e: e.affine_select(out=idf[:], in_=idf[:], pattern=[[-1, 128]], compare_op=ALU.not_equal,
                                            fill=1.0, base=0, channel_multiplier=1), [B_ident], [B_ident])
    P.add("pool", lambda e: e.tensor_copy(out=ident[:], in_=idf[:]), [B_ident], [B_ident])
    P.add("pool", lambda e: e.memset(eps_t[:], EPS), [], [B_C])
    P.add("pool", lambda e: e.memset(mhalf[:], -0.5), [], [B_C])
    P.dma(swaM[:], swaM_d, writes=[B_C], slot="c0")
    blendw = P.sbuf("blendw", [128, 2], F32)
    if fused:
        P.dma(blendw[:], blendw_d, writes=[B_C], slot="c1")
        zero_t = arena[:, 0:2 * D].bitcast(F32)
        P.add("pool", lambda e: e.memset(zero_t, 0.0), [], [B_C])
        P.dma(Zb[0], zero_t, reads=[B_C], slot="c1")

    for ps in passes:
      try:
          W = ps["W"]
          cosT_d = ps["cos"]; sinT_d = ps["sin"]
          P.barrier()
          P.dma(vvalid[:], ps["vv"], writes=[B_C], slot="c1")
          ar = Arena()
          w_in = ar.alloc([128, 8, NCOL], BF16)
          NST = 4
          stage = [ar.alloc([128, 2128], F32) for _ in range(NST)]
          B_st = [P.buf(f"st{k}") for k in range(NST)]
          mark = ar.off
          same_w = ps.get("same_w", False)
          if not same_w:
           P.dma(normg[:], W["norm_g"], writes=[B_W], slot="c0")
           P.dma(convw[:], W["conv_w"], writes=[B_W], slot="c1")
           P.dma(kvag[:], W["kva_g"], writes=[B_W], slot="c0")
           P.dma(qag[:], W["qa_g"], writes=[B_W], slot="c1")
           P.dma(gq_b[:], W["q_g"].partition_broadcast(128), writes=[B_W], slot="c0")
           P.dma(gk_b[:], W["k_g"].partition_broadcast(128), writes=[B_W], slot="c1")
           P.dma(gsq_b[:], W["sq_g"].partition_broadcast(128), writes=[B_W], slot="c0")
           P.dma(gsk_b[:], W["sk_g"].partition_broadcast(128), writes=[B_W], slot="c1")
           P.dma(esink[:], W["sinks"].partition_broadcast(128), writes=[B_W], slot="c0")
           act(esink[:], esink[:], AF.Exp, [B_W], [B_W])
           tt("pool", gqk[:], gq_b[:, 0:64], gk_b[:, 0:64], ALU.mult, [B_W], [B_W])
           tt("pool", gsqk[:], gsq_b[:], gsk_b[:], ALU.mult, [B_W], [B_W])
          n = 0
          engs = ["dve", "dve"]
          win_d = W["w_in"].rearrange("(kc p) c -> p kc c", p=128)
          for kc in range(8):
              for hf in range(2):
                  s = n % NST
                  P.dma(stage[s][:], win_d[:, kc, hf * 2128:(hf + 1) * 2128], writes=[B_st[s]], slot=f"st{s}")
                  ts(engs[n % 2], w_in[:, kc, hf * 2128:(hf + 1) * 2128], stage[s][:], normg[:, kc:kc + 1], ALU.mult,
                     [B_st[s], B_W], [B_W])
                  n += 1
          wout_d = W["w_out"].rearrange("(kc p) c -> p kc c", p=128)
          for kc in range(0 if same_w else 12):
              s = n % NST
              P.dma(stage[s][:, 0:1024], wout_d[:, kc, :], writes=[B_st[s]], slot=f"st{s}")
              ts(engs[n % 2], w_out[:, kc, :], stage[s][:, 0:1024], 0.5, ALU.mult, [B_st[s]], [B_W])
              n += 1
          s = n % NST
          if not same_w:
              P.dma(stage[s][:, 0:1024], W["w_kvb"], writes=[B_st[s]], slot=f"st{s}")
              ts(engs[n % 2], w_kvb[:], stage[s][:, 0:1024], kvag[:, 0:1], ALU.mult, [B_st[s], B_W], [B_W])
              n += 1
          wqb_d = W["w_qb"].rearrange("(c p) n -> p c n", p=128)
          for c in range(0 if same_w else 2):
              s = n % NST
              P.dma(stage[s][:, 0:768], wqb_d[:, c, :], writes=[B_st[s]], slot=f"st{s}")
              ts(engs[n % 2], w_qb[:, c, :], stage[s][:, 0:768], qag[:, c:c + 1], ALU.mult, [B_st[s], B_W], [B_W])
              n += 1

          if dbg == "w":
              raise _Stop()
          P.barrier()
          ar.off = mark - NST * ((2128 * 4 + 31) // 32 * 32)
          A = ar.alloc

          def AB(shape, dt, name):
            return A(shape, dt), P.buf(name)

          x_P = [AB([128, D], F32, f"xP{k}") for k in range(2)]
          x_O = [AB([128, D], F32, f"xO{k}") for k in range(2)]
          xb2, B_xb2 = AB([128, D], F32, "xb2")
          tabs = {(kd, k): (A([128, 32], F32), A([128, 32], F32), P.buf(f"tab{kd}{k}")) for kd in "PO" for k in range(2)}
          h_bf = [AB([128, D], BF16, f"h{k}") for k in range(2)]
          hT = {(kd, k): AB([128, 8, 130], BF16, f"hT{kd}{k}") for kd in "PO" for k in range(2)}
          kvps = {(kd, k): AB([128, 416], F32, f"kvps{kd}{k}") for kd in "PO" for k in range(2)}
          skT = {(kd, k): AB([128, 2, 128], BF16, f"skT{kd}{k}") for kd in "PO" for k in range(2)}
          sv = {(kd, k): AB([128, 2, 128], BF16, f"sv{kd}{k}") for kd in "PO" for k in range(2)}

          def kset(tag):
            d = {}
            d["sm"] = [AB([128, 8], F32, f"sm{tag}{k}") for k in range(8)]
            d["kvn"] = AB([128, 128], BF16, "kvn" + tag)
            d["kvnT"] = AB([128, 128], BF16, "kvnT" + tag)
            d["vst"] = AB([128, 8, 128], BF16, "vst" + tag)
            d["sqk"] = AB([128, 8, 64], F32, "sqk" + tag)
            d["kt"] = AB([128, 8, 96], BF16, "kt" + tag)
            d["kr"] = AB([128, 32], F32, "kr" + tag)
            d["rt1"] = AB([128, 32], F32, "rt1" + tag)
            d["rt2"] = AB([128, 32], F32, "rt2" + tag)
            d["krr"] = AB([128, 32], F32, "krr" + tag)
            d["kTst"] = AB([128, 8, 128], BF16, "kTst" + tag)
            d["sq2"] = AB([128, 2, 64], F32, "sq2" + tag)
            d["ks1"] = AB([128, 2, 64], F32, "ks1" + tag)
            d["ksd"] = AB([128, 2, 64], BF16, "ksd" + tag)
            return d

          KS = {"P": kset("A"), "O": kset("B")}
          FS = {kd: [AB([128, 8], F32, f"fs{kd}{k}") for k in range(3)] for kd in "PO"}
          QS = [AB([128, 8], F32, f"qs{k}") for k in range(6)]
          qln, B_qln = AB([128, 256], BF16, "qln")
          qlnT, B_qlnT = AB([128, 2, 128], BF16, "qlnT")
          sqq, B_sqq = AB([128, 8, 96], F32, "sqq")
          qt, B_qt = AB([128, 8, 96], BF16, "qt")
          qr, B_qr = AB([128, 8, 32], F32, "qr")
          qr1, B_qr1 = AB([128, 8, 32], F32, "qr1")
          qr2, B_qr2 = AB([128, 8, 32], F32, "qr2")
          qTst, B_qTst = AB([128, 8, 128], BF16, "qTst")
          SS = [AB([128, 8], F32, f"ss{k}") for k in range(4)]
          sqs, B_sqs = AB([128, 8, 64], F32, "sqs")
          sqb, B_sqb = AB([128, 8, 64], BF16, "sqb")
          sqT, B_sqT = AB([128, 8, 128], BF16, "sqT")
          Ebuf, B_E = AB([128, 512], F32, "E")
          Pm = [AB([128, 512], BF16, "Pm0")] * 2
          rden, B_rden = AB([128, 512], F32, "rden")
          tsw, B_tsw = AB([128, 4, 128], F32, "tsw")
          sgs, B_sgs = AB([128, 4, 128], F32, "sgs")
          sgm, B_sgm = AB([128, 4, 128], BF16, "sgm")
          sgc, B_sgc = AB([128, 4, 128], F32, "sgc")
          ch_sb, B_ch = AB([128, 130], F32, "ch")
          u_sb, B_u = AB([128, 130], F32, "u")
          cy = [AB([128, 128], F32, f"cy{k}") for k in range(2)]
          tcb, B_tcb = AB([128, 128], F32, "tcb")
          yT, B_yT = AB([128, 8, 128], BF16, "yT")
          osb, B_osb = AB([128, D], F32, "osb")
          sge, B_sge = osb[:, 0:512], B_osb
          p1_end = ar.off

          ph = bf(bk0); B_b0 = B_bk[id(bk0)]
          B_b1 = B_bk[id(bk1)]
          psm = bf(bk2)
          B_b2 = B_bk[id(bk2)]
          B_sm = {k: B_b2 for k in ("kvnP", "kvnO", "kTP", "kTO", "qln")}
          L2 = "L2"
          SMR = {"kvnP": (0, 128), "kvnO": (128, 256), "kTP": (256, 512), "kTO": (512, 768), "qln": (768, 1024)}
          ptr = bf(bk5).rearrange("p (h t) -> p h t", h=8); B_b5 = B_bk[id(bk5)]
          B_b6 = B_bk[id(bk6)]; B_b7 = B_bk[id(bk7)]
          L0, L1, LBIG, L5 = "L0", "L1", "LBIG", "L5"

          def ACQ(l):
            return ("acq", l)

          def REL(l):
            return ("rel", l)

          def drive(chains):
            st = [dict(g=g, want=None) for g in chains]
            while st:
                progressed = False
                for c in list(st):
                    while True:
                        if c["want"] is not None:
                            kind, obj = c["want"]
                            if kind == "acq":
                                if locks.get(obj) is None:
                                    locks[obj] = c["g"]
                                elif locks[obj] is not c["g"]:
                                    break
                            elif kind == "wait":
                                if obj not in events:
                                    break
                            c["want"] = None
                        try:
                            r = next(c["g"])
                        except StopIteration:
                            st.remove(c)
                            progressed = True
                            break
                        progressed = True
                        if r is None:
                            break
                        kind, obj = r
                        if kind == "rel":
                            assert locks.get(obj) is c["g"], ("release of unowned lock", obj)
                            locks[obj] = None
                        elif kind == "set":
                            events.add(obj)
                        else:
                            c["want"] = r
                if not progressed:
                    raise RuntimeError("chain deadlock at build time")

          locks = {}
          events = set()

          def load_xP(i):
            xs, Bx = x_P[i % 2]
            P.dma(xs[:], ps["prev"](i), writes=[Bx], slot=f"xP{i % 2}")

          def load_rest(i):
            xs, Bx = x_O[i % 2]
            P.dma(xs[:], ps["own"](i), writes=[Bx], slot=f"xO{i % 2}")
            for kd, n in (("P", 2 * i), ("O", 2 * i + 1)):
                cs, sn, Bt = tabs[(kd, i % 2)]
                P.dma(cs[:], cosT_d[:, n, :], writes=[Bt], slot=f"cs{kd}{i % 2}")
                P.dma(sn[:], sinT_d[:, n, :], writes=[Bt], slot=f"sn{kd}{i % 2}")
            if ps["blend"]:
                P.dma(xb2[:], ps["prev2"](i), writes=[B_xb2], slot="xb2")

          def front(kd, i):
            par = i % 2
            xs, Bx = (x_P if kd == "P" else x_O)[par]
            hb, Bh = h_bf[0 if kd == "P" else 1]
            hTk, BhT = hT[(kd, par)]
            (s0, Bs0), (s1, Bs1), (s2, Bs2) = FS[kd]
            if ps["blend"] and kd == "P":
                ts("dve", xs[:], xs[:], blendw[:, 0:1], ALU.mult, [Bx, B_C], [Bx])
                yield
                stt("dve", xs[:], xb2[:], blendw[:, 1:2], xs[:], ALU.mult, ALU.add, [B_xb2, B_C, Bx], [Bx])
                yield
            act(hb[:], xs[:], AF.Square, [Bx], [Bs0, Bh], accum=s0[:, 0:1])
            yield
            ts("pool", s1[:, 0:1], s0[:, 0:1], 1.0 / D, ALU.mult, [Bs0], [Bs1], s2=EPS, op1=ALU.add)
            yield
            tt("pool", s2[:, 0:1], s1[:, 0:1], mhalf[:, 0:1], ALU.pow, [Bs1, B_C], [Bs2])
            yield
            ts("dve", hb[:], xs[:], s2[:, 0:1], ALU.mult, [Bx, Bs2], [Bh])
            yield
            yield ACQ(L0)
            for kc in range(8):
                tr(ph[:, kc * 128:(kc + 1) * 128], hb[:, kc * 128:(kc + 1) * 128], [Bh], [B_b0])
            yield
            cp("act", hTk[:, :, 2:130], ph.rearrange("p (a b) -> p a b", a=8), [B_b0], [BhT])
            yield REL(L0)
            if kd == "P":
                yield ("set", ("hTP", i))
            if kd == "O":
                yield ("wait", ("hTP", i))
                hTp, BhTp = hT[("P", par)]
                cp("pool", hTk[:, :, 0:2], hTp[:, :, 128:130], [BhTp], [BhT])
                yield
            yield ACQ(L1)
            for kc in range(8):
                mm(bk1[:, 0:416], hTk[:, kc, 2:130], w_in[:, kc, 0:416], kc == 0, kc == 7, [BhT, B_W], [B_b1])
            yield
            kv_sb, Bkv = kvps[(kd, par)]
            cp("act", kv_sb[:], bk1[:, 0:416], [B_b1], [Bkv])
            yield REL(L1)
            yield ("set", ("kv" + kd, i))

          def kmla(kd, i):
            par = i % 2
            n = 2 * i + (0 if kd == "P" else 1)
            K = KS[kd]
            kvp, Bkvp = kvps[(kd, par)]
            cs, sn, Bcs = tabs[(kd, par)]
            sm = K["sm"]
            (kvn, Bkvn), (kvnT, BkvnT), (vst, Bvst), (sqk, Bsqk), (kt, Bkt) = K["kvn"], K["kvnT"], K["vst"], K["sqk"], K["kt"]
            (kr, Bkr), (rt1, Brt1), (rt2, Brt2), (krr, Bkrr), (kTst, BkTst) = K["kr"], K["rt1"], K["rt2"], K["krr"], K["kTst"]
            act(kvn[:], kvp[:, 0:128], AF.Square, [Bkvp], [sm[0][1], Bkvn], accum=sm[0][0][:, 0:1])
            yield
            ts("pool", sm[1][0][:, 0:1], sm[0][0][:, 0:1], 1.0 / 128, ALU.mult, [sm[0][1]], [sm[1][1]], s2=EPS, op1=ALU.add)
            yield
            tt("pool", sm[2][0][:, 0:1], sm[1][0][:, 0:1], mhalf[:, 0:1], ALU.pow, [sm[1][1], B_C], [sm[2][1]])
            yield
            ts("dve", kvn[:], kvp[:, 0:128], sm[2][0][:, 0:1], ALU.mult, [Bkvp, sm[2][1]], [Bkvn])
            yield
            act(rt1[:], kvp[:, 128:160], AF.Square, [Bkvp], [sm[3][1], Brt1], accum=sm[3][0][:, 0:1])
            yield
            tt("pool", kr[:], kvp[:, 128:160], gk_b[:, 64:96], ALU.mult, [Bkvp, B_W], [Bkr])
            yield
            tt("pool", rt1[:], kr[:], cs[:], ALU.mult, [Bkr, Bcs], [Brt1])
            yield
            tt("pool", rt2[:, 0:16], kr[:, 16:32], sn[:, 0:16], ALU.mult, [Bkr, Bcs], [Brt2])
            yield
            tt("pool", rt2[:, 16:32], kr[:, 0:16], sn[:, 16:32], ALU.mult, [Bkr, Bcs], [Brt2])
            yield
            tt("pool", krr[:], rt1[:], rt2[:], ALU.add, [Brt1, Brt2], [Bkrr])
            yield
            r0, r1 = SMR["kvn" + kd]
            Bsmr = B_sm["kvn" + kd]
            yield ACQ(L2)
            tr(psm[:, r0:r1], kvn[:], [Bkvn], [Bsmr])
            yield
            cp("act", kvnT[:], psm[:, r0:r1], [Bsmr], [BkvnT])
            yield REL(L2)
            yield ACQ(LBIG)
            for hf in range(2):
                mm(big[:, hf * 512:(hf + 1) * 512], kvnT[:], w_kvb[:, hf * 512:(hf + 1) * 512], True, True,
                   [BkvnT, B_W], [B_big])
            yield
            kv3 = big[:].rearrange("p (h d) -> p h d", h=8)
            cp("act", vst[:, :, 0:64], kv3[:, :, 64:128], [B_big], [Bvst])
            yield
            if i <= 2:
                cp("pool", vst[:, :, 64:128], vvalid[:, n:n + 1].unsqueeze(2).to_broadcast([128, 8, 64]), [B_C], [Bvst])
                yield
            P.dma(v_scr[n].rearrange("p h d -> p (h d)"), vst[:].rearrange("p h d -> p (h d)"), reads=[Bvst], slot="vst" + kd)
            act(sqk[:], kv3[:, :, 0:64], AF.Square, [B_big], [Bsqk])
            yield
            red("dve", sm[4][0][:], sqk[:], [Bsqk], [sm[4][1]])
            yield
            ts("dve", sm[4][0][:], sm[4][0][:], sm[3][0][:, 0:1], ALU.add, [sm[4][1], sm[3][1]], [sm[4][1]])
            yield
            ts("pool", sm[5][0][:], sm[4][0][:], 1.0 / 96, ALU.mult, [sm[4][1]], [sm[5][1]], s2=EPS, op1=ALU.add)
            yield
            tt("pool", sm[6][0][:], sm[5][0][:], mhalf[:, 0:8], ALU.pow, [sm[5][1], B_C], [sm[6][1]])
            yield
            tt("dve", kt[:, :, 0:64], kv3[:, :, 0:64], sm[6][0][:].unsqueeze(2).to_broadcast([128, 8, 64]), ALU.mult,
               [B_big, sm[6][1]], [Bkt])
            yield REL(LBIG)
            tt("dve", kt[:, :, 64:96], krr[:].unsqueeze(1).to_broadcast([128, 8, 32]),
               sm[6][0][:].unsqueeze(2).to_broadcast([128, 8, 32]), ALU.mult, [Bkrr, sm[6][1]], [Bkt])
            yield
            yield ACQ(L5)
            for h in range(8):
                tr(ptr[0:96, h, :], kt[:, h, :], [Bkt], [B_b5])
            yield
            cp("act", kTst[0:96], ptr[0:96], [B_b5], [BkTst])
            yield REL(L5)
            P.dma(kT_scr.rearrange("h d t -> d h t")[:, :, n * 128:(n + 1) * 128], kTst[0:96], reads=[BkTst], slot="kTst" + kd)
            yield

          def swakv(kd, i):
            par = i % 2
            n = 2 * i + (0 if kd == "P" else 1)
            K = KS[kd]
            kvp, Bkvp = kvps[(kd, par)]
            sm = K["sm"]
            (sq2, Bsq2), (ks1, Bks1), (ksd, Bksd) = K["sq2"], K["ks1"], K["ksd"]
            skTk, BskT = skT[(kd, par)]
            svk, Bsv = sv[(kd, par)]
            skp = kvp[:, 160:288].rearrange("p (g d) -> p g d", g=2)
            act(sq2[:], skp, AF.Square, [Bkvp], [Bsq2])
            yield
            red("dve", sm[7][0][:, 0:2], sq2[:], [Bsq2], [sm[7][1]])
            yield
            ts("pool", sm[7][0][:, 2:4], sm[7][0][:, 0:2], 1.0 / 64, ALU.mult, [sm[7][1]], [sm[7][1]], s2=EPS, op1=ALU.add)
            yield
            tt("pool", sm[7][0][:, 4:6], sm[7][0][:, 2:4], mhalf[:, 0:2], ALU.pow, [sm[7][1], B_C], [sm[7][1]])
            yield
            tt("dve", ksd[:], skp, sm[7][0][:, 4:6].unsqueeze(2).to_broadcast([128, 2, 64]), ALU.mult, [Bkvp, sm[7][1]], [Bksd])
            yield
            r0, r1 = SMR["kT" + kd]
            Bsmr = B_sm["kT" + kd]
            yield ACQ(L2)
            for g in range(2):
                tr(psm[0:64, r0 + g * 128:r0 + (g + 1) * 128], ksd[:, g, :], [Bksd], [Bsmr])
            yield
            cp("act", skTk[0:64], psm[0:64, r0:r1].rearrange("p (g t) -> p g t", g=2), [Bsmr], [BskT])
            yield REL(L2)
            cp("act", svk[:, :, 0:64], kvp[:, 288:416].rearrange("p (g d) -> p g d", g=2), [Bkvp], [Bsv])
            yield
            if i <= 2:
                cp("pool", svk[:, :, 64:128], vvalid[:, n:n + 1].unsqueeze(2).to_broadcast([128, 2, 64]), [B_C], [Bsv])
                yield

          def chain_pa(i):
            yield from front("P", i)
            yield from kmla("P", i)

          def chain_pb(i):
            yield from front("O", i)

          def chain_pc(i):
            yield ("wait", ("kvP", i))
            yield from swakv("P", i)

          def chain_next(i):
            return [chain_pa(i), chain_pb(i), chain_pc(i)]

          def chain_ok(i):
            yield from swakv("O", i)
            yield ("set", ("swakv", i))

          def chain_ok2(i):
            yield from kmla("O", i)

          def chain_q(i):
            par = i % 2
            hTo, BhTo = hT[("O", par)]
            cs, sn, Bcs = tabs[("O", par)]
            yield ACQ(L0)
            qp = bk0
            for kc in range(8):
                mm(qp[:, 0:256], hTo[:, kc, 2:130], w_in[:, kc, C_QL:C_QL + 256], kc == 0, kc == 7, [BhTo, B_W], [B_b0])
            yield
            act(qln[:], qp[:, 0:256], AF.Square, [B_b0], [QS[0][1], B_qln], accum=QS[0][0][:, 0:1])
            yield
            ts("pool", QS[1][0][:, 0:1], QS[0][0][:, 0:1], 1.0 / 256, ALU.mult, [QS[0][1]], [QS[1][1]], s2=EPS, op1=ALU.add)
            yield
            tt("pool", QS[2][0][:, 0:1], QS[1][0][:, 0:1], mhalf[:, 0:1], ALU.pow, [QS[1][1], B_C], [QS[2][1]])
            yield
            ts("dve", qln[:], qp[:, 0:256], QS[2][0][:, 0:1], ALU.mult, [B_b0, QS[2][1]], [B_qln])
            yield REL(L0)
            r0, r1 = SMR["qln"]
            yield ACQ(L2)
            for c in range(2):
                tr(psm[:, r0 + c * 128:r0 + (c + 1) * 128], qln[:, c * 128:(c + 1) * 128], [B_qln], [B_sm["qln"]])
            yield
            cp("act", qlnT[:], psm[:, r0:r1].rearrange("p (c t) -> p c t", c=2), [B_sm["qln"]], [B_qlnT])
            yield REL(L2)
            yield ACQ(LBIG)
            for (c0, c1) in ((0, 512), (512, 768)):
                for c in range(2):
                    mm(big[:, c0:c1], qlnT[:, c, :], w_qb[:, c, c0:c1], c == 0, c == 1, [B_qlnT, B_W], [B_big])
            yield
            q3 = big[:, 0:768].rearrange("p (h d) -> p h d", h=8)
            act(sqq[:], q3, AF.Square, [B_big], [B_sqq])
            yield
            red("dve", QS[3][0][:], sqq[:], [B_sqq], [QS[3][1]])
            yield
            ts("pool", QS[4][0][:], QS[3][0][:], 1.0 / 96, ALU.mult, [QS[3][1]], [QS[4][1]], s2=EPS, op1=ALU.add)
            yield
            tt("pool", QS[5][0][:], QS[4][0][:], mhalf[:, 0:8], ALU.pow, [QS[4][1], B_C], [QS[5][1]])
            yield
            tt("dve", sqq[:], q3, QS[5][0][:].unsqueeze(2).to_broadcast([128, 8, 96]), ALU.mult, [B_big, QS[5][1], B_sqq], [B_sqq])
            yield REL(LBIG)
            tt("dve", qt[:, :, 0:64], sqq[:, :, 0:64], gqk[:].unsqueeze(1).to_broadcast([128, 8, 64]),
               ALU.mult, [B_sqq, B_W], [B_qt])
            yield
            tt("pool", qr[:], sqq[:, :, 64:96], gq_b[:, 64:96].unsqueeze(1).to_broadcast([128, 8, 32]), ALU.mult,
               [B_sqq, B_W], [B_qr])
            yield
            tt("dve", qr1[:], qr[:], cs[:].unsqueeze(1).to_broadcast([128, 8, 32]), ALU.mult, [B_qr, Bcs], [B_qr1])
            yield
            tt("pool", qr2[:, :, 0:16], qr[:, :, 16:32], sn[:, 0:16].unsqueeze(1).to_broadcast([128, 8, 16]), ALU.mult,
               [B_qr, Bcs], [B_qr2])
            yield
            tt("pool", qr2[:, :, 16:32], qr[:, :, 0:16], sn[:, 16:32].unsqueeze(1).to_broadcast([128, 8, 16]), ALU.mult,
               [B_qr, Bcs], [B_qr2])
            yield
            tt("dve", qt[:, :, 64:96], qr1[:], qr2[:], ALU.add, [B_qr1, B_qr2], [B_qt])
            yield
            yield ACQ(L5)
            for h in range(8):
                tr(ptr[0:96, h, :], qt[:, h, :], [B_qt], [B_b5])
            yield
            cp("act", qTst[0:96], ptr[0:96], [B_b5], [B_qTst])
            yield REL(L5)
            P.dma(qT_scr.rearrange("h d t -> d h t")[:, :, i * 128:(i + 1) * 128], qTst[0:96], reads=[B_qTst], slot="qTst")
            yield

          def chain_swa(i):
            par = i % 2
            hTo, BhTo = hT[("O", par)]
            sqp = bk7
            for kc in range(8):
                mm(sqp[:], hTo[:, kc, 2:130], w_in[:, kc, C_SQ:C_SQ + 512], kc == 0, kc == 7, [BhTo, B_W], [B_b7])
            yield
            sq3 = sqp[:].rearrange("p (h d) -> p h d", h=8)
            act(sqs[:], sq3, AF.Square, [B_b7], [B_sqs])
            yield
            red("dve", SS[0][0][:], sqs[:], [B_sqs], [SS[0][1]])
            yield
            ts("pool", SS[1][0][:], SS[0][0][:], 1.0 / 64, ALU.mult, [SS[0][1]], [SS[1][1]], s2=EPS, op1=ALU.add)
            yield
            tt("pool", SS[2][0][:], SS[1][0][:], mhalf[:, 0:8], ALU.pow, [SS[1][1], B_C], [SS[2][1]])
            yield
            tt("dve", sqs[:], sq3, SS[2][0][:].unsqueeze(2).to_broadcast([128, 8, 64]), ALU.mult, [B_b7, SS[2][1], B_sqs], [B_sqs])
            yield
            tt("dve", sqb[:], sqs[:], gsqk[:].unsqueeze(1).to_broadcast([128, 8, 64]), ALU.mult, [B_sqs, B_W], [B_sqb])
            yield
            yield ACQ(L0)
            for h in range(8):
                tr(ph[0:64, h * 128:(h + 1) * 128], sqb[:, h, :], [B_sqb], [B_b0])
            yield
            cp("act", sqT[0:64], ph[0:64, :].rearrange("p (c t) -> p c t", c=8), [B_b0], [B_sqT])
            yield REL(L0)
            yield ("wait", ("swakv", i))
            ns = 0
            for g in range(2):
                yield ACQ(L5)
                Ops = bk5
                for kb, kk in enumerate(("P", "O")):
                    skTk, BskT = skT[(kk, par)]
                    svk, Bsv = sv[(kk, par)]
                    for e4 in range(4):
                        h = 4 * g + e4
                        mm(bk7[:, e4 * 128:(e4 + 1) * 128], skTk[0:64, g, :], sqT[0:64, h, :], True, True,
                           [BskT, B_sqT], [B_b7])
                    yield
                    act(Ebuf[:], bk7[:], AF.Exp, [B_b7], [B_E], scale=0.125)
                    yield
                    pm, BP = Pm[ns % 2]
                    tt("dve", pm[:], Ebuf[:], swaM[:, kb, 4 * g:4 * g + 4, :].rearrange("p h q -> p (h q)"), ALU.mult,
                       [B_E, B_C], [BP])
                    yield
                    mm(Ops[:], svk[:, g, :], pm[:], kb == 0, kb == 1, [Bsv, BP], [B_b5])
                    yield
                    ns += 1
                tt("dve", rden[64:128, :].rearrange("p (h q) -> p h q", h=4),
                   Ops[64:128, :].rearrange("p (h q) -> p h q", h=4),
                   esink[64:128, 4 * g:4 * g + 4].unsqueeze(2).to_broadcast([64, 4, 128]), ALU.add, [B_b5, B_W], [B_rden])
                yield
                rcp(rden[64:128, :], rden[64:128, :], [B_rden], [B_rden])
                yield
                for e4 in range(4):
                    h = 4 * g + e4
                    hb_ = (h % 2) * 64
                    tt("dve", tsw[hb_:hb_ + 64, h // 2, :], Ops[0:64, e4 * 128:(e4 + 1) * 128],
                       rden[64:128, e4 * 128:(e4 + 1) * 128], ALU.mult, [B_b5, B_rden], [B_tsw])
                    yield
                yield REL(L5)

          def silu_from(G, out_ap, B_out):
            act(sge[:], G[:], AF.Tanh, [B_b6], [B_sge], scale=0.5)
            yield
            stt("dve", out_ap, sge[:], 1.0, G[:], ALU.add, ALU.mult, [B_b6, B_sge], [B_out])
            yield

          def chain_gc(i):
            par = i % 2
            hTo, BhTo = hT[("O", par)]

            def fm_mm(ps_ap, col0, lo, hi, Bps):
                for kc in range(8):
                    mm(ps_ap, w_in[:, kc, col0:col0 + 128], hTo[:, kc, lo:hi], kc == 0, kc == 7, [BhTo, B_W], [Bps])

            G = bk6
            for c in range(4):
                fm_mm(G[:, c * 128:(c + 1) * 128], C_GC + c * 128, 2, 130, B_b6)
                yield
            yield from silu_from(G, sgc[:].rearrange("p c t -> p (c t)"), B_sgc)
            yield ("set", ("sgc", i))
            for c in range(4):
                fm_mm(G[:, c * 128:(c + 1) * 128], C_GM + c * 128, 2, 130, B_b6)
                yield
            yield from silu_from(G, sgm[:].rearrange("p c t -> p (c t)"), B_sgm)
            P.dma(sg_scr.rearrange("c p t -> p c t")[:, :, i * 128:(i + 1) * 128], sgm[:], reads=[B_sgm], slot="sgm")
            for c in range(4):
                fm_mm(G[:, c * 128:(c + 1) * 128], C_GS + c * 128, 2, 130, B_b6)
                yield
            yield from silu_from(G, sgs[:].rearrange("p c t -> p (c t)"), B_sgs)

          def chain_conv(i):
            par = i % 2
            hTo, BhTo = hT[("O", par)]

            def fm_mm(ps_ap, col0, lo, hi, Bps):
                for kc in range(8):
                    mm(ps_ap, w_in[:, kc, col0:col0 + 128], hTo[:, kc, lo:hi], kc == 0, kc == 7, [BhTo, B_W], [Bps])

            for cc in range(4):
                yield ACQ(L1)
                X = bk1
                fm_mm(X[:, 0:130], C_CH + cc * 128, 0, 130, B_b1)
                yield
                fm_mm(X[:, 130:260], C_CC + cc * 128, 0, 130, B_b1)
                yield
                fm_mm(X[:, 260:388], C_CB + cc * 128, 2, 130, B_b1)
                yield
                cp("act", ch_sb[:], X[:, 0:130], [B_b1], [B_ch])
                yield
                tt("dve", u_sb[:], ch_sb[:], X[:, 130:260], ALU.mult, [B_ch, B_b1], [B_u])
                yield
                if cc == 0:
                    yield ("wait", ("sgc", i))
                tt("dve", tcb[:], X[:, 260:388], sgc[:, cc, :], ALU.mult, [B_b1, B_sgc], [B_tcb])
                yield REL(L1)
                ts("pool", cy[0][0][:], u_sb[:, 2:130], convw[:, 2, cc:cc + 1], ALU.mult, [B_u, B_W], [cy[0][1]])
                yield
                stt("dve", cy[1][0][:], u_sb[:, 1:129], convw[:, 1, cc:cc + 1], cy[0][0][:], ALU.mult, ALU.add,
                    [B_u, B_W, cy[0][1]], [cy[1][1]])
                yield
                stt("dve", cy[0][0][:], u_sb[:, 0:128], convw[:, 0, cc:cc + 1], cy[1][0][:], ALU.mult, ALU.add,
                    [B_u, B_W, cy[1][1]], [cy[0][1]])
                yield
                tt("pool", yT[:, cc, :], cy[0][0][:], tcb[:], ALU.mult, [cy[0][1], B_tcb], [B_yT])
                yield

          def tail(i):
            xs, Bx = x_O[i % 2]
            tt("dve", yT[:, 4:8, :], tsw[:], sgs[:], ALU.mult, [B_tsw, B_sgs], [B_yT])
            for hf in range(2):
                for kc in range(8):
                    mm(big[:, hf * 512:(hf + 1) * 512], yT[:, kc, :], w_out[:, 4 + kc, hf * 512:(hf + 1) * 512],
                       kc == 0, kc == 7, [B_yT, B_W], [B_big])
            tt("dve", osb[:], big[:], xs[:], ALU.add, [B_big, Bx], [B_osb])
            P.dma(part_scr[i], osb[:], reads=[B_osb], slot="part")

          nb_run = NB if not (dbg is not None and dbg.startswith("b")) else int(dbg[1:])
          load_xP(0)
          load_rest(0)
          if nb_run > 1:
              load_xP(1)
          drive(chain_next(0))
          for i in range(nb_run):
              chains = [chain_ok(i), chain_ok2(i), chain_q(i), chain_swa(i), chain_gc(i), chain_conv(i)]
              if i + 1 < nb_run:
                  load_rest(i + 1)
                  if i + 2 < nb_run:
                      load_xP(i + 2)
                  chains = chain_next(i + 1) + chains
              drive(chains)
              assert all(v is None for v in locks.values()), locks
              tail(i)

          if dbg == "p1":
              raise _Stop()
          P.barrier()
          ar.off = 0
          kTp = A([128, 2, NS * 128], BF16)
          vp = A([128, NS, 2, 128], BF16)
          qTp = A([128, 2, NB * 128], BF16)
          maskM = A([128, 8, 512], BF16); B_mask = P.buf("mask")
          PT = [A([128, 512], BF16) for _ in range(4)]; B_PT = [P.buf() for _ in range(4)]
          gbuf = [A([128, 512], BF16) for _ in range(2)]; B_g = [P.buf(), P.buf()]
          ymla = A([128, 4, NB * 128], BF16); B_ym = [P.buf(f"ym{j}") for j in range(8)]
          rden2 = A([128, 512], F32); B_rd2 = P.buf()
          tn = A([128, 512], F32); B_tn = P.buf()
          ptile = [A([128, D], F32) for _ in range(2)]; B_pt = [P.buf(), P.buf()]
          osb2 = [A([128, D], F32) for _ in range(2)]; B_o2 = [P.buf(), P.buf()]
          B_kc = [P.buf(f"kTc{c}") for c in range(8)]
          B_vc = [P.buf(f"vc{c}") for c in range(8)]
          B_q = P.buf("qTp")
          P.dma(maskM[:], maskM_d, writes=[B_mask], slot="c0")
          SC = 96 ** -0.5
          Sb = [bk0, bk1, bk2, bk7]
          Ob = [bk5, bk6]
          LA = 3
          PT6 = PT + [A([128, 512], BF16) for _ in range(2)]
          B_PT6 = B_PT + [P.buf(), P.buf()]
          nO = 0
          nS = 0
          for hp in range(4):
              for c in range(8):
                  for hh in range(2):
                      P.dma(kTp[0:96, hh, c * 1024:(c + 1) * 1024], kT_scr[2 * hp + hh, :, c * 1024:(c + 1) * 1024],
                            writes=[B_kc[c]], slot=f"kl{hh}")
                  P.dma(vp[:, 8 * c:8 * c + 8, :, :], v_scr[8 * c:8 * c + 8, :, 2 * hp:2 * hp + 2, :].rearrange("s p h d -> p s h d"),
                        writes=[B_vc[c]], slot="vl")
                  if c == 0:
                      for hh in range(2):
                          P.dma(qTp[0:96, hh, :], qT_scr[2 * hp + hh], writes=[B_q], slot=f"ql{hh}")
              units = []
              for j in range(8):
                  for hh in range(2):
                      nsl = 8 * j + 8
                      for sl in range(nsl):
                          units.append((j, hh, sl, nsl))
              pend = {}
              tile_bank = {}
              for t in range(len(units) + LA):
                  if t < len(units):
                      j, hh, sl, nsl = units[t]
                      if sl == 0 and hh == 0:
                          gb = gbuf[j % 2]; Bg = B_g[j % 2]
                          P.dma(gb[:], sg_scr[hp, :, j * 512:(j + 1) * 512], writes=[Bg], slot=f"gl{j % 2}")
                      Sps = Sb[nS % 4]; BS = B_bk[id(Sps)]
                      nS += 1
                      mm(Sps[:], kTp[0:96, hh, sl * 128:(sl + 1) * 128], qTp[0:96, hh, j * 512:(j + 1) * 512],
                         True, True, [B_kc[sl // 8], B_q], [BS])
                      pend[t] = (Sps, BS)
                  if t >= LA:
                      u = t - LA
                      j, hh, sl, nsl = units[u]
                      if sl == 0:
                          tile_bank[(j, hh)] = Ob[nO % 2]
                          nO += 1
                      Ops = tile_bank[(j, hh)]; BO = B_bk[id(Ops)]
                      Sps, BS = pend.pop(u)
                      pt = PT6[u % 6]; BP = B_PT6[u % 6]
                      act(pt[:], Sps[:], AF.Exp, [BS], [BP], scale=SC)
                      if sl >= 8 * j:
                          tt("dve", pt[:], pt[:], maskM[:, sl - 8 * j, :], ALU.mult, [BP, B_mask], [BP])
                      mm(Ops[:], vp[:, sl, hh, :], pt[:], sl == 0, sl == nsl - 1, [B_vc[sl // 8], BP], [BO])
                      if sl == nsl - 1:
                          gb = gbuf[j % 2]; Bg = B_g[j % 2]
                          hb = hh * 64
                          rcp(rden2[64:128, :], Ops[64:128, :], [BO], [B_rd2])
                          tt("dve", tn[hb:hb + 64, :], Ops[0:64, :], rden2[64:128, :], ALU.mult, [BO, B_rd2], [B_tn])
                          tt("pool", ymla[hb:hb + 64, hp, j * 512:(j + 1) * 512], tn[hb:hb + 64, :], gb[hb:hb + 64, :], ALU.mult,
                             [B_tn, Bg], [B_ym[j]])

          if dbg == "p2":
              raise _Stop()
          def load_pt(i):
              P.dma(ptile[i % 2][:], part_scr[i], writes=[B_pt[i % 2]], slot=f"pl{i % 2}")

          load_pt(0)
          for i in range(NB):
              if i + 1 < NB:
                  load_pt(i + 1)
              for hf in range(2):
                  for c in range(4):
                      mm(big[:, hf * 512:(hf + 1) * 512], ymla[:, c, i * 128:(i + 1) * 128],
                         w_out[:, c, hf * 512:(hf + 1) * 512], c == 0, c == 3, [B_ym[i // 4], B_W], [B_big])
              o2 = osb2[i % 2]; Bo2 = B_o2[i % 2]
              tt("dve", o2[:], big[:], ptile[i % 2][:], ALU.add, [B_big, B_pt[i % 2]], [Bo2])
              P.dma(ps["out"](i), o2[:], reads=[Bo2], slot=f"yo{i % 2}")
      except _Stop:
        pass
    print("arena p1 end", p1_end, "total ops recorded", P.total, {e: len(v) for e, v in P.ops.items()})
    P.emit(final_slots=[s_ for s_ in P.slot_counts])
    P.close()
    return nc


def _consts(r):
    half = 16
    inv_freq = np.power(np.float32(10000.0), -np.arange(half, dtype=np.float32) / half).astype(np.float32)
    pidx = np.arange(128, dtype=np.float32)
    cosT = np.zeros((128, NS, 32), np.float32)
    sinT = np.zeros((128, NS, 32), np.float32)
    for s in range(NS):
        gb = s - 1 + r
        pos = (gb * 128 + pidx).astype(np.float32)
        ang = pos[:, None] * inv_freq[None, :]
        c = np.cos(ang).astype(np.float32)
        sn = np.sin(ang).astype(np.float32)
        cosT[:, s, :16] = c
        cosT[:, s, 16:] = c
        sinT[:, s, :16] = -sn
        sinT[:, s, 16:] = sn
    maskM = np.zeros((128, 8, 512), np.float32)
    ki = np.arange(128)[:, None]
    qi = np.arange(128)[None, :]
    tri = (ki <= qi).astype(np.float32)
    for so in range(8):
        for qb in range(4):
            d = 2 * qb + 1
            if so < d:
                maskM[:, so, qb * 128:(qb + 1) * 128] = 1.0
            elif so == d:
                maskM[:, so, qb * 128:(qb + 1) * 128] = tri
    slopes = np.exp2(-8.0 * np.arange(1, 9, dtype=np.float32) / 8).astype(np.float32)
    swaM = np.zeros((128, 2, 8, 128), np.float32)
    for h in range(8):
        d0 = (128 + qi - ki).astype(np.float32)
        swaM[:, 0, h, :] = np.where(d0 < 128, np.exp(-slopes[h] * d0), 0.0)
        d1 = (qi - ki).astype(np.float32)
        swaM[:, 1, h, :] = np.where(d1 >= 0, np.exp(-slopes[h] * np.maximum(d1, 0)), 0.0)
    vvalid = np.ones((128, NS), np.float32)
    if r == 0:
        vvalid[:, 0] = 0.0
    return {"cosT": cosT, "sinT": sinT, "maskM": maskM.astype(ml_dtypes.bfloat16), "swaM": swaM, "vvalid": vvalid}


def _layer_weights(inp, l, suffix):
    f = lambda a: np.ascontiguousarray(a, dtype=np.float32)
    return {
        f"w_in_{suffix}": f(inp["w_in"][l][:, PERM]), f"norm_g_{suffix}": f(inp["norm_g"][l].reshape(8, 128).T),
        f"w_kvb_{suffix}": f(inp["mla_w_kvb"][l]), f"kva_g_{suffix}": f(inp["mla_kv_a_norm"][l].reshape(128, 1)),
        f"w_qb_{suffix}": f(inp["mla_w_qb"][l]), f"qa_g_{suffix}": f(inp["mla_q_a_norm"][l].reshape(2, 128).T),
        f"q_g_{suffix}": f(inp["mla_q_norm"][l]), f"k_g_{suffix}": f(inp["mla_k_norm"][l]),
        f"conv_w_{suffix}": f(inp["conv_w"][l].reshape(3, 4, 128).transpose(2, 0, 1)), f"sq_g_{suffix}": f(inp["swa_q_norm"][l]),
        f"sk_g_{suffix}": f(inp["swa_k_norm"][l]), f"sinks_{suffix}": f(inp["swa_sinks"][l]),
        f"w_out_{suffix}": f(inp["w_out"][l]),
    }


def _shard_x(x):
    xb = x.reshape(4, 64, 128, D)
    outs = []
    for c in range(8):
        b, r = c // 2, c % 2
        own = xb[b, r::2]
        prev = np.zeros_like(own)
        if r == 0:
            prev[1:] = xb[b, 1:63:2]
        else:
            prev[:] = xb[b, 0::2]
        outs.append((np.ascontiguousarray(own), np.ascontiguousarray(prev)))
    return outs


def _gather(res):
    out = np.zeros((4, 64, 128, D), np.float32)
    for c in range(8):
        b, r = c // 2, c % 2
        out[b, r::2] = res[c]["y_out"]
    return out.reshape(4, 8192, D)


_CACHE = {}
FUSED = True


def kernel(**inp):
    inp = {k: np.asarray(v) for k, v in inp.items()}
    x = np.ascontiguousarray(inp["x"], dtype=np.float32)
    depth = inp["w_in"].shape[0]
    consts = [_consts(r) for r in range(2)]
    if FUSED and depth == 2:
        if "nc2" not in _CACHE:
            _CACHE["nc2"] = build_program(2)
        nc = _CACHE["nc2"]
        w = {}
        for l in range(2):
            w.update(_layer_weights(inp, l, str(l)))
        sh = _shard_x(x)
        in_maps = []
        for c in range(8):
            r = c % 2
            o = c + 1 - 2 * r
            m = {"x_own": sh[c][0], "x_prev": sh[c][1], "x_own2": sh[o][0], "x_prev2": sh[o][1]}
            m.update(w)
            m.update(consts[r])
            co = consts[1 - r]
            m["cosT2"] = co["cosT"]; m["sinT2"] = co["sinT"]; m["vvalid2"] = co["vvalid"]
            bw = np.zeros((128, 2), np.float32)
            bw[:, r] = 1.0
            m["blendw"] = bw
            in_maps.append(m)
        res = run_bass_kernel_spmd(nc, in_maps, core_ids=list(range(8)))
        return _gather(res.results).astype(np.float32)
    if "nc1" not in _CACHE:
        _CACHE["nc1"] = build_program(1)
    for l in range(depth):
        nc = _CACHE["nc1"]
        w = _layer_weights(inp, l, "0")
        sh = _shard_x(x)
        in_maps = []
        for c in range(8):
            m = {"x_own": sh[c][0], "x_prev": sh[c][1]}
            m.update(w)
            m.update(consts[c % 2])
            in_maps.append(m)
        res = run_bass_kernel_spmd(nc, in_maps, core_ids=list(range(8)))
        x = _gather(res.results)
    return x.astype(np.float32)
```

```python
import numpy as np
import ml_dtypes
from contextlib import ExitStack
import concourse.bass as bass
import concourse.mybir as mybir
from concourse.bass_utils import run_bass_kernel_spmd

F32 = mybir.dt.float32
BF16 = mybir.dt.bfloat16
ALU = mybir.AluOpType
AF = mybir.ActivationFunctionType
AX = mybir.AxisListType

NB = 32
NS = 64
D = 1024
NCOL = 4256
EPS = 1e-6
C_KVL, C_KR, C_SK, C_SV, C_QL, C_SQ, C_GM, C_GS, C_CH, C_CC, C_CB, C_GC = (
    0, 128, 160, 288, 416, 672, 1184, 1696, 2208, 2720, 3232, 3744)
PERM = np.concatenate([np.arange(256, 384), np.arange(384, 416), np.arange(3488, 3616), np.arange(3616, 3744),
                       np.arange(0, 256), np.arange(2976, 3488), np.arange(416, 928), np.arange(3744, 4256),
                       np.arange(928, 1440), np.arange(1952, 2464), np.arange(1440, 1952), np.arange(2464, 2976)])

COMPUTE = ("pe", "act", "dve", "pool")
ALLENG = COMPUTE + ("sp",)


class Buf:
    __slots__ = ("name", "lw", "rd")

    def __init__(self, name):
        self.name = name
        self.lw = None
        self.rd = []


class Op:
    __slots__ = ("eng", "fn", "waits", "signal", "dma", "slot", "slot_cnt")

    def __init__(self, eng, fn, dma=False, slot=None):
        self.eng = eng
        self.fn = fn
        self.waits = []
        self.signal = False
        self.dma = dma
        self.slot = slot
        self.slot_cnt = 0


class Prog:
    def __init__(self, nc):
        self.nc = nc
        self.ops = {e: [] for e in ALLENG}
        self.seen = {e: {} for e in ALLENG}
        self.pending = {e: [] for e in ALLENG}
        self.slot_counts = {}
        self.stack = ExitStack()
        self.nbuf = 0

    def sbuf(self, name, shape, dtype):
        return self.stack.enter_context(self.nc.sbuf_tensor("sb_" + name, list(shape), dtype))

    def psum(self, name, shape, dtype):
        return self.stack.enter_context(self.nc.psum_tensor("ps_" + name, list(shape), dtype))

    def buf(self, name=None):
        self.nbuf += 1
        return Buf(name or f"b{self.nbuf}")

    def _dep(self, op, eng, key):
        kind, k, v = key
        if kind == "eng" and k == "pe" and eng == "pe":
            return
        seen = self.seen[eng]
        if seen.get((kind, k), -1) >= v:
            return
        seen[(kind, k)] = v
        op.waits.append(key)
        if kind == "eng":
            self.ops[k][v].signal = True

    limit = None
    total = 0
    trace = None

    def add(self, eng, fn, reads=(), writes=(), dma=False, slot=None):
        self.total += 1
        if self.limit is not None and self.total > self.limit:
            return None
        op = Op(eng, fn, dma=dma, slot=slot)
        if self.trace is not None:
            import sys as _s
            f = _s._getframe(1)
            while f is not None and f.f_code.co_name != "build_program":
                f = f.f_back
            self.trace.append((self.total, eng, f.f_lineno if f else -1))
        idx = len(self.ops[eng])
        for key in self.pending[eng]:
            self._dep(op, eng, key)
        self.pending[eng] = []
        if dma:
            cnt = self.slot_counts.get(slot, 0)
            if cnt > 0:
                self._dep(op, eng, ("slot", slot, cnt))
            self.slot_counts[slot] = cnt + 1
            op.slot_cnt = cnt + 1
            me = ("slot", slot, cnt + 1)
        else:
            me = ("eng", eng, idx)
        for b in reads:
            if b.lw is not None:
                self._dep(op, eng, b.lw)
        for b in writes:
            if b.lw is not None:
                self._dep(op, eng, b.lw)
            for r in b.rd:
                self._dep(op, eng, r)
        for b in reads:
            b.rd.append(me)
        for b in writes:
            b.lw = me
            b.rd = []
        self.ops[eng].append(op)
        return op

    def barrier(self):
        keys = []
        for e in ALLENG:
            for i in range(len(self.ops[e]) - 1, -1, -1):
                if not self.ops[e][i].dma:
                    keys.append(("eng", e, i))
                    break
        for s, c in self.slot_counts.items():
            keys.append(("slot", s, c))
        for e in ALLENG:
            self.pending[e] = self.pending[e] + [k for k in keys if not (k[0] == "eng" and k[1] == e)]

    def dma(self, out, in_, reads=(), writes=(), slot="d0", eng="sp"):
        return self.add(eng, lambda e: e.dma_start(out=out, in_=in_), reads, writes, dma=True, slot=slot)

    def emit(self, final_slots=()):
        nc = self.nc
        st = self.stack
        esem = {e: st.enter_context(nc.semaphore(f"s_{e}")) for e in ALLENG}
        ssem = {s: st.enter_context(nc.semaphore(f"d_{s}")) for s in self.slot_counts}
        cnt = {}
        for e, lst in self.ops.items():
            c = 0
            for i, op in enumerate(lst):
                if op.signal and not op.dma:
                    c += 1
                cnt[(e, i)] = c

        def run(engname, handle):
            for op in self.ops[engname]:
                for kind, k, v in op.waits:
                    if kind == "eng":
                        handle.wait_ge(esem[k], cnt[(k, v)])
                    else:
                        handle.wait_ge(ssem[k], 16 * v)
                ins = op.fn(handle)
                if op.dma:
                    ins.then_inc(ssem[op.slot], 16)
                elif op.signal:
                    ins.then_inc(esem[engname], 1)
            if engname == "sp":
                for s in final_slots:
                    handle.wait_ge(ssem[s], 16 * self.slot_counts[s])

        block = st.enter_context(nc.Block())

        @block.sync
        def _(e):
            run("sp", e)

        @block.tensor
        def _(e):
            run("pe", e)

        @block.scalar
        def _(e):
            run("act", e)

        @block.vector
        def _(e):
            run("dve", e)

        @block.gpsimd
        def _(e):
            run("pool", e)

    def close(self):
        self.stack.close()


WNAMES = ["w_in", "norm_g", "w_kvb", "kva_g", "w_qb", "qa_g", "q_g", "k_g", "conv_w", "sq_g", "sk_g",
          "sinks", "w_out"]
WSHAPES = {"w_in": [D, NCOL], "norm_g": [128, 8], "w_kvb": [128, 1024], "kva_g": [128, 1], "w_qb": [256, 768],
           "qa_g": [128, 2], "q_g": [96], "k_g": [96], "conv_w": [128, 3, 4], "sq_g": [64], "sk_g": [64],
           "sinks": [8], "w_out": [1536, 1024]}


class _Stop(Exception):
    pass


def build_program(nlayers=1, dbg=None):
    nc = bass.Bass("TRN2", target_bir_lowering=False)
    P = Prog(nc)
    import os
    if os.environ.get("K_LIMIT"):
        P.limit = int(os.environ["K_LIMIT"])

    def dram_in(name, shape, dt=F32):
        return nc.dram_tensor(name, list(shape), dt, kind="ExternalInput").ap()

    x_own = dram_in("x_own", [NB, 128, D])
    x_prev = dram_in("x_prev", [NB, 128, D])
    Wd = [{n: dram_in(f"{n}_{l}", WSHAPES[n]) for n in WNAMES} for l in range(nlayers)]
    cosT_d = dram_in("cosT", [128, NS, 32])
    sinT_d = dram_in("sinT", [128, NS, 32])
    maskM_d = dram_in("maskM", [128, 8, 512], BF16)
    swaM_d = dram_in("swaM", [128, 2, 8, 128])
    vvalid_d = dram_in("vvalid", [128, NS])
    y_out = nc.dram_tensor("y_out", [NB, 128, D], F32, kind="ExternalOutput").ap()
    fused = nlayers == 2
    if not fused:
        passes = [dict(W=Wd[0], own=lambda i: x_own[i], prev=lambda i: x_prev[i], cos=cosT_d, sin=sinT_d,
                       vv=vvalid_d, out=lambda i: y_out[i], blend=False)]
    else:
        x_own2 = dram_in("x_own2", [NB, 128, D])
        x_prev2 = dram_in("x_prev2", [NB, 128, D])
        cosT2_d = dram_in("cosT2", [128, NS, 32])
        sinT2_d = dram_in("sinT2", [128, NS, 32])
        vvalid2_d = dram_in("vvalid2", [128, NS])
        blendw_d = dram_in("blendw", [128, 2])
        x1_mine = nc.dram_tensor("x1_mine", [NB, 128, D], F32, kind="ExternalOutput").ap()
        Zb = nc.dram_tensor("x1_other", [NB + 1, 128, D], F32, kind="ExternalOutput").ap()
        passes = [
            dict(W=Wd[0], own=lambda i: x_own[i], prev=lambda i: x_prev[i], cos=cosT_d, sin=sinT_d, vv=vvalid_d,
                 out=lambda i: x1_mine[i], blend=False),
            dict(W=Wd[0], own=lambda i: x_own2[i], prev=lambda i: x_prev2[i], cos=cosT2_d, sin=sinT2_d, vv=vvalid2_d,
                 out=lambda i: Zb[i + 1], blend=False, same_w=True),
            dict(W=Wd[1], own=lambda i: x1_mine[i], prev=lambda i: Zb[i], prev2=lambda i: Zb[i + 1], cos=cosT_d,
                 sin=sinT_d, vv=vvalid_d, out=lambda i: y_out[i], blend=True),
        ]

    skind = "ExternalOutput"
    kT_scr = nc.dram_tensor("kT_scr", [8, 96, NS * 128], BF16, kind=skind).ap()
    v_scr = nc.dram_tensor("v_scr", [NS, 128, 8, 128], BF16, kind=skind).ap()
    qT_scr = nc.dram_tensor("qT_scr", [8, 96, NB * 128], BF16, kind=skind).ap()
    sg_scr = nc.dram_tensor("sg_scr", [4, 128, NB * 128], BF16, kind=skind).ap()
    part_scr = nc.dram_tensor("part_scr", [NB, 128, D], F32, kind=skind).ap()

    ident = P.sbuf("ident", [128, 128], BF16); B_ident = P.buf("ident")
    idf = P.sbuf("idf", [128, 128], F32)
    w_kvb = P.sbuf("w_kvb", [128, 1024], BF16)
    w_qb = P.sbuf("w_qb", [128, 2, 768], BF16)
    w_out = P.sbuf("w_out", [128, 12, 1024], BF16)
    swaM = P.sbuf("swaM", [128, 2, 8, 128], F32)
    vvalid = P.sbuf("vvalid", [128, NS], F32)
    gq_b = P.sbuf("gq_b", [128, 96], F32)
    gk_b = P.sbuf("gk_b", [128, 96], F32)
    gsq_b = P.sbuf("gsq_b", [128, 64], F32)
    gsk_b = P.sbuf("gsk_b", [128, 64], F32)
    esink = P.sbuf("esink", [128, 8], F32)
    normg = P.sbuf("normg", [128, 8], F32)
    convw = P.sbuf("convw", [128, 3, 4], F32)
    kvag = P.sbuf("kvag", [128, 1], F32)
    qag = P.sbuf("qag", [128, 2], F32)
    eps_t = P.sbuf("eps_t", [128, 1], F32)
    gqk = P.sbuf("gqk", [128, 64], F32)
    gsqk = P.sbuf("gsqk", [128, 64], F32)
    mhalf = P.sbuf("mhalf", [128, 8], F32)

    def mhalf_like(ap):
        return mhalf[:, 0:ap.shape[-1]]
    B_W = P.buf("weights")
    B_C = P.buf("consts")

    ARENA_B = 167 * 1024
    arena = P.sbuf("arena", [128, ARENA_B // 2], BF16)

    class Arena:
        def __init__(self):
            self.off = 0

        def alloc(self, shape, dt, parts=128):
            n = int(np.prod(shape[1:]))
            nbytes = n * (4 if dt == F32 else 2)
            nbytes = (nbytes + 31) // 32 * 32
            o = self.off
            self.off += nbytes
            assert self.off <= ARENA_B, f"arena overflow {self.off}"
            v = arena[:, o // 2:(o + nbytes) // 2]
            if dt == F32:
                v = v.bitcast(F32)
            v = v[:, 0:n]
            if len(shape) == 3:
                v = v.rearrange("p (a b) -> p a b", a=shape[1])
            elif len(shape) == 4:
                v = v.rearrange("p (a b c) -> p a b c", a=shape[1], b=shape[2])
            return v

    banks = [P.psum(f"bank{i}", [128, 512], F32) for i in range(3)]
    big = P.psum("big", [128, 1024], F32)
    banks += [P.psum(f"bank{i}", [128, 512], F32) for i in range(5, 8)]
    bk0, bk1, bk2, bk5, bk6, bk7 = banks
    B_bk = {id(b): P.buf(f"bk{i}") for i, b in enumerate(banks)}
    B_big = P.buf("big")

    def bf(ps):
        return ps[:].bitcast(BF16)

    def tt(eng, out, in0, in1, op, R, W):
        return P.add(eng, lambda e: e.tensor_tensor(out=out, in0=in0, in1=in1, op=op), R, W)

    def ts(eng, out, in0, s1, op0, R, W, s2=None, op1=None):
        if op1 is None:
            return P.add(eng, lambda e: e.tensor_scalar(out=out, in0=in0, scalar1=s1, scalar2=None, op0=op0), R, W)
        return P.add(eng, lambda e: e.tensor_scalar(out=out, in0=in0, scalar1=s1, scalar2=s2, op0=op0, op1=op1), R, W)

    def stt(eng, out, in0, scalar, in1, op0, op1, R, W):
        return P.add(eng, lambda e: e.scalar_tensor_tensor(out=out, in0=in0, scalar=scalar, in1=in1, op0=op0, op1=op1), R, W)

    def act(out, in_, func, R, W, scale=None, bias=None, accum=None):
        kw = {}
        if scale is not None:
            kw["scale"] = scale
        if bias is not None:
            kw["bias"] = bias
        if accum is not None:
            kw["accum_out"] = accum
        return P.add("act", lambda e: e.activation(out=out, in_=in_, func=func, **kw), R, W)

    def cp(eng, out, in_, R, W):
        if eng == "act":
            return P.add("act", lambda e: e.activation(out=out, in_=in_, func=AF.Copy), R, W)
        return P.add(eng, lambda e: e.tensor_copy(out=out, in_=in_), R, W)

    def red(eng, out, in_, R, W):
        return P.add(eng, lambda e: e.tensor_reduce(out=out, in_=in_, axis=AX.X, op=ALU.add), R, W)

    def rcp(out, in_, R, W):
        return P.add("dve", lambda e: e.reciprocal(out=out, in_=in_), R, W)

    def mm(out, lhsT, rhs, start, stop, R, W):
        return P.add("pe", lambda e: e.matmul(out, lhsT=lhsT, rhs=rhs, start=start, stop=stop), R, W)

    def tr(out, in_, R, W):
        return P.add("pe", lambda e: e.transpose(out=out, in_=in_, identity=ident[:]), list(R) + [B_ident], W)

    def rstd(ss_ap, out_ap, scale, R_ss, B_out, tmp_ap, B_tmp):
        act(tmp_ap, ss_ap, AF.Sqrt, [R_ss, B_C], [B_tmp], scale=scale, bias=eps_t[:, 0:1])
        rcp(out_ap, tmp_ap, [B_tmp], [B_out])

    P.add("pool", lambda e: e.memset(idf[:], 0.0), [], [B_ident])
    P.add("pool", lambda e: e.affine_select(out=idf[:], in_=idf[:], pattern=[[-1, 128]], compare_op=ALU.not_equal,
                                            fill=1.0, base=0, channel_multiplier=1), [B_ident], [B_ident])
    P.add("pool", lambda e: e.tensor_copy(out=ident[:], in_=idf[:]), [B_ident], [B_ident])
    P.add("pool", lambda e: e.memset(eps_t[:], EPS), [], [B_C])
    P.add("pool", lambda e: e.memset(mhalf[:], -0.5), [], [B_C])
    P.dma(swaM[:], swaM_d, writes=[B_C], slot="c0")
    blendw = P.sbuf("blendw", [128, 2], F32)
    if fused:
        P.dma(blendw[:], blendw_d, writes=[B_C], slot="c1")
        zero_t = arena[:, 0:2 * D].bitcast(F32)
        P.add("pool", lambda e: e.memset(zero_t, 0.0), [], [B_C])
        P.dma(Zb[0], zero_t, reads=[B_C], slot="c1")

    for ps in passes:
      try:
          W = ps["W"]
          cosT_d = ps["cos"]; sinT_d = ps["sin"]
          P.barrier()
          P.dma(vvalid[:], ps["vv"], writes=[B_C], slot="c1")
          ar = Arena()
          w_in = ar.alloc([128, 8, NCOL], BF16)
          NST = 4
          stage = [ar.alloc([128, 2128], F32) for _ in range(NST)]
          B_st = [P.buf(f"st{k}") for k in range(NST)]
          mark = ar.off
          same_w = ps.get("same_w", False)
          if not same_w:
           P.dma(normg[:], W["norm_g"], writes=[B_W], slot="c0")
           P.dma(convw[:], W["conv_w"], writes=[B_W], slot="c1")
           P.dma(kvag[:], W["kva_g"], writes=[B_W], slot="c0")
           P.dma(qag[:], W["qa_g"], writes=[B_W], slot="c1")
           P.dma(gq_b[:], W["q_g"].partition_broadcast(128), writes=[B_W], slot="c0")
           P.dma(gk_b[:], W["k_g"].partition_broadcast(128), writes=[B_W], slot="c1")
           P.dma(gsq_b[:], W["sq_g"].partition_broadcast(128), writes=[B_W], slot="c0")
           P.dma(gsk_b[:], W["sk_g"].partition_broadcast(128), writes=[B_W], slot="c1")
           P.dma(esink[:], W["sinks"].partition_broadcast(128), writes=[B_W], slot="c0")
           act(esink[:], esink[:], AF.Exp, [B_W], [B_W])
           tt("pool", gqk[:], gq_b[:, 0:64], gk_b[:, 0:64], ALU.mult, [B_W], [B_W])
           tt("pool", gsqk[:], gsq_b[:], gsk_b[:], ALU.mult, [B_W], [B_W])
          n = 0
          engs = ["dve", "dve"]
          win_d = W["w_in"].rearrange("(kc p) c -> p kc c", p=128)
          for kc in range(8):
              for hf in range(2):
                  s = n % NST
                  P.dma(stage[s][:], win_d[:, kc, hf * 2128:(hf + 1) * 2128], writes=[B_st[s]], slot=f"st{s}")
                  ts(engs[n % 2], w_in[:, kc, hf * 2128:(hf + 1) * 2128], stage[s][:], normg[:, kc:kc + 1], ALU.mult,
                     [B_st[s], B_W], [B_W])
                  n += 1
          wout_d = W["w_out"].rearrange("(kc p) c -> p kc c", p=128)
          for kc in range(0 if same_w else 12):
              s = n % NST
              P.dma(stage[s][:, 0:1024], wout_d[:, kc, :], writes=[B_st[s]], slot=f"st{s}")
              ts(engs[n % 2], w_out[:, kc, :], stage[s][:, 0:1024], 0.5, ALU.mult, [B_st[s]], [B_W])
              n += 1
          s = n % NST
          if not same_w:
              P.dma(stage[s][:, 0:1024], W["w_kvb"], writes=[B_st[s]], slot=f"st{s}")
              ts(engs[n % 2], w_kvb[:], stage[s][:, 0:1024], kvag[:, 0:1], ALU.mult, [B_st[s], B_W], [B_W])
              n += 1
          wqb_d = W["w_qb"].rearrange("(c p) n -> p c n", p=128)
          for c in range(0 if same_w else 2):
              s = n % NST
              P.dma(stage[s][:, 0:768], wqb_d[:, c, :], writes=[B_st[s]], slot=f"st{s}")
              ts(engs[n % 2], w_qb[:, c, :], stage[s][:, 0:768], qag[:, c:c + 1], ALU.mult, [B_st[s], B_W], [B_W])
              n += 1

          if dbg == "w":
              raise _Stop()
          P.barrier()
          ar.off = mark - NST * ((2128 * 4 + 31) // 32 * 32)
          A = ar.alloc

          def AB(shape, dt, name):
            return A(shape, dt), P.buf(name)

          x_P = [AB([128, D], F32, f"xP{k}") for k in range(2)]
          x_O = [AB([128, D], F32, f"xO{k}") for k in range(2)]
          xb2, B_xb2 = AB([128, D], F32, "xb2")
          tabs = {(kd, k): (A([128, 32], F32), A([128, 32], F32), P.buf(f"tab{kd}{k}")) for kd in "PO" for k in range(2)}
          h_bf = [AB([128, D], BF16, f"h{k}") for k in range(2)]
          hT = {(kd, k): AB([128, 8, 130], BF16, f"hT{kd}{k}") for kd in "PO" for k in range(2)}
          kvps = {(kd, k): AB([128, 416], F32, f"kvps{kd}{k}") for kd in "PO" for k in range(2)}
          skT = {(kd, k): AB([128, 2, 128], BF16, f"skT{kd}{k}") for kd in "PO" for k in range(2)}
          sv = {(kd, k): AB([128, 2, 128], BF16, f"sv{kd}{k}") for kd in "PO" for k in range(2)}

          def kset(tag):
            d = {}
            d["sm"] = [AB([128, 8], F32, f"sm{tag}{k}") for k in range(8)]
            d["kvn"] = AB([128, 128], BF16, "kvn" + tag)
            d["kvnT"] = AB([128, 128], BF16, "kvnT" + tag)
            d["vst"] = AB([128, 8, 128], BF16, "vst" + tag)
            d["sqk"] = AB([128, 8, 64], F32, "sqk" + tag)
            d["kt"] = AB([128, 8, 96], BF16, "kt" + tag)
            d["kr"] = AB([128, 32], F32, "kr" + tag)
            d["rt1"] = AB([128, 32], F32, "rt1" + tag)
            d["rt2"] = AB([128, 32], F32, "rt2" + tag)
            d["krr"] = AB([128, 32], F32, "krr" + tag)
            d["kTst"] = AB([128, 8, 128], BF16, "kTst" + tag)
            d["sq2"] = AB([128, 2, 64], F32, "sq2" + tag)
            d["ks1"] = AB([128, 2, 64], F32, "ks1" + tag)
            d["ksd"] = AB([128, 2, 64], BF16, "ksd" + tag)
            return d

          KS = {"P": kset("A"), "O": kset("B")}
          FS = {kd: [AB([128, 8], F32, f"fs{kd}{k}") for k in range(3)] for kd in "PO"}
          QS = [AB([128, 8], F32, f"qs{k}") for k in range(6)]
          qln, B_qln = AB([128, 256], BF16, "qln")
          qlnT, B_qlnT = AB([128, 2, 128], BF16, "qlnT")
          sqq, B_sqq = AB([128, 8, 96], F32, "sqq")
          qt, B_qt = AB([128, 8, 96], BF16, "qt")
          qr, B_qr = AB([128, 8, 32], F32, "qr")
          qr1, B_qr1 = AB([128, 8, 32], F32, "qr1")
          qr2, B_qr2 = AB([128, 8, 32], F32, "qr2")
          qTst, B_qTst = AB([128, 8, 128], BF16, "qTst")
          SS = [AB([128, 8], F32, f"ss{k}") for k in range(4)]
          sqs, B_sqs = AB([128, 8, 64], F32, "sqs")
          sqb, B_sqb = AB([128, 8, 64], BF16, "sqb")
          sqT, B_sqT = AB([128, 8, 128], BF16, "sqT")
          Ebuf, B_E = AB([128, 512], F32, "E")
          Pm = [AB([128, 512], BF16, "Pm0")] * 2
          rden, B_rden = AB([128, 512], F32, "rden")
          tsw, B_tsw = AB([128, 4, 128], F32, "tsw")
          sgs, B_sgs = AB([128, 4, 128], F32, "sgs")
          sgm, B_sgm = AB([128, 4, 128], BF16, "sgm")
          sgc, B_sgc = AB([128, 4, 128], F32, "sgc")
          ch_sb, B_ch = AB([128, 130], F32, "ch")
          u_sb, B_u = AB([128, 130], F32, "u")
          cy = [AB([128, 128], F32, f"cy{k}") for k in range(2)]
          tcb, B_tcb = AB([128, 128], F32, "tcb")
          yT, B_yT = AB([128, 8, 128], BF16, "yT")
          osb, B_osb = AB([128, D], F32, "osb")
          sge, B_sge = osb[:, 0:512], B_osb
          p1_end = ar.off

          ph = bf(bk0); B_b0 = B_bk[id(bk0)]
          B_b1 = B_bk[id(bk1)]
          psm = bf(bk2)
          B_b2 = B_bk[id(bk2)]
          B_sm = {k: B_b2 for k in ("kvnP", "kvnO", "kTP", "kTO", "qln")}
          L2 = "L2"
          SMR = {"kvnP": (0, 128), "kvnO": (128, 256), "kTP": (256, 512), "kTO": (512, 768), "qln": (768, 1024)}
          ptr = bf(bk5).rearrange("p (h t) -> p h t", h=8); B_b5 = B_bk[id(bk5)]
          B_b6 = B_bk[id(bk6)]; B_b7 = B_bk[id(bk7)]
          L0, L1, LBIG, L5 = "L0", "L1", "LBIG", "L5"

          def ACQ(l):
            return ("acq", l)

          def REL(l):
            return ("rel", l)

          def drive(chains):
            st = [dict(g=g, want=None) for g in chains]
            while st:
                progressed = False
                for c in list(st):
                    while True:
                        if c["want"] is not None:
                            kind, obj = c["want"]
                            if kind == "acq":
                                if locks.get(obj) is None:
                                    locks[obj] = c["g"]
                                elif locks[obj] is not c["g"]:
                                    break
                            elif kind == "wait":
                                if obj not in events:
                                    break
                            c["want"] = None
                        try:
                            r = next(c["g"])
                        except StopIteration:
                            st.remove(c)
                            progressed = True
                            break
                        progressed = True
                        if r is None:
                            break
                        kind, obj = r
                        if kind == "rel":
                            assert locks.get(obj) is c["g"], ("release of unowned lock", obj)
                            locks[obj] = None
                        elif kind == "set":
                            events.add(obj)
                        else:
                            c["want"] = r
                if not progressed:
                    raise RuntimeError("chain deadlock at build time")

          locks = {}
          events = set()

          def load_xP(i):
            xs, Bx = x_P[i % 2]
            P.dma(xs[:], ps["prev"](i), writes=[Bx], slot=f"xP{i % 2}")

          def load_rest(i):
            xs, Bx = x_O[i % 2]
            P.dma(xs[:], ps["own"](i), writes=[Bx], slot=f"xO{i % 2}")
            for kd, n in (("P", 2 * i), ("O", 2 * i + 1)):
                cs, sn, Bt = tabs[(kd, i % 2)]
                P.dma(cs[:], cosT_d[:, n, :], writes=[Bt], slot=f"cs{kd}{i % 2}")
                P.dma(sn[:], sinT_d[:, n, :], writes=[Bt], slot=f"sn{kd}{i % 2}")
            if ps["blend"]:
                P.dma(xb2[:], ps["prev2"](i), writes=[B_xb2], slot="xb2")

          def front(kd, i):
            par = i % 2
            xs, Bx = (x_P if kd == "P" else x_O)[par]
            hb, Bh = h_bf[0 if kd == "P" else 1]
            hTk, BhT = hT[(kd, par)]
            (s0, Bs0), (s1, Bs1), (s2, Bs2) = FS[kd]
            if ps["blend"] and kd == "P":
                ts("dve", xs[:], xs[:], blendw[:, 0:1], ALU.mult, [Bx, B_C], [Bx])
                yield
                stt("dve", xs[:], xb2[:], blendw[:, 1:2], xs[:], ALU.mult, ALU.add, [B_xb2, B_C, Bx], [Bx])
                yield
            act(hb[:], xs[:], AF.Square, [Bx], [Bs0, Bh], accum=s0[:, 0:1])
            yield
            ts("pool", s1[:, 0:1], s0[:, 0:1], 1.0 / D, ALU.mult, [Bs0], [Bs1], s2=EPS, op1=ALU.add)
            yield
            tt("pool", s2[:, 0:1], s1[:, 0:1], mhalf[:, 0:1], ALU.pow, [Bs1, B_C], [Bs2])
            yield
            ts("dve", hb[:], xs[:], s2[:, 0:1], ALU.mult, [Bx, Bs2], [Bh])
            yield
            yield ACQ(L0)
            for kc in range(8):
                tr(ph[:, kc * 128:(kc + 1) * 128], hb[:, kc * 128:(kc + 1) * 128], [Bh], [B_b0])
            yield
            cp("act", hTk[:, :, 2:130], ph.rearrange("p (a b) -> p a b", a=8), [B_b0], [BhT])
            yield REL(L0)
            if kd == "P":
                yield ("set", ("hTP", i))
            if kd == "O":
                yield ("wait", ("hTP", i))
                hTp, BhTp = hT[("P", par)]
                cp("pool", hTk[:, :, 0:2], hTp[:, :, 128:130], [BhTp], [BhT])
                yield
            yield ACQ(L1)
            for kc in range(8):
                mm(bk1[:, 0:416], hTk[:, kc, 2:130], w_in[:, kc, 0:416], kc == 0, kc == 7, [BhT, B_W], [B_b1])
            yield
            kv_sb, Bkv = kvps[(kd, par)]
            cp("act", kv_sb[:], bk1[:, 0:416], [B_b1], [Bkv])
            yield REL(L1)
            yield ("set", ("kv" + kd, i))

          def kmla(kd, i):
            par = i % 2
            n = 2 * i + (0 if kd == "P" else 1)
            K = KS[kd]
            kvp, Bkvp = kvps[(kd, par)]
            cs, sn, Bcs = tabs[(kd, par)]
            sm = K["sm"]
            (kvn, Bkvn), (kvnT, BkvnT), (vst, Bvst), (sqk, Bsqk), (kt, Bkt) = K["kvn"], K["kvnT"], K["vst"], K["sqk"], K["kt"]
            (kr, Bkr), (rt1, Brt1), (rt2, Brt2), (krr, Bkrr), (kTst, BkTst) = K["kr"], K["rt1"], K["rt2"], K["krr"], K["kTst"]
            act(kvn[:], kvp[:, 0:128], AF.Square, [Bkvp], [sm[0][1], Bkvn], accum=sm[0][0][:, 0:1])
            yield
            ts("pool", sm[1][0][:, 0:1], sm[0][0][:, 0:1], 1.0 / 128, ALU.mult, [sm[0][1]], [sm[1][1]], s2=EPS, op1=ALU.add)
            yield
            tt("pool", sm[2][0][:, 0:1], sm[1][0][:, 0:1], mhalf[:, 0:1], ALU.pow, [sm[1][1], B_C], [sm[2][1]])
            yield
            ts("dve", kvn[:], kvp[:, 0:128], sm[2][0][:, 0:1], ALU.mult, [Bkvp, sm[2][1]], [Bkvn])
            yield
            act(rt1[:], kvp[:, 128:160], AF.Square, [Bkvp], [sm[3][1], Brt1], accum=sm[3][0][:, 0:1])
            yield
            tt("pool", kr[:], kvp[:, 128:160], gk_b[:, 64:96], ALU.mult, [Bkvp, B_W], [Bkr])
            yield
            tt("pool", rt1[:], kr[:], cs[:], ALU.mult, [Bkr, Bcs], [Brt1])
            yield
            tt("pool", rt2[:, 0:16], kr[:, 16:32], sn[:, 0:16], ALU.mult, [Bkr, Bcs], [Brt2])
            yield
            tt("pool", rt2[:, 16:32], kr[:, 0:16], sn[:, 16:32], ALU.mult, [Bkr, Bcs], [Brt2])
            yield
            tt("pool", krr[:], rt1[:], rt2[:], ALU.add, [Brt1, Brt2], [Bkrr])
            yield
            r0, r1 = SMR["kvn" + kd]
            Bsmr = B_sm["kvn" + kd]
            yield ACQ(L2)
            tr(psm[:, r0:r1], kvn[:], [Bkvn], [Bsmr])
            yield
            cp("act", kvnT[:], psm[:, r0:r1], [Bsmr], [BkvnT])
            yield REL(L2)
            yield ACQ(LBIG)
            for hf in range(2):
                mm(big[:, hf * 512:(hf + 1) * 512], kvnT[:], w_kvb[:, hf * 512:(hf + 1) * 512], True, True,
                   [BkvnT, B_W], [B_big])
            yield
            kv3 = big[:].rearrange("p (h d) -> p h d", h=8)
            cp("act", vst[:, :, 0:64], kv3[:, :, 64:128], [B_big], [Bvst])
            yield
            if i <= 2:
                cp("pool", vst[:, :, 64:128], vvalid[:, n:n + 1].unsqueeze(2).to_broadcast([128, 8, 64]), [B_C], [Bvst])
                yield
            P.dma(v_scr[n].rearrange("p h d -> p (h d)"), vst[:].rearrange("p h d -> p (h d)"), reads=[Bvst], slot="vst" + kd)
            act(sqk[:], kv3[:, :, 0:64], AF.Square, [B_big], [Bsqk])
            yield
            red("dve", sm[4][0][:], sqk[:], [Bsqk], [sm[4][1]])
            yield
            ts("dve", sm[4][0][:], sm[4][0][:], sm[3][0][:, 0:1], ALU.add, [sm[4][1], sm[3][1]], [sm[4][1]])
            yield
            ts("pool", sm[5][0][:], sm[4][0][:], 1.0 / 96, ALU.mult, [sm[4][1]], [sm[5][1]], s2=EPS, op1=ALU.add)
            yield
            tt("pool", sm[6][0][:], sm[5][0][:], mhalf[:, 0:8], ALU.pow, [sm[5][1], B_C], [sm[6][1]])
            yield
            tt("dve", kt[:, :, 0:64], kv3[:, :, 0:64], sm[6][0][:].unsqueeze(2).to_broadcast([128, 8, 64]), ALU.mult,
               [B_big, sm[6][1]], [Bkt])
            yield REL(LBIG)
            tt("dve", kt[:, :, 64:96], krr[:].unsqueeze(1).to_broadcast([128, 8, 32]),
               sm[6][0][:].unsqueeze(2).to_broadcast([128, 8, 32]), ALU.mult, [Bkrr, sm[6][1]], [Bkt])
            yield
            yield ACQ(L5)
            for h in range(8):
                tr(ptr[0:96, h, :], kt[:, h, :], [Bkt], [B_b5])
            yield
            cp("act", kTst[0:96], ptr[0:96], [B_b5], [BkTst])
            yield REL(L5)
            P.dma(kT_scr.rearrange("h d t -> d h t")[:, :, n * 128:(n + 1) * 128], kTst[0:96], reads=[BkTst], slot="kTst" + kd)
            yield

          def swakv(kd, i):
            par = i % 2
            n = 2 * i + (0 if kd == "P" else 1)
            K = KS[kd]
            kvp, Bkvp = kvps[(kd, par)]
            sm = K["sm"]
            (sq2, Bsq2), (ks1, Bks1), (ksd, Bksd) = K["sq2"], K["ks1"], K["ksd"]
            skTk, BskT = skT[(kd, par)]
            svk, Bsv = sv[(kd, par)]
            skp = kvp[:, 160:288].rearrange("p (g d) -> p g d", g=2)
            act(sq2[:], skp, AF.Square, [Bkvp], [Bsq2])
            yield
            red("dve", sm[7][0][:, 0:2], sq2[:], [Bsq2], [sm[7][1]])
            yield
            ts("pool", sm[7][0][:, 2:4], sm[7][0][:, 0:2], 1.0 / 64, ALU.mult, [sm[7][1]], [sm[7][1]], s2=EPS, op1=ALU.add)
            yield
            tt("pool", sm[7][0][:, 4:6], sm[7][0][:, 2:4], mhalf[:, 0:2], ALU.pow, [sm[7][1], B_C], [sm[7][1]])
            yield
            tt("dve", ksd[:], skp, sm[7][0][:, 4:6].unsqueeze(2).to_broadcast([128, 2, 64]), ALU.mult, [Bkvp, sm[7][1]], [Bksd])
            yield
            r0, r1 = SMR["kT" + kd]
            Bsmr = B_sm["kT" + kd]
            yield ACQ(L2)
            for g in range(2):
                tr(psm[0:64, r0 + g * 128:r0 + (g + 1) * 128], ksd[:, g, :], [Bksd], [Bsmr])
            yield
            cp("act", skTk[0:64], psm[0:64, r0:r1].rearrange("p (g t) -> p g t", g=2), [Bsmr], [BskT])
            yield REL(L2)
            cp("act", svk[:, :, 0:64], kvp[:, 288:416].rearrange("p (g d) -> p g d", g=2), [Bkvp], [Bsv])
            yield
            if i <= 2:
                cp("pool", svk[:, :, 64:128], vvalid[:, n:n + 1].unsqueeze(2).to_broadcast([128, 2, 64]), [B_C], [Bsv])
                yield

          def chain_pa(i):
            yield from front("P", i)
            yield from kmla("P", i)

          def chain_pb(i):
            yield from front("O", i)

          def chain_pc(i):
            yield ("wait", ("kvP", i))
            yield from swakv("P", i)

          def chain_next(i):
            return [chain_pa(i), chain_pb(i), chain_pc(i)]

          def chain_ok(i):
            yield from swakv("O", i)
            yield ("set", ("swakv", i))

          def chain_ok2(i):
            yield from kmla("O", i)

          def chain_q(i):
            par = i % 2
            hTo, BhTo = hT[("O", par)]
            cs, sn, Bcs = tabs[("O", par)]
            yield ACQ(L0)
            qp = bk0
            for kc in range(8):
                mm(qp[:, 0:256], hTo[:, kc, 2:130], w_in[:, kc, C_QL:C_QL + 256], kc == 0, kc == 7, [BhTo, B_W], [B_b0])
            yield
            act(qln[:], qp[:, 0:256], AF.Square, [B_b0], [QS[0][1], B_qln], accum=QS[0][0][:, 0:1])
            yield
            ts("pool", QS[1][0][:, 0:1], QS[0][0][:, 0:1], 1.0 / 256, ALU.mult, [QS[0][1]], [QS[1][1]], s2=EPS, op1=ALU.add)
            yield
            tt("pool", QS[2][0][:, 0:1], QS[1][0][:, 0:1], mhalf[:, 0:1], ALU.pow, [QS[1][1], B_C], [QS[2][1]])
            yield
            ts("dve", qln[:], qp[:, 0:256], QS[2][0][:, 0:1], ALU.mult, [B_b0, QS[2][1]], [B_qln])
            yield REL(L0)
            r0, r1 = SMR["qln"]
            yield ACQ(L2)
            for c in range(2):
                tr(psm[:, r0 + c * 128:r0 + (c + 1) * 128], qln[:, c * 128:(c + 1) * 128], [B_qln], [B_sm["qln"]])
            yield
            cp("act", qlnT[:], psm[:, r0:r1].rearrange("p (c t) -> p c t", c=2), [B_sm["qln"]], [B_qlnT])
            yield REL(L2)
            yield ACQ(LBIG)
            for (c0, c1) in ((0, 512), (512, 768)):
                for c in range(2):
                    mm(big[:, c0:c1], qlnT[:, c, :], w_qb[:, c, c0:c1], c == 0, c == 1, [B_qlnT, B_W], [B_big])
            yield
            q3 = big[:, 0:768].rearrange("p (h d) -> p h d", h=8)
            act(sqq[:], q3, AF.Square, [B_big], [B_sqq])
            yield
            red("dve", QS[3][0][:], sqq[:], [B_sqq], [QS[3][1]])
            yield
            ts("pool", QS[4][0][:], QS[3][0][:], 1.0 / 96, ALU.mult, [QS[3][1]], [QS[4][1]], s2=EPS, op1=ALU.add)
            yield
            tt("pool", QS[5][0][:], QS[4][0][:], mhalf[:, 0:8], ALU.pow, [QS[4][1], B_C], [QS[5][1]])
            yield
            tt("dve", sqq[:], q3, QS[5][0][:].unsqueeze(2).to_broadcast([128, 8, 96]), ALU.mult, [B_big, QS[5][1], B_sqq], [B_sqq])
            yield REL(LBIG)
            tt("dve", qt[:, :, 0:64], sqq[:, :, 0:64], gqk[:].unsqueeze(1).to_broadcast([128, 8, 64]),
               ALU.mult, [B_sqq, B_W], [B_qt])
            yield
            tt("pool", qr[:], sqq[:, :, 64:96], gq_b[:, 64:96].unsqueeze(1).to_broadcast([128, 8, 32]), ALU.mult,
               [B_sqq, B_W], [B_qr])
            yield
            tt("dve", qr1[:], qr[:], cs[:].unsqueeze(1).to_broadcast([128, 8, 32]), ALU.mult, [B_qr, Bcs], [B_qr1])
            yield
            tt("pool", qr2[:, :, 0:16], qr[:, :, 16:32], sn[:, 0:16].unsqueeze(1).to_broadcast([128, 8, 16]), ALU.mult,
               [B_qr, Bcs], [B_qr2])
            yield
            tt("pool", qr2[:, :, 16:32], qr[:, :, 0:16], sn[:, 16:32].unsqueeze(1).to_broadcast([128, 8, 16]), ALU.mult,
               [B_qr, Bcs], [B_qr2])
            yield
            tt("dve", qt[:, :, 64:96], qr1[:], qr2[:], ALU.add, [B_qr1, B_qr2], [B_qt])
            yield
            yield ACQ(L5)
            for h in range(8):
                tr(ptr[0:96, h, :], qt[:, h, :], [B_qt], [B_b5])
            yield
            cp("act", qTst[0:96], ptr[0:96], [B_b5], [B_qTst])
            yield REL(L5)
            P.dma(qT_scr.rearrange("h d t -> d h t")[:, :, i * 128:(i + 1) * 128], qTst[0:96], reads=[B_qTst], slot="qTst")
            yield

          def chain_swa(i):
            par = i % 2
            hTo, BhTo = hT[("O", par)]
            sqp = bk7
            for kc in range(8):
                mm(sqp[:], hTo[:, kc, 2:130], w_in[:, kc, C_SQ:C_SQ + 512], kc == 0, kc == 7, [BhTo, B_W], [B_b7])
            yield
            sq3 = sqp[:].rearrange("p (h d) -> p h d", h=8)
            act(sqs[:], sq3, AF.Square, [B_b7], [B_sqs])
            yield
            red("dve", SS[0][0][:], sqs[:], [B_sqs], [SS[0][1]])
            yield
            ts("pool", SS[1][0][:], SS[0][0][:], 1.0 / 64, ALU.mult, [SS[0][1]], [SS[1][1]], s2=EPS, op1=ALU.add)
            yield
            tt("pool", SS[2][0][:], SS[1][0][:], mhalf[:, 0:8], ALU.pow, [SS[1][1], B_C], [SS[2][1]])
            yield
            tt("dve", sqs[:], sq3, SS[2][0][:].unsqueeze(2).to_broadcast([128, 8, 64]), ALU.mult, [B_b7, SS[2][1], B_sqs], [B_sqs])
            yield
            tt("dve", sqb[:], sqs[:], gsqk[:].unsqueeze(1).to_broadcast([128, 8, 64]), ALU.mult, [B_sqs, B_W], [B_sqb])
            yield
            yield ACQ(L0)
            for h in range(8):
                tr(ph[0:64, h * 128:(h + 1) * 128], sqb[:, h, :], [B_sqb], [B_b0])
            yield
            cp("act", sqT[0:64], ph[0:64, :].rearrange("p (c t) -> p c t", c=8), [B_b0], [B_sqT])
            yield REL(L0)
            yield ("wait", ("swakv", i))
            ns = 0
            for g in range(2):
                yield ACQ(L5)
                Ops = bk5
                for kb, kk in enumerate(("P", "O")):
                    skTk, BskT = skT[(kk, par)]
                    svk, Bsv = sv[(kk, par)]
                    for e4 in range(4):
                        h = 4 * g + e4
                        mm(bk7[:, e4 * 128:(e4 + 1) * 128], skTk[0:64, g, :], sqT[0:64, h, :], True, True,
                           [BskT, B_sqT], [B_b7])
                    yield
                    act(Ebuf[:], bk7[:], AF.Exp, [B_b7], [B_E], scale=0.125)
                    yield
                    pm, BP = Pm[ns % 2]
                    tt("dve", pm[:], Ebuf[:], swaM[:, kb, 4 * g:4 * g + 4, :].rearrange("p h q -> p (h q)"), ALU.mult,
                       [B_E, B_C], [BP])
                    yield
                    mm(Ops[:], svk[:, g, :], pm[:], kb == 0, kb == 1, [Bsv, BP], [B_b5])
                    yield
                    ns += 1
                tt("dve", rden[64:128, :].rearrange("p (h q) -> p h q", h=4),
                   Ops[64:128, :].rearrange("p (h q) -> p h q", h=4),
                   esink[64:128, 4 * g:4 * g + 4].unsqueeze(2).to_broadcast([64, 4, 128]), ALU.add, [B_b5, B_W], [B_rden])
                yield
                rcp(rden[64:128, :], rden[64:128, :], [B_rden], [B_rden])
                yield
                for e4 in range(4):
                    h = 4 * g + e4
                    hb_ = (h % 2) * 64
                    tt("dve", tsw[hb_:hb_ + 64, h // 2, :], Ops[0:64, e4 * 128:(e4 + 1) * 128],
                       rden[64:128, e4 * 128:(e4 + 1) * 128], ALU.mult, [B_b5, B_rden], [B_tsw])
                    yield
                yield REL(L5)

          def silu_from(G, out_ap, B_out):
            act(sge[:], G[:], AF.Tanh, [B_b6], [B_sge], scale=0.5)
            yield
            stt("dve", out_ap, sge[:], 1.0, G[:], ALU.add, ALU.mult, [B_b6, B_sge], [B_out])
            yield

          def chain_gc(i):
            par = i % 2
            hTo, BhTo = hT[("O", par)]

            def fm_mm(ps_ap, col0, lo, hi, Bps):
                for kc in range(8):
                    mm(ps_ap, w_in[:, kc, col0:col0 + 128], hTo[:, kc, lo:hi], kc == 0, kc == 7, [BhTo, B_W], [Bps])

            G = bk6
            for c in range(4):
                fm_mm(G[:, c * 128:(c + 1) * 128], C_GC + c * 128, 2, 130, B_b6)
                yield
            yield from silu_from(G, sgc[:].rearrange("p c t -> p (c t)"), B_sgc)
            yield ("set", ("sgc", i))
            for c in range(4):
                fm_mm(G[:, c * 128:(c + 1) * 128], C_GM + c * 128, 2, 130, B_b6)
                yield
            yield from silu_from(G, sgm[:].rearrange("p c t -> p (c t)"), B_sgm)
            P.dma(sg_scr.rearrange("c p t -> p c t")[:, :, i * 128:(i + 1) * 128], sgm[:], reads=[B_sgm], slot="sgm")
            for c in range(4):
                fm_mm(G[:, c * 128:(c + 1) * 128], C_GS + c * 128, 2, 130, B_b6)
                yield
            yield from silu_from(G, sgs[:].rearrange("p c t -> p (c t)"), B_sgs)

          def chain_conv(i):
            par = i % 2
            hTo, BhTo = hT[("O", par)]

            def fm_mm(ps_ap, col0, lo, hi, Bps):
                for kc in range(8):
                    mm(ps_ap, w_in[:, kc, col0:col0 + 128], hTo[:, kc, lo:hi], kc == 0, kc == 7, [BhTo, B_W], [Bps])

            for cc in range(4):
                yield ACQ(L1)
                X = bk1
                fm_mm(X[:, 0:130], C_CH + cc * 128, 0, 130, B_b1)
                yield
                fm_mm(X[:, 130:260], C_CC + cc * 128, 0, 130, B_b1)
                yield
                fm_mm(X[:, 260:388], C_CB + cc * 128, 2, 130, B_b1)
                yield
                cp("act", ch_sb[:], X[:, 0:130], [B_b1], [B_ch])
                yield
                tt("dve", u_sb[:], ch_sb[:], X[:, 130:260], ALU.mult, [B_ch, B_b1], [B_u])
                yield
                if cc == 0:
                    yield ("wait", ("sgc", i))
                tt("dve", tcb[:], X[:, 260:388], sgc[:, cc, :], ALU.mult, [B_b1, B_sgc], [B_tcb])
                yield REL(L1)
                ts("pool", cy[0][0][:], u_sb[:, 2:130], convw[:, 2, cc:cc + 1], ALU.mult, [B_u, B_W], [cy[0][1]])
                yield
                stt("dve", cy[1][0][:], u_sb[:, 1:129], convw[:, 1, cc:cc + 1], cy[0][0][:], ALU.mult, ALU.add,
                    [B_u, B_W, cy[0][1]], [cy[1][1]])
                yield
                stt("dve", cy[0][0][:], u_sb[:, 0:128], convw[:, 0, cc:cc + 1], cy[1][0][:], ALU.mult, ALU.add,
                    [B_u, B_W, cy[1][1]], [cy[0][1]])
                yield
                tt("pool", yT[:, cc, :], cy[0][0][:], tcb[:], ALU.mult, [cy[0][1], B_tcb], [B_yT])
                yield

          def tail(i):
            xs, Bx = x_O[i % 2]
            tt("dve", yT[:, 4:8, :], tsw[:], sgs[:], ALU.mult, [B_tsw, B_sgs], [B_yT])
            for hf in range(2):
                for kc in range(8):
                    mm(big[:, hf * 512:(hf + 1) * 512], yT[:, kc, :], w_out[:, 4 + kc, hf * 512:(hf + 1) * 512],
                       kc == 0, kc == 7, [B_yT, B_W], [B_big])
            tt("dve", osb[:], big[:], xs[:], ALU.add, [B_big, Bx], [B_osb])
            P.dma(part_scr[i], osb[:], reads=[B_osb], slot="part")

          nb_run = NB if not (dbg is not None and dbg.startswith("b")) else int(dbg[1:])
          load_xP(0)
          load_rest(0)
          if nb_run > 1:
              load_xP(1)
          drive(chain_next(0))
          for i in range(nb_run):
              chains = [chain_ok(i), chain_ok2(i), chain_q(i), chain_swa(i), chain_gc(i), chain_conv(i)]
              if i + 1 < nb_run:
                  load_rest(i + 1)
                  if i + 2 < nb_run:
                      load_xP(i + 2)
                  chains = chain_next(i + 1) + chains
              drive(chains)
              assert all(v is None for v in locks.values()), locks
              tail(i)

          if dbg == "p1":
              raise _Stop()
          P.barrier()
          ar.off = 0
          kTp = A([128, 2, NS * 128], BF16)
          vp = A([128, NS, 2, 128], BF16)
          qTp2 = [A([128, 2, NB * 128], BF16) for _ in range(2)]
          maskM = A([128, 8, 512], BF16); B_mask = P.buf("mask")
          PT = [A([128, 512], BF16) for _ in range(4)]; B_PT = [P.buf() for _ in range(4)]
          gbuf = [A([128, 512], BF16) for _ in range(2)]; B_g = [P.buf(), P.buf()]
          ymla = A([128, 4, NB * 128], BF16); B_ym = [P.buf(f"ym{j}") for j in range(8)]
          rden2 = A([128, 512], F32); B_rd2 = P.buf()
          tn = A([128, 512], F32); B_tn = P.buf()
          ptile = [A([128, D], F32) for _ in range(2)]; B_pt = [P.buf(), P.buf()]
          osb2 = [A([128, D], F32) for _ in range(2)]; B_o2 = [P.buf(), P.buf()]
          B_kc = [P.buf(f"kTc{c}") for c in range(8)]
          B_vc = [P.buf(f"vc{c}") for c in range(8)]
          B_q2 = [P.buf("qTp0"), P.buf("qTp1")]
          P.dma(maskM[:], maskM_d, writes=[B_mask], slot="c0")
          SC = 96 ** -0.5
          Sb = [bk0, bk1, bk2, bk7]
          Ob = [bk5, bk6]
          LA = 3
          PT6 = PT + [A([128, 512], BF16) for _ in range(2)]
          B_PT6 = B_PT + [P.buf(), P.buf()]
          nO = 0
          nS = 0
          def load_kv_chunk(hp_, c):
              for hh_ in range(2):
                  P.dma(kTp[0:96, hh_, c * 1024:(c + 1) * 1024], kT_scr[2 * hp_ + hh_, :, c * 1024:(c + 1) * 1024],
                        writes=[B_kc[c]], slot=f"kl{hh_}")
              P.dma(vp[:, 8 * c:8 * c + 8, :, :], v_scr[8 * c:8 * c + 8, :, 2 * hp_:2 * hp_ + 2, :].rearrange("s p h d -> p s h d"),
                    writes=[B_vc[c]], slot="vl")

          def load_q(hp_):
              for hh_ in range(2):
                  P.dma(qTp2[hp_ % 2][0:96, hh_, :], qT_scr[2 * hp_ + hh_], writes=[B_q2[hp_ % 2]], slot=f"ql{hh_}")

          load_q(0)
          for c in range(8):
              load_kv_chunk(0, c)
          for hp in range(4):
              qTp = qTp2[hp % 2]; B_q = B_q2[hp % 2]
              if hp + 1 < 4:
                  load_q(hp + 1)
              units = []
              for j in range(8):
                  for hh in range(2):
                      nsl = 8 * j + 8
                      for sl in range(nsl):
                          units.append((j, hh, sl, nsl))
              pend = {}
              tile_bank = {}
              for t in range(len(units) + LA):
                  if t < len(units):
                      j, hh, sl, nsl = units[t]
                      if sl == 0 and hh == 0:
                          gb = gbuf[j % 2]; Bg = B_g[j % 2]
                          P.dma(gb[:], sg_scr[hp, :, j * 512:(j + 1) * 512], writes=[Bg], slot=f"gl{j % 2}")
                      Sps = Sb[nS % 4]; BS = B_bk[id(Sps)]
                      nS += 1
                      mm(Sps[:], kTp[0:96, hh, sl * 128:(sl + 1) * 128], qTp[0:96, hh, j * 512:(j + 1) * 512],
                         True, True, [B_kc[sl // 8], B_q], [BS])
                      pend[t] = (Sps, BS)
                  if t >= LA:
                      u = t - LA
                      j, hh, sl, nsl = units[u]
                      if sl == 0:
                          tile_bank[(j, hh)] = Ob[nO % 2]
                          nO += 1
                      Ops = tile_bank[(j, hh)]; BO = B_bk[id(Ops)]
                      Sps, BS = pend.pop(u)
                      pt = PT6[u % 6]; BP = B_PT6[u % 6]
                      act(pt[:], Sps[:], AF.Exp, [BS], [BP], scale=SC)
                      if sl >= 8 * j:
                          tt("dve", pt[:], pt[:], maskM[:, sl - 8 * j, :], ALU.mult, [BP, B_mask], [BP])
                      mm(Ops[:], vp[:, sl, hh, :], pt[:], sl == 0, sl == nsl - 1, [B_vc[sl // 8], BP], [BO])
                      if hp + 1 < 4 and j == 7 and hh == 1 and sl % 8 == 7:
                          load_kv_chunk(hp + 1, sl // 8)
                      if sl == nsl - 1:
                          gb = gbuf[j % 2]; Bg = B_g[j % 2]
                          hb = hh * 64
                          rcp(rden2[64:128, :], Ops[64:128, :], [BO], [B_rd2])
                          tt("dve", tn[hb:hb + 64, :], Ops[0:64, :], rden2[64:128, :], ALU.mult, [BO, B_rd2], [B_tn])
                          tt("pool", ymla[hb:hb + 64, hp, j * 512:(j + 1) * 512], tn[hb:hb + 64, :], gb[hb:hb + 64, :], ALU.mult,
                             [B_tn, Bg], [B_ym[j]])

          if dbg == "p2":
              raise _Stop()
          def load_pt(i):
              P.dma(ptile[i % 2][:], part_scr[i], writes=[B_pt[i % 2]], slot=f"pl{i % 2}")

          load_pt(0)
          for i in range(NB):
              if i + 1 < NB:
                  load_pt(i + 1)
              for hf in range(2):
                  for c in range(4):
                      mm(big[:, hf * 512:(hf + 1) * 512], ymla[:, c, i * 128:(i + 1) * 128],
                         w_out[:, c, hf * 512:(hf + 1) * 512], c == 0, c == 3, [B_ym[i // 4], B_W], [B_big])
              o2 = osb2[i % 2]; Bo2 = B_o2[i % 2]
              tt("dve", o2[:], big[:], ptile[i % 2][:], ALU.add, [B_big, B_pt[i % 2]], [Bo2])
              P.dma(ps["out"](i), o2[:], reads=[Bo2], slot=f"yo{i % 2}")
      except _Stop:
        pass
    print("arena p1 end", p1_end, "total ops recorded", P.total, {e: len(v) for e, v in P.ops.items()})
    P.emit(final_slots=[s_ for s_ in P.slot_counts])
    P.close()
    return nc


def _consts(r):
    half = 16
    inv_freq = np.power(np.float32(10000.0), -np.arange(half, dtype=np.float32) / half).astype(np.float32)
    pidx = np.arange(128, dtype=np.float32)
    cosT = np.zeros((128, NS, 32), np.float32)
    sinT = np.zeros((128, NS, 32), np.float32)
    for s in range(NS):
        gb = s - 1 + r
        pos = (gb * 128 + pidx).astype(np.float32)
        ang = pos[:, None] * inv_freq[None, :]
        c = np.cos(ang).astype(np.float32)
        sn = np.sin(ang).astype(np.float32)
        cosT[:, s, :16] = c
        cosT[:, s, 16:] = c
        sinT[:, s, :16] = -sn
        sinT[:, s, 16:] = sn
    maskM = np.zeros((128, 8, 512), np.float32)
    ki = np.arange(128)[:, None]
    qi = np.arange(128)[None, :]
    tri = (ki <= qi).astype(np.float32)
    for so in range(8):
        for qb in range(4):
            d = 2 * qb + 1
            if so < d:
                maskM[:, so, qb * 128:(qb + 1) * 128] = 1.0
            elif so == d:
                maskM[:, so, qb * 128:(qb + 1) * 128] = tri
    slopes = np.exp2(-8.0 * np.arange(1, 9, dtype=np.float32) / 8).astype(np.float32)
    swaM = np.zeros((128, 2, 8, 128), np.float32)
    for h in range(8):
        d0 = (128 + qi - ki).astype(np.float32)
        swaM[:, 0, h, :] = np.where(d0 < 128, np.exp(-slopes[h] * d0), 0.0)
        d1 = (qi - ki).astype(np.float32)
        swaM[:, 1, h, :] = np.where(d1 >= 0, np.exp(-slopes[h] * np.maximum(d1, 0)), 0.0)
    vvalid = np.ones((128, NS), np.float32)
    if r == 0:
        vvalid[:, 0] = 0.0
    return {"cosT": cosT, "sinT": sinT, "maskM": maskM.astype(ml_dtypes.bfloat16), "swaM": swaM, "vvalid": vvalid}


def _layer_weights(inp, l, suffix):
    f = lambda a: np.ascontiguousarray(a, dtype=np.float32)
    return {
        f"w_in_{suffix}": f(inp["w_in"][l][:, PERM]), f"norm_g_{suffix}": f(inp["norm_g"][l].reshape(8, 128).T),
        f"w_kvb_{suffix}": f(inp["mla_w_kvb"][l]), f"kva_g_{suffix}": f(inp["mla_kv_a_norm"][l].reshape(128, 1)),
        f"w_qb_{suffix}": f(inp["mla_w_qb"][l]), f"qa_g_{suffix}": f(inp["mla_q_a_norm"][l].reshape(2, 128).T),
        f"q_g_{suffix}": f(inp["mla_q_norm"][l]), f"k_g_{suffix}": f(inp["mla_k_norm"][l]),
        f"conv_w_{suffix}": f(inp["conv_w"][l].reshape(3, 4, 128).transpose(2, 0, 1)), f"sq_g_{suffix}": f(inp["swa_q_norm"][l]),
        f"sk_g_{suffix}": f(inp["swa_k_norm"][l]), f"sinks_{suffix}": f(inp["swa_sinks"][l]),
        f"w_out_{suffix}": f(inp["w_out"][l]),
    }


def _shard_x(x):
    xb = x.reshape(4, 64, 128, D)
    outs = []
    for c in range(8):
        b, r = c // 2, c % 2
        own = xb[b, r::2]
        prev = np.zeros_like(own)
        if r == 0:
            prev[1:] = xb[b, 1:63:2]
        else:
            prev[:] = xb[b, 0::2]
        outs.append((np.ascontiguousarray(own), np.ascontiguousarray(prev)))
    return outs


def _gather(res):
    out = np.zeros((4, 64, 128, D), np.float32)
    for c in range(8):
        b, r = c // 2, c % 2
        out[b, r::2] = res[c]["y_out"]
    return out.reshape(4, 8192, D)


_CACHE = {}
FUSED = True


def kernel(**inp):
    inp = {k: np.asarray(v) for k, v in inp.items()}
    x = np.ascontiguousarray(inp["x"], dtype=np.float32)
    depth = inp["w_in"].shape[0]
    consts = [_consts(r) for r in range(2)]
    if FUSED and depth == 2:
        if "nc2" not in _CACHE:
            _CACHE["nc2"] = build_program(2)
        nc = _CACHE["nc2"]
        w = {}
        for l in range(2):
            w.update(_layer_weights(inp, l, str(l)))
        sh = _shard_x(x)
        in_maps = []
        for c in range(8):
            r = c % 2
            o = c + 1 - 2 * r
            m = {"x_own": sh[c][0], "x_prev": sh[c][1], "x_own2": sh[o][0], "x_prev2": sh[o][1]}
            m.update(w)
            m.update(consts[r])
            co = consts[1 - r]
            m["cosT2"] = co["cosT"]; m["sinT2"] = co["sinT"]; m["vvalid2"] = co["vvalid"]
            bw = np.zeros((128, 2), np.float32)
            bw[:, r] = 1.0
            m["blendw"] = bw
            in_maps.append(m)
        res = run_bass_kernel_spmd(nc, in_maps, core_ids=list(range(8)))
        return _gather(res.results).astype(np.float32)
    if "nc1" not in _CACHE:
        _CACHE["nc1"] = build_program(1)
    for l in range(depth):
        nc = _CACHE["nc1"]
        w = _layer_weights(inp, l, "0")
        sh = _shard_x(x)
        in_maps = []
        for c in range(8):
            m = {"x_own": sh[c][0], "x_prev": sh[c][1]}
            m.update(w)
            m.update(consts[c % 2])
            in_maps.append(m)
        res = run_bass_kernel_spmd(nc, in_maps, core_ids=list(range(8)))
        x = _gather(res.results)
    return x.astype(np.float32)
```
